# Optimizing a Trainium2 kernel written in Bass

```python
import jax, jax.numpy as jnp
from jax import lax
import numpy as np

D_MODEL = 2048
BATCH = 4
SEQ = 4096
DEPTH = 1

D_MIX = D_MODEL
D_ATTN = D_MIX // 2
D_POOL = D_MIX - D_ATTN
HEAD_DIM = 128
N_HEADS = D_ATTN // HEAD_DIM
ROPE_DIM = HEAD_DIM // 4
ROPE_THETA = 500000.0
DILATED_PATTERNS = ((128, 1), (512, 4), (2048, 16))
POOL_WINDOWS = (2, 4, 8, 16)
N_POOL_GROUPS = len(POOL_WINDOWS)
POOL_GROUP_DIM = D_POOL // N_POOL_GROUPS
D_IN = 3 * D_ATTN + D_POOL + D_MIX
LN_EPS = 1e-5
DEEPNORM_ALPHA = (2.0 * DEPTH) ** 0.25
DEEPNORM_BETA = (8.0 * DEPTH) ** -0.25

kernel_name = "hybrid_dilated_attn_pool_deepnorm"


def rotary_partial(t, positions):
    half = ROPE_DIM // 2
    inv_freq = ROPE_THETA ** (-(2.0 * jnp.arange(half, dtype=jnp.float32)) / ROPE_DIM)
    ang = positions.astype(jnp.float32)[:, None] * inv_freq[None, :]
    cos = jnp.cos(ang)[None, :, None, :]
    sin = jnp.sin(ang)[None, :, None, :]
    t32 = t.astype(jnp.float32)
    t1, t2, rest = t32[..., :half], t32[..., half:ROPE_DIM], t32[..., ROPE_DIM:]
    out = jnp.concatenate([t1 * cos - t2 * sin, t1 * sin + t2 * cos, rest], axis=-1)
    return out.astype(t.dtype)


def banded_causal_attention(q, k, v, n_keys):
    G, N, Dh = q.shape
    W = n_keys
    nb = -(-N // W)
    Np = nb * W
    pad = Np - N
    qp = jnp.pad(q, ((0, 0), (0, pad), (0, 0)))
    kp = jnp.pad(k, ((0, 0), (W, pad), (0, 0)))
    vp = jnp.pad(v, ((0, 0), (W, pad), (0, 0)))
    qb = qp.reshape(G, nb, W, Dh)
    kb = jnp.concatenate([kp[:, :Np].reshape(G, nb, W, Dh), kp[:, W:].reshape(G, nb, W, Dh)], axis=2)
    vb = jnp.concatenate([vp[:, :Np].reshape(G, nb, W, Dh), vp[:, W:].reshape(G, nb, W, Dh)], axis=2)
    s = jnp.einsum('gcqd,gckd->gcqk', qb, kb).astype(jnp.float32) * (Dh ** -0.5)
    qi = jnp.arange(W)[:, None]
    kj = jnp.arange(2 * W)[None, :]
    dist = W + qi - kj
    key_pos = (jnp.arange(nb)[:, None, None] - 1) * W + kj[None]
    mask = (dist >= 0)[None] & (dist <= W)[None] & (key_pos >= 0)
    s = jnp.where(mask[None], s, -jnp.inf)
    m = jnp.max(s, axis=-1, keepdims=True)
    p = jnp.exp(s - m)
    den = jnp.sum(p, axis=-1, keepdims=True)
    o = jnp.einsum('gcqk,gckd->gcqd', p, vb.astype(jnp.float32)) / den
    lse = (m + jnp.log(den))[..., 0]
    return o.reshape(G, Np, Dh)[:, :N], lse.reshape(G, Np)[:, :N]


def dilated_attention(q, k, v, window, dilation):
    B, H, S, Dh = q.shape
    n_sub = S // dilation

    def to_residues(t):
        return t.reshape(B, H, n_sub, dilation, Dh).transpose(0, 1, 3, 2, 4).reshape(B * H * dilation, n_sub, Dh)

    o, lse = banded_causal_attention(to_residues(q), to_residues(k), to_residues(v), window // dilation)
    o = o.reshape(B, H, dilation, n_sub, Dh).transpose(0, 1, 3, 2, 4).reshape(B, H, S, Dh)
    lse = lse.reshape(B, H, dilation, n_sub).transpose(0, 1, 3, 2).reshape(B, H, S)
    return o, lse


def dilated_attention_mixture(q, k, v):
    outs, lses = [], []
    for window, dilation in DILATED_PATTERNS:
        o, lse = dilated_attention(q, k, v, window, dilation)
        outs.append(o)
        lses.append(lse)
    w = jax.nn.softmax(jnp.stack(lses, axis=0), axis=0)
    return jnp.einsum('pbhs,pbhsd->bhsd', w, jnp.stack(outs, axis=0))


def causal_pool_minus_identity(u, window):
    S = u.shape[1]
    u32 = u.astype(jnp.float32)
    c = jnp.cumsum(u32, axis=1)
    c_shift = jnp.pad(c, ((0, 0), (window, 0), (0, 0)))[:, :S]
    count = jnp.minimum(jnp.arange(S) + 1, window).astype(jnp.float32)
    return (c - c_shift) / count[None, :, None] - u32


def layer_norm(h, gain, bias):
    h32 = h.astype(jnp.float32)
    mu = jnp.mean(h32, axis=-1, keepdims=True)
    var = jnp.mean(jnp.square(h32 - mu), axis=-1, keepdims=True)
    return (h32 - mu) * lax.rsqrt(var + LN_EPS) * gain.astype(jnp.float32) + bias.astype(jnp.float32)


def setup_inputs(seed: int = 0) -> dict:
    key = jax.random.key(seed)
    ks = jax.random.split(key, 8)
    x = jax.random.normal(ks[0], (BATCH, SEQ, D_MODEL), jnp.float32)
    w_in = jax.random.normal(ks[1], (DEPTH, D_MODEL, D_IN), jnp.float32) * D_MODEL ** -0.5
    w_pool = jax.random.normal(ks[2], (DEPTH, N_POOL_GROUPS, POOL_GROUP_DIM, POOL_GROUP_DIM), jnp.float32) * POOL_GROUP_DIM ** -0.5
    pool_scale = 1.0 + 0.02 * jax.random.normal(ks[3], (DEPTH, D_POOL), jnp.float32)
    w_out = jax.random.normal(ks[4], (DEPTH, D_MIX, D_MODEL), jnp.float32) * (D_MIX ** -0.5) * DEEPNORM_BETA
    ln_gain = 1.0 + 0.02 * jax.random.normal(ks[5], (DEPTH, D_MODEL), jnp.float32)
    ln_bias = 0.02 * jax.random.normal(ks[6], (DEPTH, D_MODEL), jnp.float32)
    return {"x": x, "w_in": w_in, "w_pool": w_pool, "pool_scale": pool_scale,
            "w_out": w_out, "ln_gain": ln_gain, "ln_bias": ln_bias}


def reference(x, w_in, w_pool, pool_scale, w_out, ln_gain, ln_bias):
    B, S, _ = x.shape
    positions = jnp.arange(S, dtype=jnp.int32)
    for layer in range(DEPTH):
        h = jnp.einsum('bsd,de->bse', x, w_in[layer])
        q, k, v, u_pool, gate = jnp.split(
            h, [D_ATTN, 2 * D_ATTN, 3 * D_ATTN, 3 * D_ATTN + D_POOL], axis=-1)

        q = rotary_partial(q.reshape(B, S, N_HEADS, HEAD_DIM), positions).transpose(0, 2, 1, 3)
        k = rotary_partial(k.reshape(B, S, N_HEADS, HEAD_DIM), positions).transpose(0, 2, 1, 3)
        v = v.reshape(B, S, N_HEADS, HEAD_DIM).transpose(0, 2, 1, 3)
        attn = dilated_attention_mixture(q, k, v)
        attn = attn.transpose(0, 2, 1, 3).reshape(B, S, D_ATTN)

        u_groups = u_pool.reshape(B, S, N_POOL_GROUPS, POOL_GROUP_DIM)
        pooled = jnp.stack([causal_pool_minus_identity(u_groups[:, :, g], POOL_WINDOWS[g])
                            for g in range(N_POOL_GROUPS)], axis=2)
        pool_out = jnp.einsum('bsgc,gcd->bsgd', pooled, w_pool[layer].astype(jnp.float32))
        pool_out = pool_out.reshape(B, S, D_POOL) * pool_scale[layer].astype(jnp.float32)

        y = jnp.concatenate([attn, pool_out], axis=-1) * jax.nn.silu(gate.astype(jnp.float32))
        out = jnp.einsum('bse,ed->bsd', y.astype(x.dtype), w_out[layer])

        x = layer_norm(DEEPNORM_ALPHA * x.astype(jnp.float32) + out.astype(jnp.float32),
                       ln_gain[layer], ln_bias[layer]).astype(x.dtype)
    return x
```

```python
import numpy as np
import concourse.bass as bass
import concourse.mybir as mybir
from concourse.bass_utils import run_bass_kernel_spmd

F32 = mybir.dt.float32
BF16 = mybir.dt.bfloat16
AF = mybir.ActivationFunctionType
ALU = mybir.AluOpType
AX = mybir.AxisListType

D = 2048
T = 2048
LT = 4096
KC = 16
NEG = -30000.0
ALPHA = 2.0 ** 0.25
EPS = 1e-5
SCALE = 128.0 ** -0.5
POOL_W = (2, 4, 8, 16)

COMPUTE = ("pe", "act", "dve", "pool")


class Op:
    __slots__ = ("eng", "fn", "deps", "is_dma", "sem", "val", "need_sig", "seq")


class Sched:
    def __init__(self, prog, counters, dma_sems, dma_cnt):
        self.prog = prog
        self.counters = counters
        self.dma_sems = dma_sems
        self.dma_cnt = dma_cnt
        self.ops = []
        self.lw = {}
        self.rd = {}

    def add(self, eng, fn, reads=(), writes=(), dma=None, no_waw=False):
        idx = len(self.ops)
        deps = {}
        for r in reads:
            w = self.lw.get(r)
            if w is not None:
                deps[w] = "raw"
        for w_ in writes:
            for ridx in self.rd.get(w_, {}).values():
                deps.setdefault(ridx, "war")
            lw = self.lw.get(w_)
            if lw is not None and not no_waw:
                deps.setdefault(lw, "waw")
        op = Op()
        op.eng = eng
        op.fn = fn
        op.deps = deps
        op.is_dma = dma is not None
        op.sem = dma
        op.val = 0
        op.need_sig = False
        op.seq = 0
        if dma is not None:
            self.dma_cnt[dma] += 16
            op.val = self.dma_cnt[dma]
        key = ("dma", dma) if dma is not None else eng
        for r in reads:
            self.rd.setdefault(r, {})[key] = idx
        for w_ in writes:
            self.lw[w_] = idx
            self.rd[w_] = {}
        self.ops.append(op)
        return idx

    def _needs_wait(self, op, dop, kind):
        if dop.is_dma:
            return True
        if dop.eng != op.eng:
            return True
        if op.is_dma:
            return True
        if kind in ("raw", "waw") and op.eng in ("act", "dve", "pool"):
            return True
        return False

    def finalize(self):
        for op in self.ops:
            for d, kind in op.deps.items():
                dop = self.ops[d]
                if not dop.is_dma and self._needs_wait(op, dop, kind):
                    dop.need_sig = True
        for op in self.ops:
            if op.need_sig:
                self.counters[op.eng] += 1
                op.seq = self.counters[op.eng]

    def emit(self, eng, eh):
        waited = {}
        for op in self.ops:
            if op.eng != eng:
                continue
            waits = {}
            for d, kind in op.deps.items():
                dop = self.ops[d]
                if not self._needs_wait(op, dop, kind):
                    continue
                if dop.is_dma:
                    k, v = ("d", dop.sem), dop.val
                else:
                    k, v = ("p", dop.eng), dop.seq
                if waits.get(k, 0) < v:
                    waits[k] = v
            for k, v in waits.items():
                if waited.get(k, 0) >= v:
                    continue
                waited[k] = v
                sem = self.dma_sems[k[1]] if k[0] == "d" else self.prog[k[1]]
                eh.wait_ge(sem, v)
            if op.fn is None:
                continue
            inst = op.fn(eh)
            if op.is_dma:
                inst.then_inc(self.dma_sems[op.sem], 16)
            elif op.need_sig:
                inst.then_inc(self.prog[eng], 1)


def run_block(nc, sch):
    sch.finalize()
    with nc.Block() as block:
        @block.tensor
        def _(e):
            sch.emit("pe", e)

        @block.scalar
        def _(e):
            sch.emit("act", e)

        @block.vector
        def _(e):
            sch.emit("dve", e)

        @block.gpsimd
        def _(e):
            sch.emit("pool", e)

        @block.sync
        def _(e):
            sch.emit("sp", e)


DMA_SEM_NAMES = ["const", "constg", "wp", "xc0", "xc1", "xh", "xr0", "xr1", "out0", "out1", "out2"]


def build_program():
    nc = bass.Bass("TRN2", target_bir_lowering=False)
    dt = lambda name, shape, kind="ExternalInput": nc.dram_tensor(name, shape, F32, kind=kind).ap()
    xT_all = dt("xT_all", [D, LT])
    x_own = dt("x_own", [T, D])
    w_perm = dt("w_perm", [D, 6144])
    w_pool = dt("w_pool", [4, 256, 256])
    pscale = dt("pscale", [128, 8])
    w_out = dt("w_out", [D, D])
    ln_g = dt("ln_g", [D])
    ln_b = dt("ln_b", [D])
    cs_d = dt("cs", [128, 32 * 64])
    mask_d = dt("masks", [128, 3 * 128])
    ident_d = dt("ident", [128, 128])
    bm_d = dt("bmats", [128, 3 * 4 * 128])
    out_d = dt("out", [T, D], kind="ExternalOutput")

    import contextlib
    es = contextlib.ExitStack()
    with es:
        prog = {e: es.enter_context(nc.semaphore("prog_" + e)) for e in COMPUTE}
        dma_sems = {n: es.enter_context(nc.semaphore("dma_" + n)) for n in DMA_SEM_NAMES}
        counters = {e: 0 for e in COMPUTE}
        dma_cnt = {n: 0 for n in DMA_SEM_NAMES}
        new_sched = lambda: Sched(prog, counters, dma_sems, dma_cnt)

        import os
        PH = os.environ.get("KPHASES", "abc")
        yT_attn = es.enter_context(nc.sbuf_tensor("yT_attn", [128, 8, T], BF16))
        if "a" in PH:
            phase_a(nc, new_sched(), xT_all, w_perm, cs_d, mask_d, ident_d, yT_attn)
        yT_pool = es.enter_context(nc.sbuf_tensor("yT_pool", [128, 8, T], BF16))
        if "b" in PH:
            phase_b(nc, new_sched(), xT_all, w_perm, w_pool, pscale, bm_d, yT_pool)
        if "c" in PH:
            phase_c(nc, new_sched(), x_own, w_out, ln_g, ln_b, out_d, yT_attn, yT_pool)
    return nc


def xchunk_src(xT_all, lt0, n):
    return xT_all.rearrange("(kc p) t -> p kc t", p=128)[:, :, lt0:lt0 + n]


def phase_a(nc, sch, xT_all, w_perm, cs_d, mask_d, ident_d, yT_attn):
    import contextlib
    with contextlib.ExitStack() as es:
        sb = lambda name, shape, dtype: es.enter_context(nc.sbuf_tensor(name, shape, dtype))
        xc = [sb(f"a_xc{i}", [128, KC, 512], BF16) for i in range(2)]
        wp = sb("a_wp", [128, KC, 1024], BF16)
        QT = sb("a_QT", [128, 2, T], BF16)
        KT = sb("a_KT", [128, 2, LT], BF16)
        VT = sb("a_VT", [128, 2, LT], BF16)
        Gs = sb("a_Gs", [128, 2, T], BF16)
        Vblk = sb("a_Vblk", [128, 69, 128], BF16)
        PT = [sb(f"a_PT{i}", [128, 512], BF16) for i in range(6)]
        PT3 = sb("a_PTh3", [128, 2, T], BF16)
        qktm = [sb(f"a_qktm{i}", [128, 4, 128], BF16) for i in range(2)]
        rt1 = [sb(f"a_rt1_{i}", [128, 4, 32], F32) for i in range(2)]
        rt2 = [sb(f"a_rt2_{i}", [128, 4, 32], F32) for i in range(2)]
        cs = sb("a_cs", [128, 32, 64], F32)
        masks = sb("a_masks", [128, 3, 128], BF16)
        ident = sb("a_ident", [128, 128], BF16)
        ones = sb("a_ones", [128, 128], BF16)
        zeros = sb("a_zeros", [128, 128], BF16)
        m1g0 = sb("a_m1g0", [128, 4, 128], BF16)
        rden = [sb(f"a_rden{i}", [128, 512], F32) for i in range(2)]
        fin1 = [sb(f"a_fin{i}", [128, 512], F32) for i in range(2)]
        bank = [es.enter_context(nc.psum_tensor(f"a_b{i}", [128, 512], F32)) for i in range(6)]
        tpb = [es.enter_context(nc.psum_tensor(f"a_tp{i}", [128, 1024], BF16)) for i in range(2)]

        MCUR, MPREV, MPREVH = 0, 1, 2

        sch.add("sp", lambda e: e.dma_start(out=cs[:].rearrange("p t c -> p (t c)"), in_=cs_d),
                writes=["cs"], dma="const")
        sch.add("pool", lambda e: e.dma_start(out=masks[:].rearrange("p m q -> p (m q)"), in_=mask_d),
                writes=["masks"], dma="constg")
        sch.add("pool", lambda e: e.dma_start(out=ident[:], in_=ident_d),
                writes=["ident"], dma="constg", no_waw=True)
        sch.add("dve", lambda e: e.memset(ones[:], 1.0), writes=["ones"])
        sch.add("dve", lambda e: e.memset(zeros[:], 0.0), writes=["zeros"])
        CONST = ["cs", "masks", "ident"]
        sch.add("dve", lambda e: e.tensor_copy(out=m1g0[:, 0, :], in_=masks[:, 2, :]), reads=CONST, writes=["m1g0a"])
        sch.add("dve", lambda e: e.tensor_copy(out=m1g0[:, 1:4, :],
                                               in_=masks[:, 1, :].unsqueeze(1).broadcast_to([128, 3, 128])),
                reads=CONST, writes=["m1g0b"])

        cnt = {"xc": 0, "pqk": 0, "pf": 0, "tp": 0, "qk": 0, "st": 0, "pt": 0, "grp": 0}
        PQK_BANKS = (0, 1, 2)
        PF_BANKS = (3, 4, 5)

        import os
        DBG_PAIRS = int(os.environ.get("KA_PAIRS", "4"))
        DBG_ATTN = int(os.environ.get("KA_ATTN", "9"))
        DBG_INPROJ = int(os.environ.get("KA_INPROJ", "1"))
        for p in range(DBG_PAIRS):
            for piece in range(4):
                def f(e, piece=piece, p=p):
                    src = w_perm.rearrange("(kc p) c -> p kc c", p=128)[:, piece * 4:(piece + 1) * 4,
                                                                        p * 1024:(p + 1) * 1024]
                    return e.dma_start(out=wp[:, piece * 4:(piece + 1) * 4, :], in_=src)
                sch.add("pool", f, writes=["wp"], dma="wp", no_waw=(piece > 0))

            pending = []

            def flush(keep):
                while len(pending) > keep:
                    pending.pop(0)()

            for ci in range(8 if DBG_INPROJ else 0):
                own = ci >= 4
                slot = cnt["xc"] % 2
                cnt["xc"] += 1
                for half in range(2):
                    def f(e, slot=slot, ci=ci, half=half):
                        return e.dma_start(out=xc[slot][:, half * 8:(half + 1) * 8, :],
                                           in_=xchunk_src(xT_all, ci * 512, 512)[:, half * 8:(half + 1) * 8, :])
                    sch.add("pool", f, writes=[("xc", slot)], dma=f"xc{slot}", no_waw=(half > 0))

                nb = 4 if own else 2
                c0 = 0 if own else 256
                for tt in range(4):
                    tl = ci * 4 + tt
                    b = PQK_BANKS[cnt["pqk"] % 3]
                    cnt["pqk"] += 1

                    def f(e, slot=slot, tt=tt, b=b, nb=nb, c0=c0):
                        for kc in range(KC):
                            i = e.matmul(bank[b][:, 0:nb * 128], lhsT=xc[slot][:, kc, tt * 128:(tt + 1) * 128],
                                         rhs=wp[:, kc, c0:c0 + nb * 128], start=(kc == 0), stop=(kc == KC - 1))
                        return i
                    sch.add("pe", f, reads=[("xc", slot), "wp"], writes=[("bank", b)])
                    qs = cnt["qk"] % 2
                    cnt["qk"] += 1
                    P = bank[b][:, 0:nb * 128].rearrange("p (n d) -> p n d", n=nb)
                    cc = cs[:, tl, 0:32].unsqueeze(1).broadcast_to([128, nb, 32])
                    sn = cs[:, tl, 32:48].unsqueeze(1).broadcast_to([128, nb, 16])
                    sp_ = cs[:, tl, 48:64].unsqueeze(1).broadcast_to([128, nb, 16])
                    sch.add("dve", lambda e, P=P, qs=qs, nb=nb, cc=cc: e.tensor_tensor(
                        out=rt1[qs][:, 0:nb, :], in0=P[:, :, 0:32], in1=cc, op=ALU.mult),
                        reads=[("bank", b)] + CONST, writes=[("rt1", qs)])
                    sch.add("dve", lambda e, P=P, qs=qs, nb=nb, sn=sn: e.tensor_tensor(
                        out=rt2[qs][:, 0:nb, 0:16], in0=P[:, :, 16:32], in1=sn, op=ALU.mult),
                        reads=[("bank", b)] + CONST, writes=[("rt2a", qs)])
                    sch.add("dve", lambda e, P=P, qs=qs, nb=nb, sp_=sp_: e.tensor_tensor(
                        out=rt2[qs][:, 0:nb, 16:32], in0=P[:, :, 0:16], in1=sp_, op=ALU.mult),
                        reads=[("bank", b)] + CONST, writes=[("rt2b", qs)])
                    sch.add("act", lambda e, P=P, qs=qs, nb=nb: e.copy(out=qktm[qs][:, 0:nb, 32:128], in_=P[:, :, 32:128]),
                            reads=[("bank", b), ("rt1", qs), ("rt2a", qs), ("rt2b", qs)], writes=[("qk_nr", qs)])
                    sch.add("dve", lambda e, qs=qs, nb=nb: e.tensor_tensor(
                        out=qktm[qs][:, 0:nb, 0:32], in0=rt1[qs][:, 0:nb, :], in1=rt2[qs][:, 0:nb, :], op=ALU.add),
                        reads=[("rt1", qs), ("rt2a", qs), ("rt2b", qs)], writes=[("qk_rot", qs)])

                    def tr(qs=qs, nb=nb, own=own, tl=tl):
                        tb = cnt["tp"] % 2
                        cnt["tp"] += 1

                        def f(e):
                            for blk in range(nb):
                                i = e.transpose(tpb[tb][:, blk * 128:(blk + 1) * 128], qktm[qs][:, blk, :], ident[:])
                            return i
                        sch.add("pe", f, reads=[("qk_nr", qs), ("qk_rot", qs)] + CONST, writes=[("tp", tb)])
                        if own:
                            q0 = (tl - 16) * 128
                            sch.add("act", lambda e: e.copy(out=QT[:, :, q0:q0 + 128],
                                                            in_=tpb[tb][:, 0:256].rearrange("p (n t) -> p n t", n=2)),
                                    reads=[("tp", tb)], writes=["QT"])
                            sch.add("act", lambda e: e.copy(out=KT[:, :, tl * 128:(tl + 1) * 128],
                                                            in_=tpb[tb][:, 256:512].rearrange("p (n t) -> p n t", n=2)),
                                    reads=[("tp", tb)], writes=["KT"])
                        else:
                            sch.add("act", lambda e: e.copy(out=KT[:, :, tl * 128:(tl + 1) * 128],
                                                            in_=tpb[tb][:, 0:256].rearrange("p (n t) -> p n t", n=2)),
                                    reads=[("tp", tb)], writes=["KT"])
                    pending.append(tr)
                    flush(1)

                for cb in range(4 if own else 2):
                    b = PF_BANKS[cnt["pf"] % 3]
                    cnt["pf"] += 1

                    def f(e, slot=slot, cb=cb, b=b):
                        for kc in range(KC):
                            i = e.matmul(bank[b][:, :], lhsT=wp[:, kc, 512 + cb * 128:512 + (cb + 1) * 128],
                                         rhs=xc[slot][:, kc, :], start=(kc == 0), stop=(kc == KC - 1))
                        return i
                    sch.add("pe", f, reads=[("xc", slot), "wp"], writes=[("bank", b)])
                    if cb < 2:
                        sch.add("dve", lambda e, cb=cb, b=b, ci=ci: e.tensor_copy(
                            out=VT[:, cb, ci * 512:(ci + 1) * 512], in_=bank[b][:, :]),
                            reads=[("bank", b)], writes=["VT"])
                    else:
                        sch.add("act", lambda e, cb=cb, b=b, ci=ci: e.activation(
                            out=Gs[:, cb - 2, (ci - 4) * 512:(ci - 3) * 512], in_=bank[b][:, :], func=AF.Silu),
                            reads=[("bank", b)], writes=["Gs"])
                    flush(0)
            flush(0)

            for hh in range(2 if DBG_ATTN else 0):
                h = 2 * p + hh
                vsrc = []
                for nbk in range(15, 32):
                    vsrc.append(VT[:, hh, nbk * 128:(nbk + 1) * 128])
                for r4 in range(4):
                    for cbk in range(3, 8):
                        vsrc.append(VT[:, hh, 512 * cbk + r4:512 * (cbk + 1):4])
                for r in range(16):
                    for kb in range(2):
                        vsrc.append(VT[:, hh, 2048 * kb + r:2048 * (kb + 1):16])
                v1 = lambda nbk: nbk - 15
                v2 = lambda r4, cbk: 17 + r4 * 5 + (cbk - 3)
                v3 = lambda r, kb: 37 + r * 2 + kb
                for i0 in range(0, 69, 8):
                    n = min(8, 69 - i0)
                    tb = cnt["tp"] % 2
                    cnt["tp"] += 1

                    def f(e, i0=i0, n=n, tb=tb, vsrc=vsrc):
                        for k in range(n):
                            i = e.transpose(tpb[tb][:, k * 128:(k + 1) * 128], vsrc[i0 + k], ident[:])
                        return i
                    sch.add("pe", f, reads=["VT"] + CONST, writes=[("tp", tb)])
                    sch.add("dve", lambda e, i0=i0, n=n, tb=tb: e.tensor_copy(
                        out=Vblk[:, i0:i0 + n, :], in_=tpb[tb][:, 0:n * 128].rearrange("p (n d) -> p n d", n=n)),
                        reads=[("tp", tb)], writes=["Vblk"])

                for rq in range(4 if DBG_ATTN > 1 else 0):
                    for kb in range(2):
                        sbk = cnt["st"] % 3
                        cnt["st"] += 1
                        ST = bank[sbk]

                        def f(e, rq=rq, kb=kb, ST=ST, hh=hh):
                            m = MCUR if kb == 1 else MPREVH
                            e.matmul(ST[:, :].rearrange("p (n q) -> p n q", n=4), lhsT=ident[:],
                                     rhs=masks[:, m, :].unsqueeze(1).broadcast_to([128, 4, 128]), start=True, stop=False)
                            for rr in range(4):
                                r = 4 * rq + rr
                                i = e.matmul(ST[:, rr * 128:(rr + 1) * 128],
                                             lhsT=KT[:, hh, 2048 * kb + r:2048 * (kb + 1):16],
                                             rhs=QT[:, hh, r:2048:16], start=False, stop=(rr == 3))
                            return i
                        sch.add("pe", f, reads=["QT", "KT"] + CONST, writes=[("bank", sbk)])
                        sch.add("act", lambda e, rq=rq, kb=kb, ST=ST: e.activation(
                            out=PT3[:, kb, :].rearrange("p (u r) -> p r u", r=16)[:, 4 * rq:4 * rq + 4, :],
                            in_=ST[:, :].rearrange("p (n q) -> p n q", n=4), func=AF.Exp, scale=SCALE),
                            reads=[("bank", sbk)], writes=[("pt3", kb, rq)])
                PT3R = [[("pt3", kb, rq) for rq in range(4)] for kb in range(2)]

                for g in range(4 if DBG_ATTN > 1 else 0):
                    qb = 512 * g
                    gi = cnt["grp"]
                    cnt["grp"] += 1
                    NUM = bank[3 + gi % 2]
                    DEN = bank[5]
                    numr = ("bank", 3 + gi % 2)
                    denr = ("bank", 5)
                    cb = 4 + g
                    tiles = []

                    def qk(ti, g=g, qb=qb, hh=hh, cb=cb):
                        sbk = cnt["st"] % 3
                        cnt["st"] += 1
                        pts = cnt["pt"] % 6
                        cnt["pt"] += 1
                        tiles.append(pts)
                        ST = bank[sbk]

                        def f(e):
                            bc = lambda m, n_: masks[:, m, :].unsqueeze(1).broadcast_to([128, n_, 128])
                            if ti == 0:
                                e.matmul(ST[:, :].rearrange("p (n q) -> p n q", n=4), lhsT=ident[:],
                                         rhs=(m1g0[:] if g == 0 else bc(MPREV, 4)), start=True, stop=False)
                                for j in range(4):
                                    nbk = 16 + 4 * g + j - 1
                                    i = e.matmul(ST[:, j * 128:(j + 1) * 128], lhsT=KT[:, hh, nbk * 128:(nbk + 1) * 128],
                                                 rhs=QT[:, hh, qb + j * 128:qb + (j + 1) * 128], start=False, stop=(j == 3))
                            elif ti == 1:
                                e.matmul(ST[:, :].rearrange("p (n q) -> p n q", n=4), lhsT=ident[:],
                                         rhs=bc(MCUR, 4), start=True, stop=False)
                                for j in range(4):
                                    nbk = 16 + 4 * g + j
                                    i = e.matmul(ST[:, j * 128:(j + 1) * 128], lhsT=KT[:, hh, nbk * 128:(nbk + 1) * 128],
                                                 rhs=QT[:, hh, qb + j * 128:qb + (j + 1) * 128], start=False, stop=(j == 3))
                            else:
                                m = MCUR if ti == 3 else (MPREVH if g == 0 else MPREV)
                                cbk = cb if ti == 3 else cb - 1
                                e.matmul(ST[:, :].rearrange("p (n q) -> p n q", n=4), lhsT=ident[:],
                                         rhs=bc(m, 4), start=True, stop=False)
                                for r4 in range(4):
                                    i = e.matmul(ST[:, r4 * 128:(r4 + 1) * 128],
                                                 lhsT=KT[:, hh, 512 * cbk + r4:512 * (cbk + 1):4],
                                                 rhs=QT[:, hh, qb + r4:qb + 512:4], start=False, stop=(r4 == 3))
                            return i
                        sch.add("pe", f, reads=["QT", "KT", "m1g0a", "m1g0b"] + CONST, writes=[("bank", sbk)])
                        if ti < 2:
                            sch.add("act", lambda e: e.activation(out=PT[pts][:], in_=ST[:, :], func=AF.Exp, scale=SCALE),
                                    reads=[("bank", sbk)], writes=[("pt", pts)])
                        else:
                            sch.add("act", lambda e: e.activation(
                                out=PT[pts][:].rearrange("p (i r) -> p r i", r=4),
                                in_=ST[:, :].rearrange("p (n q) -> p n q", n=4), func=AF.Exp, scale=SCALE),
                                reads=[("bank", sbk)], writes=[("pt", pts)])

                    def pv(ti, g=g, qb=qb, hh=hh, cb=cb, NUM=NUM, DEN=DEN, numr=numr, denr=denr):
                        if ti < 4:
                            pts = tiles[ti]
                            P_ = PT[pts][:, :]
                            rds = [("pt", pts)]
                        else:
                            kb = ti - 4
                            P_ = PT3[:, kb, qb:qb + 512]
                            rds = PT3R[kb]

                        def f(e):
                            e.matmul(DEN[:, :], lhsT=ones[:], rhs=P_, start=(ti == 0), stop=(ti == 5))
                            if ti == 0:
                                e.matmul(NUM[:, :], lhsT=zeros[:], rhs=P_, start=True, stop=False)
                            if ti < 2:
                                for j in range(4):
                                    nbk = 16 + 4 * g + j - (1 if ti == 0 else 0)
                                    i = e.matmul(NUM[:, j * 128:(j + 1) * 128], lhsT=Vblk[:, v1(nbk), :],
                                                 rhs=P_[:, j * 128:(j + 1) * 128], start=False, stop=False)
                            elif ti < 4:
                                cbk = cb if ti == 3 else cb - 1
                                for r4 in range(4):
                                    i = e.matmul(NUM[:, r4:512:4], lhsT=Vblk[:, v2(r4, cbk), :], rhs=P_[:, r4:512:4],
                                                 start=False, stop=False)
                            else:
                                kb = ti - 4
                                for r in range(16):
                                    i = e.matmul(NUM[:, r:512:16], lhsT=Vblk[:, v3(r, kb), :], rhs=P_[:, r:512:16],
                                                 start=False, stop=(ti == 5 and r == 15))
                            return i
                        sch.add("pe", f, reads=rds + ["Vblk", "ones", "zeros"], writes=[numr, denr])

                    qk(0); qk(1); qk(2); pv(0); qk(3); pv(1); pv(2); pv(3); pv(4); pv(5)

                    fs = gi % 2
                    sch.add("dve", lambda e, fs=fs, DEN=DEN: e.reciprocal(out=rden[fs][:], in_=DEN[:, :]),
                            reads=[denr], writes=[("rden", fs)])
                    sch.add("dve", lambda e, fs=fs, NUM=NUM: e.tensor_tensor(out=fin1[fs][:], in0=NUM[:, :], in1=rden[fs][:],
                                                                             op=ALU.mult),
                            reads=[numr, ("rden", fs)], writes=[("fin", fs)])
                    sch.add("dve", lambda e, fs=fs, h=h, hh=hh, qb=qb: e.tensor_tensor(
                        out=yT_attn[:, h, qb:qb + 512], in0=fin1[fs][:], in1=Gs[:, hh, qb:qb + 512], op=ALU.mult),
                        reads=[("fin", fs), "Gs"], writes=["yT_attn"])
        run_block(nc, sch)


def phase_b(nc, sch, xT_all, w_perm, w_pool, pscale, bm_d, yT_pool):
    import contextlib
    with contextlib.ExitStack() as es:
        sb = lambda name, shape, dtype: es.enter_context(nc.sbuf_tensor(name, shape, dtype))
        xc = [sb(f"b_xc{i}", [128, KC, 512], BF16) for i in range(2)]
        xh = sb("b_xh", [128, KC, 128], BF16)
        wB = sb("b_w", [128, KC, 1024], BF16)
        utm = [sb(f"b_utm{i}", [128, 512], BF16) for i in range(6)]
        Gp = [sb(f"b_gp{i}", [128, 4, 512], BF16) for i in range(2)]
        pooled = [sb(f"b_pl{i}", [128, 4, 512], BF16) for i in range(2)]
        wpl = sb("b_wpool", [128, 4, 2, 256], BF16)
        bmats = sb("b_bm", [128, 3, 4, 128], BF16)
        ps = sb("b_ps", [128, 8], F32)
        bank = [es.enter_context(nc.psum_tensor(f"b_b{i}", [128, 512], F32)) for i in range(8)]

        sch.add("pool", lambda e: e.dma_start(out=wpl[:], in_=w_pool.rearrange("g (cc p) d -> p g cc d", p=128)),
                writes=["wpl"], dma="constg")
        sch.add("pool", lambda e: e.dma_start(out=bmats[:].rearrange("p k g b -> p (k g b)"), in_=bm_d),
                writes=["bmats"], dma="constg", no_waw=True)
        sch.add("sp", lambda e: e.dma_start(out=ps[:], in_=pscale), writes=["ps"], dma="const")
        CONST = ["wpl", "bmats", "ps"]
        BM, BP, BF = 0, 1, 2
        cnt = {"xc": 0, "u": 0, "gp": 0, "pp": 0, "po": 0, "ch": 0}

        for bp in range(2):
            for piece in range(4):
                def f(e, piece=piece, bp=bp):
                    src = w_perm.rearrange("(kc p) c -> p kc c", p=128)[:, piece * 4:(piece + 1) * 4,
                                                                        4096 + bp * 1024:4096 + (bp + 1) * 1024]
                    return e.dma_start(out=wB[:, piece * 4:(piece + 1) * 4, :], in_=src)
                sch.add("pool", f, writes=["wB"], dma="wp", no_waw=(piece > 0))
            sch.add("pool", lambda e: e.dma_start(out=xh[:], in_=xchunk_src(xT_all, 1920, 128)),
                    writes=["xh"], dma="xh")
            uslot = {}

            def u_tile(j, lhs_fn, rd):
                b = cnt["u"] % 2
                cnt["u"] += 1
                s = (j + 1) % 6
                uslot[j] = s

                def f(e):
                    for kc in range(KC):
                        i = e.matmul(bank[b][:, :], lhsT=lhs_fn(kc), rhs=wB[:, kc, 0:512],
                                     start=(kc == 0), stop=(kc == KC - 1))
                    return i
                sch.add("pe", f, reads=[rd, "wB"], writes=[("bank", b)])
                sch.add("act", lambda e: e.copy(out=utm[s][:], in_=bank[b][:, :]),
                        reads=[("bank", b)], writes=[("utm", s)])

            u_tile(-1, lambda kc: xh[:, kc, :], "xh")
            for ci in range(4):
                slot = cnt["xc"] % 2
                cnt["xc"] += 1
                for half in range(2):
                    def f(e, slot=slot, ci=ci, half=half):
                        return e.dma_start(out=xc[slot][:, half * 8:(half + 1) * 8, :],
                                           in_=xchunk_src(xT_all, 2048 + ci * 512, 512)[:, half * 8:(half + 1) * 8, :])
                    sch.add("pool", f, writes=[("xc", slot)], dma=f"xc{slot}", no_waw=(half > 0))
                for tt in range(4):
                    u_tile(ci * 4 + tt, lambda kc, slot=slot, tt=tt: xc[slot][:, kc, tt * 128:(tt + 1) * 128], ("xc", slot))
                cs_ = cnt["ch"] % 2
                cnt["ch"] += 1
                for cbi in range(4):
                    b = 2 + cnt["gp"] % 2
                    cnt["gp"] += 1

                    def f(e, slot=slot, cbi=cbi, b=b):
                        for kc in range(KC):
                            i = e.matmul(bank[b][:, :], lhsT=wB[:, kc, 512 + cbi * 128:512 + (cbi + 1) * 128],
                                         rhs=xc[slot][:, kc, :], start=(kc == 0), stop=(kc == KC - 1))
                        return i
                    sch.add("pe", f, reads=[("xc", slot), "wB"], writes=[("bank", b)])
                    sch.add("act", lambda e, cbi=cbi, b=b, cs_=cs_: e.activation(out=Gp[cs_][:, cbi, :], in_=bank[b][:, :],
                                                                                 func=AF.Silu),
                            reads=[("bank", b)], writes=[("gp", cs_)])
                for cbl in range(4):
                    gq = 2 * bp + cbl // 2
                    b = 4 + cnt["pp"] % 2
                    cnt["pp"] += 1

                    def f(e, cbl=cbl, gq=gq, b=b, ci=ci, uslot=uslot):
                        for jj in range(4):
                            j = ci * 4 + jj
                            e.matmul(bank[b][:, jj * 128:(jj + 1) * 128], lhsT=utm[uslot[j - 1]][:, cbl * 128:(cbl + 1) * 128],
                                     rhs=bmats[:, BP, gq, :], start=True, stop=False, skip_group_check=True)
                            i = e.matmul(bank[b][:, jj * 128:(jj + 1) * 128], lhsT=utm[uslot[j]][:, cbl * 128:(cbl + 1) * 128],
                                         rhs=bmats[:, (BF if j == 0 else BM), gq, :], start=False, stop=(jj == 3),
                                         skip_group_check=True)
                        return i
                    rds = [("utm", uslot[ci * 4 + jj]) for jj in range(-1, 4)]
                    sch.add("pe", f, reads=rds + CONST, writes=[("bank", b)])
                    sch.add("dve", lambda e, cbl=cbl, b=b, cs_=cs_: e.tensor_copy(out=pooled[cs_][:, cbl, :], in_=bank[b][:, :]),
                            reads=[("bank", b)], writes=[("pooled", cs_)])
                for obl in range(4):
                    ob = 4 * bp + obl
                    gq = ob // 2
                    hf = obl % 2
                    b = 6 + cnt["po"] % 2
                    cnt["po"] += 1

                    def f(e, obl=obl, gq=gq, hf=hf, b=b, cs_=cs_):
                        for cc in range(2):
                            i = e.matmul(bank[b][:, :], lhsT=wpl[:, gq, cc, hf * 128:(hf + 1) * 128],
                                         rhs=pooled[cs_][:, (obl // 2) * 2 + cc, :], start=(cc == 0), stop=(cc == 1))
                        return i
                    sch.add("pe", f, reads=[("pooled", cs_)] + CONST, writes=[("bank", b)])
                    sch.add("dve", lambda e, obl=obl, ob=ob, b=b, cs_=cs_, ci=ci: e.scalar_tensor_tensor(
                        out=yT_pool[:, ob, ci * 512:(ci + 1) * 512], in0=bank[b][:, :], scalar=ps[:, ob:ob + 1],
                        in1=Gp[cs_][:, obl, :], op0=ALU.mult, op1=ALU.mult),
                        reads=[("bank", b), ("gp", cs_)] + CONST, writes=["yT_pool"])
        run_block(nc, sch)


def phase_c(nc, sch, x_own, w_out, ln_g, ln_b, out_d, yT_attn, yT_pool):
    import contextlib
    with contextlib.ExitStack() as es:
        sb = lambda name, shape, dtype: es.enter_context(nc.sbuf_tensor(name, shape, dtype))
        wo = sb("c_wo", [128, KC, D], BF16)
        xr = [sb(f"c_xr{i}", [128, D], F32) for i in range(2)]
        z = [sb(f"c_z{i}", [128, D], F32) for i in range(3)]
        junk = sb("c_junk", [128, D], BF16)
        gbc = sb("c_g", [128, D], F32)
        bbc = sb("c_b", [128, D], F32)
        st = [sb(f"c_st{i}", [128, 16], F32) for i in range(3)]
        bank = [es.enter_context(nc.psum_tensor(f"c_b{i}", [128, 512], F32)) for i in range(8)]

        for piece in range(8):
            def f(e, piece=piece):
                src = w_out.rearrange("(ec p) d -> p ec d", p=128)[:, piece * 2:(piece + 1) * 2, :]
                return e.dma_start(out=wo[:, piece * 2:(piece + 1) * 2, :], in_=src)
            sch.add("pool", f, writes=["wo"], dma="wp", no_waw=(piece > 0))
        sch.add("sp", lambda e: e.dma_start(out=gbc[:], in_=ln_g.partition_broadcast(128)), writes=["gbc"], dma="const")
        sch.add("sp", lambda e: e.dma_start(out=bbc[:], in_=ln_b.partition_broadcast(128)), writes=["bbc"], dma="const",
                no_waw=True)
        CONST = ["gbc", "bbc"]

        def load_x(j):
            s = j % 2
            sch.add("sp", lambda e: e.dma_start(out=xr[s][:], in_=x_own[j * 128:(j + 1) * 128, :]),
                    writes=[("xr", s)], dma=f"xr{s}")

        load_x(0)
        for j in range(16):
            s = j % 2
            zi = j % 3
            Z = z[zi]
            S = st[zi]
            if j + 1 < 16:
                load_x(j + 1)

            def f(e, j=j, s=s):
                for ec in range(KC):
                    lhs = (yT_attn[:, ec, j * 128:(j + 1) * 128] if ec < 8 else yT_pool[:, ec - 8, j * 128:(j + 1) * 128])
                    for nb in range(4):
                        i = e.matmul(bank[4 * s + nb][:, :], lhsT=lhs, rhs=wo[:, ec, nb * 512:(nb + 1) * 512],
                                     start=(ec == 0), stop=(ec == KC - 1))
                return i
            sch.add("pe", f, reads=["wo"], writes=[("bank", 4 * s + nb) for nb in range(4)])
            for nb in range(4):
                sch.add("dve", lambda e, s=s, nb=nb, Z=Z: e.scalar_tensor_tensor(
                    out=Z[:, nb * 512:(nb + 1) * 512], in0=xr[s][:, nb * 512:(nb + 1) * 512], scalar=ALPHA,
                    in1=bank[4 * s + nb][:, :], op0=ALU.mult, op1=ALU.add),
                    reads=[("xr", s), ("bank", 4 * s + nb)], writes=[("z", zi, nb)])
            zr = [("z", zi, nb) for nb in range(4)]
            sch.add("act", lambda e, Z=Z, S=S: e.activation(out=junk[:], in_=Z[:], func=AF.Identity, accum_out=S[:, 5:6]),
                    reads=zr, writes=["junk", ("tot", zi)])
            sch.add("act", lambda e, Z=Z, S=S: e.activation(out=junk[:], in_=Z[:], func=AF.Square, accum_out=S[:, 4:5]),
                    reads=zr, writes=["junk", ("s2", zi)])
            sch.add("dve", lambda e, S=S: e.tensor_scalar(out=S[:, 6:7], in0=S[:, 5:6], scalar1=1.0 / D, scalar2=None,
                                                          op0=ALU.mult),
                    reads=[("tot", zi)], writes=[("mean", zi)])
            sch.add("dve", lambda e, S=S: e.tensor_tensor(out=S[:, 7:8], in0=S[:, 6:7], in1=S[:, 6:7], op=ALU.mult),
                    reads=[("mean", zi)], writes=[("msq", zi)])
            sch.add("dve", lambda e, S=S: e.scalar_tensor_tensor(out=S[:, 8:9], in0=S[:, 4:5], scalar=1.0 / D, in1=S[:, 7:8],
                                                                 op0=ALU.mult, op1=ALU.subtract),
                    reads=[("s2", zi), ("msq", zi)], writes=[("var", zi)])
            sch.add("dve", lambda e, S=S: e.tensor_scalar(out=S[:, 9:10], in0=S[:, 8:9], scalar1=EPS, scalar2=None,
                                                          op0=ALU.add),
                    reads=[("var", zi)], writes=[("vare", zi)])
            sch.add("act", lambda e, S=S: e.activation(out=S[:, 10:11], in_=S[:, 9:10], func=AF.Sqrt),
                    reads=[("vare", zi)], writes=[("sd", zi)])
            sch.add("dve", lambda e, S=S: e.reciprocal(out=S[:, 11:12], in_=S[:, 10:11]),
                    reads=[("sd", zi)], writes=[("rstd", zi)])
            sch.add("dve", lambda e, S=S: e.scalar_tensor_tensor(out=S[:, 12:13], in0=S[:, 6:7], scalar=-1.0, in1=S[:, 11:12],
                                                                 op0=ALU.mult, op1=ALU.mult),
                    reads=[("mean", zi), ("rstd", zi)], writes=[("nbias", zi)])
            zall = ("zall", zi)
            sch.add("act", lambda e, Z=Z, S=S: e.activation(out=Z[:], in_=Z[:], func=AF.Identity,
                                                            scale=S[:, 11:12], bias=S[:, 12:13]),
                    reads=zr + [("rstd", zi), ("nbias", zi), ("tot", zi), ("s2", zi)], writes=zr + [zall])
            sch.add("pool", lambda e, Z=Z: e.tensor_tensor(out=Z[:], in0=Z[:], in1=gbc[:], op=ALU.mult),
                    reads=[zall] + CONST, writes=zr + [zall])
            sch.add("pool", lambda e, Z=Z: e.tensor_tensor(out=Z[:], in0=Z[:], in1=bbc[:], op=ALU.add),
                    reads=[zall] + CONST, writes=zr + [zall])
            sch.add("sp", lambda e, Z=Z, j=j: e.dma_start(out=out_d[j * 128:(j + 1) * 128, :], in_=Z[:]),
                    reads=[zall] + zr, writes=[("outd", j)], dma=f"out{zi}")
        sch.add("sp", None, reads=[("outd", 13), ("outd", 14), ("outd", 15)])
        run_block(nc, sch)


def _perm_cols():
    cols = []
    for p in range(4):
        for base in (0, 1024, 2048, 4096):
            for h in (2 * p, 2 * p + 1):
                cols.extend(range(base + h * 128, base + (h + 1) * 128))
    for bp in range(2):
        cols.extend(range(3072 + bp * 512, 3072 + (bp + 1) * 512))
        cols.extend(range(5120 + bp * 512, 5120 + (bp + 1) * 512))
    return np.asarray(cols, dtype=np.int64)


def _tables(half):
    lt = np.arange(LT, dtype=np.int64)
    pos = np.maximum(lt - 2048 + 2048 * half, 0).astype(np.float32)
    inv = (np.float32(500000.0) ** (-(2.0 * np.arange(16, dtype=np.float32)) / np.float32(32.0))).astype(np.float32)
    ang = (pos[:, None] * inv[None, :]).astype(np.float32)
    c, s = np.cos(ang).astype(np.float32), np.sin(ang).astype(np.float32)
    cs = np.concatenate([c, c, -s, s], axis=1).astype(np.float32)
    cs = np.ascontiguousarray(cs.reshape(32, 128, 64).transpose(1, 0, 2).reshape(128, 32 * 64))
    k = np.arange(128)[:, None]
    q = np.arange(128)[None, :]
    mcur = np.where(k <= q, 0.0, NEG).astype(np.float32)
    mprev = np.where(k >= q, 0.0, NEG).astype(np.float32)
    mprevh = mprev if half == 1 else np.full((128, 128), NEG, np.float32)
    masks = np.ascontiguousarray(np.stack([mcur, mprev, mprevh]).astype(np.float32).transpose(1, 0, 2).reshape(128, 384))
    bm = np.zeros((3, 4, 128, 128), np.float32)
    tp = np.arange(128)[:, None]
    t = np.arange(128)[None, :]
    for g, w in enumerate(POOL_W):
        main = ((tp <= t) & (tp > t - w)).astype(np.float32) / w - (tp == t)
        prev = ((tp - 128) > (t - w)).astype(np.float32) / w
        cntf = np.minimum(t + 1, w).astype(np.float32)
        first = ((tp <= t) & (tp > t - w)).astype(np.float32) / cntf - (tp == t)
        bm[0, g] = main
        bm[1, g] = prev
        bm[2, g] = first if half == 0 else main
    bm = np.ascontiguousarray(bm.transpose(2, 0, 1, 3).reshape(128, 3 * 4 * 128))
    return cs, masks, bm


_NC_CACHE = {}


def kernel(x, w_in, w_pool, pool_scale, w_out, ln_gain, ln_bias):
    x = np.asarray(x, dtype=np.float32)
    w_perm = np.ascontiguousarray(np.asarray(w_in, np.float32)[0][:, _perm_cols()])
    w_pool0 = np.ascontiguousarray(np.asarray(w_pool, np.float32)[0])
    pscale = np.ascontiguousarray(np.asarray(pool_scale, np.float32)[0].reshape(8, 128).T)
    w_out0 = np.ascontiguousarray(np.asarray(w_out, np.float32)[0])
    g0 = np.ascontiguousarray(np.asarray(ln_gain, np.float32)[0])
    b0 = np.ascontiguousarray(np.asarray(ln_bias, np.float32)[0])
    ident = np.eye(128, dtype=np.float32)
    if "nc" not in _NC_CACHE:
        _NC_CACHE["nc"] = build_program()
    nc = _NC_CACHE["nc"]
    in_maps = []
    for c in range(8):
        b, half = c // 2, c % 2
        own = x[b, half * T:(half + 1) * T, :]
        halo = x[b, 0:T, :] if half == 1 else np.zeros((T, D), np.float32)
        xT_all = np.ascontiguousarray(np.concatenate([halo, own], axis=0).T)
        cs, masks, bm = _tables(half)
        in_maps.append({
            "xT_all": xT_all, "x_own": np.ascontiguousarray(own), "w_perm": w_perm, "w_pool": w_pool0,
            "pscale": pscale, "w_out": w_out0, "ln_g": g0, "ln_b": b0, "cs": cs, "masks": masks,
            "ident": ident, "bmats": bm,
        })
    res = run_bass_kernel_spmd(nc, in_maps, core_ids=list(range(8)))
    out = np.empty((4, 4096, D), np.float32)
    for c in range(8):
        b, half = c // 2, c % 2
        out[b, half * T:(half + 1) * T, :] = res.results[c]["out"]
    return out
```

```python
import numpy as np
import concourse.bass as bass
import concourse.mybir as mybir
from concourse.bass_utils import run_bass_kernel_spmd

F32 = mybir.dt.float32
BF16 = mybir.dt.bfloat16
AF = mybir.ActivationFunctionType
ALU = mybir.AluOpType
AX = mybir.AxisListType

D = 2048
T = 2048
LT = 4096
KC = 16
NEG = -30000.0
ALPHA = 2.0 ** 0.25
EPS = 1e-5
SCALE = 128.0 ** -0.5
POOL_W = (2, 4, 8, 16)

COMPUTE = ("pe", "act", "dve", "pool")


class Op:
    __slots__ = ("eng", "fn", "deps", "is_dma", "sem", "val", "need_sig", "seq")


class Sched:
    def __init__(self, prog, counters, dma_sems, dma_cnt):
        self.prog = prog
        self.counters = counters
        self.dma_sems = dma_sems
        self.dma_cnt = dma_cnt
        self.ops = []
        self.lw = {}
        self.rd = {}

    def add(self, eng, fn, reads=(), writes=(), dma=None, no_waw=False):
        idx = len(self.ops)
        deps = {}
        for r in reads:
            w = self.lw.get(r)
            if w is not None:
                deps[w] = "raw"
        for w_ in writes:
            for ridx in self.rd.get(w_, {}).values():
                deps.setdefault(ridx, "war")
            lw = self.lw.get(w_)
            if lw is not None and not no_waw:
                deps.setdefault(lw, "waw")
        op = Op()
        op.eng = eng
        op.fn = fn
        op.deps = deps
        op.is_dma = dma is not None
        op.sem = dma
        op.val = 0
        op.need_sig = False
        op.seq = 0
        if dma is not None:
            self.dma_cnt[dma] += 16
            op.val = self.dma_cnt[dma]
        key = ("dma", dma) if dma is not None else eng
        for r in reads:
            self.rd.setdefault(r, {})[key] = idx
        for w_ in writes:
            self.lw[w_] = idx
            self.rd[w_] = {}
        self.ops.append(op)
        return idx

    def _needs_wait(self, op, dop, kind):
        if dop.is_dma:
            return True
        if dop.eng != op.eng:
            return True
        if op.is_dma:
            return True
        if kind in ("raw", "waw") and op.eng in ("act", "dve", "pool"):
            return True
        return False

    def finalize(self):
        for op in self.ops:
            for d, kind in op.deps.items():
                dop = self.ops[d]
                if not dop.is_dma and self._needs_wait(op, dop, kind):
                    dop.need_sig = True
        for op in self.ops:
            if op.need_sig:
                self.counters[op.eng] += 1
                op.seq = self.counters[op.eng]

    def emit(self, eng, eh):
        waited = {}
        for op in self.ops:
            if op.eng != eng:
                continue
            waits = {}
            for d, kind in op.deps.items():
                dop = self.ops[d]
                if not self._needs_wait(op, dop, kind):
                    continue
                if dop.is_dma:
                    k, v = ("d", dop.sem), dop.val
                else:
                    k, v = ("p", dop.eng), dop.seq
                if waits.get(k, 0) < v:
                    waits[k] = v
            for k, v in waits.items():
                if waited.get(k, 0) >= v:
                    continue
                waited[k] = v
                sem = self.dma_sems[k[1]] if k[0] == "d" else self.prog[k[1]]
                eh.wait_ge(sem, v)
            if op.fn is None:
                continue
            inst = op.fn(eh)
            if op.is_dma:
                inst.then_inc(self.dma_sems[op.sem], 16)
            elif op.need_sig:
                inst.then_inc(self.prog[eng], 1)


def run_block(nc, sch):
    sch.finalize()
    with nc.Block() as block:
        @block.tensor
        def _(e):
            sch.emit("pe", e)

        @block.scalar
        def _(e):
            sch.emit("act", e)

        @block.vector
        def _(e):
            sch.emit("dve", e)

        @block.gpsimd
        def _(e):
            sch.emit("pool", e)

        @block.sync
        def _(e):
            sch.emit("sp", e)


DMA_SEM_NAMES = ["const", "constg", "wp", "xc0", "xc1", "xh", "xr0", "xr1", "out0", "out1", "out2"]


def build_program():
    nc = bass.Bass("TRN2", target_bir_lowering=False)
    dt = lambda name, shape, kind="ExternalInput": nc.dram_tensor(name, shape, F32, kind=kind).ap()
    xT_all = dt("xT_all", [D, LT])
    x_own = dt("x_own", [T, D])
    w_perm = dt("w_perm", [D, 6144])
    w_pool = dt("w_pool", [4, 256, 256])
    pscale = dt("pscale", [128, 8])
    w_out = dt("w_out", [D, D])
    ln_g = dt("ln_g", [D])
    ln_b = dt("ln_b", [D])
    cs_d = dt("cs", [128, 32 * 64])
    mask_d = dt("masks", [128, 3 * 128])
    ident_d = dt("ident", [128, 128])
    bm_d = dt("bmats", [128, 3 * 4 * 128])
    out_d = dt("out", [T, D], kind="ExternalOutput")

    import contextlib
    es = contextlib.ExitStack()
    with es:
        prog = {e: es.enter_context(nc.semaphore("prog_" + e)) for e in COMPUTE}
        dma_sems = {n: es.enter_context(nc.semaphore("dma_" + n)) for n in DMA_SEM_NAMES}
        counters = {e: 0 for e in COMPUTE}
        dma_cnt = {n: 0 for n in DMA_SEM_NAMES}
        new_sched = lambda: Sched(prog, counters, dma_sems, dma_cnt)

        import os
        PH = os.environ.get("KPHASES", "abc")
        yT_attn = es.enter_context(nc.sbuf_tensor("yT_attn", [128, 8, T], BF16))
        if "a" in PH:
            phase_a(nc, new_sched(), xT_all, w_perm, cs_d, mask_d, ident_d, yT_attn)
        yT_pool = es.enter_context(nc.sbuf_tensor("yT_pool", [128, 8, T], BF16))
        if "b" in PH:
            phase_b(nc, new_sched(), xT_all, w_perm, w_pool, pscale, bm_d, yT_pool)
        if "c" in PH:
            phase_c(nc, new_sched(), x_own, w_out, ln_g, ln_b, out_d, yT_attn, yT_pool)
    return nc


def xchunk_src(xT_all, lt0, n):
    return xT_all.rearrange("(kc p) t -> p kc t", p=128)[:, :, lt0:lt0 + n]


def phase_a(nc, sch, xT_all, w_perm, cs_d, mask_d, ident_d, yT_attn):
    import contextlib
    with contextlib.ExitStack() as es:
        sb = lambda name, shape, dtype: es.enter_context(nc.sbuf_tensor(name, shape, dtype))
        xc = [sb(f"a_xc{i}", [128, KC, 512], BF16) for i in range(2)]
        wp = sb("a_wp", [128, KC, 1024], BF16)
        QT = sb("a_QT", [128, 2, T], BF16)
        KT = sb("a_KT", [128, 2, LT], BF16)
        VT = sb("a_VT", [128, 2, LT], BF16)
        Gs = sb("a_Gs", [128, 2, T], BF16)
        Vblk = sb("a_Vblk", [128, 69, 128], BF16)
        PT = [sb(f"a_PT{i}", [128, 512], BF16) for i in range(3)]
        PTP = [sb(f"a_PTP{i}", [128, 4, 130], BF16) for i in range(3)]
        PT3 = sb("a_PTh3", [128, 2, 16, 130], BF16)
        qktm = [sb(f"a_qktm{i}", [128, 4, 128], BF16) for i in range(2)]
        rt1 = [sb(f"a_rt1_{i}", [128, 4, 32], F32) for i in range(2)]
        rt2 = [sb(f"a_rt2_{i}", [128, 4, 32], F32) for i in range(2)]
        cs = sb("a_cs", [128, 32, 64], F32)
        masks = sb("a_masks", [128, 3, 128], BF16)
        ident = sb("a_ident", [128, 128], BF16)
        ones = sb("a_ones", [128, 128], BF16)
        zeros = sb("a_zeros", [128, 128], BF16)
        m1g0 = sb("a_m1g0", [128, 4, 128], BF16)
        rden = [sb(f"a_rden{i}", [128, 512], F32) for i in range(2)]
        fin1 = [sb(f"a_fin{i}", [128, 512], F32) for i in range(2)]
        bank = [es.enter_context(nc.psum_tensor(f"a_b{i}", [128, 512], F32)) for i in range(6)]
        tpb = [es.enter_context(nc.psum_tensor(f"a_tp{i}", [128, 1024], BF16)) for i in range(2)]

        MCUR, MPREV, MPREVH = 0, 1, 2

        sch.add("sp", lambda e: e.dma_start(out=cs[:].rearrange("p t c -> p (t c)"), in_=cs_d),
                writes=["cs"], dma="const")
        sch.add("pool", lambda e: e.dma_start(out=masks[:].rearrange("p m q -> p (m q)"), in_=mask_d),
                writes=["masks"], dma="constg")
        sch.add("pool", lambda e: e.dma_start(out=ident[:], in_=ident_d),
                writes=["ident"], dma="constg", no_waw=True)
        sch.add("dve", lambda e: e.memset(ones[:], 1.0), writes=["ones"])
        sch.add("dve", lambda e: e.memset(zeros[:], 0.0), writes=["zeros"])
        CONST = ["cs", "masks", "ident"]
        sch.add("dve", lambda e: e.tensor_copy(out=m1g0[:, 0, :], in_=masks[:, 2, :]), reads=CONST, writes=["m1g0a"])
        sch.add("dve", lambda e: e.tensor_copy(out=m1g0[:, 1:4, :],
                                               in_=masks[:, 1, :].unsqueeze(1).broadcast_to([128, 3, 128])),
                reads=CONST, writes=["m1g0b"])

        cnt = {"xc": 0, "pqk": 0, "pf": 0, "tp": 0, "qk": 0, "st": 0, "pt": 0, "ptp": 0, "grp": 0}
        PQK_BANKS = (0, 1, 2)
        PF_BANKS = (3, 4, 5)

        import os
        DBG_PAIRS = int(os.environ.get("KA_PAIRS", "4"))
        DBG_ATTN = int(os.environ.get("KA_ATTN", "9"))
        DBG_INPROJ = int(os.environ.get("KA_INPROJ", "1"))
        for p in range(DBG_PAIRS):
            for piece in range(4):
                def f(e, piece=piece, p=p):
                    src = w_perm.rearrange("(kc p) c -> p kc c", p=128)[:, piece * 4:(piece + 1) * 4,
                                                                        p * 1024:(p + 1) * 1024]
                    return e.dma_start(out=wp[:, piece * 4:(piece + 1) * 4, :], in_=src)
                sch.add("pool", f, writes=["wp"], dma="wp", no_waw=(piece > 0))

            pending = []

            def flush(keep):
                while len(pending) > keep:
                    pending.pop(0)()

            for ci in range(8 if DBG_INPROJ else 0):
                own = ci >= 4
                slot = cnt["xc"] % 2
                cnt["xc"] += 1
                for half in range(2):
                    def f(e, slot=slot, ci=ci, half=half):
                        return e.dma_start(out=xc[slot][:, half * 8:(half + 1) * 8, :],
                                           in_=xchunk_src(xT_all, ci * 512, 512)[:, half * 8:(half + 1) * 8, :])
                    sch.add("pool", f, writes=[("xc", slot)], dma=f"xc{slot}", no_waw=(half > 0))

                nb = 4 if own else 2
                c0 = 0 if own else 256
                for tt in range(4):
                    tl = ci * 4 + tt
                    b = PQK_BANKS[cnt["pqk"] % 3]
                    cnt["pqk"] += 1

                    def f(e, slot=slot, tt=tt, b=b, nb=nb, c0=c0):
                        for kc in range(KC):
                            i = e.matmul(bank[b][:, 0:nb * 128], lhsT=xc[slot][:, kc, tt * 128:(tt + 1) * 128],
                                         rhs=wp[:, kc, c0:c0 + nb * 128], start=(kc == 0), stop=(kc == KC - 1))
                        return i
                    sch.add("pe", f, reads=[("xc", slot), "wp"], writes=[("bank", b)])
                    qs = cnt["qk"] % 2
                    cnt["qk"] += 1
                    P = bank[b][:, 0:nb * 128].rearrange("p (n d) -> p n d", n=nb)
                    cc = cs[:, tl, 0:32].unsqueeze(1).broadcast_to([128, nb, 32])
                    sn = cs[:, tl, 32:48].unsqueeze(1).broadcast_to([128, nb, 16])
                    sp_ = cs[:, tl, 48:64].unsqueeze(1).broadcast_to([128, nb, 16])
                    sch.add("dve", lambda e, P=P, qs=qs, nb=nb, cc=cc: e.tensor_tensor(
                        out=rt1[qs][:, 0:nb, :], in0=P[:, :, 0:32], in1=cc, op=ALU.mult),
                        reads=[("bank", b)] + CONST, writes=[("rt1", qs)])
                    sch.add("dve", lambda e, P=P, qs=qs, nb=nb, sn=sn: e.tensor_tensor(
                        out=rt2[qs][:, 0:nb, 0:16], in0=P[:, :, 16:32], in1=sn, op=ALU.mult),
                        reads=[("bank", b)] + CONST, writes=[("rt2a", qs)])
                    sch.add("dve", lambda e, P=P, qs=qs, nb=nb, sp_=sp_: e.tensor_tensor(
                        out=rt2[qs][:, 0:nb, 16:32], in0=P[:, :, 0:16], in1=sp_, op=ALU.mult),
                        reads=[("bank", b)] + CONST, writes=[("rt2b", qs)])
                    sch.add("act", lambda e, P=P, qs=qs, nb=nb: e.copy(out=qktm[qs][:, 0:nb, 32:128], in_=P[:, :, 32:128]),
                            reads=[("bank", b), ("rt1", qs), ("rt2a", qs), ("rt2b", qs)], writes=[("qk_nr", qs)])
                    sch.add("dve", lambda e, qs=qs, nb=nb: e.tensor_tensor(
                        out=qktm[qs][:, 0:nb, 0:32], in0=rt1[qs][:, 0:nb, :], in1=rt2[qs][:, 0:nb, :], op=ALU.add),
                        reads=[("rt1", qs), ("rt2a", qs), ("rt2b", qs)], writes=[("qk_rot", qs)])

                    def tr(qs=qs, nb=nb, own=own, tl=tl):
                        tb = cnt["tp"] % 2
                        cnt["tp"] += 1

                        def f(e):
                            for blk in range(nb):
                                i = e.transpose(tpb[tb][:, blk * 128:(blk + 1) * 128], qktm[qs][:, blk, :], ident[:])
                            return i
                        sch.add("pe", f, reads=[("qk_nr", qs), ("qk_rot", qs)] + CONST, writes=[("tp", tb)])
                        if own:
                            q0 = (tl - 16) * 128
                            sch.add("act", lambda e: e.copy(out=QT[:, :, q0:q0 + 128],
                                                            in_=tpb[tb][:, 0:256].rearrange("p (n t) -> p n t", n=2)),
                                    reads=[("tp", tb)], writes=["QT"])
                            sch.add("act", lambda e: e.copy(out=KT[:, :, tl * 128:(tl + 1) * 128],
                                                            in_=tpb[tb][:, 256:512].rearrange("p (n t) -> p n t", n=2)),
                                    reads=[("tp", tb)], writes=["KT"])
                        else:
                            sch.add("act", lambda e: e.copy(out=KT[:, :, tl * 128:(tl + 1) * 128],
                                                            in_=tpb[tb][:, 0:256].rearrange("p (n t) -> p n t", n=2)),
                                    reads=[("tp", tb)], writes=["KT"])
                    pending.append(tr)
                    flush(1)

                for cb in range(4 if own else 2):
                    b = PF_BANKS[cnt["pf"] % 3]
                    cnt["pf"] += 1

                    def f(e, slot=slot, cb=cb, b=b):
                        for kc in range(KC):
                            i = e.matmul(bank[b][:, :], lhsT=wp[:, kc, 512 + cb * 128:512 + (cb + 1) * 128],
                                         rhs=xc[slot][:, kc, :], start=(kc == 0), stop=(kc == KC - 1))
                        return i
                    sch.add("pe", f, reads=[("xc", slot), "wp"], writes=[("bank", b)])
                    if cb < 2:
                        sch.add("dve", lambda e, cb=cb, b=b, ci=ci: e.tensor_copy(
                            out=VT[:, cb, ci * 512:(ci + 1) * 512], in_=bank[b][:, :]),
                            reads=[("bank", b)], writes=["VT"])
                    else:
                        sch.add("act", lambda e, cb=cb, b=b, ci=ci: e.activation(
                            out=Gs[:, cb - 2, (ci - 4) * 512:(ci - 3) * 512], in_=bank[b][:, :], func=AF.Silu),
                            reads=[("bank", b)], writes=["Gs"])
                    flush(0)
            flush(0)

            for hh in range(2 if DBG_ATTN else 0):
                h = 2 * p + hh
                vsrc = []
                for nbk in range(15, 32):
                    vsrc.append(VT[:, hh, nbk * 128:(nbk + 1) * 128])
                for r4 in range(4):
                    for cbk in range(3, 8):
                        vsrc.append(VT[:, hh, 512 * cbk + r4:512 * (cbk + 1):4])
                for r in range(16):
                    for kb in range(2):
                        vsrc.append(VT[:, hh, 2048 * kb + r:2048 * (kb + 1):16])
                v1 = lambda nbk: nbk - 15
                v2 = lambda r4, cbk: 17 + r4 * 5 + (cbk - 3)
                v3 = lambda r, kb: 37 + r * 2 + kb
                for i0 in range(0, 69, 8):
                    n = min(8, 69 - i0)
                    tb = cnt["tp"] % 2
                    cnt["tp"] += 1

                    def f(e, i0=i0, n=n, tb=tb, vsrc=vsrc):
                        for k in range(n):
                            i = e.transpose(tpb[tb][:, k * 128:(k + 1) * 128], vsrc[i0 + k], ident[:])
                        return i
                    sch.add("pe", f, reads=["VT"] + CONST, writes=[("tp", tb)])
                    sch.add("dve", lambda e, i0=i0, n=n, tb=tb: e.tensor_copy(
                        out=Vblk[:, i0:i0 + n, :], in_=tpb[tb][:, 0:n * 128].rearrange("p (n d) -> p n d", n=n)),
                        reads=[("tp", tb)], writes=["Vblk"])

                for rq in range(4 if DBG_ATTN > 1 else 0):
                    for kb in range(2):
                        sbk = cnt["st"] % 3
                        cnt["st"] += 1
                        ST = bank[sbk]

                        def f(e, rq=rq, kb=kb, ST=ST, hh=hh):
                            m = MCUR if kb == 1 else MPREVH
                            e.matmul(ST[:, :].rearrange("p (n q) -> p n q", n=4), lhsT=ident[:],
                                     rhs=masks[:, m, :].unsqueeze(1).broadcast_to([128, 4, 128]), start=True, stop=False)
                            for rr in range(4):
                                r = 4 * rq + rr
                                i = e.matmul(ST[:, rr * 128:(rr + 1) * 128],
                                             lhsT=KT[:, hh, 2048 * kb + r:2048 * (kb + 1):16],
                                             rhs=QT[:, hh, r:2048:16], start=False, stop=(rr == 3))
                            return i
                        sch.add("pe", f, reads=["QT", "KT"] + CONST, writes=[("bank", sbk)])
                        sch.add("act", lambda e, rq=rq, kb=kb, ST=ST: e.activation(
                            out=PT3[:, kb, 4 * rq:4 * rq + 4, 0:128],
                            in_=ST[:, :].rearrange("p (n q) -> p n q", n=4), func=AF.Exp, scale=SCALE),
                            reads=[("bank", sbk)], writes=[("pt3", kb, rq)])
                PT3R = [[("pt3", kb, rq) for rq in range(4)] for kb in range(2)]

                for g in range(4 if DBG_ATTN > 1 else 0):
                    qb = 512 * g
                    gi = cnt["grp"]
                    cnt["grp"] += 1
                    NUM = bank[3 + gi % 2]
                    DEN = bank[5]
                    numr = ("bank", 3 + gi % 2)
                    denr = ("bank", 5)
                    cb = 4 + g
                    tiles = []

                    def qk(ti, g=g, qb=qb, hh=hh, cb=cb):
                        sbk = cnt["st"] % 3
                        cnt["st"] += 1
                        pts = cnt["pt"] % 3 if ti < 2 else 3 + cnt["ptp"] % 3
                        cnt["pt" if ti < 2 else "ptp"] += 1
                        tiles.append(pts)
                        ST = bank[sbk]

                        def f(e):
                            bc = lambda m, n_: masks[:, m, :].unsqueeze(1).broadcast_to([128, n_, 128])
                            if ti == 0:
                                e.matmul(ST[:, :].rearrange("p (n q) -> p n q", n=4), lhsT=ident[:],
                                         rhs=(m1g0[:] if g == 0 else bc(MPREV, 4)), start=True, stop=False)
                                for j in range(4):
                                    nbk = 16 + 4 * g + j - 1
                                    i = e.matmul(ST[:, j * 128:(j + 1) * 128], lhsT=KT[:, hh, nbk * 128:(nbk + 1) * 128],
                                                 rhs=QT[:, hh, qb + j * 128:qb + (j + 1) * 128], start=False, stop=(j == 3))
                            elif ti == 1:
                                e.matmul(ST[:, :].rearrange("p (n q) -> p n q", n=4), lhsT=ident[:],
                                         rhs=bc(MCUR, 4), start=True, stop=False)
                                for j in range(4):
                                    nbk = 16 + 4 * g + j
                                    i = e.matmul(ST[:, j * 128:(j + 1) * 128], lhsT=KT[:, hh, nbk * 128:(nbk + 1) * 128],
                                                 rhs=QT[:, hh, qb + j * 128:qb + (j + 1) * 128], start=False, stop=(j == 3))
                            else:
                                m = MCUR if ti == 3 else (MPREVH if g == 0 else MPREV)
                                cbk = cb if ti == 3 else cb - 1
                                e.matmul(ST[:, :].rearrange("p (n q) -> p n q", n=4), lhsT=ident[:],
                                         rhs=bc(m, 4), start=True, stop=False)
                                for r4 in range(4):
                                    i = e.matmul(ST[:, r4 * 128:(r4 + 1) * 128],
                                                 lhsT=KT[:, hh, 512 * cbk + r4:512 * (cbk + 1):4],
                                                 rhs=QT[:, hh, qb + r4:qb + 512:4], start=False, stop=(r4 == 3))
                            return i
                        sch.add("pe", f, reads=["QT", "KT", "m1g0a", "m1g0b"] + CONST, writes=[("bank", sbk)])
                        if ti < 2:
                            sch.add("act", lambda e: e.activation(out=PT[pts][:], in_=ST[:, :], func=AF.Exp, scale=SCALE),
                                    reads=[("bank", sbk)], writes=[("pt", pts)])
                        else:
                            sch.add("act", lambda e: e.activation(
                                out=PTP[pts - 3][:, :, 0:128],
                                in_=ST[:, :].rearrange("p (n q) -> p n q", n=4), func=AF.Exp, scale=SCALE),
                                reads=[("bank", sbk)], writes=[("pt", pts)])

                    def pv(ti, g=g, qb=qb, hh=hh, cb=cb, NUM=NUM, DEN=DEN, numr=numr, denr=denr):
                        kb = ti - 4
                        if ti < 2:
                            pts = tiles[ti]
                            P_ = PT[pts][:, :]
                            rds = [("pt", pts)]
                        elif ti < 4:
                            pts = tiles[ti]
                            P_ = PTP[pts - 3]
                            rds = [("pt", pts)]
                        else:
                            kb = ti - 4
                            P_ = PT3
                            rds = PT3R[kb]

                        def f(e):
                            if ti < 2:
                                e.matmul(DEN[:, :], lhsT=ones[:], rhs=P_, start=(ti == 0), stop=False)
                            elif ti < 4:
                                e.matmul(DEN[:, :], lhsT=ones[:], rhs=P_[:, :, 0:128].rearrange("p r i -> p i r"),
                                         start=False, stop=False)
                            else:
                                e.matmul(DEN[:, :], lhsT=ones[:],
                                         rhs=P_[:, kb, :, 32 * g:32 * g + 32].rearrange("p r j -> p j r"),
                                         start=False, stop=(ti == 5))
                            if ti == 0:
                                e.matmul(NUM[:, :], lhsT=zeros[:], rhs=P_, start=True, stop=False)
                            if ti < 2:
                                for j in range(4):
                                    nbk = 16 + 4 * g + j - (1 if ti == 0 else 0)
                                    i = e.matmul(NUM[:, j * 128:(j + 1) * 128], lhsT=Vblk[:, v1(nbk), :],
                                                 rhs=P_[:, j * 128:(j + 1) * 128], start=False, stop=False)
                            elif ti < 4:
                                cbk = cb if ti == 3 else cb - 1
                                for r4 in range(4):
                                    i = e.matmul(NUM[:, r4:512:4], lhsT=Vblk[:, v2(r4, cbk), :], rhs=P_[:, r4, 0:128],
                                                 start=False, stop=False)
                            else:
                                for r in range(16):
                                    i = e.matmul(NUM[:, r:512:16], lhsT=Vblk[:, v3(r, kb), :],
                                                 rhs=P_[:, kb, r, 32 * g:32 * g + 32],
                                                 start=False, stop=(ti == 5 and r == 15))
                            return i
                        sch.add("pe", f, reads=rds + ["Vblk", "ones", "zeros"], writes=[numr, denr])

                    qk(0); qk(1); qk(2); pv(0); qk(3); pv(1); pv(2); pv(3); pv(4); pv(5)

                    fs = gi % 2
                    sch.add("dve", lambda e, fs=fs, DEN=DEN: e.reciprocal(out=rden[fs][:], in_=DEN[:, :]),
                            reads=[denr], writes=[("rden", fs)])
                    sch.add("dve", lambda e, fs=fs, NUM=NUM: e.tensor_tensor(out=fin1[fs][:], in0=NUM[:, :], in1=rden[fs][:],
                                                                             op=ALU.mult),
                            reads=[numr, ("rden", fs)], writes=[("fin", fs)])
                    sch.add("dve", lambda e, fs=fs, h=h, hh=hh, qb=qb: e.tensor_tensor(
                        out=yT_attn[:, h, qb:qb + 512], in0=fin1[fs][:], in1=Gs[:, hh, qb:qb + 512], op=ALU.mult),
                        reads=[("fin", fs), "Gs"], writes=["yT_attn"])
        run_block(nc, sch)


def phase_b(nc, sch, xT_all, w_perm, w_pool, pscale, bm_d, yT_pool):
    import contextlib
    with contextlib.ExitStack() as es:
        sb = lambda name, shape, dtype: es.enter_context(nc.sbuf_tensor(name, shape, dtype))
        xc = [sb(f"b_xc{i}", [128, KC, 512], BF16) for i in range(2)]
        xh = sb("b_xh", [128, KC, 128], BF16)
        wB = sb("b_w", [128, KC, 1024], BF16)
        utm = [sb(f"b_utm{i}", [128, 512], BF16) for i in range(6)]
        Gp = [sb(f"b_gp{i}", [128, 4, 512], BF16) for i in range(2)]
        pooled = [sb(f"b_pl{i}", [128, 4, 512], BF16) for i in range(2)]
        wpl = sb("b_wpool", [128, 4, 2, 256], BF16)
        bmats = sb("b_bm", [128, 3, 4, 128], BF16)
        ps = sb("b_ps", [128, 8], F32)
        bank = [es.enter_context(nc.psum_tensor(f"b_b{i}", [128, 512], F32)) for i in range(8)]

        sch.add("pool", lambda e: e.dma_start(out=wpl[:], in_=w_pool.rearrange("g (cc p) d -> p g cc d", p=128)),
                writes=["wpl"], dma="constg")
        sch.add("pool", lambda e: e.dma_start(out=bmats[:].rearrange("p k g b -> p (k g b)"), in_=bm_d),
                writes=["bmats"], dma="constg", no_waw=True)
        sch.add("sp", lambda e: e.dma_start(out=ps[:], in_=pscale), writes=["ps"], dma="const")
        CONST = ["wpl", "bmats", "ps"]
        BM, BP, BF = 0, 1, 2
        cnt = {"xc": 0, "u": 0, "gp": 0, "pp": 0, "po": 0, "ch": 0}

        for bp in range(2):
            for piece in range(4):
                def f(e, piece=piece, bp=bp):
                    src = w_perm.rearrange("(kc p) c -> p kc c", p=128)[:, piece * 4:(piece + 1) * 4,
                                                                        4096 + bp * 1024:4096 + (bp + 1) * 1024]
                    return e.dma_start(out=wB[:, piece * 4:(piece + 1) * 4, :], in_=src)
                sch.add("pool", f, writes=["wB"], dma="wp", no_waw=(piece > 0))
            sch.add("pool", lambda e: e.dma_start(out=xh[:], in_=xchunk_src(xT_all, 1920, 128)),
                    writes=["xh"], dma="xh")
            uslot = {}

            def u_tile(j, lhs_fn, rd):
                b = cnt["u"] % 2
                cnt["u"] += 1
                s = (j + 1) % 6
                uslot[j] = s

                def f(e):
                    for kc in range(KC):
                        i = e.matmul(bank[b][:, :], lhsT=lhs_fn(kc), rhs=wB[:, kc, 0:512],
                                     start=(kc == 0), stop=(kc == KC - 1))
                    return i
                sch.add("pe", f, reads=[rd, "wB"], writes=[("bank", b)])
                sch.add("act", lambda e: e.copy(out=utm[s][:], in_=bank[b][:, :]),
                        reads=[("bank", b)], writes=[("utm", s)])

            u_tile(-1, lambda kc: xh[:, kc, :], "xh")
            for ci in range(4):
                slot = cnt["xc"] % 2
                cnt["xc"] += 1
                for half in range(2):
                    def f(e, slot=slot, ci=ci, half=half):
                        return e.dma_start(out=xc[slot][:, half * 8:(half + 1) * 8, :],
                                           in_=xchunk_src(xT_all, 2048 + ci * 512, 512)[:, half * 8:(half + 1) * 8, :])
                    sch.add("pool", f, writes=[("xc", slot)], dma=f"xc{slot}", no_waw=(half > 0))
                for tt in range(4):
                    u_tile(ci * 4 + tt, lambda kc, slot=slot, tt=tt: xc[slot][:, kc, tt * 128:(tt + 1) * 128], ("xc", slot))
                cs_ = cnt["ch"] % 2
                cnt["ch"] += 1
                for cbi in range(4):
                    b = 2 + cnt["gp"] % 2
                    cnt["gp"] += 1

                    def f(e, slot=slot, cbi=cbi, b=b):
                        for kc in range(KC):
                            i = e.matmul(bank[b][:, :], lhsT=wB[:, kc, 512 + cbi * 128:512 + (cbi + 1) * 128],
                                         rhs=xc[slot][:, kc, :], start=(kc == 0), stop=(kc == KC - 1))
                        return i
                    sch.add("pe", f, reads=[("xc", slot), "wB"], writes=[("bank", b)])
                    sch.add("act", lambda e, cbi=cbi, b=b, cs_=cs_: e.activation(out=Gp[cs_][:, cbi, :], in_=bank[b][:, :],
                                                                                 func=AF.Silu),
                            reads=[("bank", b)], writes=[("gp", cs_)])
                for cbl in range(4):
                    gq = 2 * bp + cbl // 2
                    b = 4 + cnt["pp"] % 2
                    cnt["pp"] += 1

                    def f(e, cbl=cbl, gq=gq, b=b, ci=ci, uslot=uslot):
                        for jj in range(4):
                            j = ci * 4 + jj
                            e.matmul(bank[b][:, jj * 128:(jj + 1) * 128], lhsT=utm[uslot[j - 1]][:, cbl * 128:(cbl + 1) * 128],
                                     rhs=bmats[:, BP, gq, :], start=True, stop=False, skip_group_check=True)
                            i = e.matmul(bank[b][:, jj * 128:(jj + 1) * 128], lhsT=utm[uslot[j]][:, cbl * 128:(cbl + 1) * 128],
                                         rhs=bmats[:, (BF if j == 0 else BM), gq, :], start=False, stop=(jj == 3),
                                         skip_group_check=True)
                        return i
                    rds = [("utm", uslot[ci * 4 + jj]) for jj in range(-1, 4)]
                    sch.add("pe", f, reads=rds + CONST, writes=[("bank", b)])
                    sch.add("dve", lambda e, cbl=cbl, b=b, cs_=cs_: e.tensor_copy(out=pooled[cs_][:, cbl, :], in_=bank[b][:, :]),
                            reads=[("bank", b)], writes=[("pooled", cs_)])
                for obl in range(4):
                    ob = 4 * bp + obl
                    gq = ob // 2
                    hf = obl % 2
                    b = 6 + cnt["po"] % 2
                    cnt["po"] += 1

                    def f(e, obl=obl, gq=gq, hf=hf, b=b, cs_=cs_):
                        for cc in range(2):
                            i = e.matmul(bank[b][:, :], lhsT=wpl[:, gq, cc, hf * 128:(hf + 1) * 128],
                                         rhs=pooled[cs_][:, (obl // 2) * 2 + cc, :], start=(cc == 0), stop=(cc == 1))
                        return i
                    sch.add("pe", f, reads=[("pooled", cs_)] + CONST, writes=[("bank", b)])
                    sch.add("dve", lambda e, obl=obl, ob=ob, b=b, cs_=cs_, ci=ci: e.scalar_tensor_tensor(
                        out=yT_pool[:, ob, ci * 512:(ci + 1) * 512], in0=bank[b][:, :], scalar=ps[:, ob:ob + 1],
                        in1=Gp[cs_][:, obl, :], op0=ALU.mult, op1=ALU.mult),
                        reads=[("bank", b), ("gp", cs_)] + CONST, writes=["yT_pool"])
        run_block(nc, sch)


def phase_c(nc, sch, x_own, w_out, ln_g, ln_b, out_d, yT_attn, yT_pool):
    import contextlib
    with contextlib.ExitStack() as es:
        sb = lambda name, shape, dtype: es.enter_context(nc.sbuf_tensor(name, shape, dtype))
        wo = sb("c_wo", [128, KC, D], BF16)
        xr = [sb(f"c_xr{i}", [128, D], F32) for i in range(2)]
        z = [sb(f"c_z{i}", [128, D], F32) for i in range(3)]
        junk = sb("c_junk", [128, D], BF16)
        gbc = sb("c_g", [128, D], F32)
        bbc = sb("c_b", [128, D], F32)
        st = [sb(f"c_st{i}", [128, 16], F32) for i in range(3)]
        bank = [es.enter_context(nc.psum_tensor(f"c_b{i}", [128, 512], F32)) for i in range(8)]

        for piece in range(8):
            def f(e, piece=piece):
                src = w_out.rearrange("(ec p) d -> p ec d", p=128)[:, piece * 2:(piece + 1) * 2, :]
                return e.dma_start(out=wo[:, piece * 2:(piece + 1) * 2, :], in_=src)
            sch.add("pool", f, writes=["wo"], dma="wp", no_waw=(piece > 0))
        sch.add("sp", lambda e: e.dma_start(out=gbc[:], in_=ln_g.partition_broadcast(128)), writes=["gbc"], dma="const")
        sch.add("sp", lambda e: e.dma_start(out=bbc[:], in_=ln_b.partition_broadcast(128)), writes=["bbc"], dma="const",
                no_waw=True)
        CONST = ["gbc", "bbc"]

        def load_x(j):
            s = j % 2
            sch.add("sp", lambda e: e.dma_start(out=xr[s][:], in_=x_own[j * 128:(j + 1) * 128, :]),
                    writes=[("xr", s)], dma=f"xr{s}")

        load_x(0)
        for j in range(16):
            s = j % 2
            zi = j % 3
            Z = z[zi]
            S = st[zi]
            if j + 1 < 16:
                load_x(j + 1)

            def f(e, j=j, s=s):
                for ec in range(KC):
                    lhs = (yT_attn[:, ec, j * 128:(j + 1) * 128] if ec < 8 else yT_pool[:, ec - 8, j * 128:(j + 1) * 128])
                    for nb in range(4):
                        i = e.matmul(bank[4 * s + nb][:, :], lhsT=lhs, rhs=wo[:, ec, nb * 512:(nb + 1) * 512],
                                     start=(ec == 0), stop=(ec == KC - 1))
                return i
            sch.add("pe", f, reads=["wo"], writes=[("bank", 4 * s + nb) for nb in range(4)])
            for nb in range(4):
                sch.add("dve", lambda e, s=s, nb=nb, Z=Z: e.scalar_tensor_tensor(
                    out=Z[:, nb * 512:(nb + 1) * 512], in0=xr[s][:, nb * 512:(nb + 1) * 512], scalar=ALPHA,
                    in1=bank[4 * s + nb][:, :], op0=ALU.mult, op1=ALU.add),
                    reads=[("xr", s), ("bank", 4 * s + nb)], writes=[("z", zi, nb)])
            zr = [("z", zi, nb) for nb in range(4)]
            sch.add("act", lambda e, Z=Z, S=S: e.activation(out=junk[:], in_=Z[:], func=AF.Identity, accum_out=S[:, 5:6]),
                    reads=zr, writes=["junk", ("tot", zi)])
            sch.add("act", lambda e, Z=Z, S=S: e.activation(out=junk[:], in_=Z[:], func=AF.Square, accum_out=S[:, 4:5]),
                    reads=zr, writes=["junk", ("s2", zi)])
            sch.add("dve", lambda e, S=S: e.tensor_scalar(out=S[:, 6:7], in0=S[:, 5:6], scalar1=1.0 / D, scalar2=None,
                                                          op0=ALU.mult),
                    reads=[("tot", zi)], writes=[("mean", zi)])
            sch.add("dve", lambda e, S=S: e.tensor_tensor(out=S[:, 7:8], in0=S[:, 6:7], in1=S[:, 6:7], op=ALU.mult),
                    reads=[("mean", zi)], writes=[("msq", zi)])
            sch.add("dve", lambda e, S=S: e.scalar_tensor_tensor(out=S[:, 8:9], in0=S[:, 4:5], scalar=1.0 / D, in1=S[:, 7:8],
                                                                 op0=ALU.mult, op1=ALU.subtract),
                    reads=[("s2", zi), ("msq", zi)], writes=[("var", zi)])
            sch.add("dve", lambda e, S=S: e.tensor_scalar(out=S[:, 9:10], in0=S[:, 8:9], scalar1=EPS, scalar2=None,
                                                          op0=ALU.add),
                    reads=[("var", zi)], writes=[("vare", zi)])
            sch.add("act", lambda e, S=S: e.activation(out=S[:, 10:11], in_=S[:, 9:10], func=AF.Sqrt),
                    reads=[("vare", zi)], writes=[("sd", zi)])
            sch.add("dve", lambda e, S=S: e.reciprocal(out=S[:, 11:12], in_=S[:, 10:11]),
                    reads=[("sd", zi)], writes=[("rstd", zi)])
            sch.add("dve", lambda e, S=S: e.scalar_tensor_tensor(out=S[:, 12:13], in0=S[:, 6:7], scalar=-1.0, in1=S[:, 11:12],
                                                                 op0=ALU.mult, op1=ALU.mult),
                    reads=[("mean", zi), ("rstd", zi)], writes=[("nbias", zi)])
            zall = ("zall", zi)
            sch.add("act", lambda e, Z=Z, S=S: e.activation(out=Z[:], in_=Z[:], func=AF.Identity,
                                                            scale=S[:, 11:12], bias=S[:, 12:13]),
                    reads=zr + [("rstd", zi), ("nbias", zi), ("tot", zi), ("s2", zi)], writes=zr + [zall])
            sch.add("pool", lambda e, Z=Z: e.tensor_tensor(out=Z[:], in0=Z[:], in1=gbc[:], op=ALU.mult),
                    reads=[zall] + CONST, writes=zr + [zall])
            sch.add("pool", lambda e, Z=Z: e.tensor_tensor(out=Z[:], in0=Z[:], in1=bbc[:], op=ALU.add),
                    reads=[zall] + CONST, writes=zr + [zall])
            sch.add("sp", lambda e, Z=Z, j=j: e.dma_start(out=out_d[j * 128:(j + 1) * 128, :], in_=Z[:]),
                    reads=[zall] + zr, writes=[("outd", j)], dma=f"out{zi}")
        sch.add("sp", None, reads=[("outd", 13), ("outd", 14), ("outd", 15)])
        run_block(nc, sch)


def _perm_cols():
    cols = []
    for p in range(4):
        for base in (0, 1024, 2048, 4096):
            for h in (2 * p, 2 * p + 1):
                cols.extend(range(base + h * 128, base + (h + 1) * 128))
    for bp in range(2):
        cols.extend(range(3072 + bp * 512, 3072 + (bp + 1) * 512))
        cols.extend(range(5120 + bp * 512, 5120 + (bp + 1) * 512))
    return np.asarray(cols, dtype=np.int64)


def _tables(half):
    lt = np.arange(LT, dtype=np.int64)
    pos = np.maximum(lt - 2048 + 2048 * half, 0).astype(np.float32)
    inv = (np.float32(500000.0) ** (-(2.0 * np.arange(16, dtype=np.float32)) / np.float32(32.0))).astype(np.float32)
    ang = (pos[:, None] * inv[None, :]).astype(np.float32)
    c, s = np.cos(ang).astype(np.float32), np.sin(ang).astype(np.float32)
    cs = np.concatenate([c, c, -s, s], axis=1).astype(np.float32)
    cs = np.ascontiguousarray(cs.reshape(32, 128, 64).transpose(1, 0, 2).reshape(128, 32 * 64))
    k = np.arange(128)[:, None]
    q = np.arange(128)[None, :]
    mcur = np.where(k <= q, 0.0, NEG).astype(np.float32)
    mprev = np.where(k >= q, 0.0, NEG).astype(np.float32)
    mprevh = mprev if half == 1 else np.full((128, 128), NEG, np.float32)
    masks = np.ascontiguousarray(np.stack([mcur, mprev, mprevh]).astype(np.float32).transpose(1, 0, 2).reshape(128, 384))
    bm = np.zeros((3, 4, 128, 128), np.float32)
    tp = np.arange(128)[:, None]
    t = np.arange(128)[None, :]
    for g, w in enumerate(POOL_W):
        main = ((tp <= t) & (tp > t - w)).astype(np.float32) / w - (tp == t)
        prev = ((tp - 128) > (t - w)).astype(np.float32) / w
        cntf = np.minimum(t + 1, w).astype(np.float32)
        first = ((tp <= t) & (tp > t - w)).astype(np.float32) / cntf - (tp == t)
        bm[0, g] = main
        bm[1, g] = prev
        bm[2, g] = first if half == 0 else main
    bm = np.ascontiguousarray(bm.transpose(2, 0, 1, 3).reshape(128, 3 * 4 * 128))
    return cs, masks, bm


_NC_CACHE = {}


def kernel(x, w_in, w_pool, pool_scale, w_out, ln_gain, ln_bias):
    x = np.asarray(x, dtype=np.float32)
    w_perm = np.ascontiguousarray(np.asarray(w_in, np.float32)[0][:, _perm_cols()])
    w_pool0 = np.ascontiguousarray(np.asarray(w_pool, np.float32)[0])
    pscale = np.ascontiguousarray(np.asarray(pool_scale, np.float32)[0].reshape(8, 128).T)
    w_out0 = np.ascontiguousarray(np.asarray(w_out, np.float32)[0])
    g0 = np.ascontiguousarray(np.asarray(ln_gain, np.float32)[0])
    b0 = np.ascontiguousarray(np.asarray(ln_bias, np.float32)[0])
    ident = np.eye(128, dtype=np.float32)
    if "nc" not in _NC_CACHE:
        _NC_CACHE["nc"] = build_program()
    nc = _NC_CACHE["nc"]
    in_maps = []
    for c in range(8):
        b, half = c // 2, c % 2
        own = x[b, half * T:(half + 1) * T, :]
        halo = x[b, 0:T, :] if half == 1 else np.zeros((T, D), np.float32)
        xT_all = np.ascontiguousarray(np.concatenate([halo, own], axis=0).T)
        cs, masks, bm = _tables(half)
        in_maps.append({
            "xT_all": xT_all, "x_own": np.ascontiguousarray(own), "w_perm": w_perm, "w_pool": w_pool0,
            "pscale": pscale, "w_out": w_out0, "ln_g": g0, "ln_b": b0, "cs": cs, "masks": masks,
            "ident": ident, "bmats": bm,
        })
    res = run_bass_kernel_spmd(nc, in_maps, core_ids=list(range(8)))
    out = np.empty((4, 4096, D), np.float32)
    for c in range(8):
        b, half = c // 2, c % 2
        out[b, half * T:(half + 1) * T, :] = res.results[c]["out"]
    return out
```

```python
import numpy as np
import concourse.bass as bass
import concourse.mybir as mybir
from concourse.bass_utils import run_bass_kernel_spmd

F32 = mybir.dt.float32
BF16 = mybir.dt.bfloat16
AF = mybir.ActivationFunctionType
ALU = mybir.AluOpType
AX = mybir.AxisListType

D = 2048
T = 2048
LT = 4096
KC = 16
NEG = -30000.0
ALPHA = 2.0 ** 0.25
EPS = 1e-5
SCALE = 128.0 ** -0.5
POOL_W = (2, 4, 8, 16)

COMPUTE = ("pe", "act", "dve", "pool")


class Op:
    __slots__ = ("eng", "fn", "deps", "is_dma", "sem", "val", "need_sig", "seq")


class Sched:
    def __init__(self, prog, counters, dma_sems, dma_cnt):
        self.prog = prog
        self.counters = counters
        self.dma_sems = dma_sems
        self.dma_cnt = dma_cnt
        self.ops = []
        self.lw = {}
        self.rd = {}

    def add(self, eng, fn, reads=(), writes=(), dma=None, no_waw=False):
        idx = len(self.ops)
        deps = {}
        for r in reads:
            w = self.lw.get(r)
            if w is not None:
                deps[w] = "raw"
        for w_ in writes:
            for ridx in self.rd.get(w_, {}).values():
                deps.setdefault(ridx, "war")
            lw = self.lw.get(w_)
            if lw is not None and not no_waw:
                deps.setdefault(lw, "waw")
        op = Op()
        op.eng = eng
        op.fn = fn
        op.deps = deps
        op.is_dma = dma is not None
        op.sem = dma
        op.val = 0
        op.need_sig = False
        op.seq = 0
        if dma is not None:
            self.dma_cnt[dma] += 16
            op.val = self.dma_cnt[dma]
        key = ("dma", dma) if dma is not None else eng
        for r in reads:
            self.rd.setdefault(r, {})[key] = idx
        for w_ in writes:
            self.lw[w_] = idx
            self.rd[w_] = {}
        self.ops.append(op)
        return idx

    def _needs_wait(self, op, dop, kind):
        if dop.is_dma:
            return True
        if dop.eng != op.eng:
            return True
        if op.is_dma:
            return True
        if kind in ("raw", "waw") and op.eng in ("act", "dve", "pool"):
            return True
        return False

    def finalize(self):
        for op in self.ops:
            for d, kind in op.deps.items():
                dop = self.ops[d]
                if not dop.is_dma and self._needs_wait(op, dop, kind):
                    dop.need_sig = True
        for op in self.ops:
            if op.need_sig:
                self.counters[op.eng] += 1
                op.seq = self.counters[op.eng]

    def emit(self, eng, eh):
        waited = {}
        for op in self.ops:
            if op.eng != eng:
                continue
            waits = {}
            for d, kind in op.deps.items():
                dop = self.ops[d]
                if not self._needs_wait(op, dop, kind):
                    continue
                if dop.is_dma:
                    k, v = ("d", dop.sem), dop.val
                else:
                    k, v = ("p", dop.eng), dop.seq
                if waits.get(k, 0) < v:
                    waits[k] = v
            for k, v in waits.items():
                if waited.get(k, 0) >= v:
                    continue
                waited[k] = v
                sem = self.dma_sems[k[1]] if k[0] == "d" else self.prog[k[1]]
                eh.wait_ge(sem, v)
            if op.fn is None:
                continue
            inst = op.fn(eh)
            if op.is_dma:
                inst.then_inc(self.dma_sems[op.sem], 16)
            elif op.need_sig:
                inst.then_inc(self.prog[eng], 1)


def run_block(nc, sch):
    sch.finalize()
    with nc.Block() as block:
        @block.tensor
        def _(e):
            sch.emit("pe", e)

        @block.scalar
        def _(e):
            sch.emit("act", e)

        @block.vector
        def _(e):
            sch.emit("dve", e)

        @block.gpsimd
        def _(e):
            sch.emit("pool", e)

        @block.sync
        def _(e):
            sch.emit("sp", e)


DMA_SEM_NAMES = ["const", "constg", "wp", "xc0", "xc1", "xh", "xr0", "xr1", "out0", "out1", "out2"]


def build_program():
    nc = bass.Bass("TRN2", target_bir_lowering=False)
    dt = lambda name, shape, kind="ExternalInput": nc.dram_tensor(name, shape, F32, kind=kind).ap()
    xT_all = dt("xT_all", [D, LT])
    x_own = dt("x_own", [T, D])
    w_perm = dt("w_perm", [D, 6144])
    w_pool = dt("w_pool", [4, 256, 256])
    pscale = dt("pscale", [128, 8])
    w_out = dt("w_out", [D, D])
    ln_g = dt("ln_g", [D])
    ln_b = dt("ln_b", [D])
    cs_d = dt("cs", [128, 32 * 64])
    mask_d = dt("masks", [128, 3 * 128])
    ident_d = dt("ident", [128, 128])
    bm_d = dt("bmats", [128, 3 * 4 * 128])
    out_d = dt("out", [T, D], kind="ExternalOutput")

    import contextlib
    es = contextlib.ExitStack()
    with es:
        prog = {e: es.enter_context(nc.semaphore("prog_" + e)) for e in COMPUTE}
        dma_sems = {n: es.enter_context(nc.semaphore("dma_" + n)) for n in DMA_SEM_NAMES}
        counters = {e: 0 for e in COMPUTE}
        dma_cnt = {n: 0 for n in DMA_SEM_NAMES}
        new_sched = lambda: Sched(prog, counters, dma_sems, dma_cnt)

        import os
        PH = os.environ.get("KPHASES", "abc")
        yT_attn = es.enter_context(nc.sbuf_tensor("yT_attn", [128, 8, T], BF16))
        if "a" in PH:
            phase_a(nc, new_sched(), xT_all, w_perm, cs_d, mask_d, ident_d, yT_attn)
        yT_pool = es.enter_context(nc.sbuf_tensor("yT_pool", [128, 8, T], BF16))
        if "b" in PH:
            phase_b(nc, new_sched(), xT_all, w_perm, w_pool, pscale, bm_d, yT_pool)
        if "c" in PH:
            phase_c(nc, new_sched(), x_own, w_out, ln_g, ln_b, out_d, yT_attn, yT_pool)
    return nc


def xchunk_src(xT_all, lt0, n):
    return xT_all.rearrange("(kc p) t -> p kc t", p=128)[:, :, lt0:lt0 + n]


def phase_a(nc, sch, xT_all, w_perm, cs_d, mask_d, ident_d, yT_attn):
    import contextlib
    with contextlib.ExitStack() as es:
        sb = lambda name, shape, dtype: es.enter_context(nc.sbuf_tensor(name, shape, dtype))
        xc = [sb(f"a_xc{i}", [128, KC, 512], BF16) for i in range(2)]
        wp = sb("a_wp", [128, KC, 1024], BF16)
        QT = sb("a_QT", [128, 2, T], BF16)
        KT = sb("a_KT", [128, 2, LT], BF16)
        VT = sb("a_VT", [128, 2, LT], BF16)
        Gs = sb("a_Gs", [128, 2, T], BF16)
        Vblk = sb("a_Vblk", [128, 69, 128], BF16)
        PT = [sb(f"a_PT{i}", [128, 512], BF16) for i in range(3)]
        PTP = [sb(f"a_PTP{i}", [128, 4, 130], BF16) for i in range(3)]
        PT3 = sb("a_PTh3", [128, 2, 16, 130], BF16)
        qktm = [sb(f"a_qktm{i}", [128, 4, 128], BF16) for i in range(2)]
        rt1 = [sb(f"a_rt1_{i}", [128, 4, 32], F32) for i in range(2)]
        rt2 = [sb(f"a_rt2_{i}", [128, 4, 32], F32) for i in range(2)]
        cs = sb("a_cs", [128, 32, 64], F32)
        masks = sb("a_masks", [128, 3, 128], BF16)
        ident = sb("a_ident", [128, 128], BF16)
        ones = sb("a_ones", [128, 128], BF16)
        zeros = sb("a_zeros", [128, 128], BF16)
        m1g0 = sb("a_m1g0", [128, 4, 128], BF16)
        rden = [sb(f"a_rden{i}", [128, 512], F32) for i in range(2)]
        lden = [sb(f"a_lden{i}", [128, 512], F32) for i in range(2)]
        bank = [es.enter_context(nc.psum_tensor(f"a_b{i}", [128, 512], F32)) for i in range(6)]
        tpb = [es.enter_context(nc.psum_tensor(f"a_tp{i}", [128, 1024], BF16)) for i in range(2)]

        MCUR, MPREV, MPREVH = 0, 1, 2

        sch.add("sp", lambda e: e.dma_start(out=cs[:].rearrange("p t c -> p (t c)"), in_=cs_d),
                writes=["cs"], dma="const")
        sch.add("pool", lambda e: e.dma_start(out=masks[:].rearrange("p m q -> p (m q)"), in_=mask_d),
                writes=["masks"], dma="constg")
        sch.add("pool", lambda e: e.dma_start(out=ident[:], in_=ident_d),
                writes=["ident"], dma="constg", no_waw=True)
        sch.add("dve", lambda e: e.memset(ones[:], 1.0), writes=["ones"])
        sch.add("dve", lambda e: e.memset(zeros[:], 0.0), writes=["zeros"])
        CONST = ["cs", "masks", "ident"]
        sch.add("dve", lambda e: e.tensor_copy(out=m1g0[:, 0, :], in_=masks[:, 2, :]), reads=CONST, writes=["m1g0a"])
        sch.add("dve", lambda e: e.tensor_copy(out=m1g0[:, 1:4, :],
                                               in_=masks[:, 1, :].unsqueeze(1).broadcast_to([128, 3, 128])),
                reads=CONST, writes=["m1g0b"])

        cnt = {"xc": 0, "pqk": 0, "pf": 0, "tp": 0, "qk": 0, "st": 0, "pt": 0, "ptp": 0, "grp": 0}
        PQK_BANKS = (0, 1, 2)
        PF_BANKS = (3, 4, 5)

        import os
        DBG_PAIRS = int(os.environ.get("KA_PAIRS", "4"))
        DBG_ATTN = int(os.environ.get("KA_ATTN", "9"))
        DBG_INPROJ = int(os.environ.get("KA_INPROJ", "1"))
        for p in range(DBG_PAIRS):
            for piece in range(4):
                def f(e, piece=piece, p=p):
                    src = w_perm.rearrange("(kc p) c -> p kc c", p=128)[:, piece * 4:(piece + 1) * 4,
                                                                        p * 1024:(p + 1) * 1024]
                    return e.dma_start(out=wp[:, piece * 4:(piece + 1) * 4, :], in_=src)
                sch.add("pool", f, writes=["wp"], dma="wp", no_waw=(piece > 0))

            pending = []

            def flush(keep):
                while len(pending) > keep:
                    pending.pop(0)()

            for ci in range(8 if DBG_INPROJ else 0):
                own = ci >= 4
                slot = cnt["xc"] % 2
                cnt["xc"] += 1
                for half in range(2):
                    def f(e, slot=slot, ci=ci, half=half):
                        return e.dma_start(out=xc[slot][:, half * 8:(half + 1) * 8, :],
                                           in_=xchunk_src(xT_all, ci * 512, 512)[:, half * 8:(half + 1) * 8, :])
                    sch.add("pool", f, writes=[("xc", slot)], dma=f"xc{slot}", no_waw=(half > 0))

                nb = 4 if own else 2
                c0 = 0 if own else 256
                for tt in range(4):
                    tl = ci * 4 + tt
                    b = PQK_BANKS[cnt["pqk"] % 3]
                    cnt["pqk"] += 1

                    def f(e, slot=slot, tt=tt, b=b, nb=nb, c0=c0):
                        for kc in range(KC):
                            i = e.matmul(bank[b][:, 0:nb * 128], lhsT=xc[slot][:, kc, tt * 128:(tt + 1) * 128],
                                         rhs=wp[:, kc, c0:c0 + nb * 128], start=(kc == 0), stop=(kc == KC - 1))
                        return i
                    sch.add("pe", f, reads=[("xc", slot), "wp"], writes=[("bank", b)])
                    qs = cnt["qk"] % 2
                    cnt["qk"] += 1
                    P = bank[b][:, 0:nb * 128].rearrange("p (n d) -> p n d", n=nb)
                    cc = cs[:, tl, 0:32].unsqueeze(1).broadcast_to([128, nb, 32])
                    sn = cs[:, tl, 32:48].unsqueeze(1).broadcast_to([128, nb, 16])
                    sp_ = cs[:, tl, 48:64].unsqueeze(1).broadcast_to([128, nb, 16])
                    sch.add("dve", lambda e, P=P, qs=qs, nb=nb, cc=cc: e.tensor_tensor(
                        out=rt1[qs][:, 0:nb, :], in0=P[:, :, 0:32], in1=cc, op=ALU.mult),
                        reads=[("bank", b)] + CONST, writes=[("rt1", qs)])
                    sch.add("dve", lambda e, P=P, qs=qs, nb=nb, sn=sn: e.tensor_tensor(
                        out=rt2[qs][:, 0:nb, 0:16], in0=P[:, :, 16:32], in1=sn, op=ALU.mult),
                        reads=[("bank", b)] + CONST, writes=[("rt2a", qs)])
                    sch.add("dve", lambda e, P=P, qs=qs, nb=nb, sp_=sp_: e.tensor_tensor(
                        out=rt2[qs][:, 0:nb, 16:32], in0=P[:, :, 0:16], in1=sp_, op=ALU.mult),
                        reads=[("bank", b)] + CONST, writes=[("rt2b", qs)])
                    sch.add("act", lambda e, P=P, qs=qs, nb=nb: e.copy(out=qktm[qs][:, 0:nb, 32:128], in_=P[:, :, 32:128]),
                            reads=[("bank", b), ("rt1", qs), ("rt2a", qs), ("rt2b", qs)], writes=[("qk_nr", qs)])
                    sch.add("dve", lambda e, qs=qs, nb=nb: e.tensor_tensor(
                        out=qktm[qs][:, 0:nb, 0:32], in0=rt1[qs][:, 0:nb, :], in1=rt2[qs][:, 0:nb, :], op=ALU.add),
                        reads=[("rt1", qs), ("rt2a", qs), ("rt2b", qs)], writes=[("qk_rot", qs)])

                    def tr(qs=qs, nb=nb, own=own, tl=tl):
                        tb = cnt["tp"] % 2
                        cnt["tp"] += 1

                        def f(e):
                            for blk in range(nb):
                                i = e.transpose(tpb[tb][:, blk * 128:(blk + 1) * 128], qktm[qs][:, blk, :], ident[:])
                            return i
                        sch.add("pe", f, reads=[("qk_nr", qs), ("qk_rot", qs)] + CONST, writes=[("tp", tb)])
                        if own:
                            q0 = (tl - 16) * 128
                            sch.add("act", lambda e: e.copy(out=QT[:, :, q0:q0 + 128],
                                                            in_=tpb[tb][:, 0:256].rearrange("p (n t) -> p n t", n=2)),
                                    reads=[("tp", tb)], writes=["QT"])
                            sch.add("act", lambda e: e.copy(out=KT[:, :, tl * 128:(tl + 1) * 128],
                                                            in_=tpb[tb][:, 256:512].rearrange("p (n t) -> p n t", n=2)),
                                    reads=[("tp", tb)], writes=["KT"])
                        else:
                            sch.add("act", lambda e: e.copy(out=KT[:, :, tl * 128:(tl + 1) * 128],
                                                            in_=tpb[tb][:, 0:256].rearrange("p (n t) -> p n t", n=2)),
                                    reads=[("tp", tb)], writes=["KT"])
                    pending.append(tr)
                    flush(1)

                for cb in range(4 if own else 2):
                    b = PF_BANKS[cnt["pf"] % 3]
                    cnt["pf"] += 1

                    def f(e, slot=slot, cb=cb, b=b):
                        for kc in range(KC):
                            i = e.matmul(bank[b][:, :], lhsT=wp[:, kc, 512 + cb * 128:512 + (cb + 1) * 128],
                                         rhs=xc[slot][:, kc, :], start=(kc == 0), stop=(kc == KC - 1))
                        return i
                    sch.add("pe", f, reads=[("xc", slot), "wp"], writes=[("bank", b)])
                    if cb < 2:
                        sch.add("dve", lambda e, cb=cb, b=b, ci=ci: e.tensor_copy(
                            out=VT[:, cb, ci * 512:(ci + 1) * 512], in_=bank[b][:, :]),
                            reads=[("bank", b)], writes=["VT"])
                    else:
                        sch.add("act", lambda e, cb=cb, b=b, ci=ci: e.activation(
                            out=Gs[:, cb - 2, (ci - 4) * 512:(ci - 3) * 512], in_=bank[b][:, :], func=AF.Silu),
                            reads=[("bank", b)], writes=["Gs"])
                    flush(0)
            flush(0)

            for hh in range(2 if DBG_ATTN else 0):
                h = 2 * p + hh
                vsrc = []
                for nbk in range(15, 32):
                    vsrc.append(VT[:, hh, nbk * 128:(nbk + 1) * 128])
                for r4 in range(4):
                    for cbk in range(3, 8):
                        vsrc.append(VT[:, hh, 512 * cbk + r4:512 * (cbk + 1):4])
                for r in range(16):
                    for kb in range(2):
                        vsrc.append(VT[:, hh, 2048 * kb + r:2048 * (kb + 1):16])
                v1 = lambda nbk: nbk - 15
                v2 = lambda r4, cbk: 17 + r4 * 5 + (cbk - 3)
                v3 = lambda r, kb: 37 + r * 2 + kb
                for i0 in range(0, 69, 8):
                    n = min(8, 69 - i0)
                    tb = cnt["tp"] % 2
                    cnt["tp"] += 1

                    def f(e, i0=i0, n=n, tb=tb, vsrc=vsrc):
                        for k in range(n):
                            i = e.transpose(tpb[tb][:, k * 128:(k + 1) * 128], vsrc[i0 + k], ident[:])
                        return i
                    sch.add("pe", f, reads=["VT"] + CONST, writes=[("tp", tb)])
                    sch.add("dve", lambda e, i0=i0, n=n, tb=tb: e.tensor_copy(
                        out=Vblk[:, i0:i0 + n, :], in_=tpb[tb][:, 0:n * 128].rearrange("p (n d) -> p n d", n=n)),
                        reads=[("tp", tb)], writes=["Vblk"])

                for rq in range(4 if DBG_ATTN > 1 else 0):
                    for kb in range(2):
                        sbk = cnt["st"] % 2
                        cnt["st"] += 1
                        ST = bank[sbk]

                        def f(e, rq=rq, kb=kb, ST=ST, hh=hh):
                            m = MCUR if kb == 1 else MPREVH
                            e.matmul(ST[:, :].rearrange("p (n q) -> p n q", n=4), lhsT=ident[:],
                                     rhs=masks[:, m, :].unsqueeze(1).broadcast_to([128, 4, 128]), start=True, stop=False)
                            for rr in range(4):
                                r = 4 * rq + rr
                                i = e.matmul(ST[:, rr * 128:(rr + 1) * 128],
                                             lhsT=KT[:, hh, 2048 * kb + r:2048 * (kb + 1):16],
                                             rhs=QT[:, hh, r:2048:16], start=False, stop=(rr == 3))
                            return i
                        sch.add("pe", f, reads=["QT", "KT"] + CONST, writes=[("bank", sbk)])
                        sch.add("act", lambda e, rq=rq, kb=kb, ST=ST: e.activation(
                            out=PT3[:, kb, 4 * rq:4 * rq + 4, 0:128],
                            in_=ST[:, :].rearrange("p (n q) -> p n q", n=4), func=AF.Exp, scale=SCALE),
                            reads=[("bank", sbk)], writes=[("pt3", kb, rq)])
                PT3R = [[("pt3", kb, rq) for rq in range(4)] for kb in range(2)]

                for g in range(4 if DBG_ATTN > 1 else 0):
                    qb = 512 * g
                    gi = cnt["grp"]
                    cnt["grp"] += 1
                    NUM = bank[2 + gi % 2]
                    DEN = bank[4 + gi % 2]
                    numr = ("bank", 2 + gi % 2)
                    denr = ("bank", 4 + gi % 2)
                    cb = 4 + g
                    tiles = []

                    def qk(ti, g=g, qb=qb, hh=hh, cb=cb):
                        sbk = cnt["st"] % 2
                        cnt["st"] += 1
                        pts = cnt["pt"] % 3 if ti < 2 else 3 + cnt["ptp"] % 3
                        cnt["pt" if ti < 2 else "ptp"] += 1
                        tiles.append(pts)
                        ST = bank[sbk]

                        def f(e):
                            bc = lambda m, n_: masks[:, m, :].unsqueeze(1).broadcast_to([128, n_, 128])
                            if ti == 0:
                                e.matmul(ST[:, :].rearrange("p (n q) -> p n q", n=4), lhsT=ident[:],
                                         rhs=(m1g0[:] if g == 0 else bc(MPREV, 4)), start=True, stop=False)
                                for j in range(4):
                                    nbk = 16 + 4 * g + j - 1
                                    i = e.matmul(ST[:, j * 128:(j + 1) * 128], lhsT=KT[:, hh, nbk * 128:(nbk + 1) * 128],
                                                 rhs=QT[:, hh, qb + j * 128:qb + (j + 1) * 128], start=False, stop=(j == 3))
                            elif ti == 1:
                                e.matmul(ST[:, :].rearrange("p (n q) -> p n q", n=4), lhsT=ident[:],
                                         rhs=bc(MCUR, 4), start=True, stop=False)
                                for j in range(4):
                                    nbk = 16 + 4 * g + j
                                    i = e.matmul(ST[:, j * 128:(j + 1) * 128], lhsT=KT[:, hh, nbk * 128:(nbk + 1) * 128],
                                                 rhs=QT[:, hh, qb + j * 128:qb + (j + 1) * 128], start=False, stop=(j == 3))
                            else:
                                m = MCUR if ti == 3 else (MPREVH if g == 0 else MPREV)
                                cbk = cb if ti == 3 else cb - 1
                                e.matmul(ST[:, :].rearrange("p (n q) -> p n q", n=4), lhsT=ident[:],
                                         rhs=bc(m, 4), start=True, stop=False)
                                for r4 in range(4):
                                    i = e.matmul(ST[:, r4 * 128:(r4 + 1) * 128],
                                                 lhsT=KT[:, hh, 512 * cbk + r4:512 * (cbk + 1):4],
                                                 rhs=QT[:, hh, qb + r4:qb + 512:4], start=False, stop=(r4 == 3))
                            return i
                        sch.add("pe", f, reads=["QT", "KT", "m1g0a", "m1g0b"] + CONST, writes=[("bank", sbk)])
                        if ti < 2:
                            sch.add("act", lambda e: e.activation(out=PT[pts][:], in_=ST[:, :], func=AF.Exp, scale=SCALE),
                                    reads=[("bank", sbk)], writes=[("pt", pts)])
                        else:
                            sch.add("act", lambda e: e.activation(
                                out=PTP[pts - 3][:, :, 0:128],
                                in_=ST[:, :].rearrange("p (n q) -> p n q", n=4), func=AF.Exp, scale=SCALE),
                                reads=[("bank", sbk)], writes=[("pt", pts)])

                    def pv(ti, g=g, qb=qb, hh=hh, cb=cb, NUM=NUM, DEN=DEN, numr=numr, denr=denr):
                        kb = ti - 4
                        if ti < 2:
                            pts = tiles[ti]
                            P_ = PT[pts][:, :]
                            rds = [("pt", pts)]
                        elif ti < 4:
                            pts = tiles[ti]
                            P_ = PTP[pts - 3]
                            rds = [("pt", pts)]
                        else:
                            kb = ti - 4
                            P_ = PT3
                            rds = PT3R[kb]

                        def f(e):
                            if ti < 2:
                                e.matmul(DEN[:, :], lhsT=ones[:], rhs=P_, start=(ti == 0), stop=False)
                            elif ti < 4:
                                e.matmul(DEN[:, :], lhsT=ones[:], rhs=P_[:, :, 0:128].rearrange("p r i -> p i r"),
                                         start=False, stop=(ti == 3))
                            else:
                                e.matmul(DEN[:, :], lhsT=ones[:],
                                         rhs=P_[:, kb, :, 32 * g:32 * g + 32].rearrange("p r j -> p j r"),
                                         start=False, stop=False)
                            if ti == 0:
                                e.matmul(NUM[:, :], lhsT=zeros[:], rhs=P_, start=True, stop=False)
                            if ti < 2:
                                for j in range(4):
                                    nbk = 16 + 4 * g + j - (1 if ti == 0 else 0)
                                    i = e.matmul(NUM[:, j * 128:(j + 1) * 128], lhsT=Vblk[:, v1(nbk), :],
                                                 rhs=P_[:, j * 128:(j + 1) * 128], start=False, stop=False)
                            elif ti < 4:
                                cbk = cb if ti == 3 else cb - 1
                                for r4 in range(4):
                                    i = e.matmul(NUM[:, r4:512:4], lhsT=Vblk[:, v2(r4, cbk), :], rhs=P_[:, r4, 0:128],
                                                 start=False, stop=(ti == 3 and r4 == 3))
                            else:
                                for r in range(16):
                                    i = e.matmul(NUM[:, r:512:16], lhsT=Vblk[:, v3(r, kb), :],
                                                 rhs=P_[:, kb, r, 32 * g:32 * g + 32],
                                                 start=False, stop=False)
                            return i
                        sch.add("pe", f, reads=rds + ["Vblk", "ones", "zeros"], writes=[numr, denr])

                    qk(0); qk(1); pv(0); qk(2); pv(1); qk(3); pv(4); pv(2); pv(5); pv(3)

                    fs = gi % 2
                    sch.add("act", lambda e, fs=fs, DEN=DEN: e.activation(out=lden[fs][:], in_=DEN[:, :], func=AF.Ln),
                            reads=[denr], writes=[("lden", fs)])
                    sch.add("act", lambda e, fs=fs: e.activation(out=rden[fs][:], in_=lden[fs][:], func=AF.Exp, scale=-1.0),
                            reads=[("lden", fs)], writes=[("rden", fs)])
                    sch.add("dve", lambda e, fs=fs, NUM=NUM: e.tensor_tensor(out=lden[fs][:], in0=NUM[:, :], in1=rden[fs][:],
                                                                             op=ALU.mult),
                            reads=[numr, ("rden", fs)], writes=[("lden", fs)])
                    sch.add("dve", lambda e, fs=fs, h=h, hh=hh, qb=qb: e.tensor_tensor(
                        out=yT_attn[:, h, qb:qb + 512], in0=lden[fs][:], in1=Gs[:, hh, qb:qb + 512], op=ALU.mult),
                        reads=[("lden", fs), "Gs"], writes=["yT_attn"])
        run_block(nc, sch)


def phase_b(nc, sch, xT_all, w_perm, w_pool, pscale, bm_d, yT_pool):
    import contextlib
    with contextlib.ExitStack() as es:
        sb = lambda name, shape, dtype: es.enter_context(nc.sbuf_tensor(name, shape, dtype))
        xc = [sb(f"b_xc{i}", [128, KC, 512], BF16) for i in range(2)]
        xh = sb("b_xh", [128, KC, 128], BF16)
        wB = sb("b_w", [128, KC, 1024], BF16)
        utm = [sb(f"b_utm{i}", [128, 512], BF16) for i in range(6)]
        Gp = [sb(f"b_gp{i}", [128, 4, 512], BF16) for i in range(2)]
        pooled = [sb(f"b_pl{i}", [128, 4, 512], BF16) for i in range(2)]
        wpl = sb("b_wpool", [128, 4, 2, 256], BF16)
        bmats = sb("b_bm", [128, 3, 4, 128], BF16)
        ps = sb("b_ps", [128, 8], F32)
        bank = [es.enter_context(nc.psum_tensor(f"b_b{i}", [128, 512], F32)) for i in range(8)]

        sch.add("pool", lambda e: e.dma_start(out=wpl[:], in_=w_pool.rearrange("g (cc p) d -> p g cc d", p=128)),
                writes=["wpl"], dma="constg")
        sch.add("pool", lambda e: e.dma_start(out=bmats[:].rearrange("p k g b -> p (k g b)"), in_=bm_d),
                writes=["bmats"], dma="constg", no_waw=True)
        sch.add("sp", lambda e: e.dma_start(out=ps[:], in_=pscale), writes=["ps"], dma="const")
        CONST = ["wpl", "bmats", "ps"]
        BM, BP, BF = 0, 1, 2
        cnt = {"xc": 0, "u": 0, "gp": 0, "pp": 0, "po": 0, "ch": 0}

        for bp in range(2):
            for piece in range(4):
                def f(e, piece=piece, bp=bp):
                    src = w_perm.rearrange("(kc p) c -> p kc c", p=128)[:, piece * 4:(piece + 1) * 4,
                                                                        4096 + bp * 1024:4096 + (bp + 1) * 1024]
                    return e.dma_start(out=wB[:, piece * 4:(piece + 1) * 4, :], in_=src)
                sch.add("pool", f, writes=["wB"], dma="wp", no_waw=(piece > 0))
            sch.add("pool", lambda e: e.dma_start(out=xh[:], in_=xchunk_src(xT_all, 1920, 128)),
                    writes=["xh"], dma="xh")
            uslot = {}

            def u_tile(j, lhs_fn, rd):
                b = cnt["u"] % 2
                cnt["u"] += 1
                s = (j + 1) % 6
                uslot[j] = s

                def f(e):
                    for kc in range(KC):
                        i = e.matmul(bank[b][:, :], lhsT=lhs_fn(kc), rhs=wB[:, kc, 0:512],
                                     start=(kc == 0), stop=(kc == KC - 1))
                    return i
                sch.add("pe", f, reads=[rd, "wB"], writes=[("bank", b)])
                sch.add("act", lambda e: e.copy(out=utm[s][:], in_=bank[b][:, :]),
                        reads=[("bank", b)], writes=[("utm", s)])

            u_tile(-1, lambda kc: xh[:, kc, :], "xh")
            for ci in range(4):
                slot = cnt["xc"] % 2
                cnt["xc"] += 1
                for half in range(2):
                    def f(e, slot=slot, ci=ci, half=half):
                        return e.dma_start(out=xc[slot][:, half * 8:(half + 1) * 8, :],
                                           in_=xchunk_src(xT_all, 2048 + ci * 512, 512)[:, half * 8:(half + 1) * 8, :])
                    sch.add("pool", f, writes=[("xc", slot)], dma=f"xc{slot}", no_waw=(half > 0))
                for tt in range(4):
                    u_tile(ci * 4 + tt, lambda kc, slot=slot, tt=tt: xc[slot][:, kc, tt * 128:(tt + 1) * 128], ("xc", slot))
                cs_ = cnt["ch"] % 2
                cnt["ch"] += 1
                for cbi in range(4):
                    b = 2 + cnt["gp"] % 2
                    cnt["gp"] += 1

                    def f(e, slot=slot, cbi=cbi, b=b):
                        for kc in range(KC):
                            i = e.matmul(bank[b][:, :], lhsT=wB[:, kc, 512 + cbi * 128:512 + (cbi + 1) * 128],
                                         rhs=xc[slot][:, kc, :], start=(kc == 0), stop=(kc == KC - 1))
                        return i
                    sch.add("pe", f, reads=[("xc", slot), "wB"], writes=[("bank", b)])
                    sch.add("act", lambda e, cbi=cbi, b=b, cs_=cs_: e.activation(out=Gp[cs_][:, cbi, :], in_=bank[b][:, :],
                                                                                 func=AF.Silu),
                            reads=[("bank", b)], writes=[("gp", cs_)])
                for cbl in range(4):
                    gq = 2 * bp + cbl // 2
                    b = 4 + cnt["pp"] % 2
                    cnt["pp"] += 1

                    def f(e, cbl=cbl, gq=gq, b=b, ci=ci, uslot=uslot):
                        for jj in range(4):
                            j = ci * 4 + jj
                            e.matmul(bank[b][:, jj * 128:(jj + 1) * 128], lhsT=utm[uslot[j - 1]][:, cbl * 128:(cbl + 1) * 128],
                                     rhs=bmats[:, BP, gq, :], start=True, stop=False, skip_group_check=True)
                            i = e.matmul(bank[b][:, jj * 128:(jj + 1) * 128], lhsT=utm[uslot[j]][:, cbl * 128:(cbl + 1) * 128],
                                         rhs=bmats[:, (BF if j == 0 else BM), gq, :], start=False, stop=(jj == 3),
                                         skip_group_check=True)
                        return i
                    rds = [("utm", uslot[ci * 4 + jj]) for jj in range(-1, 4)]
                    sch.add("pe", f, reads=rds + CONST, writes=[("bank", b)])
                    sch.add("dve", lambda e, cbl=cbl, b=b, cs_=cs_: e.tensor_copy(out=pooled[cs_][:, cbl, :], in_=bank[b][:, :]),
                            reads=[("bank", b)], writes=[("pooled", cs_)])
                for obl in range(4):
                    ob = 4 * bp + obl
                    gq = ob // 2
                    hf = obl % 2
                    b = 6 + cnt["po"] % 2
                    cnt["po"] += 1

                    def f(e, obl=obl, gq=gq, hf=hf, b=b, cs_=cs_):
                        for cc in range(2):
                            i = e.matmul(bank[b][:, :], lhsT=wpl[:, gq, cc, hf * 128:(hf + 1) * 128],
                                         rhs=pooled[cs_][:, (obl // 2) * 2 + cc, :], start=(cc == 0), stop=(cc == 1))
                        return i
                    sch.add("pe", f, reads=[("pooled", cs_)] + CONST, writes=[("bank", b)])
                    sch.add("dve", lambda e, obl=obl, ob=ob, b=b, cs_=cs_, ci=ci: e.scalar_tensor_tensor(
                        out=yT_pool[:, ob, ci * 512:(ci + 1) * 512], in0=bank[b][:, :], scalar=ps[:, ob:ob + 1],
                        in1=Gp[cs_][:, obl, :], op0=ALU.mult, op1=ALU.mult),
                        reads=[("bank", b), ("gp", cs_)] + CONST, writes=["yT_pool"])
        run_block(nc, sch)


def phase_c(nc, sch, x_own, w_out, ln_g, ln_b, out_d, yT_attn, yT_pool):
    import contextlib
    with contextlib.ExitStack() as es:
        sb = lambda name, shape, dtype: es.enter_context(nc.sbuf_tensor(name, shape, dtype))
        wo = sb("c_wo", [128, KC, D], BF16)
        xr = [sb(f"c_xr{i}", [128, D], F32) for i in range(2)]
        z = [sb(f"c_z{i}", [128, D], F32) for i in range(3)]
        junk = sb("c_junk", [128, D], BF16)
        gbc = sb("c_g", [128, D], F32)
        bbc = sb("c_b", [128, D], F32)
        st = [sb(f"c_st{i}", [128, 16], F32) for i in range(3)]
        bank = [es.enter_context(nc.psum_tensor(f"c_b{i}", [128, 512], F32)) for i in range(8)]

        for piece in range(8):
            def f(e, piece=piece):
                src = w_out.rearrange("(ec p) d -> p ec d", p=128)[:, piece * 2:(piece + 1) * 2, :]
                return e.dma_start(out=wo[:, piece * 2:(piece + 1) * 2, :], in_=src)
            sch.add("pool", f, writes=["wo"], dma="wp", no_waw=(piece > 0))
        sch.add("sp", lambda e: e.dma_start(out=gbc[:], in_=ln_g.partition_broadcast(128)), writes=["gbc"], dma="const")
        sch.add("sp", lambda e: e.dma_start(out=bbc[:], in_=ln_b.partition_broadcast(128)), writes=["bbc"], dma="const",
                no_waw=True)
        CONST = ["gbc", "bbc"]

        def load_x(j):
            s = j % 2
            sch.add("sp", lambda e: e.dma_start(out=xr[s][:], in_=x_own[j * 128:(j + 1) * 128, :]),
                    writes=[("xr", s)], dma=f"xr{s}")

        load_x(0)
        for j in range(16):
            s = j % 2
            zi = j % 3
            Z = z[zi]
            S = st[zi]
            if j + 1 < 16:
                load_x(j + 1)

            def f(e, j=j, s=s):
                for ec in range(KC):
                    lhs = (yT_attn[:, ec, j * 128:(j + 1) * 128] if ec < 8 else yT_pool[:, ec - 8, j * 128:(j + 1) * 128])
                    for nb in range(4):
                        i = e.matmul(bank[4 * s + nb][:, :], lhsT=lhs, rhs=wo[:, ec, nb * 512:(nb + 1) * 512],
                                     start=(ec == 0), stop=(ec == KC - 1))
                return i
            sch.add("pe", f, reads=["wo"], writes=[("bank", 4 * s + nb) for nb in range(4)])
            for nb in range(4):
                sch.add("dve", lambda e, s=s, nb=nb, Z=Z: e.scalar_tensor_tensor(
                    out=Z[:, nb * 512:(nb + 1) * 512], in0=xr[s][:, nb * 512:(nb + 1) * 512], scalar=ALPHA,
                    in1=bank[4 * s + nb][:, :], op0=ALU.mult, op1=ALU.add),
                    reads=[("xr", s), ("bank", 4 * s + nb)], writes=[("z", zi, nb)])
            zr = [("z", zi, nb) for nb in range(4)]
            sch.add("act", lambda e, Z=Z, S=S: e.activation(out=junk[:], in_=Z[:], func=AF.Identity, accum_out=S[:, 5:6]),
                    reads=zr, writes=["junk", ("tot", zi)])
            sch.add("act", lambda e, Z=Z, S=S: e.activation(out=junk[:], in_=Z[:], func=AF.Square, accum_out=S[:, 4:5]),
                    reads=zr, writes=["junk", ("s2", zi)])
            sch.add("dve", lambda e, S=S: e.tensor_scalar(out=S[:, 6:7], in0=S[:, 5:6], scalar1=1.0 / D, scalar2=None,
                                                          op0=ALU.mult),
                    reads=[("tot", zi)], writes=[("mean", zi)])
            sch.add("dve", lambda e, S=S: e.tensor_tensor(out=S[:, 7:8], in0=S[:, 6:7], in1=S[:, 6:7], op=ALU.mult),
                    reads=[("mean", zi)], writes=[("msq", zi)])
            sch.add("dve", lambda e, S=S: e.scalar_tensor_tensor(out=S[:, 8:9], in0=S[:, 4:5], scalar=1.0 / D, in1=S[:, 7:8],
                                                                 op0=ALU.mult, op1=ALU.subtract),
                    reads=[("s2", zi), ("msq", zi)], writes=[("var", zi)])
            sch.add("dve", lambda e, S=S: e.tensor_scalar(out=S[:, 9:10], in0=S[:, 8:9], scalar1=EPS, scalar2=None,
                                                          op0=ALU.add),
                    reads=[("var", zi)], writes=[("vare", zi)])
            sch.add("act", lambda e, S=S: e.activation(out=S[:, 10:11], in_=S[:, 9:10], func=AF.Sqrt),
                    reads=[("vare", zi)], writes=[("sd", zi)])
            sch.add("dve", lambda e, S=S: e.reciprocal(out=S[:, 11:12], in_=S[:, 10:11]),
                    reads=[("sd", zi)], writes=[("rstd", zi)])
            sch.add("dve", lambda e, S=S: e.scalar_tensor_tensor(out=S[:, 12:13], in0=S[:, 6:7], scalar=-1.0, in1=S[:, 11:12],
                                                                 op0=ALU.mult, op1=ALU.mult),
                    reads=[("mean", zi), ("rstd", zi)], writes=[("nbias", zi)])
            zall = ("zall", zi)
            sch.add("act", lambda e, Z=Z, S=S: e.activation(out=Z[:], in_=Z[:], func=AF.Identity,
                                                            scale=S[:, 11:12], bias=S[:, 12:13]),
                    reads=zr + [("rstd", zi), ("nbias", zi), ("tot", zi), ("s2", zi)], writes=zr + [zall])
            sch.add("pool", lambda e, Z=Z: e.tensor_tensor(out=Z[:], in0=Z[:], in1=gbc[:], op=ALU.mult),
                    reads=[zall] + CONST, writes=zr + [zall])
            sch.add("pool", lambda e, Z=Z: e.tensor_tensor(out=Z[:], in0=Z[:], in1=bbc[:], op=ALU.add),
                    reads=[zall] + CONST, writes=zr + [zall])
            sch.add("sp", lambda e, Z=Z, j=j: e.dma_start(out=out_d[j * 128:(j + 1) * 128, :], in_=Z[:]),
                    reads=[zall] + zr, writes=[("outd", j)], dma=f"out{zi}")
        sch.add("sp", None, reads=[("outd", 13), ("outd", 14), ("outd", 15)])
        run_block(nc, sch)


def _perm_cols():
    cols = []
    for p in range(4):
        for base in (0, 1024, 2048, 4096):
            for h in (2 * p, 2 * p + 1):
                cols.extend(range(base + h * 128, base + (h + 1) * 128))
    for bp in range(2):
        cols.extend(range(3072 + bp * 512, 3072 + (bp + 1) * 512))
        cols.extend(range(5120 + bp * 512, 5120 + (bp + 1) * 512))
    return np.asarray(cols, dtype=np.int64)


def _tables(half):
    lt = np.arange(LT, dtype=np.int64)
    pos = np.maximum(lt - 2048 + 2048 * half, 0).astype(np.float32)
    inv = (np.float32(500000.0) ** (-(2.0 * np.arange(16, dtype=np.float32)) / np.float32(32.0))).astype(np.float32)
    ang = (pos[:, None] * inv[None, :]).astype(np.float32)
    c, s = np.cos(ang).astype(np.float32), np.sin(ang).astype(np.float32)
    cs = np.concatenate([c, c, -s, s], axis=1).astype(np.float32)
    cs = np.ascontiguousarray(cs.reshape(32, 128, 64).transpose(1, 0, 2).reshape(128, 32 * 64))
    k = np.arange(128)[:, None]
    q = np.arange(128)[None, :]
    mcur = np.where(k <= q, 0.0, NEG).astype(np.float32)
    mprev = np.where(k >= q, 0.0, NEG).astype(np.float32)
    mprevh = mprev if half == 1 else np.full((128, 128), NEG, np.float32)
    masks = np.ascontiguousarray(np.stack([mcur, mprev, mprevh]).astype(np.float32).transpose(1, 0, 2).reshape(128, 384))
    bm = np.zeros((3, 4, 128, 128), np.float32)
    tp = np.arange(128)[:, None]
    t = np.arange(128)[None, :]
    for g, w in enumerate(POOL_W):
        main = ((tp <= t) & (tp > t - w)).astype(np.float32) / w - (tp == t)
        prev = ((tp - 128) > (t - w)).astype(np.float32) / w
        cntf = np.minimum(t + 1, w).astype(np.float32)
        first = ((tp <= t) & (tp > t - w)).astype(np.float32) / cntf - (tp == t)
        bm[0, g] = main
        bm[1, g] = prev
        bm[2, g] = first if half == 0 else main
    bm = np.ascontiguousarray(bm.transpose(2, 0, 1, 3).reshape(128, 3 * 4 * 128))
    return cs, masks, bm


_NC_CACHE = {}


def kernel(x, w_in, w_pool, pool_scale, w_out, ln_gain, ln_bias):
    x = np.asarray(x, dtype=np.float32)
    w_perm = np.ascontiguousarray(np.asarray(w_in, np.float32)[0][:, _perm_cols()])
    w_pool0 = np.ascontiguousarray(np.asarray(w_pool, np.float32)[0])
    pscale = np.ascontiguousarray(np.asarray(pool_scale, np.float32)[0].reshape(8, 128).T)
    w_out0 = np.ascontiguousarray(np.asarray(w_out, np.float32)[0])
    g0 = np.ascontiguousarray(np.asarray(ln_gain, np.float32)[0])
    b0 = np.ascontiguousarray(np.asarray(ln_bias, np.float32)[0])
    ident = np.eye(128, dtype=np.float32)
    if "nc" not in _NC_CACHE:
        _NC_CACHE["nc"] = build_program()
    nc = _NC_CACHE["nc"]
    in_maps = []
    for c in range(8):
        b, half = c // 2, c % 2
        own = x[b, half * T:(half + 1) * T, :]
        halo = x[b, 0:T, :] if half == 1 else np.zeros((T, D), np.float32)
        xT_all = np.ascontiguousarray(np.concatenate([halo, own], axis=0).T)
        cs, masks, bm = _tables(half)
        in_maps.append({
            "xT_all": xT_all, "x_own": np.ascontiguousarray(own), "w_perm": w_perm, "w_pool": w_pool0,
            "pscale": pscale, "w_out": w_out0, "ln_g": g0, "ln_b": b0, "cs": cs, "masks": masks,
            "ident": ident, "bmats": bm,
        })
    res = run_bass_kernel_spmd(nc, in_maps, core_ids=list(range(8)))
    out = np.empty((4, 4096, D), np.float32)
    for c in range(8):
        b, half = c // 2, c % 2
        out[b, half * T:(half + 1) * T, :] = res.results[c]["out"]
    return out
```

```python
import numpy as np
import concourse.bass as bass
import concourse.mybir as mybir
from concourse.bass_utils import run_bass_kernel_spmd

F32 = mybir.dt.float32
BF16 = mybir.dt.bfloat16
AF = mybir.ActivationFunctionType
ALU = mybir.AluOpType
AX = mybir.AxisListType

D = 2048
T = 2048
LT = 4096
KC = 16
NEG = -30000.0
ALPHA = 2.0 ** 0.25
EPS = 1e-5
SCALE = 128.0 ** -0.5
POOL_W = (2, 4, 8, 16)

COMPUTE = ("pe", "act", "dve", "pool")


class Op:
    __slots__ = ("eng", "fn", "deps", "is_dma", "sem", "val", "need_sig", "seq")


class Sched:
    def __init__(self, prog, counters, dma_sems, dma_cnt):
        self.prog = prog
        self.counters = counters
        self.dma_sems = dma_sems
        self.dma_cnt = dma_cnt
        self.ops = []
        self.lw = {}
        self.rd = {}

    def add(self, eng, fn, reads=(), writes=(), dma=None, no_waw=False):
        idx = len(self.ops)
        deps = {}
        for r in reads:
            w = self.lw.get(r)
            if w is not None:
                deps[w] = "raw"
        for w_ in writes:
            for ridx in self.rd.get(w_, {}).values():
                deps.setdefault(ridx, "war")
            lw = self.lw.get(w_)
            if lw is not None and not no_waw:
                deps.setdefault(lw, "waw")
        op = Op()
        op.eng = eng
        op.fn = fn
        op.deps = deps
        op.is_dma = dma is not None
        op.sem = dma
        op.val = 0
        op.need_sig = False
        op.seq = 0
        if dma is not None:
            self.dma_cnt[dma] += 16
            op.val = self.dma_cnt[dma]
        key = ("dma", dma) if dma is not None else eng
        for r in reads:
            self.rd.setdefault(r, {})[key] = idx
        for w_ in writes:
            self.lw[w_] = idx
            self.rd[w_] = {}
        self.ops.append(op)
        return idx

    def _needs_wait(self, op, dop, kind):
        if dop.is_dma:
            return True
        if dop.eng != op.eng:
            return True
        if op.is_dma:
            return True
        if kind in ("raw", "waw") and op.eng in ("act", "dve", "pool"):
            return True
        return False

    def finalize(self):
        for op in self.ops:
            for d, kind in op.deps.items():
                dop = self.ops[d]
                if not dop.is_dma and self._needs_wait(op, dop, kind):
                    dop.need_sig = True
        for op in self.ops:
            if op.need_sig:
                self.counters[op.eng] += 1
                op.seq = self.counters[op.eng]

    def emit(self, eng, eh):
        waited = {}
        for op in self.ops:
            if op.eng != eng:
                continue
            waits = {}
            for d, kind in op.deps.items():
                dop = self.ops[d]
                if not self._needs_wait(op, dop, kind):
                    continue
                if dop.is_dma:
                    k, v = ("d", dop.sem), dop.val
                else:
                    k, v = ("p", dop.eng), dop.seq
                if waits.get(k, 0) < v:
                    waits[k] = v
            for k, v in waits.items():
                if waited.get(k, 0) >= v:
                    continue
                waited[k] = v
                sem = self.dma_sems[k[1]] if k[0] == "d" else self.prog[k[1]]
                eh.wait_ge(sem, v)
            if op.fn is None:
                continue
            inst = op.fn(eh)
            if op.is_dma:
                inst.then_inc(self.dma_sems[op.sem], 16)
            elif op.need_sig:
                inst.then_inc(self.prog[eng], 1)


def run_block(nc, sch):
    sch.finalize()
    with nc.Block() as block:
        @block.tensor
        def _(e):
            sch.emit("pe", e)

        @block.scalar
        def _(e):
            sch.emit("act", e)

        @block.vector
        def _(e):
            sch.emit("dve", e)

        @block.gpsimd
        def _(e):
            sch.emit("pool", e)

        @block.sync
        def _(e):
            sch.emit("sp", e)


DMA_SEM_NAMES = ["const", "constg", "wp", "xc0", "xc1", "xh", "xr0", "xr1", "out0", "out1", "out2"]


def build_program():
    nc = bass.Bass("TRN2", target_bir_lowering=False)
    dt = lambda name, shape, kind="ExternalInput": nc.dram_tensor(name, shape, F32, kind=kind).ap()
    xT_all = dt("xT_all", [D, LT])
    x_own = dt("x_own", [T, D])
    w_perm = dt("w_perm", [D, 6144])
    w_pool = dt("w_pool", [4, 256, 256])
    pscale = dt("pscale", [128, 8])
    w_out = dt("w_out", [D, D])
    ln_g = dt("ln_g", [D])
    ln_b = dt("ln_b", [D])
    cs_d = dt("cs", [128, 32 * 64])
    mask_d = dt("masks", [128, 3 * 128])
    ident_d = dt("ident", [128, 128])
    bm_d = dt("bmats", [128, 3 * 4 * 128])
    out_d = dt("out", [T, D], kind="ExternalOutput")

    import contextlib
    es = contextlib.ExitStack()
    with es:
        prog = {e: es.enter_context(nc.semaphore("prog_" + e)) for e in COMPUTE}
        dma_sems = {n: es.enter_context(nc.semaphore("dma_" + n)) for n in DMA_SEM_NAMES}
        counters = {e: 0 for e in COMPUTE}
        dma_cnt = {n: 0 for n in DMA_SEM_NAMES}
        new_sched = lambda: Sched(prog, counters, dma_sems, dma_cnt)

        import os
        PH = os.environ.get("KPHASES", "abc")
        yT_attn = es.enter_context(nc.sbuf_tensor("yT_attn", [128, 8, T], BF16))
        if "a" in PH:
            phase_a(nc, new_sched(), xT_all, w_perm, cs_d, mask_d, ident_d, yT_attn)
        yT_pool = es.enter_context(nc.sbuf_tensor("yT_pool", [128, 8, T], BF16))
        if "b" in PH:
            phase_b(nc, new_sched(), xT_all, w_perm, w_pool, pscale, bm_d, yT_pool)
        if "c" in PH:
            phase_c(nc, new_sched(), x_own, w_out, ln_g, ln_b, out_d, yT_attn, yT_pool)
    return nc


def xchunk_src(xT_all, lt0, n):
    return xT_all.rearrange("(kc p) t -> p kc t", p=128)[:, :, lt0:lt0 + n]


def phase_a(nc, sch, xT_all, w_perm, cs_d, mask_d, ident_d, yT_attn):
    import contextlib
    with contextlib.ExitStack() as es:
        sb = lambda name, shape, dtype: es.enter_context(nc.sbuf_tensor(name, shape, dtype))
        xc = [sb(f"a_xc{i}", [128, KC, 512], BF16) for i in range(2)]
        wp = sb("a_wp", [128, KC, 1024], BF16)
        QT = sb("a_QT", [128, 2, T], BF16)
        KT = sb("a_KT", [128, 2, LT], BF16)
        VT = sb("a_VT", [128, 2, LT], BF16)
        Gs = sb("a_Gs", [128, 2, T], BF16)
        Vblk = sb("a_Vblk", [128, 69, 128], BF16)
        PT = [sb(f"a_PT{i}", [128, 512], BF16) for i in range(3)]
        PTP = [sb(f"a_PTP{i}", [128, 4, 130], BF16) for i in range(3)]
        PT3 = sb("a_PTh3", [128, 2, 16, 130], BF16)
        qktm = [sb(f"a_qktm{i}", [128, 4, 128], BF16) for i in range(2)]
        rt1 = [sb(f"a_rt1_{i}", [128, 4, 32], F32) for i in range(2)]
        rt2 = [sb(f"a_rt2_{i}", [128, 4, 32], F32) for i in range(2)]
        cs = sb("a_cs", [128, 32, 64], F32)
        masks = sb("a_masks", [128, 3, 128], BF16)
        ident = sb("a_ident", [128, 128], BF16)
        ones = sb("a_ones", [128, 128], BF16)
        zeros = sb("a_zeros", [128, 128], BF16)
        m1g0 = sb("a_m1g0", [128, 4, 128], BF16)
        rden = [sb(f"a_rden{i}", [128, 512], F32) for i in range(2)]
        lden = [sb(f"a_lden{i}", [128, 512], F32) for i in range(2)]
        bank = [es.enter_context(nc.psum_tensor(f"a_b{i}", [128, 512], F32)) for i in range(6)]
        tpb = [es.enter_context(nc.psum_tensor(f"a_tp{i}", [128, 1024], BF16)) for i in range(2)]

        MCUR, MPREV, MPREVH = 0, 1, 2

        sch.add("sp", lambda e: e.dma_start(out=cs[:].rearrange("p t c -> p (t c)"), in_=cs_d),
                writes=["cs"], dma="const")
        sch.add("pool", lambda e: e.dma_start(out=masks[:].rearrange("p m q -> p (m q)"), in_=mask_d),
                writes=["masks"], dma="constg")
        sch.add("pool", lambda e: e.dma_start(out=ident[:], in_=ident_d),
                writes=["ident"], dma="constg", no_waw=True)
        sch.add("dve", lambda e: e.memset(ones[:], 1.0), writes=["ones"])
        sch.add("dve", lambda e: e.memset(zeros[:], 0.0), writes=["zeros"])
        CONST = ["cs", "masks", "ident"]
        sch.add("dve", lambda e: e.tensor_copy(out=m1g0[:, 0, :], in_=masks[:, 2, :]), reads=CONST, writes=["m1g0a"])
        sch.add("dve", lambda e: e.tensor_copy(out=m1g0[:, 1:4, :],
                                               in_=masks[:, 1, :].unsqueeze(1).broadcast_to([128, 3, 128])),
                reads=CONST, writes=["m1g0b"])

        cnt = {"xc": 0, "pqk": 0, "pf": 0, "tp": 0, "qk": 0, "st": 0, "pt": 0, "ptp": 0, "grp": 0}
        PQK_BANKS = (0, 1, 2)
        PF_BANKS = (3, 4, 5)

        import os
        DBG_PAIRS = int(os.environ.get("KA_PAIRS", "4"))
        DBG_ATTN = int(os.environ.get("KA_ATTN", "9"))
        DBG_INPROJ = int(os.environ.get("KA_INPROJ", "1"))
        for p in range(DBG_PAIRS):
            for piece in range(4):
                def f(e, piece=piece, p=p):
                    src = w_perm.rearrange("(kc p) c -> p kc c", p=128)[:, piece * 4:(piece + 1) * 4,
                                                                        p * 1024:(p + 1) * 1024]
                    return e.dma_start(out=wp[:, piece * 4:(piece + 1) * 4, :], in_=src)
                sch.add("pool", f, writes=["wp"], dma="wp", no_waw=(piece > 0))

            pending = []

            def flush(keep):
                while len(pending) > keep:
                    pending.pop(0)()

            for ci in range(8 if DBG_INPROJ else 0):
                own = ci >= 4
                slot = cnt["xc"] % 2
                cnt["xc"] += 1
                for half in range(2):
                    def f(e, slot=slot, ci=ci, half=half):
                        return e.dma_start(out=xc[slot][:, half * 8:(half + 1) * 8, :],
                                           in_=xchunk_src(xT_all, ci * 512, 512)[:, half * 8:(half + 1) * 8, :])
                    sch.add("pool", f, writes=[("xc", slot)], dma=f"xc{slot}", no_waw=(half > 0))

                nb = 4 if own else 2
                c0 = 0 if own else 256
                for tt in range(4):
                    tl = ci * 4 + tt
                    b = PQK_BANKS[cnt["pqk"] % 3]
                    cnt["pqk"] += 1

                    def f(e, slot=slot, tt=tt, b=b, nb=nb, c0=c0):
                        for kc in range(KC):
                            i = e.matmul(bank[b][:, 0:nb * 128], lhsT=xc[slot][:, kc, tt * 128:(tt + 1) * 128],
                                         rhs=wp[:, kc, c0:c0 + nb * 128], start=(kc == 0), stop=(kc == KC - 1))
                        return i
                    sch.add("pe", f, reads=[("xc", slot), "wp"], writes=[("bank", b)])
                    qs = cnt["qk"] % 2
                    cnt["qk"] += 1
                    P = bank[b][:, 0:nb * 128].rearrange("p (n d) -> p n d", n=nb)
                    cc = cs[:, tl, 0:32].unsqueeze(1).broadcast_to([128, nb, 32])
                    sn = cs[:, tl, 32:48].unsqueeze(1).broadcast_to([128, nb, 16])
                    sp_ = cs[:, tl, 48:64].unsqueeze(1).broadcast_to([128, nb, 16])
                    sch.add("dve", lambda e, P=P, qs=qs, nb=nb, cc=cc: e.tensor_tensor(
                        out=rt1[qs][:, 0:nb, :], in0=P[:, :, 0:32], in1=cc, op=ALU.mult),
                        reads=[("bank", b)] + CONST, writes=[("rt1", qs)])
                    sch.add("dve", lambda e, P=P, qs=qs, nb=nb, sn=sn: e.tensor_tensor(
                        out=rt2[qs][:, 0:nb, 0:16], in0=P[:, :, 16:32], in1=sn, op=ALU.mult),
                        reads=[("bank", b)] + CONST, writes=[("rt2a", qs)])
                    sch.add("dve", lambda e, P=P, qs=qs, nb=nb, sp_=sp_: e.tensor_tensor(
                        out=rt2[qs][:, 0:nb, 16:32], in0=P[:, :, 0:16], in1=sp_, op=ALU.mult),
                        reads=[("bank", b)] + CONST, writes=[("rt2b", qs)])
                    sch.add("act", lambda e, P=P, qs=qs, nb=nb: e.copy(out=qktm[qs][:, 0:nb, 32:128], in_=P[:, :, 32:128]),
                            reads=[("bank", b), ("rt1", qs), ("rt2a", qs), ("rt2b", qs)], writes=[("qk_nr", qs)])
                    sch.add("dve", lambda e, qs=qs, nb=nb: e.tensor_tensor(
                        out=qktm[qs][:, 0:nb, 0:32], in0=rt1[qs][:, 0:nb, :], in1=rt2[qs][:, 0:nb, :], op=ALU.add),
                        reads=[("rt1", qs), ("rt2a", qs), ("rt2b", qs)], writes=[("qk_rot", qs)])

                    def tr(qs=qs, nb=nb, own=own, tl=tl):
                        tb = cnt["tp"] % 2
                        cnt["tp"] += 1

                        def f(e):
                            for blk in range(nb):
                                i = e.transpose(tpb[tb][:, blk * 128:(blk + 1) * 128], qktm[qs][:, blk, :], ident[:])
                            return i
                        sch.add("pe", f, reads=[("qk_nr", qs), ("qk_rot", qs)] + CONST, writes=[("tp", tb)])
                        if own:
                            q0 = (tl - 16) * 128
                            sch.add("act", lambda e: e.copy(out=QT[:, :, q0:q0 + 128],
                                                            in_=tpb[tb][:, 0:256].rearrange("p (n t) -> p n t", n=2)),
                                    reads=[("tp", tb)], writes=["QT"])
                            sch.add("act", lambda e: e.copy(out=KT[:, :, tl * 128:(tl + 1) * 128],
                                                            in_=tpb[tb][:, 256:512].rearrange("p (n t) -> p n t", n=2)),
                                    reads=[("tp", tb)], writes=["KT"])
                        else:
                            sch.add("act", lambda e: e.copy(out=KT[:, :, tl * 128:(tl + 1) * 128],
                                                            in_=tpb[tb][:, 0:256].rearrange("p (n t) -> p n t", n=2)),
                                    reads=[("tp", tb)], writes=["KT"])
                    pending.append(tr)
                    flush(1)

                for cb in range(4 if own else 2):
                    b = PF_BANKS[cnt["pf"] % 3]
                    cnt["pf"] += 1

                    def f(e, slot=slot, cb=cb, b=b):
                        for kc in range(KC):
                            i = e.matmul(bank[b][:, :], lhsT=wp[:, kc, 512 + cb * 128:512 + (cb + 1) * 128],
                                         rhs=xc[slot][:, kc, :], start=(kc == 0), stop=(kc == KC - 1))
                        return i
                    sch.add("pe", f, reads=[("xc", slot), "wp"], writes=[("bank", b)])
                    if cb < 2:
                        sch.add("dve", lambda e, cb=cb, b=b, ci=ci: e.tensor_copy(
                            out=VT[:, cb, ci * 512:(ci + 1) * 512], in_=bank[b][:, :]),
                            reads=[("bank", b)], writes=["VT"])
                    else:
                        sch.add("act", lambda e, cb=cb, b=b, ci=ci: e.activation(
                            out=Gs[:, cb - 2, (ci - 4) * 512:(ci - 3) * 512], in_=bank[b][:, :], func=AF.Silu),
                            reads=[("bank", b)], writes=["Gs"])
                    flush(0)
            flush(0)

            for hh in range(2 if DBG_ATTN else 0):
                h = 2 * p + hh
                vsrc = []
                for nbk in range(15, 32):
                    vsrc.append(VT[:, hh, nbk * 128:(nbk + 1) * 128])
                for r4 in range(4):
                    for cbk in range(3, 8):
                        vsrc.append(VT[:, hh, 512 * cbk + r4:512 * (cbk + 1):4])
                for r in range(16):
                    for kb in range(2):
                        vsrc.append(VT[:, hh, 2048 * kb + r:2048 * (kb + 1):16])
                v1 = lambda nbk: nbk - 15
                v2 = lambda r4, cbk: 17 + r4 * 5 + (cbk - 3)
                v3 = lambda r, kb: 37 + r * 2 + kb
                for i0 in range(0, 69, 8):
                    n = min(8, 69 - i0)
                    tb = cnt["tp"] % 2
                    cnt["tp"] += 1

                    def f(e, i0=i0, n=n, tb=tb, vsrc=vsrc):
                        for k in range(n):
                            i = e.transpose(tpb[tb][:, k * 128:(k + 1) * 128], vsrc[i0 + k], ident[:])
                        return i
                    sch.add("pe", f, reads=["VT"] + CONST, writes=[("tp", tb)])
                    sch.add("dve", lambda e, i0=i0, n=n, tb=tb: e.tensor_copy(
                        out=Vblk[:, i0:i0 + n, :], in_=tpb[tb][:, 0:n * 128].rearrange("p (n d) -> p n d", n=n)),
                        reads=[("tp", tb)], writes=["Vblk"])

                for rq in range(4 if DBG_ATTN > 1 else 0):
                    for kb in range(2):
                        sbk = cnt["st"] % 2
                        cnt["st"] += 1
                        ST = bank[sbk]

                        def f(e, rq=rq, kb=kb, ST=ST, hh=hh):
                            m = MCUR if kb == 1 else MPREVH
                            e.matmul(ST[:, :].rearrange("p (n q) -> p n q", n=4), lhsT=ident[:],
                                     rhs=masks[:, m, :].unsqueeze(1).broadcast_to([128, 4, 128]), start=True, stop=False)
                            for rr in range(4):
                                r = 4 * rq + rr
                                i = e.matmul(ST[:, rr * 128:(rr + 1) * 128],
                                             lhsT=KT[:, hh, 2048 * kb + r:2048 * (kb + 1):16],
                                             rhs=QT[:, hh, r:2048:16], start=False, stop=(rr == 3))
                            return i
                        sch.add("pe", f, reads=["QT", "KT"] + CONST, writes=[("bank", sbk)])
                        sch.add("act", lambda e, rq=rq, kb=kb, ST=ST: e.activation(
                            out=PT3[:, kb, 4 * rq:4 * rq + 4, 0:128],
                            in_=ST[:, :].rearrange("p (n q) -> p n q", n=4), func=AF.Exp, scale=SCALE),
                            reads=[("bank", sbk)], writes=[("pt3", kb, rq)])
                PT3R = [[("pt3", kb, rq) for rq in range(4)] for kb in range(2)]
                fin_pending = []

                for g in range(4 if DBG_ATTN > 1 else 0):
                    qb = 512 * g
                    gi = cnt["grp"]
                    cnt["grp"] += 1
                    NUM = bank[2 + gi % 2]
                    DEN = bank[4 + gi % 2]
                    numr = ("bank", 2 + gi % 2)
                    denr = ("bank", 4 + gi % 2)
                    cb = 4 + g
                    tiles = []

                    def qk(ti, g=g, qb=qb, hh=hh, cb=cb):
                        sbk = cnt["st"] % 2
                        cnt["st"] += 1
                        pts = cnt["pt"] % 3 if ti < 2 else 3 + cnt["ptp"] % 3
                        cnt["pt" if ti < 2 else "ptp"] += 1
                        tiles.append(pts)
                        ST = bank[sbk]

                        def f(e):
                            bc = lambda m, n_: masks[:, m, :].unsqueeze(1).broadcast_to([128, n_, 128])
                            if ti == 0:
                                e.matmul(ST[:, :].rearrange("p (n q) -> p n q", n=4), lhsT=ident[:],
                                         rhs=(m1g0[:] if g == 0 else bc(MPREV, 4)), start=True, stop=False)
                                for j in range(4):
                                    nbk = 16 + 4 * g + j - 1
                                    i = e.matmul(ST[:, j * 128:(j + 1) * 128], lhsT=KT[:, hh, nbk * 128:(nbk + 1) * 128],
                                                 rhs=QT[:, hh, qb + j * 128:qb + (j + 1) * 128], start=False, stop=(j == 3))
                            elif ti == 1:
                                e.matmul(ST[:, :].rearrange("p (n q) -> p n q", n=4), lhsT=ident[:],
                                         rhs=bc(MCUR, 4), start=True, stop=False)
                                for j in range(4):
                                    nbk = 16 + 4 * g + j
                                    i = e.matmul(ST[:, j * 128:(j + 1) * 128], lhsT=KT[:, hh, nbk * 128:(nbk + 1) * 128],
                                                 rhs=QT[:, hh, qb + j * 128:qb + (j + 1) * 128], start=False, stop=(j == 3))
                            else:
                                m = MCUR if ti == 3 else (MPREVH if g == 0 else MPREV)
                                cbk = cb if ti == 3 else cb - 1
                                e.matmul(ST[:, :].rearrange("p (n q) -> p n q", n=4), lhsT=ident[:],
                                         rhs=bc(m, 4), start=True, stop=False)
                                for r4 in range(4):
                                    i = e.matmul(ST[:, r4 * 128:(r4 + 1) * 128],
                                                 lhsT=KT[:, hh, 512 * cbk + r4:512 * (cbk + 1):4],
                                                 rhs=QT[:, hh, qb + r4:qb + 512:4], start=False, stop=(r4 == 3))
                            return i
                        sch.add("pe", f, reads=["QT", "KT", "m1g0a", "m1g0b"] + CONST, writes=[("bank", sbk)])
                        if ti < 2:
                            sch.add("act", lambda e: e.activation(out=PT[pts][:], in_=ST[:, :], func=AF.Exp, scale=SCALE),
                                    reads=[("bank", sbk)], writes=[("pt", pts)])
                        else:
                            sch.add("act", lambda e: e.activation(
                                out=PTP[pts - 3][:, :, 0:128],
                                in_=ST[:, :].rearrange("p (n q) -> p n q", n=4), func=AF.Exp, scale=SCALE),
                                reads=[("bank", sbk)], writes=[("pt", pts)])

                    def pv(ti, g=g, qb=qb, hh=hh, cb=cb, NUM=NUM, DEN=DEN, numr=numr, denr=denr):
                        kb = ti - 4
                        if ti < 2:
                            pts = tiles[ti]
                            P_ = PT[pts][:, :]
                            rds = [("pt", pts)]
                        elif ti < 4:
                            pts = tiles[ti]
                            P_ = PTP[pts - 3]
                            rds = [("pt", pts)]
                        else:
                            kb = ti - 4
                            P_ = PT3
                            rds = PT3R[kb]

                        def f(e):
                            if ti < 2:
                                e.matmul(DEN[:, :], lhsT=ones[:], rhs=P_, start=(ti == 0), stop=False)
                            elif ti < 4:
                                e.matmul(DEN[:, :], lhsT=ones[:], rhs=P_[:, :, 0:128].rearrange("p r i -> p i r"),
                                         start=False, stop=(ti == 3))
                            else:
                                e.matmul(DEN[:, :], lhsT=ones[:],
                                         rhs=P_[:, kb, :, 32 * g:32 * g + 32].rearrange("p r j -> p j r"),
                                         start=False, stop=False)
                            if ti == 0:
                                e.matmul(NUM[:, :], lhsT=zeros[:], rhs=P_, start=True, stop=False)
                            if ti < 2:
                                for j in range(4):
                                    nbk = 16 + 4 * g + j - (1 if ti == 0 else 0)
                                    i = e.matmul(NUM[:, j * 128:(j + 1) * 128], lhsT=Vblk[:, v1(nbk), :],
                                                 rhs=P_[:, j * 128:(j + 1) * 128], start=False, stop=False)
                            elif ti < 4:
                                cbk = cb if ti == 3 else cb - 1
                                for r4 in range(4):
                                    i = e.matmul(NUM[:, r4:512:4], lhsT=Vblk[:, v2(r4, cbk), :], rhs=P_[:, r4, 0:128],
                                                 start=False, stop=(ti == 3 and r4 == 3))
                            else:
                                for r in range(16):
                                    i = e.matmul(NUM[:, r:512:16], lhsT=Vblk[:, v3(r, kb), :],
                                                 rhs=P_[:, kb, r, 32 * g:32 * g + 32],
                                                 start=False, stop=False)
                            return i
                        sch.add("pe", f, reads=rds + ["Vblk", "ones", "zeros"], writes=[numr, denr])

                    qk(0); qk(1)
                    if fin_pending:
                        fin_pending.pop()()
                    pv(0); qk(2); pv(1); qk(3); pv(4); pv(2); pv(5); pv(3)

                    def finalize(gi=gi, NUM=NUM, DEN=DEN, numr=numr, denr=denr, h=h, hh=hh, qb=qb):
                        fs = gi % 2
                        sch.add("act", lambda e, fs=fs, DEN=DEN: e.activation(out=lden[fs][:], in_=DEN[:, :], func=AF.Ln),
                                reads=[denr], writes=[("lden", fs)])
                        sch.add("act", lambda e, fs=fs: e.activation(out=rden[fs][:], in_=lden[fs][:], func=AF.Exp, scale=-1.0),
                                reads=[("lden", fs)], writes=[("rden", fs)])
                        sch.add("dve", lambda e, fs=fs, NUM=NUM: e.tensor_tensor(out=lden[fs][:], in0=NUM[:, :], in1=rden[fs][:],
                                                                                 op=ALU.mult),
                                reads=[numr, ("rden", fs)], writes=[("lden", fs)])
                        sch.add("dve", lambda e, fs=fs, h=h, hh=hh, qb=qb: e.tensor_tensor(
                            out=yT_attn[:, h, qb:qb + 512], in0=lden[fs][:], in1=Gs[:, hh, qb:qb + 512], op=ALU.mult),
                            reads=[("lden", fs), "Gs"], writes=["yT_attn"])

                    fin_pending.append(finalize)
                if fin_pending:
                    fin_pending.pop()()
        run_block(nc, sch)


def phase_b(nc, sch, xT_all, w_perm, w_pool, pscale, bm_d, yT_pool):
    import contextlib
    with contextlib.ExitStack() as es:
        sb = lambda name, shape, dtype: es.enter_context(nc.sbuf_tensor(name, shape, dtype))
        xc = [sb(f"b_xc{i}", [128, KC, 512], BF16) for i in range(2)]
        xh = sb("b_xh", [128, KC, 128], BF16)
        wB = sb("b_w", [128, KC, 1024], BF16)
        utm = [sb(f"b_utm{i}", [128, 512], BF16) for i in range(6)]
        Gp = [sb(f"b_gp{i}", [128, 4, 512], BF16) for i in range(2)]
        pooled = [sb(f"b_pl{i}", [128, 4, 512], BF16) for i in range(2)]
        wpl = sb("b_wpool", [128, 4, 2, 256], BF16)
        bmats = sb("b_bm", [128, 3, 4, 128], BF16)
        ps = sb("b_ps", [128, 8], F32)
        bank = [es.enter_context(nc.psum_tensor(f"b_b{i}", [128, 512], F32)) for i in range(8)]

        sch.add("pool", lambda e: e.dma_start(out=wpl[:], in_=w_pool.rearrange("g (cc p) d -> p g cc d", p=128)),
                writes=["wpl"], dma="constg")
        sch.add("pool", lambda e: e.dma_start(out=bmats[:].rearrange("p k g b -> p (k g b)"), in_=bm_d),
                writes=["bmats"], dma="constg", no_waw=True)
        sch.add("sp", lambda e: e.dma_start(out=ps[:], in_=pscale), writes=["ps"], dma="const")
        CONST = ["wpl", "bmats", "ps"]
        BM, BP, BF = 0, 1, 2
        cnt = {"xc": 0, "u": 0, "gp": 0, "pp": 0, "po": 0, "ch": 0}

        for bp in range(2):
            for piece in range(4):
                def f(e, piece=piece, bp=bp):
                    src = w_perm.rearrange("(kc p) c -> p kc c", p=128)[:, piece * 4:(piece + 1) * 4,
                                                                        4096 + bp * 1024:4096 + (bp + 1) * 1024]
                    return e.dma_start(out=wB[:, piece * 4:(piece + 1) * 4, :], in_=src)
                sch.add("pool", f, writes=["wB"], dma="wp", no_waw=(piece > 0))
            sch.add("pool", lambda e: e.dma_start(out=xh[:], in_=xchunk_src(xT_all, 1920, 128)),
                    writes=["xh"], dma="xh")
            uslot = {}

            def u_tile(j, lhs_fn, rd):
                b = cnt["u"] % 2
                cnt["u"] += 1
                s = (j + 1) % 6
                uslot[j] = s

                def f(e):
                    for kc in range(KC):
                        i = e.matmul(bank[b][:, :], lhsT=lhs_fn(kc), rhs=wB[:, kc, 0:512],
                                     start=(kc == 0), stop=(kc == KC - 1))
                    return i
                sch.add("pe", f, reads=[rd, "wB"], writes=[("bank", b)])
                sch.add("act", lambda e: e.copy(out=utm[s][:], in_=bank[b][:, :]),
                        reads=[("bank", b)], writes=[("utm", s)])

            u_tile(-1, lambda kc: xh[:, kc, :], "xh")
            for ci in range(4):
                slot = cnt["xc"] % 2
                cnt["xc"] += 1
                for half in range(2):
                    def f(e, slot=slot, ci=ci, half=half):
                        return e.dma_start(out=xc[slot][:, half * 8:(half + 1) * 8, :],
                                           in_=xchunk_src(xT_all, 2048 + ci * 512, 512)[:, half * 8:(half + 1) * 8, :])
                    sch.add("pool", f, writes=[("xc", slot)], dma=f"xc{slot}", no_waw=(half > 0))
                for tt in range(4):
                    u_tile(ci * 4 + tt, lambda kc, slot=slot, tt=tt: xc[slot][:, kc, tt * 128:(tt + 1) * 128], ("xc", slot))
                cs_ = cnt["ch"] % 2
                cnt["ch"] += 1
                for cbi in range(4):
                    b = 2 + cnt["gp"] % 2
                    cnt["gp"] += 1

                    def f(e, slot=slot, cbi=cbi, b=b):
                        for kc in range(KC):
                            i = e.matmul(bank[b][:, :], lhsT=wB[:, kc, 512 + cbi * 128:512 + (cbi + 1) * 128],
                                         rhs=xc[slot][:, kc, :], start=(kc == 0), stop=(kc == KC - 1))
                        return i
                    sch.add("pe", f, reads=[("xc", slot), "wB"], writes=[("bank", b)])
                    sch.add("act", lambda e, cbi=cbi, b=b, cs_=cs_: e.activation(out=Gp[cs_][:, cbi, :], in_=bank[b][:, :],
                                                                                 func=AF.Silu),
                            reads=[("bank", b)], writes=[("gp", cs_)])
                for cbl in range(4):
                    gq = 2 * bp + cbl // 2
                    b = 4 + cnt["pp"] % 2
                    cnt["pp"] += 1

                    def f(e, cbl=cbl, gq=gq, b=b, ci=ci, uslot=uslot):
                        for jj in range(4):
                            j = ci * 4 + jj
                            e.matmul(bank[b][:, jj * 128:(jj + 1) * 128], lhsT=utm[uslot[j - 1]][:, cbl * 128:(cbl + 1) * 128],
                                     rhs=bmats[:, BP, gq, :], start=True, stop=False, skip_group_check=True)
                            i = e.matmul(bank[b][:, jj * 128:(jj + 1) * 128], lhsT=utm[uslot[j]][:, cbl * 128:(cbl + 1) * 128],
                                         rhs=bmats[:, (BF if j == 0 else BM), gq, :], start=False, stop=(jj == 3),
                                         skip_group_check=True)
                        return i
                    rds = [("utm", uslot[ci * 4 + jj]) for jj in range(-1, 4)]
                    sch.add("pe", f, reads=rds + CONST, writes=[("bank", b)])
                    sch.add("dve", lambda e, cbl=cbl, b=b, cs_=cs_: e.tensor_copy(out=pooled[cs_][:, cbl, :], in_=bank[b][:, :]),
                            reads=[("bank", b)], writes=[("pooled", cs_)])
                for obl in range(4):
                    ob = 4 * bp + obl
                    gq = ob // 2
                    hf = obl % 2
                    b = 6 + cnt["po"] % 2
                    cnt["po"] += 1

                    def f(e, obl=obl, gq=gq, hf=hf, b=b, cs_=cs_):
                        for cc in range(2):
                            i = e.matmul(bank[b][:, :], lhsT=wpl[:, gq, cc, hf * 128:(hf + 1) * 128],
                                         rhs=pooled[cs_][:, (obl // 2) * 2 + cc, :], start=(cc == 0), stop=(cc == 1))
                        return i
                    sch.add("pe", f, reads=[("pooled", cs_)] + CONST, writes=[("bank", b)])
                    sch.add("dve", lambda e, obl=obl, ob=ob, b=b, cs_=cs_, ci=ci: e.scalar_tensor_tensor(
                        out=yT_pool[:, ob, ci * 512:(ci + 1) * 512], in0=bank[b][:, :], scalar=ps[:, ob:ob + 1],
                        in1=Gp[cs_][:, obl, :], op0=ALU.mult, op1=ALU.mult),
                        reads=[("bank", b), ("gp", cs_)] + CONST, writes=["yT_pool"])
        run_block(nc, sch)


def phase_c(nc, sch, x_own, w_out, ln_g, ln_b, out_d, yT_attn, yT_pool):
    import contextlib
    with contextlib.ExitStack() as es:
        sb = lambda name, shape, dtype: es.enter_context(nc.sbuf_tensor(name, shape, dtype))
        wo = sb("c_wo", [128, KC, D], BF16)
        xr = [sb(f"c_xr{i}", [128, D], F32) for i in range(2)]
        z = [sb(f"c_z{i}", [128, D], F32) for i in range(3)]
        junk = sb("c_junk", [128, D], BF16)
        gbc = sb("c_g", [128, D], F32)
        bbc = sb("c_b", [128, D], F32)
        st = [sb(f"c_st{i}", [128, 16], F32) for i in range(3)]
        bank = [es.enter_context(nc.psum_tensor(f"c_b{i}", [128, 512], F32)) for i in range(8)]

        for piece in range(8):
            def f(e, piece=piece):
                src = w_out.rearrange("(ec p) d -> p ec d", p=128)[:, piece * 2:(piece + 1) * 2, :]
                return e.dma_start(out=wo[:, piece * 2:(piece + 1) * 2, :], in_=src)
            sch.add("pool", f, writes=["wo"], dma="wp", no_waw=(piece > 0))
        sch.add("sp", lambda e: e.dma_start(out=gbc[:], in_=ln_g.partition_broadcast(128)), writes=["gbc"], dma="const")
        sch.add("sp", lambda e: e.dma_start(out=bbc[:], in_=ln_b.partition_broadcast(128)), writes=["bbc"], dma="const",
                no_waw=True)
        CONST = ["gbc", "bbc"]

        def load_x(j):
            s = j % 2
            sch.add("sp", lambda e: e.dma_start(out=xr[s][:], in_=x_own[j * 128:(j + 1) * 128, :]),
                    writes=[("xr", s)], dma=f"xr{s}")

        load_x(0)
        for j in range(16):
            s = j % 2
            zi = j % 3
            Z = z[zi]
            S = st[zi]
            if j + 1 < 16:
                load_x(j + 1)

            def f(e, j=j, s=s):
                for ec in range(KC):
                    lhs = (yT_attn[:, ec, j * 128:(j + 1) * 128] if ec < 8 else yT_pool[:, ec - 8, j * 128:(j + 1) * 128])
                    for nb in range(4):
                        i = e.matmul(bank[4 * s + nb][:, :], lhsT=lhs, rhs=wo[:, ec, nb * 512:(nb + 1) * 512],
                                     start=(ec == 0), stop=(ec == KC - 1))
                return i
            sch.add("pe", f, reads=["wo"], writes=[("bank", 4 * s + nb) for nb in range(4)])
            for nb in range(4):
                sch.add("dve", lambda e, s=s, nb=nb, Z=Z: e.scalar_tensor_tensor(
                    out=Z[:, nb * 512:(nb + 1) * 512], in0=xr[s][:, nb * 512:(nb + 1) * 512], scalar=ALPHA,
                    in1=bank[4 * s + nb][:, :], op0=ALU.mult, op1=ALU.add),
                    reads=[("xr", s), ("bank", 4 * s + nb)], writes=[("z", zi, nb)])
            zr = [("z", zi, nb) for nb in range(4)]
            sch.add("act", lambda e, Z=Z, S=S: e.activation(out=junk[:], in_=Z[:], func=AF.Identity, accum_out=S[:, 5:6]),
                    reads=zr, writes=["junk", ("tot", zi)])
            sch.add("act", lambda e, Z=Z, S=S: e.activation(out=junk[:], in_=Z[:], func=AF.Square, accum_out=S[:, 4:5]),
                    reads=zr, writes=["junk", ("s2", zi)])
            sch.add("dve", lambda e, S=S: e.tensor_scalar(out=S[:, 6:7], in0=S[:, 5:6], scalar1=1.0 / D, scalar2=None,
                                                          op0=ALU.mult),
                    reads=[("tot", zi)], writes=[("mean", zi)])
            sch.add("dve", lambda e, S=S: e.tensor_tensor(out=S[:, 7:8], in0=S[:, 6:7], in1=S[:, 6:7], op=ALU.mult),
                    reads=[("mean", zi)], writes=[("msq", zi)])
            sch.add("dve", lambda e, S=S: e.scalar_tensor_tensor(out=S[:, 8:9], in0=S[:, 4:5], scalar=1.0 / D, in1=S[:, 7:8],
                                                                 op0=ALU.mult, op1=ALU.subtract),
                    reads=[("s2", zi), ("msq", zi)], writes=[("var", zi)])
            sch.add("dve", lambda e, S=S: e.tensor_scalar(out=S[:, 9:10], in0=S[:, 8:9], scalar1=EPS, scalar2=None,
                                                          op0=ALU.add),
                    reads=[("var", zi)], writes=[("vare", zi)])
            sch.add("act", lambda e, S=S: e.activation(out=S[:, 10:11], in_=S[:, 9:10], func=AF.Sqrt),
                    reads=[("vare", zi)], writes=[("sd", zi)])
            sch.add("dve", lambda e, S=S: e.reciprocal(out=S[:, 11:12], in_=S[:, 10:11]),
                    reads=[("sd", zi)], writes=[("rstd", zi)])
            sch.add("dve", lambda e, S=S: e.scalar_tensor_tensor(out=S[:, 12:13], in0=S[:, 6:7], scalar=-1.0, in1=S[:, 11:12],
                                                                 op0=ALU.mult, op1=ALU.mult),
                    reads=[("mean", zi), ("rstd", zi)], writes=[("nbias", zi)])
            zall = ("zall", zi)
            sch.add("act", lambda e, Z=Z, S=S: e.activation(out=Z[:], in_=Z[:], func=AF.Identity,
                                                            scale=S[:, 11:12], bias=S[:, 12:13]),
                    reads=zr + [("rstd", zi), ("nbias", zi), ("tot", zi), ("s2", zi)], writes=zr + [zall])
            sch.add("pool", lambda e, Z=Z: e.tensor_tensor(out=Z[:], in0=Z[:], in1=gbc[:], op=ALU.mult),
                    reads=[zall] + CONST, writes=zr + [zall])
            sch.add("pool", lambda e, Z=Z: e.tensor_tensor(out=Z[:], in0=Z[:], in1=bbc[:], op=ALU.add),
                    reads=[zall] + CONST, writes=zr + [zall])
            sch.add("sp", lambda e, Z=Z, j=j: e.dma_start(out=out_d[j * 128:(j + 1) * 128, :], in_=Z[:]),
                    reads=[zall] + zr, writes=[("outd", j)], dma=f"out{zi}")
        sch.add("sp", None, reads=[("outd", 13), ("outd", 14), ("outd", 15)])
        run_block(nc, sch)


def _perm_cols():
    cols = []
    for p in range(4):
        for base in (0, 1024, 2048, 4096):
            for h in (2 * p, 2 * p + 1):
                cols.extend(range(base + h * 128, base + (h + 1) * 128))
    for bp in range(2):
        cols.extend(range(3072 + bp * 512, 3072 + (bp + 1) * 512))
        cols.extend(range(5120 + bp * 512, 5120 + (bp + 1) * 512))
    return np.asarray(cols, dtype=np.int64)


def _tables(half):
    lt = np.arange(LT, dtype=np.int64)
    pos = np.maximum(lt - 2048 + 2048 * half, 0).astype(np.float32)
    inv = (np.float32(500000.0) ** (-(2.0 * np.arange(16, dtype=np.float32)) / np.float32(32.0))).astype(np.float32)
    ang = (pos[:, None] * inv[None, :]).astype(np.float32)
    c, s = np.cos(ang).astype(np.float32), np.sin(ang).astype(np.float32)
    cs = np.concatenate([c, c, -s, s], axis=1).astype(np.float32)
    cs = np.ascontiguousarray(cs.reshape(32, 128, 64).transpose(1, 0, 2).reshape(128, 32 * 64))
    k = np.arange(128)[:, None]
    q = np.arange(128)[None, :]
    mcur = np.where(k <= q, 0.0, NEG).astype(np.float32)
    mprev = np.where(k >= q, 0.0, NEG).astype(np.float32)
    mprevh = mprev if half == 1 else np.full((128, 128), NEG, np.float32)
    masks = np.ascontiguousarray(np.stack([mcur, mprev, mprevh]).astype(np.float32).transpose(1, 0, 2).reshape(128, 384))
    bm = np.zeros((3, 4, 128, 128), np.float32)
    tp = np.arange(128)[:, None]
    t = np.arange(128)[None, :]
    for g, w in enumerate(POOL_W):
        main = ((tp <= t) & (tp > t - w)).astype(np.float32) / w - (tp == t)
        prev = ((tp - 128) > (t - w)).astype(np.float32) / w
        cntf = np.minimum(t + 1, w).astype(np.float32)
        first = ((tp <= t) & (tp > t - w)).astype(np.float32) / cntf - (tp == t)
        bm[0, g] = main
        bm[1, g] = prev
        bm[2, g] = first if half == 0 else main
    bm = np.ascontiguousarray(bm.transpose(2, 0, 1, 3).reshape(128, 3 * 4 * 128))
    return cs, masks, bm


_NC_CACHE = {}


def kernel(x, w_in, w_pool, pool_scale, w_out, ln_gain, ln_bias):
    x = np.asarray(x, dtype=np.float32)
    w_perm = np.ascontiguousarray(np.asarray(w_in, np.float32)[0][:, _perm_cols()])
    w_pool0 = np.ascontiguousarray(np.asarray(w_pool, np.float32)[0])
    pscale = np.ascontiguousarray(np.asarray(pool_scale, np.float32)[0].reshape(8, 128).T)
    w_out0 = np.ascontiguousarray(np.asarray(w_out, np.float32)[0])
    g0 = np.ascontiguousarray(np.asarray(ln_gain, np.float32)[0])
    b0 = np.ascontiguousarray(np.asarray(ln_bias, np.float32)[0])
    ident = np.eye(128, dtype=np.float32)
    if "nc" not in _NC_CACHE:
        _NC_CACHE["nc"] = build_program()
    nc = _NC_CACHE["nc"]
    in_maps = []
    for c in range(8):
        b, half = c // 2, c % 2
        own = x[b, half * T:(half + 1) * T, :]
        halo = x[b, 0:T, :] if half == 1 else np.zeros((T, D), np.float32)
        xT_all = np.ascontiguousarray(np.concatenate([halo, own], axis=0).T)
        cs, masks, bm = _tables(half)
        in_maps.append({
            "xT_all": xT_all, "x_own": np.ascontiguousarray(own), "w_perm": w_perm, "w_pool": w_pool0,
            "pscale": pscale, "w_out": w_out0, "ln_g": g0, "ln_b": b0, "cs": cs, "masks": masks,
            "ident": ident, "bmats": bm,
        })
    res = run_bass_kernel_spmd(nc, in_maps, core_ids=list(range(8)))
    out = np.empty((4, 4096, D), np.float32)
    for c in range(8):
        b, half = c // 2, c % 2
        out[b, half * T:(half + 1) * T, :] = res.results[c]["out"]
    return out
```

```python
import numpy as np
import concourse.bass as bass
import concourse.mybir as mybir
from concourse.bass_utils import run_bass_kernel_spmd

F32 = mybir.dt.float32
BF16 = mybir.dt.bfloat16
AF = mybir.ActivationFunctionType
ALU = mybir.AluOpType
AX = mybir.AxisListType

D = 2048
T = 2048
LT = 4096
KC = 16
NEG = -30000.0
ALPHA = 2.0 ** 0.25
EPS = 1e-5
SCALE = 128.0 ** -0.5
POOL_W = (2, 4, 8, 16)

COMPUTE = ("pe", "act", "dve", "pool")


class Op:
    __slots__ = ("eng", "fn", "deps", "is_dma", "sem", "val", "need_sig", "seq")


class Sched:
    def __init__(self, prog, counters, dma_sems, dma_cnt):
        self.prog = prog
        self.counters = counters
        self.dma_sems = dma_sems
        self.dma_cnt = dma_cnt
        self.ops = []
        self.lw = {}
        self.rd = {}

    def add(self, eng, fn, reads=(), writes=(), dma=None, no_waw=False):
        idx = len(self.ops)
        deps = {}
        for r in reads:
            w = self.lw.get(r)
            if w is not None:
                deps[w] = "raw"
        for w_ in writes:
            for ridx in self.rd.get(w_, {}).values():
                deps.setdefault(ridx, "war")
            lw = self.lw.get(w_)
            if lw is not None and not no_waw:
                deps.setdefault(lw, "waw")
        op = Op()
        op.eng = eng
        op.fn = fn
        op.deps = deps
        op.is_dma = dma is not None
        op.sem = dma
        op.val = 0
        op.need_sig = False
        op.seq = 0
        if dma is not None:
            self.dma_cnt[dma] += 16
            op.val = self.dma_cnt[dma]
        key = ("dma", dma) if dma is not None else eng
        for r in reads:
            self.rd.setdefault(r, {})[key] = idx
        for w_ in writes:
            self.lw[w_] = idx
            self.rd[w_] = {}
        self.ops.append(op)
        return idx

    def _needs_wait(self, op, dop, kind):
        if dop.is_dma:
            return True
        if dop.eng != op.eng:
            return True
        if op.is_dma:
            return True
        if kind in ("raw", "waw") and op.eng in ("act", "dve", "pool"):
            return True
        return False

    def finalize(self):
        for op in self.ops:
            for d, kind in op.deps.items():
                dop = self.ops[d]
                if not dop.is_dma and self._needs_wait(op, dop, kind):
                    dop.need_sig = True
        for op in self.ops:
            if op.need_sig:
                self.counters[op.eng] += 1
                op.seq = self.counters[op.eng]

    def emit(self, eng, eh):
        waited = {}
        for op in self.ops:
            if op.eng != eng:
                continue
            waits = {}
            for d, kind in op.deps.items():
                dop = self.ops[d]
                if not self._needs_wait(op, dop, kind):
                    continue
                if dop.is_dma:
                    k, v = ("d", dop.sem), dop.val
                else:
                    k, v = ("p", dop.eng), dop.seq
                if waits.get(k, 0) < v:
                    waits[k] = v
            for k, v in waits.items():
                if waited.get(k, 0) >= v:
                    continue
                waited[k] = v
                sem = self.dma_sems[k[1]] if k[0] == "d" else self.prog[k[1]]
                eh.wait_ge(sem, v)
            if op.fn is None:
                continue
            inst = op.fn(eh)
            if op.is_dma:
                inst.then_inc(self.dma_sems[op.sem], 16)
            elif op.need_sig:
                inst.then_inc(self.prog[eng], 1)


def run_block(nc, sch):
    sch.finalize()
    with nc.Block() as block:
        @block.tensor
        def _(e):
            sch.emit("pe", e)

        @block.scalar
        def _(e):
            sch.emit("act", e)

        @block.vector
        def _(e):
            sch.emit("dve", e)

        @block.gpsimd
        def _(e):
            sch.emit("pool", e)

        @block.sync
        def _(e):
            sch.emit("sp", e)


DMA_SEM_NAMES = ["const", "constg", "wp", "woa", "xc0", "xc1", "xh", "xr0", "xr1", "out0", "out1", "out2"]


def build_program():
    nc = bass.Bass("TRN2", target_bir_lowering=False)
    dt = lambda name, shape, kind="ExternalInput": nc.dram_tensor(name, shape, F32, kind=kind).ap()
    xT_all = dt("xT_all", [D, LT])
    x_own = dt("x_own", [T, D])
    w_perm = dt("w_perm", [D, 6144])
    w_pool = dt("w_pool", [4, 256, 256])
    pscale = dt("pscale", [128, 8])
    w_out = dt("w_out", [D, D])
    ln_g = dt("ln_g", [D])
    ln_b = dt("ln_b", [D])
    cs_d = dt("cs", [128, 32 * 64])
    mask_d = dt("masks", [128, 3 * 128])
    ident_d = dt("ident", [128, 128])
    bm_d = dt("bmats", [128, 3 * 4 * 128])
    out_d = dt("out", [T, D], kind="ExternalOutput")

    import contextlib
    es = contextlib.ExitStack()
    with es:
        prog = {e: es.enter_context(nc.semaphore("prog_" + e)) for e in COMPUTE}
        dma_sems = {n: es.enter_context(nc.semaphore("dma_" + n)) for n in DMA_SEM_NAMES}
        counters = {e: 0 for e in COMPUTE}
        dma_cnt = {n: 0 for n in DMA_SEM_NAMES}
        new_sched = lambda: Sched(prog, counters, dma_sems, dma_cnt)

        import os
        PH = os.environ.get("KPHASES", "abc")
        yT_attn = es.enter_context(nc.sbuf_tensor("yT_attn", [128, 8, T], BF16))
        if "a" in PH:
            phase_a(nc, new_sched(), xT_all, w_perm, cs_d, mask_d, ident_d, yT_attn)
        yT_pool = es.enter_context(nc.sbuf_tensor("yT_pool", [128, 8, T], BF16))
        wo_a = es.enter_context(nc.sbuf_tensor("wo_a", [128, 8, D], BF16))
        if "b" in PH:
            phase_b(nc, new_sched(), xT_all, w_perm, w_pool, pscale, bm_d, yT_pool, w_out, wo_a)
        if "c" in PH:
            phase_c(nc, new_sched(), x_own, w_out, ln_g, ln_b, out_d, yT_attn, yT_pool, wo_a, "b" in PH)
    return nc


def xchunk_src(xT_all, lt0, n):
    return xT_all.rearrange("(kc p) t -> p kc t", p=128)[:, :, lt0:lt0 + n]


def phase_a(nc, sch, xT_all, w_perm, cs_d, mask_d, ident_d, yT_attn):
    import contextlib
    with contextlib.ExitStack() as es:
        sb = lambda name, shape, dtype: es.enter_context(nc.sbuf_tensor(name, shape, dtype))
        xc = [sb(f"a_xc{i}", [128, KC, 512], BF16) for i in range(2)]
        wp = sb("a_wp", [128, KC, 1024], BF16)
        QT = sb("a_QT", [128, 2, T], BF16)
        KT = sb("a_KT", [128, 2, LT], BF16)
        VT = sb("a_VT", [128, 2, LT], BF16)
        Gs = sb("a_Gs", [128, 2, T], BF16)
        Vblk = sb("a_Vblk", [128, 69, 128], BF16)
        PT = [sb(f"a_PT{i}", [128, 512], BF16) for i in range(3)]
        PTP = [sb(f"a_PTP{i}", [128, 4, 130], BF16) for i in range(3)]
        PT3 = sb("a_PTh3", [128, 2, 16, 130], BF16)
        qktm = [sb(f"a_qktm{i}", [128, 4, 128], BF16) for i in range(2)]
        rt1 = [sb(f"a_rt1_{i}", [128, 4, 32], F32) for i in range(2)]
        rt2 = [sb(f"a_rt2_{i}", [128, 4, 32], F32) for i in range(2)]
        cs = sb("a_cs", [128, 32, 64], F32)
        masks = sb("a_masks", [128, 3, 128], BF16)
        ident = sb("a_ident", [128, 128], BF16)
        ones = sb("a_ones", [128, 128], BF16)
        zeros = sb("a_zeros", [128, 128], BF16)
        m1g0 = sb("a_m1g0", [128, 4, 128], BF16)
        rden = [sb(f"a_rden{i}", [128, 512], F32) for i in range(2)]
        lden = [sb(f"a_lden{i}", [128, 512], F32) for i in range(2)]
        bank = [es.enter_context(nc.psum_tensor(f"a_b{i}", [128, 512], F32)) for i in range(6)]
        tpb = [es.enter_context(nc.psum_tensor(f"a_tp{i}", [128, 1024], BF16)) for i in range(2)]

        MCUR, MPREV, MPREVH = 0, 1, 2

        sch.add("sp", lambda e: e.dma_start(out=cs[:].rearrange("p t c -> p (t c)"), in_=cs_d),
                writes=["cs"], dma="const")
        sch.add("pool", lambda e: e.dma_start(out=masks[:].rearrange("p m q -> p (m q)"), in_=mask_d),
                writes=["masks"], dma="constg")
        sch.add("pool", lambda e: e.dma_start(out=ident[:], in_=ident_d),
                writes=["ident"], dma="constg", no_waw=True)
        sch.add("dve", lambda e: e.memset(ones[:], 1.0), writes=["ones"])
        sch.add("dve", lambda e: e.memset(zeros[:], 0.0), writes=["zeros"])
        CONST = ["cs", "masks", "ident"]
        sch.add("dve", lambda e: e.tensor_copy(out=m1g0[:, 0, :], in_=masks[:, 2, :]), reads=CONST, writes=["m1g0a"])
        sch.add("dve", lambda e: e.tensor_copy(out=m1g0[:, 1:4, :],
                                               in_=masks[:, 1, :].unsqueeze(1).broadcast_to([128, 3, 128])),
                reads=CONST, writes=["m1g0b"])

        cnt = {"xc": 0, "pqk": 0, "pf": 0, "tp": 0, "qk": 0, "st": 0, "pt": 0, "ptp": 0, "grp": 0}
        PQK_BANKS = (0, 1, 2)
        PF_BANKS = (3, 4, 5)

        import os
        DBG_PAIRS = int(os.environ.get("KA_PAIRS", "4"))
        DBG_ATTN = int(os.environ.get("KA_ATTN", "9"))
        DBG_INPROJ = int(os.environ.get("KA_INPROJ", "1"))
        for p in range(DBG_PAIRS):
            for piece in range(4):
                def f(e, piece=piece, p=p):
                    src = w_perm.rearrange("(kc p) c -> p kc c", p=128)[:, piece * 4:(piece + 1) * 4,
                                                                        p * 1024:(p + 1) * 1024]
                    return e.dma_start(out=wp[:, piece * 4:(piece + 1) * 4, :], in_=src)
                sch.add("pool", f, writes=["wp"], dma="wp", no_waw=(piece > 0))

            pending = []

            def flush(keep):
                while len(pending) > keep:
                    pending.pop(0)()

            for ci in range(8 if DBG_INPROJ else 0):
                own = ci >= 4
                slot = cnt["xc"] % 2
                cnt["xc"] += 1
                for half in range(2):
                    def f(e, slot=slot, ci=ci, half=half):
                        return e.dma_start(out=xc[slot][:, half * 8:(half + 1) * 8, :],
                                           in_=xchunk_src(xT_all, ci * 512, 512)[:, half * 8:(half + 1) * 8, :])
                    sch.add("pool", f, writes=[("xc", slot)], dma=f"xc{slot}", no_waw=(half > 0))

                nb = 4 if own else 2
                c0 = 0 if own else 256
                for tt in range(4):
                    tl = ci * 4 + tt
                    b = PQK_BANKS[cnt["pqk"] % 3]
                    cnt["pqk"] += 1

                    def f(e, slot=slot, tt=tt, b=b, nb=nb, c0=c0):
                        for kc in range(KC):
                            i = e.matmul(bank[b][:, 0:nb * 128], lhsT=xc[slot][:, kc, tt * 128:(tt + 1) * 128],
                                         rhs=wp[:, kc, c0:c0 + nb * 128], start=(kc == 0), stop=(kc == KC - 1))
                        return i
                    sch.add("pe", f, reads=[("xc", slot), "wp"], writes=[("bank", b)])
                    qs = cnt["qk"] % 2
                    cnt["qk"] += 1
                    P = bank[b][:, 0:nb * 128].rearrange("p (n d) -> p n d", n=nb)
                    cc = cs[:, tl, 0:32].unsqueeze(1).broadcast_to([128, nb, 32])
                    sn = cs[:, tl, 32:48].unsqueeze(1).broadcast_to([128, nb, 16])
                    sp_ = cs[:, tl, 48:64].unsqueeze(1).broadcast_to([128, nb, 16])
                    sch.add("dve", lambda e, P=P, qs=qs, nb=nb, cc=cc: e.tensor_tensor(
                        out=rt1[qs][:, 0:nb, :], in0=P[:, :, 0:32], in1=cc, op=ALU.mult),
                        reads=[("bank", b)] + CONST, writes=[("rt1", qs)])
                    sch.add("dve", lambda e, P=P, qs=qs, nb=nb, sn=sn: e.tensor_tensor(
                        out=rt2[qs][:, 0:nb, 0:16], in0=P[:, :, 16:32], in1=sn, op=ALU.mult),
                        reads=[("bank", b)] + CONST, writes=[("rt2a", qs)])
                    sch.add("dve", lambda e, P=P, qs=qs, nb=nb, sp_=sp_: e.tensor_tensor(
                        out=rt2[qs][:, 0:nb, 16:32], in0=P[:, :, 0:16], in1=sp_, op=ALU.mult),
                        reads=[("bank", b)] + CONST, writes=[("rt2b", qs)])
                    sch.add("act", lambda e, P=P, qs=qs, nb=nb: e.copy(out=qktm[qs][:, 0:nb, 32:128], in_=P[:, :, 32:128]),
                            reads=[("bank", b), ("rt1", qs), ("rt2a", qs), ("rt2b", qs)], writes=[("qk_nr", qs)])
                    sch.add("dve", lambda e, qs=qs, nb=nb: e.tensor_tensor(
                        out=qktm[qs][:, 0:nb, 0:32], in0=rt1[qs][:, 0:nb, :], in1=rt2[qs][:, 0:nb, :], op=ALU.add),
                        reads=[("rt1", qs), ("rt2a", qs), ("rt2b", qs)], writes=[("qk_rot", qs)])

                    def tr(qs=qs, nb=nb, own=own, tl=tl):
                        tb = cnt["tp"] % 2
                        cnt["tp"] += 1

                        def f(e):
                            for blk in range(nb):
                                i = e.transpose(tpb[tb][:, blk * 128:(blk + 1) * 128], qktm[qs][:, blk, :], ident[:])
                            return i
                        sch.add("pe", f, reads=[("qk_nr", qs), ("qk_rot", qs)] + CONST, writes=[("tp", tb)])
                        if own:
                            q0 = (tl - 16) * 128
                            sch.add("act", lambda e: e.copy(out=QT[:, :, q0:q0 + 128],
                                                            in_=tpb[tb][:, 0:256].rearrange("p (n t) -> p n t", n=2)),
                                    reads=[("tp", tb)], writes=["QT"])
                            sch.add("act", lambda e: e.copy(out=KT[:, :, tl * 128:(tl + 1) * 128],
                                                            in_=tpb[tb][:, 256:512].rearrange("p (n t) -> p n t", n=2)),
                                    reads=[("tp", tb)], writes=["KT"])
                        else:
                            sch.add("act", lambda e: e.copy(out=KT[:, :, tl * 128:(tl + 1) * 128],
                                                            in_=tpb[tb][:, 0:256].rearrange("p (n t) -> p n t", n=2)),
                                    reads=[("tp", tb)], writes=["KT"])
                    pending.append(tr)
                    flush(1)

                for cb in range(4 if own else 2):
                    b = PF_BANKS[cnt["pf"] % 3]
                    cnt["pf"] += 1

                    def f(e, slot=slot, cb=cb, b=b):
                        for kc in range(KC):
                            i = e.matmul(bank[b][:, :], lhsT=wp[:, kc, 512 + cb * 128:512 + (cb + 1) * 128],
                                         rhs=xc[slot][:, kc, :], start=(kc == 0), stop=(kc == KC - 1))
                        return i
                    sch.add("pe", f, reads=[("xc", slot), "wp"], writes=[("bank", b)])
                    if cb < 2:
                        sch.add("dve", lambda e, cb=cb, b=b, ci=ci: e.tensor_copy(
                            out=VT[:, cb, ci * 512:(ci + 1) * 512], in_=bank[b][:, :]),
                            reads=[("bank", b)], writes=["VT"])
                    else:
                        sch.add("act", lambda e, cb=cb, b=b, ci=ci: e.activation(
                            out=Gs[:, cb - 2, (ci - 4) * 512:(ci - 3) * 512], in_=bank[b][:, :], func=AF.Silu),
                            reads=[("bank", b)], writes=["Gs"])
                    flush(0)
            flush(0)

            for hh in range(2 if DBG_ATTN else 0):
                h = 2 * p + hh
                vsrc = []
                for nbk in range(15, 32):
                    vsrc.append(VT[:, hh, nbk * 128:(nbk + 1) * 128])
                for r4 in range(4):
                    for cbk in range(3, 8):
                        vsrc.append(VT[:, hh, 512 * cbk + r4:512 * (cbk + 1):4])
                for r in range(16):
                    for kb in range(2):
                        vsrc.append(VT[:, hh, 2048 * kb + r:2048 * (kb + 1):16])
                v1 = lambda nbk: nbk - 15
                v2 = lambda r4, cbk: 17 + r4 * 5 + (cbk - 3)
                v3 = lambda r, kb: 37 + r * 2 + kb
                for i0 in range(0, 69, 8):
                    n = min(8, 69 - i0)
                    tb = cnt["tp"] % 2
                    cnt["tp"] += 1

                    def f(e, i0=i0, n=n, tb=tb, vsrc=vsrc):
                        for k in range(n):
                            i = e.transpose(tpb[tb][:, k * 128:(k + 1) * 128], vsrc[i0 + k], ident[:])
                        return i
                    sch.add("pe", f, reads=["VT"] + CONST, writes=[("tp", tb)])
                    sch.add("dve", lambda e, i0=i0, n=n, tb=tb: e.tensor_copy(
                        out=Vblk[:, i0:i0 + n, :], in_=tpb[tb][:, 0:n * 128].rearrange("p (n d) -> p n d", n=n)),
                        reads=[("tp", tb)], writes=["Vblk"])

                for rq in range(4 if DBG_ATTN > 1 else 0):
                    for kb in range(2):
                        sbk = cnt["st"] % 2
                        cnt["st"] += 1
                        ST = bank[sbk]

                        def f(e, rq=rq, kb=kb, ST=ST, hh=hh):
                            m = MCUR if kb == 1 else MPREVH
                            e.matmul(ST[:, :].rearrange("p (n q) -> p n q", n=4), lhsT=ident[:],
                                     rhs=masks[:, m, :].unsqueeze(1).broadcast_to([128, 4, 128]), start=True, stop=False)
                            for rr in range(4):
                                r = 4 * rq + rr
                                i = e.matmul(ST[:, rr * 128:(rr + 1) * 128],
                                             lhsT=KT[:, hh, 2048 * kb + r:2048 * (kb + 1):16],
                                             rhs=QT[:, hh, r:2048:16], start=False, stop=(rr == 3))
                            return i
                        sch.add("pe", f, reads=["QT", "KT"] + CONST, writes=[("bank", sbk)])
                        sch.add("act", lambda e, rq=rq, kb=kb, ST=ST: e.activation(
                            out=PT3[:, kb, 4 * rq:4 * rq + 4, 0:128],
                            in_=ST[:, :].rearrange("p (n q) -> p n q", n=4), func=AF.Exp, scale=SCALE),
                            reads=[("bank", sbk)], writes=[("pt3", kb, rq)])
                PT3R = [[("pt3", kb, rq) for rq in range(4)] for kb in range(2)]
                fin_pending = []

                for g in range(4 if DBG_ATTN > 1 else 0):
                    qb = 512 * g
                    gi = cnt["grp"]
                    cnt["grp"] += 1
                    NUM = bank[2 + gi % 2]
                    DEN = bank[4 + gi % 2]
                    numr = ("bank", 2 + gi % 2)
                    denr = ("bank", 4 + gi % 2)
                    cb = 4 + g
                    tiles = []

                    def qk(ti, g=g, qb=qb, hh=hh, cb=cb):
                        sbk = cnt["st"] % 2
                        cnt["st"] += 1
                        pts = cnt["pt"] % 3 if ti < 2 else 3 + cnt["ptp"] % 3
                        cnt["pt" if ti < 2 else "ptp"] += 1
                        tiles.append(pts)
                        ST = bank[sbk]

                        def f(e):
                            bc = lambda m, n_: masks[:, m, :].unsqueeze(1).broadcast_to([128, n_, 128])
                            if ti == 0:
                                e.matmul(ST[:, :].rearrange("p (n q) -> p n q", n=4), lhsT=ident[:],
                                         rhs=(m1g0[:] if g == 0 else bc(MPREV, 4)), start=True, stop=False)
                                for j in range(4):
                                    nbk = 16 + 4 * g + j - 1
                                    i = e.matmul(ST[:, j * 128:(j + 1) * 128], lhsT=KT[:, hh, nbk * 128:(nbk + 1) * 128],
                                                 rhs=QT[:, hh, qb + j * 128:qb + (j + 1) * 128], start=False, stop=(j == 3))
                            elif ti == 1:
                                e.matmul(ST[:, :].rearrange("p (n q) -> p n q", n=4), lhsT=ident[:],
                                         rhs=bc(MCUR, 4), start=True, stop=False)
                                for j in range(4):
                                    nbk = 16 + 4 * g + j
                                    i = e.matmul(ST[:, j * 128:(j + 1) * 128], lhsT=KT[:, hh, nbk * 128:(nbk + 1) * 128],
                                                 rhs=QT[:, hh, qb + j * 128:qb + (j + 1) * 128], start=False, stop=(j == 3))
                            else:
                                m = MCUR if ti == 3 else (MPREVH if g == 0 else MPREV)
                                cbk = cb if ti == 3 else cb - 1
                                e.matmul(ST[:, :].rearrange("p (n q) -> p n q", n=4), lhsT=ident[:],
                                         rhs=bc(m, 4), start=True, stop=False)
                                for r4 in range(4):
                                    i = e.matmul(ST[:, r4 * 128:(r4 + 1) * 128],
                                                 lhsT=KT[:, hh, 512 * cbk + r4:512 * (cbk + 1):4],
                                                 rhs=QT[:, hh, qb + r4:qb + 512:4], start=False, stop=(r4 == 3))
                            return i
                        sch.add("pe", f, reads=["QT", "KT", "m1g0a", "m1g0b"] + CONST, writes=[("bank", sbk)])
                        if ti < 2:
                            sch.add("act", lambda e: e.activation(out=PT[pts][:], in_=ST[:, :], func=AF.Exp, scale=SCALE),
                                    reads=[("bank", sbk)], writes=[("pt", pts)])
                        else:
                            sch.add("act", lambda e: e.activation(
                                out=PTP[pts - 3][:, :, 0:128],
                                in_=ST[:, :].rearrange("p (n q) -> p n q", n=4), func=AF.Exp, scale=SCALE),
                                reads=[("bank", sbk)], writes=[("pt", pts)])

                    def pv(ti, g=g, qb=qb, hh=hh, cb=cb, NUM=NUM, DEN=DEN, numr=numr, denr=denr):
                        kb = ti - 4
                        if ti < 2:
                            pts = tiles[ti]
                            P_ = PT[pts][:, :]
                            rds = [("pt", pts)]
                        elif ti < 4:
                            pts = tiles[ti]
                            P_ = PTP[pts - 3]
                            rds = [("pt", pts)]
                        else:
                            kb = ti - 4
                            P_ = PT3
                            rds = PT3R[kb]

                        def f(e):
                            if ti < 2:
                                e.matmul(DEN[:, :], lhsT=ones[:], rhs=P_, start=(ti == 0), stop=False)
                            elif ti < 4:
                                e.matmul(DEN[:, :], lhsT=ones[:], rhs=P_[:, :, 0:128].rearrange("p r i -> p i r"),
                                         start=False, stop=(ti == 3))
                            else:
                                e.matmul(DEN[:, :], lhsT=ones[:],
                                         rhs=P_[:, kb, :, 32 * g:32 * g + 32].rearrange("p r j -> p j r"),
                                         start=False, stop=False)
                            if ti == 0:
                                e.matmul(NUM[:, :], lhsT=zeros[:], rhs=P_, start=True, stop=False)
                            if ti < 2:
                                for j in range(4):
                                    nbk = 16 + 4 * g + j - (1 if ti == 0 else 0)
                                    i = e.matmul(NUM[:, j * 128:(j + 1) * 128], lhsT=Vblk[:, v1(nbk), :],
                                                 rhs=P_[:, j * 128:(j + 1) * 128], start=False, stop=False)
                            elif ti < 4:
                                cbk = cb if ti == 3 else cb - 1
                                for r4 in range(4):
                                    i = e.matmul(NUM[:, r4:512:4], lhsT=Vblk[:, v2(r4, cbk), :], rhs=P_[:, r4, 0:128],
                                                 start=False, stop=(ti == 3 and r4 == 3))
                            else:
                                for r in range(16):
                                    i = e.matmul(NUM[:, r:512:16], lhsT=Vblk[:, v3(r, kb), :],
                                                 rhs=P_[:, kb, r, 32 * g:32 * g + 32],
                                                 start=False, stop=False)
                            return i
                        sch.add("pe", f, reads=rds + ["Vblk", "ones", "zeros"], writes=[numr, denr])

                    qk(0); qk(1)
                    if fin_pending:
                        fin_pending.pop()()
                    pv(0); qk(2); pv(1); qk(3); pv(4); pv(2); pv(5); pv(3)

                    def finalize(gi=gi, NUM=NUM, DEN=DEN, numr=numr, denr=denr, h=h, hh=hh, qb=qb):
                        fs = gi % 2
                        sch.add("act", lambda e, fs=fs, DEN=DEN: e.activation(out=lden[fs][:], in_=DEN[:, :], func=AF.Ln),
                                reads=[denr], writes=[("lden", fs)])
                        sch.add("act", lambda e, fs=fs: e.activation(out=rden[fs][:], in_=lden[fs][:], func=AF.Exp, scale=-1.0),
                                reads=[("lden", fs)], writes=[("rden", fs)])
                        sch.add("dve", lambda e, fs=fs, NUM=NUM: e.tensor_tensor(out=lden[fs][:], in0=NUM[:, :], in1=rden[fs][:],
                                                                                 op=ALU.mult),
                                reads=[numr, ("rden", fs)], writes=[("lden", fs)])
                        sch.add("dve", lambda e, fs=fs, h=h, hh=hh, qb=qb: e.tensor_tensor(
                            out=yT_attn[:, h, qb:qb + 512], in0=lden[fs][:], in1=Gs[:, hh, qb:qb + 512], op=ALU.mult),
                            reads=[("lden", fs), "Gs"], writes=["yT_attn"])

                    fin_pending.append(finalize)
                if fin_pending:
                    fin_pending.pop()()
        run_block(nc, sch)


def phase_b(nc, sch, xT_all, w_perm, w_pool, pscale, bm_d, yT_pool, w_out, wo_a):
    import contextlib
    with contextlib.ExitStack() as es:
        sb = lambda name, shape, dtype: es.enter_context(nc.sbuf_tensor(name, shape, dtype))
        xc = [sb(f"b_xc{i}", [128, KC, 512], BF16) for i in range(2)]
        xh = sb("b_xh", [128, KC, 128], BF16)
        wB = sb("b_w", [128, KC, 1024], BF16)
        utm = [sb(f"b_utm{i}", [128, 512], BF16) for i in range(10)]
        Gp = [sb(f"b_gp{i}", [128, 4, 512], BF16) for i in range(2)]
        pooled = [sb(f"b_pl{i}", [128, 4, 512], BF16) for i in range(2)]
        wpl = sb("b_wpool", [128, 4, 2, 256], BF16)
        bmats = sb("b_bm", [128, 3, 4, 128], BF16)
        ps = sb("b_ps", [128, 8], F32)
        bank = [es.enter_context(nc.psum_tensor(f"b_b{i}", [128, 512], F32)) for i in range(8)]

        sch.add("pool", lambda e: e.dma_start(out=wpl[:], in_=w_pool.rearrange("g (cc p) d -> p g cc d", p=128)),
                writes=["wpl"], dma="constg")
        sch.add("pool", lambda e: e.dma_start(out=bmats[:].rearrange("p k g b -> p (k g b)"), in_=bm_d),
                writes=["bmats"], dma="constg", no_waw=True)
        sch.add("sp", lambda e: e.dma_start(out=ps[:], in_=pscale), writes=["ps"], dma="const")
        CONST = ["wpl", "bmats", "ps"]
        BM, BP, BF = 0, 1, 2
        cnt = {"xc": 0, "u": 0, "gp": 0, "pp": 0, "po": 0, "ch": 0}

        for bp in range(2):
            for piece in range(4):
                def f(e, piece=piece, bp=bp):
                    src = w_perm.rearrange("(kc p) c -> p kc c", p=128)[:, piece * 4:(piece + 1) * 4,
                                                                        4096 + bp * 1024:4096 + (bp + 1) * 1024]
                    return e.dma_start(out=wB[:, piece * 4:(piece + 1) * 4, :], in_=src)
                sch.add("pool", f, writes=["wB"], dma="wp", no_waw=(piece > 0))
            if bp == 0:
                for piece in range(4):
                    def f(e, piece=piece):
                        src = w_out.rearrange("(ec p) d -> p ec d", p=128)[:, piece * 2:(piece + 1) * 2, :]
                        return e.dma_start(out=wo_a[:, piece * 2:(piece + 1) * 2, :], in_=src)
                    sch.add("pool", f, writes=["wo_a"], dma="woa", no_waw=(piece > 0))
            sch.add("pool", lambda e: e.dma_start(out=xh[:], in_=xchunk_src(xT_all, 1920, 128)),
                    writes=["xh"], dma="xh")
            uslot = {}
            tails = []

            def u_tile(j, lhs_fn, rd):
                b = cnt["u"] % 2
                cnt["u"] += 1
                s = (j + 1) % 10
                uslot[j] = s

                def f(e):
                    for kc in range(KC):
                        i = e.matmul(bank[b][:, :], lhsT=lhs_fn(kc), rhs=wB[:, kc, 0:512],
                                     start=(kc == 0), stop=(kc == KC - 1))
                    return i
                sch.add("pe", f, reads=[rd, "wB"], writes=[("bank", b)])
                sch.add("act", lambda e: e.copy(out=utm[s][:], in_=bank[b][:, :]),
                        reads=[("bank", b)], writes=[("utm", s)])

            u_tile(-1, lambda kc: xh[:, kc, :], "xh")
            for ci in range(4):
                slot = cnt["xc"] % 2
                cnt["xc"] += 1
                for half in range(2):
                    def f(e, slot=slot, ci=ci, half=half):
                        return e.dma_start(out=xc[slot][:, half * 8:(half + 1) * 8, :],
                                           in_=xchunk_src(xT_all, 2048 + ci * 512, 512)[:, half * 8:(half + 1) * 8, :])
                    sch.add("pool", f, writes=[("xc", slot)], dma=f"xc{slot}", no_waw=(half > 0))
                for tt in range(4):
                    u_tile(ci * 4 + tt, lambda kc, slot=slot, tt=tt: xc[slot][:, kc, tt * 128:(tt + 1) * 128], ("xc", slot))
                cs_ = cnt["ch"] % 2
                cnt["ch"] += 1
                if tails:
                    tails[0][0]()
                for cbi in range(4):
                    b = 2 + cnt["gp"] % 2
                    cnt["gp"] += 1

                    def f(e, slot=slot, cbi=cbi, b=b):
                        for kc in range(KC):
                            i = e.matmul(bank[b][:, :], lhsT=wB[:, kc, 512 + cbi * 128:512 + (cbi + 1) * 128],
                                         rhs=xc[slot][:, kc, :], start=(kc == 0), stop=(kc == KC - 1))
                        return i
                    sch.add("pe", f, reads=[("xc", slot), "wB"], writes=[("bank", b)])
                    sch.add("act", lambda e, cbi=cbi, b=b, cs_=cs_: e.activation(out=Gp[cs_][:, cbi, :], in_=bank[b][:, :],
                                                                                 func=AF.Silu),
                            reads=[("bank", b)], writes=[("gp", cs_)])
                if tails:
                    tails.pop(0)[1]()
                def tail_pooled(ci=ci, cs_=cs_, bp=bp, uslot=uslot):
                    for cbl in range(4):
                        gq = 2 * bp + cbl // 2
                        b = 4 + cnt["pp"] % 2
                        cnt["pp"] += 1

                        def f(e, cbl=cbl, gq=gq, b=b, ci=ci, uslot=uslot):
                            for jj in range(4):
                                j = ci * 4 + jj
                                e.matmul(bank[b][:, jj * 128:(jj + 1) * 128], lhsT=utm[uslot[j - 1]][:, cbl * 128:(cbl + 1) * 128],
                                         rhs=bmats[:, BP, gq, :], start=True, stop=False, skip_group_check=True)
                                i = e.matmul(bank[b][:, jj * 128:(jj + 1) * 128], lhsT=utm[uslot[j]][:, cbl * 128:(cbl + 1) * 128],
                                             rhs=bmats[:, (BF if j == 0 else BM), gq, :], start=False, stop=(jj == 3),
                                             skip_group_check=True)
                            return i
                        rds = [("utm", uslot[ci * 4 + jj]) for jj in range(-1, 4)]
                        sch.add("pe", f, reads=rds + CONST, writes=[("bank", b)])
                        sch.add("dve", lambda e, cbl=cbl, b=b, cs_=cs_: e.tensor_copy(out=pooled[cs_][:, cbl, :], in_=bank[b][:, :]),
                                reads=[("bank", b)], writes=[("pooled", cs_)])

                def tail_po(ci=ci, cs_=cs_, bp=bp):
                    for obl in range(4):
                        ob = 4 * bp + obl
                        gq = ob // 2
                        hf = obl % 2
                        b = 6 + cnt["po"] % 2
                        cnt["po"] += 1

                        def f(e, obl=obl, gq=gq, hf=hf, b=b, cs_=cs_):
                            for cc in range(2):
                                i = e.matmul(bank[b][:, :], lhsT=wpl[:, gq, cc, hf * 128:(hf + 1) * 128],
                                             rhs=pooled[cs_][:, (obl // 2) * 2 + cc, :], start=(cc == 0), stop=(cc == 1))
                            return i
                        sch.add("pe", f, reads=[("pooled", cs_)] + CONST, writes=[("bank", b)])
                        sch.add("dve", lambda e, obl=obl, ob=ob, b=b, cs_=cs_, ci=ci: e.scalar_tensor_tensor(
                            out=yT_pool[:, ob, ci * 512:(ci + 1) * 512], in0=bank[b][:, :], scalar=ps[:, ob:ob + 1],
                            in1=Gp[cs_][:, obl, :], op0=ALU.mult, op1=ALU.mult),
                            reads=[("bank", b), ("gp", cs_)] + CONST, writes=["yT_pool"])

                tails.append((tail_pooled, tail_po))
            while tails:
                tp_, to_ = tails.pop(0)
                tp_()
                to_()
        run_block(nc, sch)


def phase_c(nc, sch, x_own, w_out, ln_g, ln_b, out_d, yT_attn, yT_pool, wo_a, prefetched):
    import contextlib
    with contextlib.ExitStack() as es:
        sb = lambda name, shape, dtype: es.enter_context(nc.sbuf_tensor(name, shape, dtype))
        wo_b = sb("c_wo_b", [128, 8, D], BF16)
        xr = [sb(f"c_xr{i}", [128, D], F32) for i in range(2)]
        z = [sb(f"c_z{i}", [128, D], F32) for i in range(3)]
        junk = sb("c_junk", [128, D], BF16)
        gbc = sb("c_g", [128, D], F32)
        bbc = sb("c_b", [128, D], F32)
        st = [sb(f"c_st{i}", [128, 16], F32) for i in range(3)]
        bank = [es.enter_context(nc.psum_tensor(f"c_b{i}", [128, 512], F32)) for i in range(8)]

        for piece in range(8):
            if piece < 4 and prefetched:
                continue
            tgt = wo_a if piece < 4 else wo_b

            def f(e, piece=piece, tgt=tgt):
                src = w_out.rearrange("(ec p) d -> p ec d", p=128)[:, piece * 2:(piece + 1) * 2, :]
                return e.dma_start(out=tgt[:, (piece % 4) * 2:(piece % 4 + 1) * 2, :], in_=src)
            sch.add("pool", f, writes=["wo_a" if piece < 4 else "wo_b"], dma=("woa" if piece < 4 else "wp"),
                    no_waw=(piece % 4 > 0))
        if prefetched:
            woa_val = sch.dma_cnt["woa"]
            sch.add("pe", lambda e: e.wait_ge(sch.dma_sems["woa"], woa_val) and None)
        sch.add("sp", lambda e: e.dma_start(out=gbc[:], in_=ln_g.partition_broadcast(128)), writes=["gbc"], dma="const")
        sch.add("sp", lambda e: e.dma_start(out=bbc[:], in_=ln_b.partition_broadcast(128)), writes=["bbc"], dma="const",
                no_waw=True)
        CONST = ["gbc", "bbc"]

        def load_x(j):
            s = j % 2
            sch.add("sp", lambda e: e.dma_start(out=xr[s][:], in_=x_own[j * 128:(j + 1) * 128, :]),
                    writes=[("xr", s)], dma=f"xr{s}")

        load_x(0)
        for j in range(16):
            s = j % 2
            zi = j % 3
            Z = z[zi]
            S = st[zi]
            if j + 1 < 16:
                load_x(j + 1)

            for part in range(2):
                def f(e, j=j, s=s, part=part):
                    for ec in range(8 * part, 8 * part + 8):
                        lhs = (yT_attn[:, ec, j * 128:(j + 1) * 128] if ec < 8 else yT_pool[:, ec - 8, j * 128:(j + 1) * 128])
                        for nb in range(4):
                            i = e.matmul(bank[4 * s + nb][:, :], lhsT=lhs,
                                         rhs=(wo_a if ec < 8 else wo_b)[:, ec % 8, nb * 512:(nb + 1) * 512],
                                         start=(ec == 0), stop=(ec == KC - 1))
                    return i
                sch.add("pe", f, reads=["wo_a" if part == 0 else "wo_b"], writes=[("bank", 4 * s + nb) for nb in range(4)])
            for nb in range(4):
                sch.add("dve", lambda e, s=s, nb=nb, Z=Z: e.scalar_tensor_tensor(
                    out=Z[:, nb * 512:(nb + 1) * 512], in0=xr[s][:, nb * 512:(nb + 1) * 512], scalar=ALPHA,
                    in1=bank[4 * s + nb][:, :], op0=ALU.mult, op1=ALU.add),
                    reads=[("xr", s), ("bank", 4 * s + nb)], writes=[("z", zi, nb)])
            zr = [("z", zi, nb) for nb in range(4)]
            sch.add("act", lambda e, Z=Z, S=S: e.activation(out=junk[:], in_=Z[:], func=AF.Identity, accum_out=S[:, 5:6]),
                    reads=zr, writes=["junk", ("tot", zi)])
            sch.add("act", lambda e, Z=Z, S=S: e.activation(out=junk[:], in_=Z[:], func=AF.Square, accum_out=S[:, 4:5]),
                    reads=zr, writes=["junk", ("s2", zi)])
            sch.add("dve", lambda e, S=S: e.tensor_scalar(out=S[:, 6:7], in0=S[:, 5:6], scalar1=1.0 / D, scalar2=None,
                                                          op0=ALU.mult),
                    reads=[("tot", zi)], writes=[("mean", zi)])
            sch.add("dve", lambda e, S=S: e.tensor_tensor(out=S[:, 7:8], in0=S[:, 6:7], in1=S[:, 6:7], op=ALU.mult),
                    reads=[("mean", zi)], writes=[("msq", zi)])
            sch.add("dve", lambda e, S=S: e.scalar_tensor_tensor(out=S[:, 8:9], in0=S[:, 4:5], scalar=1.0 / D, in1=S[:, 7:8],
                                                                 op0=ALU.mult, op1=ALU.subtract),
                    reads=[("s2", zi), ("msq", zi)], writes=[("var", zi)])
            sch.add("dve", lambda e, S=S: e.tensor_scalar(out=S[:, 9:10], in0=S[:, 8:9], scalar1=EPS, scalar2=None,
                                                          op0=ALU.add),
                    reads=[("var", zi)], writes=[("vare", zi)])
            sch.add("act", lambda e, S=S: e.activation(out=S[:, 10:11], in_=S[:, 9:10], func=AF.Sqrt),
                    reads=[("vare", zi)], writes=[("sd", zi)])
            sch.add("dve", lambda e, S=S: e.reciprocal(out=S[:, 11:12], in_=S[:, 10:11]),
                    reads=[("sd", zi)], writes=[("rstd", zi)])
            sch.add("dve", lambda e, S=S: e.scalar_tensor_tensor(out=S[:, 12:13], in0=S[:, 6:7], scalar=-1.0, in1=S[:, 11:12],
                                                                 op0=ALU.mult, op1=ALU.mult),
                    reads=[("mean", zi), ("rstd", zi)], writes=[("nbias", zi)])
            zall = ("zall", zi)
            sch.add("act", lambda e, Z=Z, S=S: e.activation(out=Z[:], in_=Z[:], func=AF.Identity,
                                                            scale=S[:, 11:12], bias=S[:, 12:13]),
                    reads=zr + [("rstd", zi), ("nbias", zi), ("tot", zi), ("s2", zi)], writes=zr + [zall])
            sch.add("pool", lambda e, Z=Z: e.tensor_tensor(out=Z[:], in0=Z[:], in1=gbc[:], op=ALU.mult),
                    reads=[zall] + CONST, writes=zr + [zall])
            sch.add("pool", lambda e, Z=Z: e.tensor_tensor(out=Z[:], in0=Z[:], in1=bbc[:], op=ALU.add),
                    reads=[zall] + CONST, writes=zr + [zall])
            sch.add("sp", lambda e, Z=Z, j=j: e.dma_start(out=out_d[j * 128:(j + 1) * 128, :], in_=Z[:]),
                    reads=[zall] + zr, writes=[("outd", j)], dma=f"out{zi}")
        sch.add("sp", None, reads=[("outd", 13), ("outd", 14), ("outd", 15)])
        run_block(nc, sch)


def _perm_cols():
    cols = []
    for p in range(4):
        for base in (0, 1024, 2048, 4096):
            for h in (2 * p, 2 * p + 1):
                cols.extend(range(base + h * 128, base + (h + 1) * 128))
    for bp in range(2):
        cols.extend(range(3072 + bp * 512, 3072 + (bp + 1) * 512))
        cols.extend(range(5120 + bp * 512, 5120 + (bp + 1) * 512))
    return np.asarray(cols, dtype=np.int64)


def _tables(half):
    lt = np.arange(LT, dtype=np.int64)
    pos = np.maximum(lt - 2048 + 2048 * half, 0).astype(np.float32)
    inv = (np.float32(500000.0) ** (-(2.0 * np.arange(16, dtype=np.float32)) / np.float32(32.0))).astype(np.float32)
    ang = (pos[:, None] * inv[None, :]).astype(np.float32)
    c, s = np.cos(ang).astype(np.float32), np.sin(ang).astype(np.float32)
    cs = np.concatenate([c, c, -s, s], axis=1).astype(np.float32)
    cs = np.ascontiguousarray(cs.reshape(32, 128, 64).transpose(1, 0, 2).reshape(128, 32 * 64))
    k = np.arange(128)[:, None]
    q = np.arange(128)[None, :]
    mcur = np.where(k <= q, 0.0, NEG).astype(np.float32)
    mprev = np.where(k >= q, 0.0, NEG).astype(np.float32)
    mprevh = mprev if half == 1 else np.full((128, 128), NEG, np.float32)
    masks = np.ascontiguousarray(np.stack([mcur, mprev, mprevh]).astype(np.float32).transpose(1, 0, 2).reshape(128, 384))
    bm = np.zeros((3, 4, 128, 128), np.float32)
    tp = np.arange(128)[:, None]
    t = np.arange(128)[None, :]
    for g, w in enumerate(POOL_W):
        main = ((tp <= t) & (tp > t - w)).astype(np.float32) / w - (tp == t)
        prev = ((tp - 128) > (t - w)).astype(np.float32) / w
        cntf = np.minimum(t + 1, w).astype(np.float32)
        first = ((tp <= t) & (tp > t - w)).astype(np.float32) / cntf - (tp == t)
        bm[0, g] = main
        bm[1, g] = prev
        bm[2, g] = first if half == 0 else main
    bm = np.ascontiguousarray(bm.transpose(2, 0, 1, 3).reshape(128, 3 * 4 * 128))
    return cs, masks, bm


_NC_CACHE = {}


def kernel(x, w_in, w_pool, pool_scale, w_out, ln_gain, ln_bias):
    x = np.asarray(x, dtype=np.float32)
    w_perm = np.ascontiguousarray(np.asarray(w_in, np.float32)[0][:, _perm_cols()])
    w_pool0 = np.ascontiguousarray(np.asarray(w_pool, np.float32)[0])
    pscale = np.ascontiguousarray(np.asarray(pool_scale, np.float32)[0].reshape(8, 128).T)
    w_out0 = np.ascontiguousarray(np.asarray(w_out, np.float32)[0])
    g0 = np.ascontiguousarray(np.asarray(ln_gain, np.float32)[0])
    b0 = np.ascontiguousarray(np.asarray(ln_bias, np.float32)[0])
    ident = np.eye(128, dtype=np.float32)
    if "nc" not in _NC_CACHE:
        _NC_CACHE["nc"] = build_program()
    nc = _NC_CACHE["nc"]
    in_maps = []
    for c in range(8):
        b, half = c // 2, c % 2
        own = x[b, half * T:(half + 1) * T, :]
        halo = x[b, 0:T, :] if half == 1 else np.zeros((T, D), np.float32)
        xT_all = np.ascontiguousarray(np.concatenate([halo, own], axis=0).T)
        cs, masks, bm = _tables(half)
        in_maps.append({
            "xT_all": xT_all, "x_own": np.ascontiguousarray(own), "w_perm": w_perm, "w_pool": w_pool0,
            "pscale": pscale, "w_out": w_out0, "ln_g": g0, "ln_b": b0, "cs": cs, "masks": masks,
            "ident": ident, "bmats": bm,
        })
    res = run_bass_kernel_spmd(nc, in_maps, core_ids=list(range(8)))
    out = np.empty((4, 4096, D), np.float32)
    for c in range(8):
        b, half = c // 2, c % 2
        out[b, half * T:(half + 1) * T, :] = res.results[c]["out"]
    return out
```

```python
import numpy as np
import concourse.bass as bass
import concourse.mybir as mybir
from concourse.bass_utils import run_bass_kernel_spmd

F32 = mybir.dt.float32
BF16 = mybir.dt.bfloat16
AF = mybir.ActivationFunctionType
ALU = mybir.AluOpType
AX = mybir.AxisListType

D = 2048
T = 2048
LT = 4096
KC = 16
NEG = -30000.0
ALPHA = 2.0 ** 0.25
EPS = 1e-5
SCALE = 128.0 ** -0.5
POOL_W = (2, 4, 8, 16)

COMPUTE = ("pe", "act", "dve", "pool")


class Op:
    __slots__ = ("eng", "fn", "deps", "is_dma", "sem", "val", "need_sig", "seq")


class Sched:
    def __init__(self, prog, counters, dma_sems, dma_cnt):
        self.prog = prog
        self.counters = counters
        self.dma_sems = dma_sems
        self.dma_cnt = dma_cnt
        self.ops = []
        self.lw = {}
        self.rd = {}

    def add(self, eng, fn, reads=(), writes=(), dma=None, no_waw=False):
        idx = len(self.ops)
        deps = {}
        for r in reads:
            w = self.lw.get(r)
            if w is not None:
                deps[w] = "raw"
        for w_ in writes:
            for ridx in self.rd.get(w_, {}).values():
                deps.setdefault(ridx, "war")
            lw = self.lw.get(w_)
            if lw is not None and not no_waw:
                deps.setdefault(lw, "waw")
        op = Op()
        op.eng = eng
        op.fn = fn
        op.deps = deps
        op.is_dma = dma is not None
        op.sem = dma
        op.val = 0
        op.need_sig = False
        op.seq = 0
        if dma is not None:
            self.dma_cnt[dma] += 16
            op.val = self.dma_cnt[dma]
        key = ("dma", dma) if dma is not None else eng
        for r in reads:
            self.rd.setdefault(r, {})[key] = idx
        for w_ in writes:
            self.lw[w_] = idx
            self.rd[w_] = {}
        self.ops.append(op)
        return idx

    def _needs_wait(self, op, dop, kind):
        if dop.is_dma:
            return True
        if dop.eng != op.eng:
            return True
        if op.is_dma:
            return True
        if kind in ("raw", "waw") and op.eng in ("act", "dve", "pool"):
            return True
        return False

    def finalize(self):
        for op in self.ops:
            for d, kind in op.deps.items():
                dop = self.ops[d]
                if not dop.is_dma and self._needs_wait(op, dop, kind):
                    dop.need_sig = True
        for op in self.ops:
            if op.need_sig:
                self.counters[op.eng] += 1
                op.seq = self.counters[op.eng]

    def emit(self, eng, eh):
        waited = {}
        for op in self.ops:
            if op.eng != eng:
                continue
            waits = {}
            for d, kind in op.deps.items():
                dop = self.ops[d]
                if not self._needs_wait(op, dop, kind):
                    continue
                if dop.is_dma:
                    k, v = ("d", dop.sem), dop.val
                else:
                    k, v = ("p", dop.eng), dop.seq
                if waits.get(k, 0) < v:
                    waits[k] = v
            for k, v in waits.items():
                if waited.get(k, 0) >= v:
                    continue
                waited[k] = v
                sem = self.dma_sems[k[1]] if k[0] == "d" else self.prog[k[1]]
                eh.wait_ge(sem, v)
            if op.fn is None:
                continue
            inst = op.fn(eh)
            if op.is_dma:
                inst.then_inc(self.dma_sems[op.sem], 16)
            elif op.need_sig:
                inst.then_inc(self.prog[eng], 1)


def run_block(nc, sch):
    sch.finalize()
    with nc.Block() as block:
        @block.tensor
        def _(e):
            sch.emit("pe", e)

        @block.scalar
        def _(e):
            sch.emit("act", e)

        @block.vector
        def _(e):
            sch.emit("dve", e)

        @block.gpsimd
        def _(e):
            sch.emit("pool", e)

        @block.sync
        def _(e):
            sch.emit("sp", e)


DMA_SEM_NAMES = ["const", "constg", "wp", "woa", "xc0", "xc1", "xh", "xr0", "xr1", "out0", "out1", "out2"]


def build_program():
    nc = bass.Bass("TRN2", target_bir_lowering=False)
    dt = lambda name, shape, kind="ExternalInput": nc.dram_tensor(name, shape, F32, kind=kind).ap()
    xT_all = dt("xT_all", [D, LT])
    x_own = dt("x_own", [T, D])
    w_perm = dt("w_perm", [D, 6144])
    w_pool = dt("w_pool", [4, 256, 256])
    pscale = dt("pscale", [128, 8])
    w_out = dt("w_out", [D, D])
    ln_g = dt("ln_g", [D])
    ln_b = dt("ln_b", [D])
    cs_d = dt("cs", [128, 32 * 64])
    mask_d = dt("masks", [128, 3 * 128])
    ident_d = dt("ident", [128, 128])
    bm_d = dt("bmats", [128, 3 * 4 * 128])
    out_d = dt("out", [T, D], kind="ExternalOutput")

    import contextlib
    es = contextlib.ExitStack()
    with es:
        prog = {e: es.enter_context(nc.semaphore("prog_" + e)) for e in COMPUTE}
        dma_sems = {n: es.enter_context(nc.semaphore("dma_" + n)) for n in DMA_SEM_NAMES}
        counters = {e: 0 for e in COMPUTE}
        dma_cnt = {n: 0 for n in DMA_SEM_NAMES}
        new_sched = lambda: Sched(prog, counters, dma_sems, dma_cnt)

        import os
        PH = os.environ.get("KPHASES", "abc")
        bufW = es.enter_context(nc.sbuf_tensor("bufW", [128, KC, 1024], BF16))
        bufX = [es.enter_context(nc.sbuf_tensor(f"bufX{i}", [128, KC, 512], BF16)) for i in range(2)]
        yT_attn = es.enter_context(nc.sbuf_tensor("yT_attn", [128, 8, T], BF16))
        if "a" in PH:
            phase_a(nc, new_sched(), xT_all, w_perm, cs_d, mask_d, ident_d, yT_attn, bufW, bufX, "b" in PH)
        yT_pool = es.enter_context(nc.sbuf_tensor("yT_pool", [128, 8, T], BF16))
        wo_a = es.enter_context(nc.sbuf_tensor("wo_a", [128, 8, D], BF16))
        if "b" in PH:
            phase_b(nc, new_sched(), xT_all, w_perm, w_pool, pscale, bm_d, yT_pool, w_out, wo_a, bufW, bufX, "a" in PH)
        if "c" in PH:
            phase_c(nc, new_sched(), x_own, w_out, ln_g, ln_b, out_d, yT_attn, yT_pool, wo_a, bufW, bufX, "b" in PH)
    return nc


def xchunk_src(xT_all, lt0, n):
    return xT_all.rearrange("(kc p) t -> p kc t", p=128)[:, :, lt0:lt0 + n]


def phase_a(nc, sch, xT_all, w_perm, cs_d, mask_d, ident_d, yT_attn, bufW, bufX, prefetch_b):
    import contextlib
    with contextlib.ExitStack() as es:
        sb = lambda name, shape, dtype: es.enter_context(nc.sbuf_tensor(name, shape, dtype))
        xc = bufX
        wp = bufW
        QT = sb("a_QT", [128, 2, T], BF16)
        KT = sb("a_KT", [128, 2, LT], BF16)
        VT = sb("a_VT", [128, 2, LT], BF16)
        Gs = sb("a_Gs", [128, 2, T], BF16)
        Vblk = sb("a_Vblk", [128, 69, 128], BF16)
        PT = [sb(f"a_PT{i}", [128, 512], BF16) for i in range(3)]
        PTP = [sb(f"a_PTP{i}", [128, 4, 130], BF16) for i in range(3)]
        PT3 = sb("a_PTh3", [128, 2, 16, 130], BF16)
        qktm = [sb(f"a_qktm{i}", [128, 4, 128], BF16) for i in range(2)]
        rt1 = [sb(f"a_rt1_{i}", [128, 4, 32], F32) for i in range(2)]
        rt2 = [sb(f"a_rt2_{i}", [128, 4, 32], F32) for i in range(2)]
        cs = sb("a_cs", [128, 32, 64], F32)
        masks = sb("a_masks", [128, 3, 128], BF16)
        ident = sb("a_ident", [128, 128], BF16)
        ones = sb("a_ones", [128, 128], BF16)
        zeros = sb("a_zeros", [128, 128], BF16)
        m1g0 = sb("a_m1g0", [128, 4, 128], BF16)
        rden = [sb(f"a_rden{i}", [128, 512], F32) for i in range(2)]
        lden = [sb(f"a_lden{i}", [128, 512], F32) for i in range(2)]
        bank = [es.enter_context(nc.psum_tensor(f"a_b{i}", [128, 512], F32)) for i in range(6)]
        tpb = [es.enter_context(nc.psum_tensor(f"a_tp{i}", [128, 1024], BF16)) for i in range(2)]

        MCUR, MPREV, MPREVH = 0, 1, 2

        sch.add("sp", lambda e: e.dma_start(out=cs[:].rearrange("p t c -> p (t c)"), in_=cs_d),
                writes=["cs"], dma="const")
        sch.add("pool", lambda e: e.dma_start(out=masks[:].rearrange("p m q -> p (m q)"), in_=mask_d),
                writes=["masks"], dma="constg")
        sch.add("pool", lambda e: e.dma_start(out=ident[:], in_=ident_d),
                writes=["ident"], dma="constg", no_waw=True)
        sch.add("dve", lambda e: e.memset(ones[:], 1.0), writes=["ones"])
        sch.add("dve", lambda e: e.memset(zeros[:], 0.0), writes=["zeros"])
        CONST = ["cs", "masks", "ident"]
        sch.add("dve", lambda e: e.tensor_copy(out=m1g0[:, 0, :], in_=masks[:, 2, :]), reads=CONST, writes=["m1g0a"])
        sch.add("dve", lambda e: e.tensor_copy(out=m1g0[:, 1:4, :],
                                               in_=masks[:, 1, :].unsqueeze(1).broadcast_to([128, 3, 128])),
                reads=CONST, writes=["m1g0b"])

        cnt = {"xc": 0, "pqk": 0, "pf": 0, "tp": 0, "qk": 0, "st": 0, "pt": 0, "ptp": 0, "grp": 0}
        PQK_BANKS = (0, 1, 2)
        PF_BANKS = (3, 4, 5)

        import os
        DBG_PAIRS = int(os.environ.get("KA_PAIRS", "4"))
        DBG_ATTN = int(os.environ.get("KA_ATTN", "9"))
        DBG_INPROJ = int(os.environ.get("KA_INPROJ", "1"))
        for p in range(DBG_PAIRS):
            for piece in range(4):
                def f(e, piece=piece, p=p):
                    src = w_perm.rearrange("(kc p) c -> p kc c", p=128)[:, piece * 4:(piece + 1) * 4,
                                                                        p * 1024:(p + 1) * 1024]
                    return e.dma_start(out=wp[:, piece * 4:(piece + 1) * 4, :], in_=src)
                sch.add("pool", f, writes=["wp"], dma="wp", no_waw=(piece > 0))

            pending = []

            def flush(keep):
                while len(pending) > keep:
                    pending.pop(0)()

            for ci in range(8 if DBG_INPROJ else 0):
                own = ci >= 4
                slot = cnt["xc"] % 2
                cnt["xc"] += 1
                for half in range(2):
                    def f(e, slot=slot, ci=ci, half=half):
                        return e.dma_start(out=xc[slot][:, half * 8:(half + 1) * 8, :],
                                           in_=xchunk_src(xT_all, ci * 512, 512)[:, half * 8:(half + 1) * 8, :])
                    sch.add("pool", f, writes=[("xc", slot)], dma=f"xc{slot}", no_waw=(half > 0))

                nb = 4 if own else 2
                c0 = 0 if own else 256
                for tt in range(4):
                    tl = ci * 4 + tt
                    b = PQK_BANKS[cnt["pqk"] % 3]
                    cnt["pqk"] += 1

                    def f(e, slot=slot, tt=tt, b=b, nb=nb, c0=c0):
                        for kc in range(KC):
                            i = e.matmul(bank[b][:, 0:nb * 128], lhsT=xc[slot][:, kc, tt * 128:(tt + 1) * 128],
                                         rhs=wp[:, kc, c0:c0 + nb * 128], start=(kc == 0), stop=(kc == KC - 1))
                        return i
                    sch.add("pe", f, reads=[("xc", slot), "wp"], writes=[("bank", b)])
                    qs = cnt["qk"] % 2
                    cnt["qk"] += 1
                    P = bank[b][:, 0:nb * 128].rearrange("p (n d) -> p n d", n=nb)
                    cc = cs[:, tl, 0:32].unsqueeze(1).broadcast_to([128, nb, 32])
                    sn = cs[:, tl, 32:48].unsqueeze(1).broadcast_to([128, nb, 16])
                    sp_ = cs[:, tl, 48:64].unsqueeze(1).broadcast_to([128, nb, 16])
                    sch.add("dve", lambda e, P=P, qs=qs, nb=nb, cc=cc: e.tensor_tensor(
                        out=rt1[qs][:, 0:nb, :], in0=P[:, :, 0:32], in1=cc, op=ALU.mult),
                        reads=[("bank", b)] + CONST, writes=[("rt1", qs)])
                    sch.add("dve", lambda e, P=P, qs=qs, nb=nb, sn=sn: e.tensor_tensor(
                        out=rt2[qs][:, 0:nb, 0:16], in0=P[:, :, 16:32], in1=sn, op=ALU.mult),
                        reads=[("bank", b)] + CONST, writes=[("rt2a", qs)])
                    sch.add("dve", lambda e, P=P, qs=qs, nb=nb, sp_=sp_: e.tensor_tensor(
                        out=rt2[qs][:, 0:nb, 16:32], in0=P[:, :, 0:16], in1=sp_, op=ALU.mult),
                        reads=[("bank", b)] + CONST, writes=[("rt2b", qs)])
                    sch.add("act", lambda e, P=P, qs=qs, nb=nb: e.copy(out=qktm[qs][:, 0:nb, 32:128], in_=P[:, :, 32:128]),
                            reads=[("bank", b), ("rt1", qs), ("rt2a", qs), ("rt2b", qs)], writes=[("qk_nr", qs)])
                    sch.add("dve", lambda e, qs=qs, nb=nb: e.tensor_tensor(
                        out=qktm[qs][:, 0:nb, 0:32], in0=rt1[qs][:, 0:nb, :], in1=rt2[qs][:, 0:nb, :], op=ALU.add),
                        reads=[("rt1", qs), ("rt2a", qs), ("rt2b", qs)], writes=[("qk_rot", qs)])

                    def tr(qs=qs, nb=nb, own=own, tl=tl):
                        tb = cnt["tp"] % 2
                        cnt["tp"] += 1

                        def f(e):
                            for blk in range(nb):
                                i = e.transpose(tpb[tb][:, blk * 128:(blk + 1) * 128], qktm[qs][:, blk, :], ident[:])
                            return i
                        sch.add("pe", f, reads=[("qk_nr", qs), ("qk_rot", qs)] + CONST, writes=[("tp", tb)])
                        if own:
                            q0 = (tl - 16) * 128
                            sch.add("act", lambda e: e.copy(out=QT[:, :, q0:q0 + 128],
                                                            in_=tpb[tb][:, 0:256].rearrange("p (n t) -> p n t", n=2)),
                                    reads=[("tp", tb)], writes=["QT"])
                            sch.add("act", lambda e: e.copy(out=KT[:, :, tl * 128:(tl + 1) * 128],
                                                            in_=tpb[tb][:, 256:512].rearrange("p (n t) -> p n t", n=2)),
                                    reads=[("tp", tb)], writes=["KT"])
                        else:
                            sch.add("act", lambda e: e.copy(out=KT[:, :, tl * 128:(tl + 1) * 128],
                                                            in_=tpb[tb][:, 0:256].rearrange("p (n t) -> p n t", n=2)),
                                    reads=[("tp", tb)], writes=["KT"])
                    pending.append(tr)
                    flush(1)

                for cb in range(4 if own else 2):
                    b = PF_BANKS[cnt["pf"] % 3]
                    cnt["pf"] += 1

                    def f(e, slot=slot, cb=cb, b=b):
                        for kc in range(KC):
                            i = e.matmul(bank[b][:, :], lhsT=wp[:, kc, 512 + cb * 128:512 + (cb + 1) * 128],
                                         rhs=xc[slot][:, kc, :], start=(kc == 0), stop=(kc == KC - 1))
                        return i
                    sch.add("pe", f, reads=[("xc", slot), "wp"], writes=[("bank", b)])
                    if cb < 2:
                        sch.add("dve", lambda e, cb=cb, b=b, ci=ci: e.tensor_copy(
                            out=VT[:, cb, ci * 512:(ci + 1) * 512], in_=bank[b][:, :]),
                            reads=[("bank", b)], writes=["VT"])
                    else:
                        sch.add("act", lambda e, cb=cb, b=b, ci=ci: e.activation(
                            out=Gs[:, cb - 2, (ci - 4) * 512:(ci - 3) * 512], in_=bank[b][:, :], func=AF.Silu),
                            reads=[("bank", b)], writes=["Gs"])
                    flush(0)
            flush(0)
            if p == DBG_PAIRS - 1 and prefetch_b:
                for piece in range(4):
                    def f(e, piece=piece):
                        src = w_perm.rearrange("(kc p) c -> p kc c", p=128)[:, piece * 4:(piece + 1) * 4, 4096:5120]
                        return e.dma_start(out=wp[:, piece * 4:(piece + 1) * 4, :], in_=src)
                    sch.add("pool", f, writes=["wp"], dma="wp", no_waw=(piece > 0))
                for half in range(2):
                    def f(e, half=half):
                        return e.dma_start(out=xc[0][:, half * 8:(half + 1) * 8, :],
                                           in_=xchunk_src(xT_all, 2048, 512)[:, half * 8:(half + 1) * 8, :])
                    sch.add("pool", f, writes=[("xc", 0)], dma="xc0", no_waw=(half > 0))
                sch.add("pool", lambda e: e.dma_start(out=xc[1][:, :, 0:128], in_=xchunk_src(xT_all, 1920, 128)),
                        writes=[("xc", 1)], dma="xc1")

            for hh in range(2 if DBG_ATTN else 0):
                h = 2 * p + hh
                vsrc = []
                for nbk in range(15, 32):
                    vsrc.append(VT[:, hh, nbk * 128:(nbk + 1) * 128])
                for r4 in range(4):
                    for cbk in range(3, 8):
                        vsrc.append(VT[:, hh, 512 * cbk + r4:512 * (cbk + 1):4])
                for r in range(16):
                    for kb in range(2):
                        vsrc.append(VT[:, hh, 2048 * kb + r:2048 * (kb + 1):16])
                v1 = lambda nbk: nbk - 15
                v2 = lambda r4, cbk: 17 + r4 * 5 + (cbk - 3)
                v3 = lambda r, kb: 37 + r * 2 + kb
                for i0 in range(0, 69, 8):
                    n = min(8, 69 - i0)
                    tb = cnt["tp"] % 2
                    cnt["tp"] += 1

                    def f(e, i0=i0, n=n, tb=tb, vsrc=vsrc):
                        for k in range(n):
                            i = e.transpose(tpb[tb][:, k * 128:(k + 1) * 128], vsrc[i0 + k], ident[:])
                        return i
                    sch.add("pe", f, reads=["VT"] + CONST, writes=[("tp", tb)])
                    sch.add("dve", lambda e, i0=i0, n=n, tb=tb: e.tensor_copy(
                        out=Vblk[:, i0:i0 + n, :], in_=tpb[tb][:, 0:n * 128].rearrange("p (n d) -> p n d", n=n)),
                        reads=[("tp", tb)], writes=["Vblk"])

                for rq in range(4 if DBG_ATTN > 1 else 0):
                    for kb in range(2):
                        sbk = cnt["st"] % 2
                        cnt["st"] += 1
                        ST = bank[sbk]

                        def f(e, rq=rq, kb=kb, ST=ST, hh=hh):
                            m = MCUR if kb == 1 else MPREVH
                            e.matmul(ST[:, :].rearrange("p (n q) -> p n q", n=4), lhsT=ident[:],
                                     rhs=masks[:, m, :].unsqueeze(1).broadcast_to([128, 4, 128]), start=True, stop=False)
                            for rr in range(4):
                                r = 4 * rq + rr
                                i = e.matmul(ST[:, rr * 128:(rr + 1) * 128],
                                             lhsT=KT[:, hh, 2048 * kb + r:2048 * (kb + 1):16],
                                             rhs=QT[:, hh, r:2048:16], start=False, stop=(rr == 3))
                            return i
                        sch.add("pe", f, reads=["QT", "KT"] + CONST, writes=[("bank", sbk)])
                        sch.add("act", lambda e, rq=rq, kb=kb, ST=ST: e.activation(
                            out=PT3[:, kb, 4 * rq:4 * rq + 4, 0:128],
                            in_=ST[:, :].rearrange("p (n q) -> p n q", n=4), func=AF.Exp, scale=SCALE),
                            reads=[("bank", sbk)], writes=[("pt3", kb, rq)])
                PT3R = [[("pt3", kb, rq) for rq in range(4)] for kb in range(2)]
                fin_pending = []

                for g in range(4 if DBG_ATTN > 1 else 0):
                    qb = 512 * g
                    gi = cnt["grp"]
                    cnt["grp"] += 1
                    NUM = bank[2 + gi % 2]
                    DEN = bank[4 + gi % 2]
                    numr = ("bank", 2 + gi % 2)
                    denr = ("bank", 4 + gi % 2)
                    cb = 4 + g
                    tiles = []

                    def qk(ti, g=g, qb=qb, hh=hh, cb=cb):
                        sbk = cnt["st"] % 2
                        cnt["st"] += 1
                        pts = cnt["pt"] % 3 if ti < 2 else 3 + cnt["ptp"] % 3
                        cnt["pt" if ti < 2 else "ptp"] += 1
                        tiles.append(pts)
                        ST = bank[sbk]

                        def f(e):
                            bc = lambda m, n_: masks[:, m, :].unsqueeze(1).broadcast_to([128, n_, 128])
                            if ti == 0:
                                e.matmul(ST[:, :].rearrange("p (n q) -> p n q", n=4), lhsT=ident[:],
                                         rhs=(m1g0[:] if g == 0 else bc(MPREV, 4)), start=True, stop=False)
                                for j in range(4):
                                    nbk = 16 + 4 * g + j - 1
                                    i = e.matmul(ST[:, j * 128:(j + 1) * 128], lhsT=KT[:, hh, nbk * 128:(nbk + 1) * 128],
                                                 rhs=QT[:, hh, qb + j * 128:qb + (j + 1) * 128], start=False, stop=(j == 3))
                            elif ti == 1:
                                e.matmul(ST[:, :].rearrange("p (n q) -> p n q", n=4), lhsT=ident[:],
                                         rhs=bc(MCUR, 4), start=True, stop=False)
                                for j in range(4):
                                    nbk = 16 + 4 * g + j
                                    i = e.matmul(ST[:, j * 128:(j + 1) * 128], lhsT=KT[:, hh, nbk * 128:(nbk + 1) * 128],
                                                 rhs=QT[:, hh, qb + j * 128:qb + (j + 1) * 128], start=False, stop=(j == 3))
                            else:
                                m = MCUR if ti == 3 else (MPREVH if g == 0 else MPREV)
                                cbk = cb if ti == 3 else cb - 1
                                e.matmul(ST[:, :].rearrange("p (n q) -> p n q", n=4), lhsT=ident[:],
                                         rhs=bc(m, 4), start=True, stop=False)
                                for r4 in range(4):
                                    i = e.matmul(ST[:, r4 * 128:(r4 + 1) * 128],
                                                 lhsT=KT[:, hh, 512 * cbk + r4:512 * (cbk + 1):4],
                                                 rhs=QT[:, hh, qb + r4:qb + 512:4], start=False, stop=(r4 == 3))
                            return i
                        sch.add("pe", f, reads=["QT", "KT", "m1g0a", "m1g0b"] + CONST, writes=[("bank", sbk)])
                        if ti < 2:
                            sch.add("act", lambda e: e.activation(out=PT[pts][:], in_=ST[:, :], func=AF.Exp, scale=SCALE),
                                    reads=[("bank", sbk)], writes=[("pt", pts)])
                        else:
                            sch.add("act", lambda e: e.activation(
                                out=PTP[pts - 3][:, :, 0:128],
                                in_=ST[:, :].rearrange("p (n q) -> p n q", n=4), func=AF.Exp, scale=SCALE),
                                reads=[("bank", sbk)], writes=[("pt", pts)])

                    def pv(ti, g=g, qb=qb, hh=hh, cb=cb, NUM=NUM, DEN=DEN, numr=numr, denr=denr):
                        kb = ti - 4
                        if ti < 2:
                            pts = tiles[ti]
                            P_ = PT[pts][:, :]
                            rds = [("pt", pts)]
                        elif ti < 4:
                            pts = tiles[ti]
                            P_ = PTP[pts - 3]
                            rds = [("pt", pts)]
                        else:
                            kb = ti - 4
                            P_ = PT3
                            rds = PT3R[kb]

                        def f(e):
                            if ti < 2:
                                e.matmul(DEN[:, :], lhsT=ones[:], rhs=P_, start=(ti == 0), stop=False)
                            elif ti < 4:
                                e.matmul(DEN[:, :], lhsT=ones[:], rhs=P_[:, :, 0:128].rearrange("p r i -> p i r"),
                                         start=False, stop=(ti == 3))
                            else:
                                e.matmul(DEN[:, :], lhsT=ones[:],
                                         rhs=P_[:, kb, :, 32 * g:32 * g + 32].rearrange("p r j -> p j r"),
                                         start=False, stop=False)
                            if ti == 0:
                                e.matmul(NUM[:, :], lhsT=zeros[:], rhs=P_, start=True, stop=False)
                            if ti < 2:
                                for j in range(4):
                                    nbk = 16 + 4 * g + j - (1 if ti == 0 else 0)
                                    i = e.matmul(NUM[:, j * 128:(j + 1) * 128], lhsT=Vblk[:, v1(nbk), :],
                                                 rhs=P_[:, j * 128:(j + 1) * 128], start=False, stop=False)
                            elif ti < 4:
                                cbk = cb if ti == 3 else cb - 1
                                for r4 in range(4):
                                    i = e.matmul(NUM[:, r4:512:4], lhsT=Vblk[:, v2(r4, cbk), :], rhs=P_[:, r4, 0:128],
                                                 start=False, stop=(ti == 3 and r4 == 3))
                            else:
                                for r in range(16):
                                    i = e.matmul(NUM[:, r:512:16], lhsT=Vblk[:, v3(r, kb), :],
                                                 rhs=P_[:, kb, r, 32 * g:32 * g + 32],
                                                 start=False, stop=False)
                            return i
                        sch.add("pe", f, reads=rds + ["Vblk", "ones", "zeros"], writes=[numr, denr])

                    qk(0); qk(1)
                    if fin_pending:
                        fin_pending.pop()()
                    pv(0); qk(2); pv(1); qk(3); pv(4); pv(2); pv(5); pv(3)

                    def finalize(gi=gi, NUM=NUM, DEN=DEN, numr=numr, denr=denr, h=h, hh=hh, qb=qb):
                        fs = gi % 2
                        sch.add("act", lambda e, fs=fs, DEN=DEN: e.activation(out=lden[fs][:], in_=DEN[:, :], func=AF.Ln),
                                reads=[denr], writes=[("lden", fs)])
                        sch.add("act", lambda e, fs=fs: e.activation(out=rden[fs][:], in_=lden[fs][:], func=AF.Exp, scale=-1.0),
                                reads=[("lden", fs)], writes=[("rden", fs)])
                        sch.add("dve", lambda e, fs=fs, NUM=NUM: e.tensor_tensor(out=lden[fs][:], in0=NUM[:, :], in1=rden[fs][:],
                                                                                 op=ALU.mult),
                                reads=[numr, ("rden", fs)], writes=[("lden", fs)])
                        sch.add("dve", lambda e, fs=fs, h=h, hh=hh, qb=qb: e.tensor_tensor(
                            out=yT_attn[:, h, qb:qb + 512], in0=lden[fs][:], in1=Gs[:, hh, qb:qb + 512], op=ALU.mult),
                            reads=[("lden", fs), "Gs"], writes=["yT_attn"])

                    fin_pending.append(finalize)
                if fin_pending:
                    fin_pending.pop()()
        run_block(nc, sch)


def phase_b(nc, sch, xT_all, w_perm, w_pool, pscale, bm_d, yT_pool, w_out, wo_a, bufW, bufX, prefetched):
    import contextlib
    with contextlib.ExitStack() as es:
        sb = lambda name, shape, dtype: es.enter_context(nc.sbuf_tensor(name, shape, dtype))
        xc = bufX
        xh = sb("b_xh", [128, KC, 128], BF16)
        wB = bufW
        utm = [sb(f"b_utm{i}", [128, 512], BF16) for i in range(10)]
        Gp = [sb(f"b_gp{i}", [128, 4, 512], BF16) for i in range(2)]
        pooled = [sb(f"b_pl{i}", [128, 4, 512], BF16) for i in range(2)]
        wpl = sb("b_wpool", [128, 4, 2, 256], BF16)
        bmats = sb("b_bm", [128, 3, 4, 128], BF16)
        ps = sb("b_ps", [128, 8], F32)
        bank = [es.enter_context(nc.psum_tensor(f"b_b{i}", [128, 512], F32)) for i in range(8)]

        sch.add("pool", lambda e: e.dma_start(out=wpl[:], in_=w_pool.rearrange("g (cc p) d -> p g cc d", p=128)),
                writes=["wpl"], dma="constg")
        sch.add("pool", lambda e: e.dma_start(out=bmats[:].rearrange("p k g b -> p (k g b)"), in_=bm_d),
                writes=["bmats"], dma="constg", no_waw=True)
        sch.add("sp", lambda e: e.dma_start(out=ps[:], in_=pscale), writes=["ps"], dma="const")
        CONST = ["wpl", "bmats", "ps"]
        BM, BP, BF = 0, 1, 2
        cnt = {"xc": 0, "u": 0, "gp": 0, "pp": 0, "po": 0, "ch": 0}

        if prefetched:
            for nm in ("wp", "xc0", "xc1"):
                sch.add("pe", lambda e, nm=nm, v=sch.dma_cnt[nm]: e.wait_ge(sch.dma_sems[nm], v) and None)

        def prefetch_wo_a():
            for piece in range(4):
                def f(e, piece=piece):
                    src = w_out.rearrange("(ec p) d -> p ec d", p=128)[:, piece * 2:(piece + 1) * 2, :]
                    return e.dma_start(out=wo_a[:, piece * 2:(piece + 1) * 2, :], in_=src)
                sch.add("pool", f, writes=["wo_a"], dma="woa", no_waw=(piece > 0))

        for bp in range(2):
            pre = prefetched and bp == 0
            if not pre:
                for piece in range(4):
                    def f(e, piece=piece, bp=bp):
                        src = w_perm.rearrange("(kc p) c -> p kc c", p=128)[:, piece * 4:(piece + 1) * 4,
                                                                            4096 + bp * 1024:4096 + (bp + 1) * 1024]
                        return e.dma_start(out=wB[:, piece * 4:(piece + 1) * 4, :], in_=src)
                    sch.add("pool", f, writes=["wB"], dma="wp", no_waw=(piece > 0))
                sch.add("pool", lambda e: e.dma_start(out=xh[:], in_=xchunk_src(xT_all, 1920, 128)),
                        writes=["xh"], dma="xh")
            uslot = {}
            tails = []

            def u_tile(j, lhs_fn, rd):
                b = cnt["u"] % 2
                cnt["u"] += 1
                s = (j + 1) % 10
                uslot[j] = s

                def f(e):
                    for kc in range(KC):
                        i = e.matmul(bank[b][:, :], lhsT=lhs_fn(kc), rhs=wB[:, kc, 0:512],
                                     start=(kc == 0), stop=(kc == KC - 1))
                    return i
                sch.add("pe", f, reads=[rd, "wB"], writes=[("bank", b)])
                sch.add("act", lambda e: e.copy(out=utm[s][:], in_=bank[b][:, :]),
                        reads=[("bank", b)], writes=[("utm", s)])

            if pre:
                u_tile(-1, lambda kc: xc[1][:, kc, 0:128], ("xc", 1))
            else:
                u_tile(-1, lambda kc: xh[:, kc, :], "xh")
            for ci in range(4):
                slot = cnt["xc"] % 2
                cnt["xc"] += 1
                for half in range(2):
                    if pre and ci == 0:
                        continue
                    def f(e, slot=slot, ci=ci, half=half):
                        return e.dma_start(out=xc[slot][:, half * 8:(half + 1) * 8, :],
                                           in_=xchunk_src(xT_all, 2048 + ci * 512, 512)[:, half * 8:(half + 1) * 8, :])
                    sch.add("pool", f, writes=[("xc", slot)], dma=f"xc{slot}", no_waw=(half > 0))
                if bp == 0 and ci == 1:
                    prefetch_wo_a()
                for tt in range(4):
                    u_tile(ci * 4 + tt, lambda kc, slot=slot, tt=tt: xc[slot][:, kc, tt * 128:(tt + 1) * 128], ("xc", slot))
                cs_ = cnt["ch"] % 2
                cnt["ch"] += 1
                if tails:
                    tails[0][0]()
                for cbi in range(4):
                    b = 2 + cnt["gp"] % 2
                    cnt["gp"] += 1

                    def f(e, slot=slot, cbi=cbi, b=b):
                        for kc in range(KC):
                            i = e.matmul(bank[b][:, :], lhsT=wB[:, kc, 512 + cbi * 128:512 + (cbi + 1) * 128],
                                         rhs=xc[slot][:, kc, :], start=(kc == 0), stop=(kc == KC - 1))
                        return i
                    sch.add("pe", f, reads=[("xc", slot), "wB"], writes=[("bank", b)])
                    sch.add("act", lambda e, cbi=cbi, b=b, cs_=cs_: e.activation(out=Gp[cs_][:, cbi, :], in_=bank[b][:, :],
                                                                                 func=AF.Silu),
                            reads=[("bank", b)], writes=[("gp", cs_)])
                if tails:
                    tails.pop(0)[1]()
                def tail_pooled(ci=ci, cs_=cs_, bp=bp, uslot=uslot):
                    for cbl in range(4):
                        gq = 2 * bp + cbl // 2
                        b = 4 + cnt["pp"] % 2
                        cnt["pp"] += 1

                        def f(e, cbl=cbl, gq=gq, b=b, ci=ci, uslot=uslot):
                            for jj in range(4):
                                j = ci * 4 + jj
                                e.matmul(bank[b][:, jj * 128:(jj + 1) * 128], lhsT=utm[uslot[j - 1]][:, cbl * 128:(cbl + 1) * 128],
                                         rhs=bmats[:, BP, gq, :], start=True, stop=False, skip_group_check=True)
                                i = e.matmul(bank[b][:, jj * 128:(jj + 1) * 128], lhsT=utm[uslot[j]][:, cbl * 128:(cbl + 1) * 128],
                                             rhs=bmats[:, (BF if j == 0 else BM), gq, :], start=False, stop=(jj == 3),
                                             skip_group_check=True)
                            return i
                        rds = [("utm", uslot[ci * 4 + jj]) for jj in range(-1, 4)]
                        sch.add("pe", f, reads=rds + CONST, writes=[("bank", b)])
                        sch.add("dve", lambda e, cbl=cbl, b=b, cs_=cs_: e.tensor_copy(out=pooled[cs_][:, cbl, :], in_=bank[b][:, :]),
                                reads=[("bank", b)], writes=[("pooled", cs_)])

                def tail_po(ci=ci, cs_=cs_, bp=bp):
                    for obl in range(4):
                        ob = 4 * bp + obl
                        gq = ob // 2
                        hf = obl % 2
                        b = 6 + cnt["po"] % 2
                        cnt["po"] += 1

                        def f(e, obl=obl, gq=gq, hf=hf, b=b, cs_=cs_):
                            for cc in range(2):
                                i = e.matmul(bank[b][:, :], lhsT=wpl[:, gq, cc, hf * 128:(hf + 1) * 128],
                                             rhs=pooled[cs_][:, (obl // 2) * 2 + cc, :], start=(cc == 0), stop=(cc == 1))
                            return i
                        sch.add("pe", f, reads=[("pooled", cs_)] + CONST, writes=[("bank", b)])
                        sch.add("dve", lambda e, obl=obl, ob=ob, b=b, cs_=cs_, ci=ci: e.scalar_tensor_tensor(
                            out=yT_pool[:, ob, ci * 512:(ci + 1) * 512], in0=bank[b][:, :], scalar=ps[:, ob:ob + 1],
                            in1=Gp[cs_][:, obl, :], op0=ALU.mult, op1=ALU.mult),
                            reads=[("bank", b), ("gp", cs_)] + CONST, writes=["yT_pool"])

                tails.append((tail_pooled, tail_po))
            if bp == 1:
                for piece in range(4, 8):
                    def f(e, piece=piece):
                        src = w_out.rearrange("(ec p) d -> p ec d", p=128)[:, piece * 2:(piece + 1) * 2, :]
                        dst = wB[:].rearrange("p a b -> p (a b)").rearrange("p (e d) -> p e d", d=D)
                        return e.dma_start(out=dst[:, (piece - 4) * 2:(piece - 3) * 2, :], in_=src)
                    sch.add("pool", f, writes=["wB"], dma="wp", no_waw=(piece > 4))
            while tails:
                tp_, to_ = tails.pop(0)
                tp_()
                to_()
        run_block(nc, sch)


def phase_c(nc, sch, x_own, w_out, ln_g, ln_b, out_d, yT_attn, yT_pool, wo_a, bufW, bufX, prefetched):
    import contextlib
    with contextlib.ExitStack() as es:
        sb = lambda name, shape, dtype: es.enter_context(nc.sbuf_tensor(name, shape, dtype))
        wo_b = bufW[:].rearrange("p a b -> p (a b)").rearrange("p (e d) -> p e d", d=D)
        fx = [b_[:].rearrange("p a b -> p (a b)").bitcast(F32) for b_ in bufX]
        xr = [fx[0][:, 0:D], fx[0][:, D:2 * D]]
        z = [fx[1][:, 0:D], fx[1][:, D:2 * D], sb("c_z2", [128, D], F32)[:]]
        junk = sb("c_junk", [128, D], BF16)
        gbc = sb("c_g", [128, D], F32)
        bbc = sb("c_b", [128, D], F32)
        st = [sb(f"c_st{i}", [128, 16], F32) for i in range(3)]
        bank = [es.enter_context(nc.psum_tensor(f"c_b{i}", [128, 512], F32)) for i in range(8)]

        for piece in range(8):
            if prefetched:
                continue
            tgt = wo_a if piece < 4 else wo_b

            def f(e, piece=piece, tgt=tgt):
                src = w_out.rearrange("(ec p) d -> p ec d", p=128)[:, piece * 2:(piece + 1) * 2, :]
                return e.dma_start(out=tgt[:, (piece % 4) * 2:(piece % 4 + 1) * 2, :], in_=src)
            sch.add("pool", f, writes=["wo_a" if piece < 4 else "wo_b"], dma=("woa" if piece < 4 else "wp"),
                    no_waw=(piece % 4 > 0))
        if prefetched:
            for nm in ("woa", "wp"):
                sch.add("pe", lambda e, nm=nm, v=sch.dma_cnt[nm]: e.wait_ge(sch.dma_sems[nm], v) and None)
        sch.add("sp", lambda e: e.dma_start(out=gbc[:], in_=ln_g.partition_broadcast(128)), writes=["gbc"], dma="const")
        sch.add("sp", lambda e: e.dma_start(out=bbc[:], in_=ln_b.partition_broadcast(128)), writes=["bbc"], dma="const",
                no_waw=True)
        CONST = ["gbc", "bbc"]

        def load_x(j):
            s = j % 2
            sch.add("sp", lambda e: e.dma_start(out=xr[s], in_=x_own[j * 128:(j + 1) * 128, :]),
                    writes=[("xr", s)], dma=f"xr{s}")

        load_x(0)
        for j in range(16):
            s = j % 2
            zi = j % 3
            Z = z[zi]
            S = st[zi]
            if j + 1 < 16:
                load_x(j + 1)

            for part in range(2):
                def f(e, j=j, s=s, part=part):
                    for ec in range(8 * part, 8 * part + 8):
                        lhs = (yT_attn[:, ec, j * 128:(j + 1) * 128] if ec < 8 else yT_pool[:, ec - 8, j * 128:(j + 1) * 128])
                        for nb in range(4):
                            i = e.matmul(bank[4 * s + nb][:, :], lhsT=lhs,
                                         rhs=(wo_a if ec < 8 else wo_b)[:, ec % 8, nb * 512:(nb + 1) * 512],
                                         start=(ec == 0), stop=(ec == KC - 1))
                    return i
                sch.add("pe", f, reads=["wo_a" if part == 0 else "wo_b"], writes=[("bank", 4 * s + nb) for nb in range(4)])
            for nb in range(4):
                sch.add("dve", lambda e, s=s, nb=nb, Z=Z: e.scalar_tensor_tensor(
                    out=Z[:, nb * 512:(nb + 1) * 512], in0=xr[s][:, nb * 512:(nb + 1) * 512], scalar=ALPHA,
                    in1=bank[4 * s + nb][:, :], op0=ALU.mult, op1=ALU.add),
                    reads=[("xr", s), ("bank", 4 * s + nb)], writes=[("z", zi, nb)])
            zr = [("z", zi, nb) for nb in range(4)]
            sch.add("act", lambda e, Z=Z, S=S: e.activation(out=junk[:], in_=Z[:], func=AF.Identity, accum_out=S[:, 5:6]),
                    reads=zr, writes=["junk", ("tot", zi)])
            sch.add("act", lambda e, Z=Z, S=S: e.activation(out=junk[:], in_=Z[:], func=AF.Square, accum_out=S[:, 4:5]),
                    reads=zr, writes=["junk", ("s2", zi)])
            sch.add("dve", lambda e, S=S: e.tensor_scalar(out=S[:, 6:7], in0=S[:, 5:6], scalar1=1.0 / D, scalar2=None,
                                                          op0=ALU.mult),
                    reads=[("tot", zi)], writes=[("mean", zi)])
            sch.add("dve", lambda e, S=S: e.tensor_tensor(out=S[:, 7:8], in0=S[:, 6:7], in1=S[:, 6:7], op=ALU.mult),
                    reads=[("mean", zi)], writes=[("msq", zi)])
            sch.add("dve", lambda e, S=S: e.scalar_tensor_tensor(out=S[:, 8:9], in0=S[:, 4:5], scalar=1.0 / D, in1=S[:, 7:8],
                                                                 op0=ALU.mult, op1=ALU.subtract),
                    reads=[("s2", zi), ("msq", zi)], writes=[("var", zi)])
            sch.add("dve", lambda e, S=S: e.tensor_scalar(out=S[:, 9:10], in0=S[:, 8:9], scalar1=EPS, scalar2=None,
                                                          op0=ALU.add),
                    reads=[("var", zi)], writes=[("vare", zi)])
            sch.add("act", lambda e, S=S: e.activation(out=S[:, 10:11], in_=S[:, 9:10], func=AF.Sqrt),
                    reads=[("vare", zi)], writes=[("sd", zi)])
            sch.add("dve", lambda e, S=S: e.reciprocal(out=S[:, 11:12], in_=S[:, 10:11]),
                    reads=[("sd", zi)], writes=[("rstd", zi)])
            sch.add("dve", lambda e, S=S: e.scalar_tensor_tensor(out=S[:, 12:13], in0=S[:, 6:7], scalar=-1.0, in1=S[:, 11:12],
                                                                 op0=ALU.mult, op1=ALU.mult),
                    reads=[("mean", zi), ("rstd", zi)], writes=[("nbias", zi)])
            zall = ("zall", zi)
            sch.add("act", lambda e, Z=Z, S=S: e.activation(out=Z[:], in_=Z[:], func=AF.Identity,
                                                            scale=S[:, 11:12], bias=S[:, 12:13]),
                    reads=zr + [("rstd", zi), ("nbias", zi), ("tot", zi), ("s2", zi)], writes=zr + [zall])
            sch.add("pool", lambda e, Z=Z: e.tensor_tensor(out=Z[:], in0=Z[:], in1=gbc[:], op=ALU.mult),
                    reads=[zall] + CONST, writes=zr + [zall])
            sch.add("pool", lambda e, Z=Z: e.tensor_tensor(out=Z[:], in0=Z[:], in1=bbc[:], op=ALU.add),
                    reads=[zall] + CONST, writes=zr + [zall])
            sch.add("sp", lambda e, Z=Z, j=j: e.dma_start(out=out_d[j * 128:(j + 1) * 128, :], in_=Z[:]),
                    reads=[zall] + zr, writes=[("outd", j)], dma=f"out{zi}")
        sch.add("sp", None, reads=[("outd", 13), ("outd", 14), ("outd", 15)])
        run_block(nc, sch)


def _perm_cols():
    cols = []
    for p in range(4):
        for base in (0, 1024, 2048, 4096):
            for h in (2 * p, 2 * p + 1):
                cols.extend(range(base + h * 128, base + (h + 1) * 128))
    for bp in range(2):
        cols.extend(range(3072 + bp * 512, 3072 + (bp + 1) * 512))
        cols.extend(range(5120 + bp * 512, 5120 + (bp + 1) * 512))
    return np.asarray(cols, dtype=np.int64)


def _tables(half):
    lt = np.arange(LT, dtype=np.int64)
    pos = np.maximum(lt - 2048 + 2048 * half, 0).astype(np.float32)
    inv = (np.float32(500000.0) ** (-(2.0 * np.arange(16, dtype=np.float32)) / np.float32(32.0))).astype(np.float32)
    ang = (pos[:, None] * inv[None, :]).astype(np.float32)
    c, s = np.cos(ang).astype(np.float32), np.sin(ang).astype(np.float32)
    cs = np.concatenate([c, c, -s, s], axis=1).astype(np.float32)
    cs = np.ascontiguousarray(cs.reshape(32, 128, 64).transpose(1, 0, 2).reshape(128, 32 * 64))
    k = np.arange(128)[:, None]
    q = np.arange(128)[None, :]
    mcur = np.where(k <= q, 0.0, NEG).astype(np.float32)
    mprev = np.where(k >= q, 0.0, NEG).astype(np.float32)
    mprevh = mprev if half == 1 else np.full((128, 128), NEG, np.float32)
    masks = np.ascontiguousarray(np.stack([mcur, mprev, mprevh]).astype(np.float32).transpose(1, 0, 2).reshape(128, 384))
    bm = np.zeros((3, 4, 128, 128), np.float32)
    tp = np.arange(128)[:, None]
    t = np.arange(128)[None, :]
    for g, w in enumerate(POOL_W):
        main = ((tp <= t) & (tp > t - w)).astype(np.float32) / w - (tp == t)
        prev = ((tp - 128) > (t - w)).astype(np.float32) / w
        cntf = np.minimum(t + 1, w).astype(np.float32)
        first = ((tp <= t) & (tp > t - w)).astype(np.float32) / cntf - (tp == t)
        bm[0, g] = main
        bm[1, g] = prev
        bm[2, g] = first if half == 0 else main
    bm = np.ascontiguousarray(bm.transpose(2, 0, 1, 3).reshape(128, 3 * 4 * 128))
    return cs, masks, bm


_NC_CACHE = {}


def kernel(x, w_in, w_pool, pool_scale, w_out, ln_gain, ln_bias):
    x = np.asarray(x, dtype=np.float32)
    w_perm = np.ascontiguousarray(np.asarray(w_in, np.float32)[0][:, _perm_cols()])
    w_pool0 = np.ascontiguousarray(np.asarray(w_pool, np.float32)[0])
    pscale = np.ascontiguousarray(np.asarray(pool_scale, np.float32)[0].reshape(8, 128).T)
    w_out0 = np.ascontiguousarray(np.asarray(w_out, np.float32)[0])
    g0 = np.ascontiguousarray(np.asarray(ln_gain, np.float32)[0])
    b0 = np.ascontiguousarray(np.asarray(ln_bias, np.float32)[0])
    ident = np.eye(128, dtype=np.float32)
    if "nc" not in _NC_CACHE:
        _NC_CACHE["nc"] = build_program()
    nc = _NC_CACHE["nc"]
    in_maps = []
    for c in range(8):
        b, half = c // 2, c % 2
        own = x[b, half * T:(half + 1) * T, :]
        halo = x[b, 0:T, :] if half == 1 else np.zeros((T, D), np.float32)
        xT_all = np.ascontiguousarray(np.concatenate([halo, own], axis=0).T)
        cs, masks, bm = _tables(half)
        in_maps.append({
            "xT_all": xT_all, "x_own": np.ascontiguousarray(own), "w_perm": w_perm, "w_pool": w_pool0,
            "pscale": pscale, "w_out": w_out0, "ln_g": g0, "ln_b": b0, "cs": cs, "masks": masks,
            "ident": ident, "bmats": bm,
        })
    res = run_bass_kernel_spmd(nc, in_maps, core_ids=list(range(8)))
    out = np.empty((4, 4096, D), np.float32)
    for c in range(8):
        b, half = c // 2, c % 2
        out[b, half * T:(half + 1) * T, :] = res.results[c]["out"]
    return out
```

```python
import numpy as np
import concourse.bass as bass
import concourse.mybir as mybir
from concourse.bass_utils import run_bass_kernel_spmd

F32 = mybir.dt.float32
BF16 = mybir.dt.bfloat16
AF = mybir.ActivationFunctionType
ALU = mybir.AluOpType
AX = mybir.AxisListType

D = 2048
T = 2048
LT = 4096
KC = 16
NEG = -30000.0
ALPHA = 2.0 ** 0.25
EPS = 1e-5
SCALE = 128.0 ** -0.5
POOL_W = (2, 4, 8, 16)

COMPUTE = ("pe", "act", "dve", "pool")


class Op:
    __slots__ = ("eng", "fn", "deps", "is_dma", "sem", "val", "need_sig", "seq")


class Sched:
    def __init__(self, prog, counters, dma_sems, dma_cnt):
        self.prog = prog
        self.counters = counters
        self.dma_sems = dma_sems
        self.dma_cnt = dma_cnt
        self.ops = []
        self.lw = {}
        self.rd = {}

    def add(self, eng, fn, reads=(), writes=(), dma=None, no_waw=False):
        idx = len(self.ops)
        deps = {}
        for r in reads:
            w = self.lw.get(r)
            if w is not None:
                deps[w] = "raw"
        for w_ in writes:
            for ridx in self.rd.get(w_, {}).values():
                deps.setdefault(ridx, "war")
            lw = self.lw.get(w_)
            if lw is not None and not no_waw:
                deps.setdefault(lw, "waw")
        op = Op()
        op.eng = eng
        op.fn = fn
        op.deps = deps
        op.is_dma = dma is not None
        op.sem = dma
        op.val = 0
        op.need_sig = False
        op.seq = 0
        if dma is not None:
            self.dma_cnt[dma] += 16
            op.val = self.dma_cnt[dma]
        key = ("dma", dma) if dma is not None else eng
        for r in reads:
            self.rd.setdefault(r, {})[key] = idx
        for w_ in writes:
            self.lw[w_] = idx
            self.rd[w_] = {}
        self.ops.append(op)
        return idx

    def _needs_wait(self, op, dop, kind):
        if dop.is_dma:
            return True
        if dop.eng != op.eng:
            return True
        if op.is_dma:
            return True
        if kind in ("raw", "waw") and op.eng in ("act", "dve", "pool"):
            return True
        return False

    def finalize(self):
        for op in self.ops:
            for d, kind in op.deps.items():
                dop = self.ops[d]
                if not dop.is_dma and self._needs_wait(op, dop, kind):
                    dop.need_sig = True
        for op in self.ops:
            if op.need_sig:
                self.counters[op.eng] += 1
                op.seq = self.counters[op.eng]

    def emit(self, eng, eh):
        waited = {}
        for op in self.ops:
            if op.eng != eng:
                continue
            waits = {}
            for d, kind in op.deps.items():
                dop = self.ops[d]
                if not self._needs_wait(op, dop, kind):
                    continue
                if dop.is_dma:
                    k, v = ("d", dop.sem), dop.val
                else:
                    k, v = ("p", dop.eng), dop.seq
                if waits.get(k, 0) < v:
                    waits[k] = v
            for k, v in waits.items():
                if waited.get(k, 0) >= v:
                    continue
                waited[k] = v
                sem = self.dma_sems[k[1]] if k[0] == "d" else self.prog[k[1]]
                eh.wait_ge(sem, v)
            if op.fn is None:
                continue
            inst = op.fn(eh)
            if op.is_dma:
                inst.then_inc(self.dma_sems[op.sem], 16)
            elif op.need_sig:
                inst.then_inc(self.prog[eng], 1)


def run_block(nc, sch):
    sch.finalize()
    with nc.Block() as block:
        @block.tensor
        def _(e):
            sch.emit("pe", e)

        @block.scalar
        def _(e):
            sch.emit("act", e)

        @block.vector
        def _(e):
            sch.emit("dve", e)

        @block.gpsimd
        def _(e):
            sch.emit("pool", e)

        @block.sync
        def _(e):
            sch.emit("sp", e)


DMA_SEM_NAMES = ["const", "constg", "wp", "woa", "xc0", "xc1", "xh", "xr0", "xr1", "out0", "out1", "out2"]


def build_program():
    nc = bass.Bass("TRN2", target_bir_lowering=False)
    dt = lambda name, shape, kind="ExternalInput": nc.dram_tensor(name, shape, F32, kind=kind).ap()
    xT_all = dt("xT_all", [D, LT])
    x_own = dt("x_own", [T, D])
    w_perm = dt("w_perm", [D, 6144])
    w_pool = dt("w_pool", [4, 256, 256])
    pscale = dt("pscale", [128, 8])
    w_out = dt("w_out", [D, D])
    ln_g = dt("ln_g", [D])
    ln_b = dt("ln_b", [D])
    cs_d = dt("cs", [128, 32 * 48])
    mask_d = dt("masks", [128, 3 * 128])
    ident_d = dt("ident", [128, 128])
    bm_d = dt("bmats", [128, 3 * 4 * 128])
    out_d = dt("out", [T, D], kind="ExternalOutput")

    import contextlib
    es = contextlib.ExitStack()
    with es:
        prog = {e: es.enter_context(nc.semaphore("prog_" + e)) for e in COMPUTE}
        dma_sems = {n: es.enter_context(nc.semaphore("dma_" + n)) for n in DMA_SEM_NAMES}
        counters = {e: 0 for e in COMPUTE}
        dma_cnt = {n: 0 for n in DMA_SEM_NAMES}
        new_sched = lambda: Sched(prog, counters, dma_sems, dma_cnt)

        import os
        PH = os.environ.get("KPHASES", "abc")
        bufW = es.enter_context(nc.sbuf_tensor("bufW", [128, KC, 1024], BF16))
        bufX = [es.enter_context(nc.sbuf_tensor(f"bufX{i}", [128, KC, 512], BF16)) for i in range(2)]
        yT_attn = es.enter_context(nc.sbuf_tensor("yT_attn", [128, 8, T], BF16))
        if "a" in PH:
            phase_a(nc, new_sched(), xT_all, w_perm, cs_d, mask_d, ident_d, yT_attn, bufW, bufX, "b" in PH)
        yT_pool = es.enter_context(nc.sbuf_tensor("yT_pool", [128, 8, T], BF16))
        wo_a = es.enter_context(nc.sbuf_tensor("wo_a", [128, 8, D], BF16))
        if "b" in PH:
            phase_b(nc, new_sched(), xT_all, w_perm, w_pool, pscale, bm_d, yT_pool, w_out, wo_a, bufW, bufX, "a" in PH)
        if "c" in PH:
            phase_c(nc, new_sched(), x_own, w_out, ln_g, ln_b, out_d, yT_attn, yT_pool, wo_a, bufW, bufX, "b" in PH)
    return nc


def xchunk_src(xT_all, lt0, n):
    return xT_all.rearrange("(kc p) t -> p kc t", p=128)[:, :, lt0:lt0 + n]


def phase_a(nc, sch, xT_all, w_perm, cs_d, mask_d, ident_d, yT_attn, bufW, bufX, prefetch_b):
    import contextlib
    with contextlib.ExitStack() as es:
        sb = lambda name, shape, dtype: es.enter_context(nc.sbuf_tensor(name, shape, dtype))
        xc = bufX
        wp = bufW
        QT = sb("a_QT", [128, 2, T], BF16)
        QT2 = sb("a_QT2", [128, 2, 4, 512], BF16)
        QT3 = sb("a_QT3", [128, 2, 16, 128], BF16)
        KT = sb("a_KT", [128, 2, LT], BF16)
        VT = sb("a_VT", [128, 2, LT], BF16)
        Gs = sb("a_Gs", [128, 2, T], BF16)
        Vblk = sb("a_Vblk", [128, 69, 128], BF16)
        PT = [sb(f"a_PT{i}", [128, 512], BF16) for i in range(3)]
        PTP = [sb(f"a_PTP{i}", [128, 4, 130], BF16) for i in range(2)]
        PT3 = sb("a_PTh3", [128, 2, 16, 130], BF16)
        qktm = [sb(f"a_qktm{i}", [128, 4, 128], BF16) for i in range(2)]
        rt1 = [sb(f"a_rt1_{i}", [128, 4, 32], F32) for i in range(2)]
        rt2 = [sb(f"a_rt2_{i}", [128, 4, 32], F32) for i in range(2)]
        cs = sb("a_cs", [128, 32, 48], F32)
        masks = sb("a_masks", [128, 3, 128], BF16)
        ident = sb("a_ident", [128, 128], BF16)
        ones = sb("a_ones", [128, 128], BF16)
        zeros = sb("a_zeros", [128, 128], BF16)
        m1g0 = sb("a_m1g0", [128, 4, 128], BF16)
        lden = [sb(f"a_lden{i}", [128, 512], F32) for i in range(2)]
        bank = [es.enter_context(nc.psum_tensor(f"a_b{i}", [128, 512], F32)) for i in range(6)]
        tpb = [es.enter_context(nc.psum_tensor(f"a_tp{i}", [128, 1024], BF16)) for i in range(2)]

        MCUR, MPREV, MPREVH = 0, 1, 2

        sch.add("sp", lambda e: e.dma_start(out=cs[:].rearrange("p t c -> p (t c)"), in_=cs_d),
                writes=["cs"], dma="const")
        sch.add("pool", lambda e: e.dma_start(out=masks[:].rearrange("p m q -> p (m q)"), in_=mask_d),
                writes=["masks"], dma="constg")
        sch.add("pool", lambda e: e.dma_start(out=ident[:], in_=ident_d),
                writes=["ident"], dma="constg", no_waw=True)
        sch.add("dve", lambda e: e.memset(ones[:], 1.0), writes=["ones"])
        sch.add("dve", lambda e: e.memset(zeros[:], 0.0), writes=["zeros"])
        CONST = ["cs", "masks", "ident"]
        sch.add("dve", lambda e: e.tensor_copy(out=m1g0[:, 0, :], in_=masks[:, 2, :]), reads=CONST, writes=["m1g0a"])
        sch.add("dve", lambda e: e.tensor_copy(out=m1g0[:, 1:4, :],
                                               in_=masks[:, 1, :].unsqueeze(1).broadcast_to([128, 3, 128])),
                reads=CONST, writes=["m1g0b"])

        cnt = {"xc": 0, "pqk": 0, "pf": 0, "tp": 0, "qk": 0, "st": 0, "pt": 0, "ptp": 0, "grp": 0}
        PQK_BANKS = (0, 1, 2)
        PF_BANKS = (3, 4, 5)

        import os
        DBG_PAIRS = int(os.environ.get("KA_PAIRS", "4"))
        DBG_ATTN = int(os.environ.get("KA_ATTN", "9"))
        DBG_INPROJ = int(os.environ.get("KA_INPROJ", "1"))
        for p in range(DBG_PAIRS):
            for piece in range(4):
                def f(e, piece=piece, p=p):
                    src = w_perm.rearrange("(kc p) c -> p kc c", p=128)[:, piece * 4:(piece + 1) * 4,
                                                                        p * 1024:(p + 1) * 1024]
                    return e.dma_start(out=wp[:, piece * 4:(piece + 1) * 4, :], in_=src)
                sch.add("pool", f, writes=["wp"], dma="wp", no_waw=(piece > 0))

            pending = []

            def flush(keep):
                while len(pending) > keep:
                    pending.pop(0)()

            for ci in range(8 if DBG_INPROJ else 0):
                own = ci >= 4
                slot = cnt["xc"] % 2
                cnt["xc"] += 1
                for half in range(2):
                    def f(e, slot=slot, ci=ci, half=half):
                        return e.dma_start(out=xc[slot][:, half * 8:(half + 1) * 8, :],
                                           in_=xchunk_src(xT_all, ci * 512, 512)[:, half * 8:(half + 1) * 8, :])
                    sch.add("pool", f, writes=[("xc", slot)], dma=f"xc{slot}", no_waw=(half > 0))

                nb = 4 if own else 2
                c0 = 0 if own else 256
                for tt in range(4):
                    tl = ci * 4 + tt
                    b = PQK_BANKS[cnt["pqk"] % 3]
                    cnt["pqk"] += 1

                    def f(e, slot=slot, tt=tt, b=b, nb=nb, c0=c0):
                        for kc in range(KC):
                            i = e.matmul(bank[b][:, 0:nb * 128], lhsT=xc[slot][:, kc, tt * 128:(tt + 1) * 128],
                                         rhs=wp[:, kc, c0:c0 + nb * 128], start=(kc == 0), stop=(kc == KC - 1))
                        return i
                    sch.add("pe", f, reads=[("xc", slot), "wp"], writes=[("bank", b)])
                    qs = cnt["qk"] % 2
                    cnt["qk"] += 1
                    P = bank[b][:, 0:nb * 128].rearrange("p (n d) -> p n d", n=nb)
                    cc = cs[:, tl, 0:16].unsqueeze(1).unsqueeze(1).broadcast_to([128, nb, 2, 16])
                    sn = cs[:, tl, 16:32].unsqueeze(1).broadcast_to([128, nb, 16])
                    sp_ = cs[:, tl, 32:48].unsqueeze(1).broadcast_to([128, nb, 16])
                    sch.add("dve", lambda e, P=P, qs=qs, nb=nb, cc=cc: e.tensor_tensor(
                        out=rt1[qs][:, 0:nb, :].rearrange("p n (h d) -> p n h d", h=2),
                        in0=P[:, :, 0:32].rearrange("p n (h d) -> p n h d", h=2), in1=cc, op=ALU.mult),
                        reads=[("bank", b)] + CONST, writes=[("rt1", qs)])
                    sch.add("dve", lambda e, P=P, qs=qs, nb=nb, sn=sn: e.tensor_tensor(
                        out=rt2[qs][:, 0:nb, 0:16], in0=P[:, :, 16:32], in1=sn, op=ALU.mult),
                        reads=[("bank", b)] + CONST, writes=[("rt2a", qs)])
                    sch.add("dve", lambda e, P=P, qs=qs, nb=nb, sp_=sp_: e.tensor_tensor(
                        out=rt2[qs][:, 0:nb, 16:32], in0=P[:, :, 0:16], in1=sp_, op=ALU.mult),
                        reads=[("bank", b)] + CONST, writes=[("rt2b", qs)])
                    sch.add("act", lambda e, P=P, qs=qs, nb=nb: e.copy(out=qktm[qs][:, 0:nb, 32:128], in_=P[:, :, 32:128]),
                            reads=[("bank", b), ("rt1", qs), ("rt2a", qs), ("rt2b", qs)], writes=[("qk_nr", qs)])
                    sch.add("dve", lambda e, qs=qs, nb=nb: e.tensor_tensor(
                        out=qktm[qs][:, 0:nb, 0:32], in0=rt1[qs][:, 0:nb, :], in1=rt2[qs][:, 0:nb, :], op=ALU.add),
                        reads=[("rt1", qs), ("rt2a", qs), ("rt2b", qs)], writes=[("qk_rot", qs)])

                    def tr(qs=qs, nb=nb, own=own, tl=tl):
                        tb = cnt["tp"] % 2
                        cnt["tp"] += 1

                        def f(e):
                            for blk in range(nb):
                                i = e.transpose(tpb[tb][:, blk * 128:(blk + 1) * 128], qktm[qs][:, blk, :], ident[:])
                            return i
                        sch.add("pe", f, reads=[("qk_nr", qs), ("qk_rot", qs)] + CONST, writes=[("tp", tb)])
                        if own:
                            q0 = (tl - 16) * 128
                            sch.add("act", lambda e: e.copy(out=QT[:, :, q0:q0 + 128],
                                                            in_=tpb[tb][:, 0:256].rearrange("p (n t) -> p n t", n=2)),
                                    reads=[("tp", tb)], writes=["QT"])
                            t16 = tl - 16
                            for n_ in range(2):
                                sch.add("act", lambda e, n_=n_: e.copy(
                                    out=QT2[:, n_, :, 32 * t16:32 * t16 + 32],
                                    in_=tpb[tb][:, n_ * 128:(n_ + 1) * 128].rearrange("p (v r) -> p r v", r=4)),
                                    reads=[("tp", tb)], writes=["QT"])
                                sch.add("act", lambda e, n_=n_: e.copy(
                                    out=QT3[:, n_, :, 8 * t16:8 * t16 + 8],
                                    in_=tpb[tb][:, n_ * 128:(n_ + 1) * 128].rearrange("p (u r) -> p r u", r=16)),
                                    reads=[("tp", tb)], writes=["QT"])
                            sch.add("act", lambda e: e.copy(out=KT[:, :, tl * 128:(tl + 1) * 128],
                                                            in_=tpb[tb][:, 256:512].rearrange("p (n t) -> p n t", n=2)),
                                    reads=[("tp", tb)], writes=["KT"])
                        else:
                            sch.add("act", lambda e: e.copy(out=KT[:, :, tl * 128:(tl + 1) * 128],
                                                            in_=tpb[tb][:, 0:256].rearrange("p (n t) -> p n t", n=2)),
                                    reads=[("tp", tb)], writes=["KT"])
                    pending.append(tr)
                    flush(1)

                for cb in range(4 if own else 2):
                    b = PF_BANKS[cnt["pf"] % 3]
                    cnt["pf"] += 1

                    def f(e, slot=slot, cb=cb, b=b):
                        for kc in range(KC):
                            i = e.matmul(bank[b][:, :], lhsT=wp[:, kc, 512 + cb * 128:512 + (cb + 1) * 128],
                                         rhs=xc[slot][:, kc, :], start=(kc == 0), stop=(kc == KC - 1))
                        return i
                    sch.add("pe", f, reads=[("xc", slot), "wp"], writes=[("bank", b)])
                    if cb < 2:
                        sch.add("dve", lambda e, cb=cb, b=b, ci=ci: e.tensor_copy(
                            out=VT[:, cb, ci * 512:(ci + 1) * 512], in_=bank[b][:, :]),
                            reads=[("bank", b)], writes=["VT"])
                    else:
                        sch.add("act", lambda e, cb=cb, b=b, ci=ci: e.activation(
                            out=Gs[:, cb - 2, (ci - 4) * 512:(ci - 3) * 512], in_=bank[b][:, :], func=AF.Silu),
                            reads=[("bank", b)], writes=["Gs"])
                    flush(0)
            flush(0)
            if p == DBG_PAIRS - 1 and prefetch_b:
                for piece in range(4):
                    def f(e, piece=piece):
                        src = w_perm.rearrange("(kc p) c -> p kc c", p=128)[:, piece * 4:(piece + 1) * 4, 4096:5120]
                        return e.dma_start(out=wp[:, piece * 4:(piece + 1) * 4, :], in_=src)
                    sch.add("pool", f, writes=["wp"], dma="wp", no_waw=(piece > 0))
                for half in range(2):
                    def f(e, half=half):
                        return e.dma_start(out=xc[0][:, half * 8:(half + 1) * 8, :],
                                           in_=xchunk_src(xT_all, 2048, 512)[:, half * 8:(half + 1) * 8, :])
                    sch.add("pool", f, writes=[("xc", 0)], dma="xc0", no_waw=(half > 0))
                sch.add("pool", lambda e: e.dma_start(out=xc[1][:, :, 0:128], in_=xchunk_src(xT_all, 1920, 128)),
                        writes=[("xc", 1)], dma="xc1")

            for hh in range(2 if DBG_ATTN else 0):
                h = 2 * p + hh
                vsrc = []
                for nbk in range(15, 32):
                    vsrc.append(VT[:, hh, nbk * 128:(nbk + 1) * 128])
                for r4 in range(4):
                    for cbk in range(3, 8):
                        vsrc.append(VT[:, hh, 512 * cbk + r4:512 * (cbk + 1):4])
                for r in range(16):
                    for kb in range(2):
                        vsrc.append(VT[:, hh, 2048 * kb + r:2048 * (kb + 1):16])
                v1 = lambda nbk: nbk - 15
                v2 = lambda r4, cbk: 17 + r4 * 5 + (cbk - 3)
                v3 = lambda r, kb: 37 + r * 2 + kb
                for i0 in range(0, 69, 8):
                    n = min(8, 69 - i0)
                    tb = cnt["tp"] % 2
                    cnt["tp"] += 1

                    def f(e, i0=i0, n=n, tb=tb, vsrc=vsrc):
                        for k in range(n):
                            i = e.transpose(tpb[tb][:, k * 128:(k + 1) * 128], vsrc[i0 + k], ident[:])
                        return i
                    sch.add("pe", f, reads=["VT"] + CONST, writes=[("tp", tb)])
                    sch.add("dve", lambda e, i0=i0, n=n, tb=tb: e.tensor_copy(
                        out=Vblk[:, i0:i0 + n, :], in_=tpb[tb][:, 0:n * 128].rearrange("p (n d) -> p n d", n=n)),
                        reads=[("tp", tb)], writes=["Vblk"])

                for rq in range(4 if DBG_ATTN > 1 else 0):
                    for kb in range(2):
                        sbk = cnt["st"] % 2
                        cnt["st"] += 1
                        ST = bank[sbk]

                        def f(e, rq=rq, kb=kb, ST=ST, hh=hh):
                            m = MCUR if kb == 1 else MPREVH
                            e.matmul(ST[:, :].rearrange("p (n q) -> p n q", n=4), lhsT=ident[:],
                                     rhs=masks[:, m, :].unsqueeze(1).broadcast_to([128, 4, 128]), start=True, stop=False)
                            for rr in range(4):
                                r = 4 * rq + rr
                                i = e.matmul(ST[:, rr * 128:(rr + 1) * 128],
                                             lhsT=KT[:, hh, 2048 * kb + r:2048 * (kb + 1):16],
                                             rhs=QT3[:, hh, r, :], start=False, stop=(rr == 3))
                            return i
                        sch.add("pe", f, reads=["QT", "KT"] + CONST, writes=[("bank", sbk)])
                        sch.add("act", lambda e, rq=rq, kb=kb, ST=ST: e.activation(
                            out=PT3[:, kb, 4 * rq:4 * rq + 4, 0:128],
                            in_=ST[:, :].rearrange("p (n q) -> p n q", n=4), func=AF.Exp, scale=SCALE),
                            reads=[("bank", sbk)], writes=[("pt3", kb, rq)])
                PT3R = [[("pt3", kb, rq) for rq in range(4)] for kb in range(2)]
                fin_pending = []

                for g in range(4 if DBG_ATTN > 1 else 0):
                    qb = 512 * g
                    gi = cnt["grp"]
                    cnt["grp"] += 1
                    NUM = bank[2 + gi % 2]
                    DEN = bank[4 + gi % 2]
                    numr = ("bank", 2 + gi % 2)
                    denr = ("bank", 4 + gi % 2)
                    cb = 4 + g
                    tiles = []

                    def qk(ti, g=g, qb=qb, hh=hh, cb=cb):
                        sbk = cnt["st"] % 2
                        cnt["st"] += 1
                        pts = cnt["pt"] % 3 if ti < 2 else 3 + cnt["ptp"] % 2
                        cnt["pt" if ti < 2 else "ptp"] += 1
                        tiles.append(pts)
                        ST = bank[sbk]

                        def f(e):
                            bc = lambda m, n_: masks[:, m, :].unsqueeze(1).broadcast_to([128, n_, 128])
                            if ti == 0:
                                e.matmul(ST[:, :].rearrange("p (n q) -> p n q", n=4), lhsT=ident[:],
                                         rhs=(m1g0[:] if g == 0 else bc(MPREV, 4)), start=True, stop=False)
                                for j in range(4):
                                    nbk = 16 + 4 * g + j - 1
                                    i = e.matmul(ST[:, j * 128:(j + 1) * 128], lhsT=KT[:, hh, nbk * 128:(nbk + 1) * 128],
                                                 rhs=QT[:, hh, qb + j * 128:qb + (j + 1) * 128], start=False, stop=(j == 3))
                            elif ti == 1:
                                e.matmul(ST[:, :].rearrange("p (n q) -> p n q", n=4), lhsT=ident[:],
                                         rhs=bc(MCUR, 4), start=True, stop=False)
                                for j in range(4):
                                    nbk = 16 + 4 * g + j
                                    i = e.matmul(ST[:, j * 128:(j + 1) * 128], lhsT=KT[:, hh, nbk * 128:(nbk + 1) * 128],
                                                 rhs=QT[:, hh, qb + j * 128:qb + (j + 1) * 128], start=False, stop=(j == 3))
                            else:
                                m = MCUR if ti == 3 else (MPREVH if g == 0 else MPREV)
                                cbk = cb if ti == 3 else cb - 1
                                e.matmul(ST[:, :].rearrange("p (n q) -> p n q", n=4), lhsT=ident[:],
                                         rhs=bc(m, 4), start=True, stop=False)
                                for r4 in range(4):
                                    i = e.matmul(ST[:, r4 * 128:(r4 + 1) * 128],
                                                 lhsT=KT[:, hh, 512 * cbk + r4:512 * (cbk + 1):4],
                                                 rhs=QT2[:, hh, r4, 128 * g:128 * g + 128], start=False, stop=(r4 == 3))
                            return i
                        sch.add("pe", f, reads=["QT", "KT", "m1g0a", "m1g0b"] + CONST, writes=[("bank", sbk)])
                        if ti < 2:
                            sch.add("act", lambda e: e.activation(out=PT[pts][:], in_=ST[:, :], func=AF.Exp, scale=SCALE),
                                    reads=[("bank", sbk)], writes=[("pt", pts)])
                        else:
                            sch.add("act", lambda e: e.activation(
                                out=PTP[pts - 3][:, :, 0:128],
                                in_=ST[:, :].rearrange("p (n q) -> p n q", n=4), func=AF.Exp, scale=SCALE),
                                reads=[("bank", sbk)], writes=[("pt", pts)])

                    def pv(ti, g=g, qb=qb, hh=hh, cb=cb, NUM=NUM, DEN=DEN, numr=numr, denr=denr):
                        kb = ti - 4
                        if ti < 2:
                            pts = tiles[ti]
                            P_ = PT[pts][:, :]
                            rds = [("pt", pts)]
                        elif ti < 4:
                            pts = tiles[ti]
                            P_ = PTP[pts - 3]
                            rds = [("pt", pts)]
                        else:
                            kb = ti - 4
                            P_ = PT3
                            rds = PT3R[kb]

                        def f(e):
                            if ti < 2:
                                e.matmul(DEN[:, :], lhsT=ones[:], rhs=P_, start=(ti == 0), stop=False)
                            elif ti < 4:
                                e.matmul(DEN[:, :], lhsT=ones[:], rhs=P_[:, :, 0:128].rearrange("p r i -> p i r"),
                                         start=False, stop=(ti == 3))
                            else:
                                e.matmul(DEN[:, :], lhsT=ones[:],
                                         rhs=P_[:, kb, :, 32 * g:32 * g + 32].rearrange("p r j -> p j r"),
                                         start=False, stop=False)
                            if ti == 0:
                                e.matmul(NUM[:, :], lhsT=zeros[:], rhs=P_, start=True, stop=False)
                            if ti < 2:
                                for j in range(4):
                                    nbk = 16 + 4 * g + j - (1 if ti == 0 else 0)
                                    i = e.matmul(NUM[:, j * 128:(j + 1) * 128], lhsT=Vblk[:, v1(nbk), :],
                                                 rhs=P_[:, j * 128:(j + 1) * 128], start=False, stop=False)
                            elif ti < 4:
                                cbk = cb if ti == 3 else cb - 1
                                for r4 in range(4):
                                    i = e.matmul(NUM[:, r4:512:4], lhsT=Vblk[:, v2(r4, cbk), :], rhs=P_[:, r4, 0:128],
                                                 start=False, stop=(ti == 3 and r4 == 3))
                            else:
                                for r in range(16):
                                    i = e.matmul(NUM[:, r:512:16], lhsT=Vblk[:, v3(r, kb), :],
                                                 rhs=P_[:, kb, r, 32 * g:32 * g + 32],
                                                 start=False, stop=False)
                            return i
                        sch.add("pe", f, reads=rds + ["Vblk", "ones", "zeros"], writes=[numr, denr])

                    qk(0); qk(1)
                    if fin_pending:
                        fin_pending.pop()()
                    pv(0); qk(2); pv(1); qk(3); pv(4); pv(2); pv(5); pv(3)

                    def finalize(gi=gi, NUM=NUM, DEN=DEN, numr=numr, denr=denr, h=h, hh=hh, qb=qb):
                        fs = gi % 2
                        sch.add("act", lambda e, fs=fs, DEN=DEN: e.activation(out=lden[fs][:], in_=DEN[:, :], func=AF.Ln),
                                reads=[denr], writes=[("lden", fs)])
                        sch.add("act", lambda e, fs=fs: e.activation(out=lden[fs][:], in_=lden[fs][:], func=AF.Exp, scale=-1.0),
                                reads=[("lden", fs)], writes=[("lden", fs)])
                        sch.add("dve", lambda e, fs=fs, NUM=NUM: e.tensor_tensor(out=lden[fs][:], in0=NUM[:, :], in1=lden[fs][:],
                                                                                 op=ALU.mult),
                                reads=[numr, ("lden", fs)], writes=[("lden", fs)])
                        sch.add("dve", lambda e, fs=fs, h=h, hh=hh, qb=qb: e.tensor_tensor(
                            out=yT_attn[:, h, qb:qb + 512], in0=lden[fs][:], in1=Gs[:, hh, qb:qb + 512], op=ALU.mult),
                            reads=[("lden", fs), "Gs"], writes=["yT_attn"])

                    fin_pending.append(finalize)
                if fin_pending:
                    fin_pending.pop()()
        run_block(nc, sch)


def phase_b(nc, sch, xT_all, w_perm, w_pool, pscale, bm_d, yT_pool, w_out, wo_a, bufW, bufX, prefetched):
    import contextlib
    with contextlib.ExitStack() as es:
        sb = lambda name, shape, dtype: es.enter_context(nc.sbuf_tensor(name, shape, dtype))
        xc = bufX
        xh = sb("b_xh", [128, KC, 128], BF16)
        wB = bufW
        utm = [sb(f"b_utm{i}", [128, 512], BF16) for i in range(10)]
        Gp = [sb(f"b_gp{i}", [128, 4, 512], BF16) for i in range(2)]
        pooled = [sb(f"b_pl{i}", [128, 4, 512], BF16) for i in range(2)]
        wpl = sb("b_wpool", [128, 4, 2, 256], BF16)
        bmats = sb("b_bm", [128, 3, 4, 128], BF16)
        ps = sb("b_ps", [128, 8], F32)
        bank = [es.enter_context(nc.psum_tensor(f"b_b{i}", [128, 512], F32)) for i in range(8)]

        sch.add("pool", lambda e: e.dma_start(out=wpl[:], in_=w_pool.rearrange("g (cc p) d -> p g cc d", p=128)),
                writes=["wpl"], dma="constg")
        sch.add("pool", lambda e: e.dma_start(out=bmats[:].rearrange("p k g b -> p (k g b)"), in_=bm_d),
                writes=["bmats"], dma="constg", no_waw=True)
        sch.add("sp", lambda e: e.dma_start(out=ps[:], in_=pscale), writes=["ps"], dma="const")
        CONST = ["wpl", "bmats", "ps"]
        BM, BP, BF = 0, 1, 2
        cnt = {"xc": 0, "u": 0, "gp": 0, "pp": 0, "po": 0, "ch": 0}

        if prefetched:
            for nm in ("wp", "xc0", "xc1"):
                sch.add("pe", lambda e, nm=nm, v=sch.dma_cnt[nm]: e.wait_ge(sch.dma_sems[nm], v) and None)

        def prefetch_wo_a():
            for piece in range(4):
                def f(e, piece=piece):
                    src = w_out.rearrange("(ec p) d -> p ec d", p=128)[:, piece * 2:(piece + 1) * 2, :]
                    return e.dma_start(out=wo_a[:, piece * 2:(piece + 1) * 2, :], in_=src)
                sch.add("pool", f, writes=["wo_a"], dma="woa", no_waw=(piece > 0))

        for bp in range(2):
            pre = prefetched and bp == 0
            if not pre:
                for piece in range(4):
                    def f(e, piece=piece, bp=bp):
                        src = w_perm.rearrange("(kc p) c -> p kc c", p=128)[:, piece * 4:(piece + 1) * 4,
                                                                            4096 + bp * 1024:4096 + (bp + 1) * 1024]
                        return e.dma_start(out=wB[:, piece * 4:(piece + 1) * 4, :], in_=src)
                    sch.add("pool", f, writes=["wB"], dma="wp", no_waw=(piece > 0))
                sch.add("pool", lambda e: e.dma_start(out=xh[:], in_=xchunk_src(xT_all, 1920, 128)),
                        writes=["xh"], dma="xh")
            uslot = {}
            tails = []

            def u_tile(j, lhs_fn, rd):
                b = cnt["u"] % 2
                cnt["u"] += 1
                s = (j + 1) % 10
                uslot[j] = s

                def f(e):
                    for kc in range(KC):
                        i = e.matmul(bank[b][:, :], lhsT=lhs_fn(kc), rhs=wB[:, kc, 0:512],
                                     start=(kc == 0), stop=(kc == KC - 1))
                    return i
                sch.add("pe", f, reads=[rd, "wB"], writes=[("bank", b)])
                sch.add("act", lambda e: e.copy(out=utm[s][:], in_=bank[b][:, :]),
                        reads=[("bank", b)], writes=[("utm", s)])

            if pre:
                u_tile(-1, lambda kc: xc[1][:, kc, 0:128], ("xc", 1))
            else:
                u_tile(-1, lambda kc: xh[:, kc, :], "xh")
            for ci in range(4):
                slot = cnt["xc"] % 2
                cnt["xc"] += 1
                for half in range(2):
                    if pre and ci == 0:
                        continue
                    def f(e, slot=slot, ci=ci, half=half):
                        return e.dma_start(out=xc[slot][:, half * 8:(half + 1) * 8, :],
                                           in_=xchunk_src(xT_all, 2048 + ci * 512, 512)[:, half * 8:(half + 1) * 8, :])
                    sch.add("pool", f, writes=[("xc", slot)], dma=f"xc{slot}", no_waw=(half > 0))
                if bp == 0 and ci == 1:
                    prefetch_wo_a()
                for tt in range(4):
                    u_tile(ci * 4 + tt, lambda kc, slot=slot, tt=tt: xc[slot][:, kc, tt * 128:(tt + 1) * 128], ("xc", slot))
                cs_ = cnt["ch"] % 2
                cnt["ch"] += 1
                if tails:
                    tails[0][0]()
                for cbi in range(4):
                    b = 2 + cnt["gp"] % 2
                    cnt["gp"] += 1

                    def f(e, slot=slot, cbi=cbi, b=b):
                        for kc in range(KC):
                            i = e.matmul(bank[b][:, :], lhsT=wB[:, kc, 512 + cbi * 128:512 + (cbi + 1) * 128],
                                         rhs=xc[slot][:, kc, :], start=(kc == 0), stop=(kc == KC - 1))
                        return i
                    sch.add("pe", f, reads=[("xc", slot), "wB"], writes=[("bank", b)])
                    sch.add("act", lambda e, cbi=cbi, b=b, cs_=cs_: e.activation(out=Gp[cs_][:, cbi, :], in_=bank[b][:, :],
                                                                                 func=AF.Silu),
                            reads=[("bank", b)], writes=[("gp", cs_)])
                if tails:
                    tails.pop(0)[1]()
                def tail_pooled(ci=ci, cs_=cs_, bp=bp, uslot=uslot):
                    for cbl in range(4):
                        gq = 2 * bp + cbl // 2
                        b = 4 + cnt["pp"] % 2
                        cnt["pp"] += 1

                        def f(e, cbl=cbl, gq=gq, b=b, ci=ci, uslot=uslot):
                            for jj in range(4):
                                j = ci * 4 + jj
                                e.matmul(bank[b][:, jj * 128:(jj + 1) * 128], lhsT=utm[uslot[j - 1]][:, cbl * 128:(cbl + 1) * 128],
                                         rhs=bmats[:, BP, gq, :], start=True, stop=False, skip_group_check=True)
                                i = e.matmul(bank[b][:, jj * 128:(jj + 1) * 128], lhsT=utm[uslot[j]][:, cbl * 128:(cbl + 1) * 128],
                                             rhs=bmats[:, (BF if j == 0 else BM), gq, :], start=False, stop=(jj == 3),
                                             skip_group_check=True)
                            return i
                        rds = [("utm", uslot[ci * 4 + jj]) for jj in range(-1, 4)]
                        sch.add("pe", f, reads=rds + CONST, writes=[("bank", b)])
                        sch.add("dve", lambda e, cbl=cbl, b=b, cs_=cs_: e.tensor_copy(out=pooled[cs_][:, cbl, :], in_=bank[b][:, :]),
                                reads=[("bank", b)], writes=[("pooled", cs_)])

                def tail_po(ci=ci, cs_=cs_, bp=bp):
                    for obl in range(4):
                        ob = 4 * bp + obl
                        gq = ob // 2
                        hf = obl % 2
                        b = 6 + cnt["po"] % 2
                        cnt["po"] += 1

                        def f(e, obl=obl, gq=gq, hf=hf, b=b, cs_=cs_):
                            for cc in range(2):
                                i = e.matmul(bank[b][:, :], lhsT=wpl[:, gq, cc, hf * 128:(hf + 1) * 128],
                                             rhs=pooled[cs_][:, (obl // 2) * 2 + cc, :], start=(cc == 0), stop=(cc == 1))
                            return i
                        sch.add("pe", f, reads=[("pooled", cs_)] + CONST, writes=[("bank", b)])
                        sch.add("dve", lambda e, obl=obl, ob=ob, b=b, cs_=cs_, ci=ci: e.scalar_tensor_tensor(
                            out=yT_pool[:, ob, ci * 512:(ci + 1) * 512], in0=bank[b][:, :], scalar=ps[:, ob:ob + 1],
                            in1=Gp[cs_][:, obl, :], op0=ALU.mult, op1=ALU.mult),
                            reads=[("bank", b), ("gp", cs_)] + CONST, writes=["yT_pool"])

                tails.append((tail_pooled, tail_po))
            if bp == 1:
                for piece in range(4, 8):
                    def f(e, piece=piece):
                        src = w_out.rearrange("(ec p) d -> p ec d", p=128)[:, piece * 2:(piece + 1) * 2, :]
                        dst = wB[:].rearrange("p a b -> p (a b)").rearrange("p (e d) -> p e d", d=D)
                        return e.dma_start(out=dst[:, (piece - 4) * 2:(piece - 3) * 2, :], in_=src)
                    sch.add("pool", f, writes=["wB"], dma="wp", no_waw=(piece > 4))
            while tails:
                tp_, to_ = tails.pop(0)
                tp_()
                to_()
        run_block(nc, sch)


def phase_c(nc, sch, x_own, w_out, ln_g, ln_b, out_d, yT_attn, yT_pool, wo_a, bufW, bufX, prefetched):
    import contextlib
    with contextlib.ExitStack() as es:
        sb = lambda name, shape, dtype: es.enter_context(nc.sbuf_tensor(name, shape, dtype))
        wo_b = bufW[:].rearrange("p a b -> p (a b)").rearrange("p (e d) -> p e d", d=D)
        fx = [b_[:].rearrange("p a b -> p (a b)").bitcast(F32) for b_ in bufX]
        xr = [fx[0][:, 0:D], fx[0][:, D:2 * D]]
        z = [fx[1][:, 0:D], fx[1][:, D:2 * D], sb("c_z2", [128, D], F32)[:]]
        junk = sb("c_junk", [128, D], BF16)
        gbc = sb("c_g", [128, D], F32)
        bbc = sb("c_b", [128, D], F32)
        st = [sb(f"c_st{i}", [128, 16], F32) for i in range(3)]
        bank = [es.enter_context(nc.psum_tensor(f"c_b{i}", [128, 512], F32)) for i in range(8)]

        for piece in range(8):
            if prefetched:
                continue
            tgt = wo_a if piece < 4 else wo_b

            def f(e, piece=piece, tgt=tgt):
                src = w_out.rearrange("(ec p) d -> p ec d", p=128)[:, piece * 2:(piece + 1) * 2, :]
                return e.dma_start(out=tgt[:, (piece % 4) * 2:(piece % 4 + 1) * 2, :], in_=src)
            sch.add("pool", f, writes=["wo_a" if piece < 4 else "wo_b"], dma=("woa" if piece < 4 else "wp"),
                    no_waw=(piece % 4 > 0))
        if prefetched:
            for nm in ("woa", "wp"):
                sch.add("pe", lambda e, nm=nm, v=sch.dma_cnt[nm]: e.wait_ge(sch.dma_sems[nm], v) and None)
        sch.add("sp", lambda e: e.dma_start(out=gbc[:], in_=ln_g.partition_broadcast(128)), writes=["gbc"], dma="const")
        sch.add("sp", lambda e: e.dma_start(out=bbc[:], in_=ln_b.partition_broadcast(128)), writes=["bbc"], dma="const",
                no_waw=True)
        CONST = ["gbc", "bbc"]

        def load_x(j):
            s = j % 2
            sch.add("sp", lambda e: e.dma_start(out=xr[s], in_=x_own[j * 128:(j + 1) * 128, :]),
                    writes=[("xr", s)], dma=f"xr{s}")

        load_x(0)
        for j in range(16):
            s = j % 2
            zi = j % 3
            Z = z[zi]
            S = st[zi]
            if j + 1 < 16:
                load_x(j + 1)

            for part in range(2):
                def f(e, j=j, s=s, part=part):
                    for ec in range(8 * part, 8 * part + 8):
                        lhs = (yT_attn[:, ec, j * 128:(j + 1) * 128] if ec < 8 else yT_pool[:, ec - 8, j * 128:(j + 1) * 128])
                        for nb in range(4):
                            i = e.matmul(bank[4 * s + nb][:, :], lhsT=lhs,
                                         rhs=(wo_a if ec < 8 else wo_b)[:, ec % 8, nb * 512:(nb + 1) * 512],
                                         start=(ec == 0), stop=(ec == KC - 1))
                    return i
                sch.add("pe", f, reads=["wo_a" if part == 0 else "wo_b"], writes=[("bank", 4 * s + nb) for nb in range(4)])
            for nb in range(4):
                sch.add("dve", lambda e, s=s, nb=nb, Z=Z: e.scalar_tensor_tensor(
                    out=Z[:, nb * 512:(nb + 1) * 512], in0=xr[s][:, nb * 512:(nb + 1) * 512], scalar=ALPHA,
                    in1=bank[4 * s + nb][:, :], op0=ALU.mult, op1=ALU.add),
                    reads=[("xr", s), ("bank", 4 * s + nb)], writes=[("z", zi, nb)])
            zr = [("z", zi, nb) for nb in range(4)]
            sch.add("act", lambda e, Z=Z, S=S: e.activation(out=junk[:], in_=Z[:], func=AF.Identity, accum_out=S[:, 5:6]),
                    reads=zr, writes=["junk", ("tot", zi)])
            sch.add("act", lambda e, Z=Z, S=S: e.activation(out=junk[:], in_=Z[:], func=AF.Square, accum_out=S[:, 4:5]),
                    reads=zr, writes=["junk", ("s2", zi)])
            sch.add("dve", lambda e, S=S: e.tensor_scalar(out=S[:, 6:7], in0=S[:, 5:6], scalar1=1.0 / D, scalar2=None,
                                                          op0=ALU.mult),
                    reads=[("tot", zi)], writes=[("mean", zi)])
            sch.add("dve", lambda e, S=S: e.tensor_tensor(out=S[:, 7:8], in0=S[:, 6:7], in1=S[:, 6:7], op=ALU.mult),
                    reads=[("mean", zi)], writes=[("msq", zi)])
            sch.add("dve", lambda e, S=S: e.scalar_tensor_tensor(out=S[:, 8:9], in0=S[:, 4:5], scalar=1.0 / D, in1=S[:, 7:8],
                                                                 op0=ALU.mult, op1=ALU.subtract),
                    reads=[("s2", zi), ("msq", zi)], writes=[("var", zi)])
            sch.add("dve", lambda e, S=S: e.tensor_scalar(out=S[:, 9:10], in0=S[:, 8:9], scalar1=EPS, scalar2=None,
                                                          op0=ALU.add),
                    reads=[("var", zi)], writes=[("vare", zi)])
            sch.add("act", lambda e, S=S: e.activation(out=S[:, 10:11], in_=S[:, 9:10], func=AF.Sqrt),
                    reads=[("vare", zi)], writes=[("sd", zi)])
            sch.add("dve", lambda e, S=S: e.reciprocal(out=S[:, 11:12], in_=S[:, 10:11]),
                    reads=[("sd", zi)], writes=[("rstd", zi)])
            sch.add("dve", lambda e, S=S: e.scalar_tensor_tensor(out=S[:, 12:13], in0=S[:, 6:7], scalar=-1.0, in1=S[:, 11:12],
                                                                 op0=ALU.mult, op1=ALU.mult),
                    reads=[("mean", zi), ("rstd", zi)], writes=[("nbias", zi)])
            zall = ("zall", zi)
            sch.add("act", lambda e, Z=Z, S=S: e.activation(out=Z[:], in_=Z[:], func=AF.Identity,
                                                            scale=S[:, 11:12], bias=S[:, 12:13]),
                    reads=zr + [("rstd", zi), ("nbias", zi), ("tot", zi), ("s2", zi)], writes=zr + [zall])
            sch.add("pool", lambda e, Z=Z: e.tensor_tensor(out=Z[:], in0=Z[:], in1=gbc[:], op=ALU.mult),
                    reads=[zall] + CONST, writes=zr + [zall])
            sch.add("pool", lambda e, Z=Z: e.tensor_tensor(out=Z[:], in0=Z[:], in1=bbc[:], op=ALU.add),
                    reads=[zall] + CONST, writes=zr + [zall])
            sch.add("sp", lambda e, Z=Z, j=j: e.dma_start(out=out_d[j * 128:(j + 1) * 128, :], in_=Z[:]),
                    reads=[zall] + zr, writes=[("outd", j)], dma=f"out{zi}")
        sch.add("sp", None, reads=[("outd", 13), ("outd", 14), ("outd", 15)])
        run_block(nc, sch)


def _perm_cols():
    cols = []
    for p in range(4):
        for base in (0, 1024, 2048, 4096):
            for h in (2 * p, 2 * p + 1):
                cols.extend(range(base + h * 128, base + (h + 1) * 128))
    for bp in range(2):
        cols.extend(range(3072 + bp * 512, 3072 + (bp + 1) * 512))
        cols.extend(range(5120 + bp * 512, 5120 + (bp + 1) * 512))
    return np.asarray(cols, dtype=np.int64)


def _tables(half):
    lt = np.arange(LT, dtype=np.int64)
    pos = np.maximum(lt - 2048 + 2048 * half, 0).astype(np.float32)
    inv = (np.float32(500000.0) ** (-(2.0 * np.arange(16, dtype=np.float32)) / np.float32(32.0))).astype(np.float32)
    ang = (pos[:, None] * inv[None, :]).astype(np.float32)
    c, s = np.cos(ang).astype(np.float32), np.sin(ang).astype(np.float32)
    cs = np.concatenate([c, -s, s], axis=1).astype(np.float32)
    cs = np.ascontiguousarray(cs.reshape(32, 128, 48).transpose(1, 0, 2).reshape(128, 32 * 48))
    k = np.arange(128)[:, None]
    q = np.arange(128)[None, :]
    mcur = np.where(k <= q, 0.0, NEG).astype(np.float32)
    mprev = np.where(k >= q, 0.0, NEG).astype(np.float32)
    mprevh = mprev if half == 1 else np.full((128, 128), NEG, np.float32)
    masks = np.ascontiguousarray(np.stack([mcur, mprev, mprevh]).astype(np.float32).transpose(1, 0, 2).reshape(128, 384))
    bm = np.zeros((3, 4, 128, 128), np.float32)
    tp = np.arange(128)[:, None]
    t = np.arange(128)[None, :]
    for g, w in enumerate(POOL_W):
        main = ((tp <= t) & (tp > t - w)).astype(np.float32) / w - (tp == t)
        prev = ((tp - 128) > (t - w)).astype(np.float32) / w
        cntf = np.minimum(t + 1, w).astype(np.float32)
        first = ((tp <= t) & (tp > t - w)).astype(np.float32) / cntf - (tp == t)
        bm[0, g] = main
        bm[1, g] = prev
        bm[2, g] = first if half == 0 else main
    bm = np.ascontiguousarray(bm.transpose(2, 0, 1, 3).reshape(128, 3 * 4 * 128))
    return cs, masks, bm


_NC_CACHE = {}


def kernel(x, w_in, w_pool, pool_scale, w_out, ln_gain, ln_bias):
    x = np.asarray(x, dtype=np.float32)
    w_perm = np.ascontiguousarray(np.asarray(w_in, np.float32)[0][:, _perm_cols()])
    w_pool0 = np.ascontiguousarray(np.asarray(w_pool, np.float32)[0])
    pscale = np.ascontiguousarray(np.asarray(pool_scale, np.float32)[0].reshape(8, 128).T)
    w_out0 = np.ascontiguousarray(np.asarray(w_out, np.float32)[0])
    g0 = np.ascontiguousarray(np.asarray(ln_gain, np.float32)[0])
    b0 = np.ascontiguousarray(np.asarray(ln_bias, np.float32)[0])
    ident = np.eye(128, dtype=np.float32)
    if "nc" not in _NC_CACHE:
        _NC_CACHE["nc"] = build_program()
    nc = _NC_CACHE["nc"]
    in_maps = []
    for c in range(8):
        b, half = c // 2, c % 2
        own = x[b, half * T:(half + 1) * T, :]
        halo = x[b, 0:T, :] if half == 1 else np.zeros((T, D), np.float32)
        xT_all = np.ascontiguousarray(np.concatenate([halo, own], axis=0).T)
        cs, masks, bm = _tables(half)
        in_maps.append({
            "xT_all": xT_all, "x_own": np.ascontiguousarray(own), "w_perm": w_perm, "w_pool": w_pool0,
            "pscale": pscale, "w_out": w_out0, "ln_g": g0, "ln_b": b0, "cs": cs, "masks": masks,
            "ident": ident, "bmats": bm,
        })
    res = run_bass_kernel_spmd(nc, in_maps, core_ids=list(range(8)))
    out = np.empty((4, 4096, D), np.float32)
    for c in range(8):
        b, half = c // 2, c % 2
        out[b, half * T:(half + 1) * T, :] = res.results[c]["out"]
    return out
```

```python
import numpy as np
import concourse.bass as bass
import concourse.mybir as mybir
from concourse.bass_utils import run_bass_kernel_spmd

F32 = mybir.dt.float32
BF16 = mybir.dt.bfloat16
AF = mybir.ActivationFunctionType
ALU = mybir.AluOpType
AX = mybir.AxisListType

D = 2048
T = 2048
LT = 4096
KC = 16
NEG = -30000.0
ALPHA = 2.0 ** 0.25
EPS = 1e-5
SCALE = 128.0 ** -0.5
POOL_W = (2, 4, 8, 16)

COMPUTE = ("pe", "act", "dve", "pool")


class Op:
    __slots__ = ("eng", "fn", "deps", "is_dma", "sem", "val", "need_sig", "seq")


class Sched:
    def __init__(self, prog, counters, dma_sems, dma_cnt):
        self.prog = prog
        self.counters = counters
        self.dma_sems = dma_sems
        self.dma_cnt = dma_cnt
        self.ops = []
        self.lw = {}
        self.rd = {}

    def add(self, eng, fn, reads=(), writes=(), dma=None, no_waw=False):
        idx = len(self.ops)
        deps = {}
        for r in reads:
            w = self.lw.get(r)
            if w is not None:
                deps[w] = "raw"
        for w_ in writes:
            for ridx in self.rd.get(w_, {}).values():
                deps.setdefault(ridx, "war")
            lw = self.lw.get(w_)
            if lw is not None and not no_waw:
                deps.setdefault(lw, "waw")
        op = Op()
        op.eng = eng
        op.fn = fn
        op.deps = deps
        op.is_dma = dma is not None
        op.sem = dma
        op.val = 0
        op.need_sig = False
        op.seq = 0
        if dma is not None:
            self.dma_cnt[dma] += 16
            op.val = self.dma_cnt[dma]
        key = ("dma", dma) if dma is not None else eng
        for r in reads:
            self.rd.setdefault(r, {})[key] = idx
        for w_ in writes:
            self.lw[w_] = idx
            self.rd[w_] = {}
        self.ops.append(op)
        return idx

    def _needs_wait(self, op, dop, kind):
        if dop.is_dma:
            return True
        if dop.eng != op.eng:
            return True
        if op.is_dma:
            return True
        if op.eng in ("act", "dve", "pool"):
            return True
        return False

    def finalize(self):
        for op in self.ops:
            for d, kind in op.deps.items():
                dop = self.ops[d]
                if not dop.is_dma and self._needs_wait(op, dop, kind):
                    dop.need_sig = True
        for op in self.ops:
            if op.need_sig:
                self.counters[op.eng] += 1
                op.seq = self.counters[op.eng]

    def emit(self, eng, eh):
        waited = {}
        for op in self.ops:
            if op.eng != eng:
                continue
            waits = {}
            for d, kind in op.deps.items():
                dop = self.ops[d]
                if not self._needs_wait(op, dop, kind):
                    continue
                if dop.is_dma:
                    k, v = ("d", dop.sem), dop.val
                else:
                    k, v = ("p", dop.eng), dop.seq
                if waits.get(k, 0) < v:
                    waits[k] = v
            for k, v in waits.items():
                if waited.get(k, 0) >= v:
                    continue
                waited[k] = v
                sem = self.dma_sems[k[1]] if k[0] == "d" else self.prog[k[1]]
                eh.wait_ge(sem, v)
            if op.fn is None:
                continue
            inst = op.fn(eh)
            if op.is_dma:
                inst.then_inc(self.dma_sems[op.sem], 16)
            elif op.need_sig:
                inst.then_inc(self.prog[eng], 1)


def run_block(nc, sch):
    sch.finalize()
    with nc.Block() as block:
        @block.tensor
        def _(e):
            sch.emit("pe", e)

        @block.scalar
        def _(e):
            sch.emit("act", e)

        @block.vector
        def _(e):
            sch.emit("dve", e)

        @block.gpsimd
        def _(e):
            sch.emit("pool", e)

        @block.sync
        def _(e):
            sch.emit("sp", e)


DMA_SEM_NAMES = ["const", "constg", "wp", "woa", "xc0", "xc1", "xh", "xr0", "xr1", "out0", "out1", "out2"]


def build_program():
    nc = bass.Bass("TRN2", target_bir_lowering=False)
    dt = lambda name, shape, kind="ExternalInput": nc.dram_tensor(name, shape, F32, kind=kind).ap()
    xT_all = dt("xT_all", [D, LT])
    x_own = dt("x_own", [T, D])
    w_perm = dt("w_perm", [D, 6144])
    w_pool = dt("w_pool", [4, 256, 256])
    pscale = dt("pscale", [128, 8])
    w_out = dt("w_out", [D, D])
    ln_g = dt("ln_g", [D])
    ln_b = dt("ln_b", [D])
    cs_d = dt("cs", [128, 32 * 48])
    mask_d = dt("masks", [128, 3 * 128])
    ident_d = dt("ident", [128, 128])
    bm_d = dt("bmats", [128, 3 * 4 * 128])
    out_d = dt("out", [T, D], kind="ExternalOutput")

    import contextlib
    es = contextlib.ExitStack()
    with es:
        prog = {e: es.enter_context(nc.semaphore("prog_" + e)) for e in COMPUTE}
        dma_sems = {n: es.enter_context(nc.semaphore("dma_" + n)) for n in DMA_SEM_NAMES}
        counters = {e: 0 for e in COMPUTE}
        dma_cnt = {n: 0 for n in DMA_SEM_NAMES}
        new_sched = lambda: Sched(prog, counters, dma_sems, dma_cnt)

        import os
        PH = os.environ.get("KPHASES", "abc")
        bufW = es.enter_context(nc.sbuf_tensor("bufW", [128, KC, 1024], BF16))
        bufX = [es.enter_context(nc.sbuf_tensor(f"bufX{i}", [128, KC, 512], BF16)) for i in range(2)]
        yT_attn = es.enter_context(nc.sbuf_tensor("yT_attn", [128, 8, T], BF16))
        if "a" in PH:
            phase_a(nc, new_sched(), xT_all, w_perm, cs_d, mask_d, ident_d, yT_attn, bufW, bufX, "b" in PH)
        yT_pool = es.enter_context(nc.sbuf_tensor("yT_pool", [128, 8, T], BF16))
        wo_a = es.enter_context(nc.sbuf_tensor("wo_a", [128, 8, D], BF16))
        if "b" in PH:
            phase_b(nc, new_sched(), xT_all, w_perm, w_pool, pscale, bm_d, yT_pool, w_out, wo_a, bufW, bufX, "a" in PH)
        if "c" in PH:
            phase_c(nc, new_sched(), x_own, w_out, ln_g, ln_b, out_d, yT_attn, yT_pool, wo_a, bufW, bufX, "b" in PH)
    return nc


def xchunk_src(xT_all, lt0, n):
    return xT_all.rearrange("(kc p) t -> p kc t", p=128)[:, :, lt0:lt0 + n]


def phase_a(nc, sch, xT_all, w_perm, cs_d, mask_d, ident_d, yT_attn, bufW, bufX, prefetch_b):
    import contextlib
    with contextlib.ExitStack() as es:
        sb = lambda name, shape, dtype: es.enter_context(nc.sbuf_tensor(name, shape, dtype))
        xc = bufX
        wp = bufW
        QT = sb("a_QT", [128, 2, T], BF16)
        QT2 = sb("a_QT2", [128, 2, 4, 512], BF16)
        QT3 = sb("a_QT3", [128, 2, 16, 128], BF16)
        KT = sb("a_KT", [128, 2, LT], BF16)
        VT = sb("a_VT", [128, 2, LT], BF16)
        Gs = sb("a_Gs", [128, 2, T], BF16)
        Vblk = sb("a_Vblk", [128, 69, 128], BF16)
        PT = [sb(f"a_PT{i}", [128, 512], BF16) for i in range(2)]
        PTP = [sb(f"a_PTP{i}", [128, 4, 130], BF16) for i in range(2)]
        PT3 = sb("a_PTh3", [128, 2, 16, 130], BF16)
        PTN = [sb(f"a_PTN{i}", [128, 512], BF16) for i in range(2)]
        qktm = [sb(f"a_qktm{i}", [128, 4, 128], BF16) for i in range(2)]
        rt1 = [sb("a_rt1_0", [128, 4, 32], F32)] * 2
        rt2 = [sb("a_rt2_0", [128, 4, 32], F32)] * 2
        cs = sb("a_cs", [128, 32, 48], F32)
        masks = sb("a_masks", [128, 3, 128], BF16)
        ident = sb("a_ident", [128, 128], BF16)
        ones = sb("a_ones", [128, 128], BF16)
        zeros = sb("a_zeros", [128, 128], BF16)
        m1g0 = sb("a_m1g0", [128, 4, 128], BF16)
        lden = [sb(f"a_lden{i}", [128, 512], F32) for i in range(2)]
        bank = [es.enter_context(nc.psum_tensor(f"a_b{i}", [128, 512], F32)) for i in range(6)]
        tpb = [es.enter_context(nc.psum_tensor(f"a_tp{i}", [128, 1024], BF16)) for i in range(2)]

        MCUR, MPREV, MPREVH = 0, 1, 2

        sch.add("sp", lambda e: e.dma_start(out=cs[:].rearrange("p t c -> p (t c)"), in_=cs_d),
                writes=["cs"], dma="const")
        sch.add("pool", lambda e: e.dma_start(out=masks[:].rearrange("p m q -> p (m q)"), in_=mask_d),
                writes=["masks"], dma="constg")
        sch.add("pool", lambda e: e.dma_start(out=ident[:], in_=ident_d),
                writes=["ident"], dma="constg", no_waw=True)
        sch.add("dve", lambda e: e.memset(ones[:], 1.0), writes=["ones"])
        sch.add("dve", lambda e: e.memset(zeros[:], 0.0), writes=["zeros"])
        CONST = ["cs", "masks", "ident"]
        sch.add("dve", lambda e: e.tensor_copy(out=m1g0[:, 0, :], in_=masks[:, 2, :]), reads=CONST, writes=["m1g0a"])
        sch.add("dve", lambda e: e.tensor_copy(out=m1g0[:, 1:4, :],
                                               in_=masks[:, 1, :].unsqueeze(1).broadcast_to([128, 3, 128])),
                reads=CONST, writes=["m1g0b"])

        cnt = {"xc": 0, "pqk": 0, "pf": 0, "tp": 0, "qk": 0, "st": 0, "pt": 0, "ptp": 0, "ptn": 0, "grp": 0}
        PQK_BANKS = (0, 1, 2)
        PF_BANKS = (3, 4, 5)

        import os
        DBG_PAIRS = int(os.environ.get("KA_PAIRS", "4"))
        DBG_ATTN = int(os.environ.get("KA_ATTN", "9"))
        DBG_INPROJ = int(os.environ.get("KA_INPROJ", "1"))
        for p in range(DBG_PAIRS):
            for piece in range(4):
                def f(e, piece=piece, p=p):
                    src = w_perm.rearrange("(kc p) c -> p kc c", p=128)[:, piece * 4:(piece + 1) * 4,
                                                                        p * 1024:(p + 1) * 1024]
                    return e.dma_start(out=wp[:, piece * 4:(piece + 1) * 4, :], in_=src)
                sch.add("pool", f, writes=["wp"], dma="wp", no_waw=(piece > 0))

            pending = []

            def flush(keep):
                while len(pending) > keep:
                    pending.pop(0)()

            for ci in range(8 if DBG_INPROJ else 0):
                own = ci >= 4
                slot = cnt["xc"] % 2
                cnt["xc"] += 1
                for half in range(2):
                    def f(e, slot=slot, ci=ci, half=half):
                        return e.dma_start(out=xc[slot][:, half * 8:(half + 1) * 8, :],
                                           in_=xchunk_src(xT_all, ci * 512, 512)[:, half * 8:(half + 1) * 8, :])
                    sch.add("pool", f, writes=[("xc", slot)], dma=f"xc{slot}", no_waw=(half > 0))

                nb = 4 if own else 2
                c0 = 0 if own else 256
                for tt in range(4):
                    tl = ci * 4 + tt
                    b = PQK_BANKS[cnt["pqk"] % 3]
                    cnt["pqk"] += 1

                    def f(e, slot=slot, tt=tt, b=b, nb=nb, c0=c0):
                        for kc in range(KC):
                            i = e.matmul(bank[b][:, 0:nb * 128], lhsT=xc[slot][:, kc, tt * 128:(tt + 1) * 128],
                                         rhs=wp[:, kc, c0:c0 + nb * 128], start=(kc == 0), stop=(kc == KC - 1))
                        return i
                    sch.add("pe", f, reads=[("xc", slot), "wp"], writes=[("bank", b)])
                    qs = cnt["qk"] % 2
                    cnt["qk"] += 1
                    P = bank[b][:, 0:nb * 128].rearrange("p (n d) -> p n d", n=nb)
                    cc = cs[:, tl, 0:16].unsqueeze(1).unsqueeze(1).broadcast_to([128, nb, 2, 16])
                    sn = cs[:, tl, 16:32].unsqueeze(1).broadcast_to([128, nb, 16])
                    sp_ = cs[:, tl, 32:48].unsqueeze(1).broadcast_to([128, nb, 16])
                    sch.add("dve", lambda e, P=P, qs=qs, nb=nb, cc=cc: e.tensor_tensor(
                        out=rt1[qs][:, 0:nb, :].rearrange("p n (h d) -> p n h d", h=2),
                        in0=P[:, :, 0:32].rearrange("p n (h d) -> p n h d", h=2), in1=cc, op=ALU.mult),
                        reads=[("bank", b)] + CONST, writes=["rt1"])
                    sch.add("dve", lambda e, P=P, qs=qs, nb=nb, sn=sn: e.tensor_tensor(
                        out=rt2[qs][:, 0:nb, 0:16], in0=P[:, :, 16:32], in1=sn, op=ALU.mult),
                        reads=[("bank", b)] + CONST, writes=["rt2a"])
                    sch.add("dve", lambda e, P=P, qs=qs, nb=nb, sp_=sp_: e.tensor_tensor(
                        out=rt2[qs][:, 0:nb, 16:32], in0=P[:, :, 0:16], in1=sp_, op=ALU.mult),
                        reads=[("bank", b)] + CONST, writes=["rt2b"])
                    sch.add("act", lambda e, P=P, qs=qs, nb=nb: e.copy(out=qktm[qs][:, 0:nb, 32:128], in_=P[:, :, 32:128]),
                            reads=[("bank", b), "rt1", "rt2a", "rt2b"], writes=[("qk_nr", qs)])
                    sch.add("dve", lambda e, qs=qs, nb=nb: e.tensor_tensor(
                        out=qktm[qs][:, 0:nb, 0:32], in0=rt1[qs][:, 0:nb, :], in1=rt2[qs][:, 0:nb, :], op=ALU.add),
                        reads=["rt1", "rt2a", "rt2b"], writes=[("qk_rot", qs)])

                    def tr(qs=qs, nb=nb, own=own, tl=tl):
                        tb = cnt["tp"] % 2
                        cnt["tp"] += 1

                        def f(e):
                            for blk in range(nb):
                                i = e.transpose(tpb[tb][:, blk * 128:(blk + 1) * 128], qktm[qs][:, blk, :], ident[:])
                            return i
                        sch.add("pe", f, reads=[("qk_nr", qs), ("qk_rot", qs)] + CONST, writes=[("tp", tb)])
                        if own:
                            q0 = (tl - 16) * 128
                            sch.add("act", lambda e: e.copy(out=QT[:, :, q0:q0 + 128],
                                                            in_=tpb[tb][:, 0:256].rearrange("p (n t) -> p n t", n=2)),
                                    reads=[("tp", tb)], writes=["QT"])
                            t16 = tl - 16
                            for n_ in range(2):
                                sch.add("act", lambda e, n_=n_: e.copy(
                                    out=QT2[:, n_, :, 32 * t16:32 * t16 + 32],
                                    in_=tpb[tb][:, n_ * 128:(n_ + 1) * 128].rearrange("p (v r) -> p r v", r=4)),
                                    reads=[("tp", tb)], writes=["QT"])
                                sch.add("act", lambda e, n_=n_: e.copy(
                                    out=QT3[:, n_, :, 8 * t16:8 * t16 + 8],
                                    in_=tpb[tb][:, n_ * 128:(n_ + 1) * 128].rearrange("p (u r) -> p r u", r=16)),
                                    reads=[("tp", tb)], writes=["QT"])
                            sch.add("act", lambda e: e.copy(out=KT[:, :, tl * 128:(tl + 1) * 128],
                                                            in_=tpb[tb][:, 256:512].rearrange("p (n t) -> p n t", n=2)),
                                    reads=[("tp", tb)], writes=["KT"])
                        else:
                            sch.add("act", lambda e: e.copy(out=KT[:, :, tl * 128:(tl + 1) * 128],
                                                            in_=tpb[tb][:, 0:256].rearrange("p (n t) -> p n t", n=2)),
                                    reads=[("tp", tb)], writes=["KT"])
                    pending.append(tr)
                    flush(1)

                for cb in range(4 if own else 2):
                    b = PF_BANKS[cnt["pf"] % 3]
                    cnt["pf"] += 1

                    def f(e, slot=slot, cb=cb, b=b):
                        for kc in range(KC):
                            i = e.matmul(bank[b][:, :], lhsT=wp[:, kc, 512 + cb * 128:512 + (cb + 1) * 128],
                                         rhs=xc[slot][:, kc, :], start=(kc == 0), stop=(kc == KC - 1))
                        return i
                    sch.add("pe", f, reads=[("xc", slot), "wp"], writes=[("bank", b)])
                    if cb < 2:
                        sch.add("dve", lambda e, cb=cb, b=b, ci=ci: e.tensor_copy(
                            out=VT[:, cb, ci * 512:(ci + 1) * 512], in_=bank[b][:, :]),
                            reads=[("bank", b)], writes=["VT"])
                    else:
                        sch.add("act", lambda e, cb=cb, b=b, ci=ci: e.activation(
                            out=Gs[:, cb - 2, (ci - 4) * 512:(ci - 3) * 512], in_=bank[b][:, :], func=AF.Silu),
                            reads=[("bank", b)], writes=["Gs"])
                    flush(0)
            flush(0)
            if p == DBG_PAIRS - 1 and prefetch_b:
                for piece in range(4):
                    def f(e, piece=piece):
                        src = w_perm.rearrange("(kc p) c -> p kc c", p=128)[:, piece * 4:(piece + 1) * 4, 4096:5120]
                        return e.dma_start(out=wp[:, piece * 4:(piece + 1) * 4, :], in_=src)
                    sch.add("pool", f, writes=["wp"], dma="wp", no_waw=(piece > 0))
                for half in range(2):
                    def f(e, half=half):
                        return e.dma_start(out=xc[0][:, half * 8:(half + 1) * 8, :],
                                           in_=xchunk_src(xT_all, 2048, 512)[:, half * 8:(half + 1) * 8, :])
                    sch.add("pool", f, writes=[("xc", 0)], dma="xc0", no_waw=(half > 0))
                sch.add("pool", lambda e: e.dma_start(out=xc[1][:, :, 0:128], in_=xchunk_src(xT_all, 1920, 128)),
                        writes=[("xc", 1)], dma="xc1")

            for hh in range(2 if DBG_ATTN else 0):
                h = 2 * p + hh
                vsrc = []
                for nbk in range(15, 32):
                    vsrc.append(VT[:, hh, nbk * 128:(nbk + 1) * 128])
                for r4 in range(4):
                    for cbk in range(3, 8):
                        vsrc.append(VT[:, hh, 512 * cbk + r4:512 * (cbk + 1):4])
                for r in range(16):
                    for kb in range(2):
                        vsrc.append(VT[:, hh, 2048 * kb + r:2048 * (kb + 1):16])
                v1 = lambda nbk: nbk - 15
                v2 = lambda r4, cbk: 17 + r4 * 5 + (cbk - 3)
                v3 = lambda r, kb: 37 + r * 2 + kb
                for i0 in range(0, 69, 8):
                    n = min(8, 69 - i0)
                    tb = cnt["tp"] % 2
                    cnt["tp"] += 1

                    def f(e, i0=i0, n=n, tb=tb, vsrc=vsrc):
                        for k in range(n):
                            i = e.transpose(tpb[tb][:, k * 128:(k + 1) * 128], vsrc[i0 + k], ident[:])
                        return i
                    sch.add("pe", f, reads=["VT"] + CONST, writes=[("tp", tb)])
                    sch.add("dve", lambda e, i0=i0, n=n, tb=tb: e.tensor_copy(
                        out=Vblk[:, i0:i0 + n, :], in_=tpb[tb][:, 0:n * 128].rearrange("p (n d) -> p n d", n=n)),
                        reads=[("tp", tb)], writes=["Vblk"])

                for rq in range(4 if DBG_ATTN > 1 else 0):
                    for kb in range(2):
                        sbk = cnt["st"] % 2
                        cnt["st"] += 1
                        ST = bank[sbk]

                        def f(e, rq=rq, kb=kb, ST=ST, hh=hh):
                            m = MCUR if kb == 1 else MPREVH
                            e.matmul(ST[:, :].rearrange("p (n q) -> p n q", n=4), lhsT=ident[:],
                                     rhs=masks[:, m, :].unsqueeze(1).broadcast_to([128, 4, 128]), start=True, stop=False)
                            for rr in range(4):
                                r = 4 * rq + rr
                                i = e.matmul(ST[:, rr * 128:(rr + 1) * 128],
                                             lhsT=KT[:, hh, 2048 * kb + r:2048 * (kb + 1):16],
                                             rhs=QT3[:, hh, r, :], start=False, stop=(rr == 3))
                            return i
                        sch.add("pe", f, reads=["QT", "KT"] + CONST, writes=[("bank", sbk)])
                        sch.add("act", lambda e, rq=rq, kb=kb, ST=ST: e.activation(
                            out=PT3[:, kb, 4 * rq:4 * rq + 4, 0:128],
                            in_=ST[:, :].rearrange("p (n q) -> p n q", n=4), func=AF.Exp, scale=SCALE),
                            reads=[("bank", sbk)], writes=[("pt3", kb, rq)])
                PT3R = [[("pt3", kb, rq) for rq in range(4)] for kb in range(2)]
                fin_pending = []

                for g in range(4 if DBG_ATTN > 1 else 0):
                    qb = 512 * g
                    gi = cnt["grp"]
                    cnt["grp"] += 1
                    NUM = bank[2 + gi % 2]
                    DEN = bank[4 + gi % 2]
                    numr = ("bank", 2 + gi % 2)
                    denr = ("bank", 4 + gi % 2)
                    cb = 4 + g
                    tiles = []

                    def qk(ti, g=g, qb=qb, hh=hh, cb=cb):
                        sbk = cnt["st"] % 2
                        cnt["st"] += 1
                        pts = cnt["pt"] % 2 if ti < 2 else 3 + cnt["ptp"] % 2
                        cnt["pt" if ti < 2 else "ptp"] += 1
                        tiles.append(pts)
                        ST = bank[sbk]

                        def f(e):
                            bc = lambda m, n_: masks[:, m, :].unsqueeze(1).broadcast_to([128, n_, 128])
                            if ti == 0:
                                e.matmul(ST[:, :].rearrange("p (n q) -> p n q", n=4), lhsT=ident[:],
                                         rhs=(m1g0[:] if g == 0 else bc(MPREV, 4)), start=True, stop=False)
                                for j in range(4):
                                    nbk = 16 + 4 * g + j - 1
                                    i = e.matmul(ST[:, j * 128:(j + 1) * 128], lhsT=KT[:, hh, nbk * 128:(nbk + 1) * 128],
                                                 rhs=QT[:, hh, qb + j * 128:qb + (j + 1) * 128], start=False, stop=(j == 3))
                            elif ti == 1:
                                e.matmul(ST[:, :].rearrange("p (n q) -> p n q", n=4), lhsT=ident[:],
                                         rhs=bc(MCUR, 4), start=True, stop=False)
                                for j in range(4):
                                    nbk = 16 + 4 * g + j
                                    i = e.matmul(ST[:, j * 128:(j + 1) * 128], lhsT=KT[:, hh, nbk * 128:(nbk + 1) * 128],
                                                 rhs=QT[:, hh, qb + j * 128:qb + (j + 1) * 128], start=False, stop=(j == 3))
                            else:
                                m = MCUR if ti == 3 else (MPREVH if g == 0 else MPREV)
                                cbk = cb if ti == 3 else cb - 1
                                e.matmul(ST[:, :].rearrange("p (n q) -> p n q", n=4), lhsT=ident[:],
                                         rhs=bc(m, 4), start=True, stop=False)
                                for r4 in range(4):
                                    i = e.matmul(ST[:, r4 * 128:(r4 + 1) * 128],
                                                 lhsT=KT[:, hh, 512 * cbk + r4:512 * (cbk + 1):4],
                                                 rhs=QT2[:, hh, r4, 128 * g:128 * g + 128], start=False, stop=(r4 == 3))
                            return i
                        sch.add("pe", f, reads=["QT", "KT", "m1g0a", "m1g0b"] + CONST, writes=[("bank", sbk)])
                        if ti < 2:
                            sch.add("act", lambda e: e.activation(out=PT[pts][:], in_=ST[:, :], func=AF.Exp, scale=SCALE),
                                    reads=[("bank", sbk)], writes=[("pt", pts)])
                        else:
                            sch.add("act", lambda e: e.activation(
                                out=PTP[pts - 3][:, :, 0:128],
                                in_=ST[:, :].rearrange("p (n q) -> p n q", n=4), func=AF.Exp, scale=SCALE),
                                reads=[("bank", sbk)], writes=[("pt", pts)])

                    def pv(ti, g=g, qb=qb, hh=hh, cb=cb, NUM=NUM, DEN=DEN, numr=numr, denr=denr):
                        kb = ti - 4
                        if ti < 2:
                            pts = tiles[ti]
                            P_ = PT[pts][:, :]
                            rds = [("pt", pts)]
                        elif ti < 4:
                            pts = tiles[ti]
                            P_ = PTP[pts - 3]
                            rds = [("pt", pts)]
                        else:
                            kb = ti - 4
                            P_ = PT3
                            rds = PT3R[kb]

                        if ti >= 2:
                            ns = cnt["ptn"] % 2
                            cnt["ptn"] += 1
                            if ti < 4:
                                src = P_[:, :, 0:128].rearrange("p r i -> p i r")
                                dst = PTN[ns][:, :].rearrange("p (i r) -> p i r", r=4)
                            else:
                                src = P_[:, kb, :, 32 * g:32 * g + 32].rearrange("p r j -> p j r")
                                dst = PTN[ns][:, :].rearrange("p (j r) -> p j r", r=16)
                            sch.add("dve", lambda e, src=src, dst=dst: e.tensor_copy(out=dst, in_=src),
                                    reads=rds, writes=[("ptn", ns)])
                            rds = rds + [("ptn", ns)]

                        def f(e):
                            if ti < 2:
                                e.matmul(DEN[:, :], lhsT=ones[:], rhs=P_, start=(ti == 0), stop=False)
                            if ti == 0:
                                e.matmul(NUM[:, :], lhsT=zeros[:], rhs=P_, start=True, stop=False)
                            if ti < 2:
                                for j in range(4):
                                    nbk = 16 + 4 * g + j - (1 if ti == 0 else 0)
                                    i = e.matmul(NUM[:, j * 128:(j + 1) * 128], lhsT=Vblk[:, v1(nbk), :],
                                                 rhs=P_[:, j * 128:(j + 1) * 128], start=False, stop=False)
                            elif ti < 4:
                                cbk = cb if ti == 3 else cb - 1
                                for r4 in range(4):
                                    i = e.matmul(NUM[:, r4:512:4], lhsT=Vblk[:, v2(r4, cbk), :], rhs=P_[:, r4, 0:128],
                                                 start=False, stop=(ti == 3 and r4 == 3))
                            else:
                                for r in range(16):
                                    i = e.matmul(NUM[:, r:512:16], lhsT=Vblk[:, v3(r, kb), :],
                                                 rhs=P_[:, kb, r, 32 * g:32 * g + 32],
                                                 start=False, stop=False)
                            if ti >= 2:
                                i = e.matmul(DEN[:, :], lhsT=ones[:], rhs=PTN[ns][:, :], start=False, stop=(ti == 3))
                            return i
                        sch.add("pe", f, reads=rds + ["Vblk", "ones", "zeros"], writes=[numr, denr])

                    qk(0); qk(1)
                    if fin_pending:
                        fin_pending.pop()()
                    pv(0); qk(2); pv(1); qk(3); pv(4); pv(2); pv(5); pv(3)

                    def finalize(gi=gi, NUM=NUM, DEN=DEN, numr=numr, denr=denr, h=h, hh=hh, qb=qb):
                        fs = gi % 2
                        sch.add("act", lambda e, fs=fs, DEN=DEN: e.activation(out=lden[fs][:], in_=DEN[:, :], func=AF.Ln),
                                reads=[denr], writes=[("lden", fs)])
                        sch.add("act", lambda e, fs=fs: e.activation(out=lden[fs][:], in_=lden[fs][:], func=AF.Exp, scale=-1.0),
                                reads=[("lden", fs)], writes=[("lden", fs)])
                        sch.add("dve", lambda e, fs=fs, NUM=NUM: e.tensor_tensor(out=lden[fs][:], in0=NUM[:, :], in1=lden[fs][:],
                                                                                 op=ALU.mult),
                                reads=[numr, ("lden", fs)], writes=[("lden", fs)])
                        sch.add("dve", lambda e, fs=fs, h=h, hh=hh, qb=qb: e.tensor_tensor(
                            out=yT_attn[:, h, qb:qb + 512], in0=lden[fs][:], in1=Gs[:, hh, qb:qb + 512], op=ALU.mult),
                            reads=[("lden", fs), "Gs"], writes=["yT_attn"])

                    fin_pending.append(finalize)
                if fin_pending:
                    fin_pending.pop()()
        run_block(nc, sch)


def phase_b(nc, sch, xT_all, w_perm, w_pool, pscale, bm_d, yT_pool, w_out, wo_a, bufW, bufX, prefetched):
    import contextlib
    with contextlib.ExitStack() as es:
        sb = lambda name, shape, dtype: es.enter_context(nc.sbuf_tensor(name, shape, dtype))
        xc = bufX
        xh = sb("b_xh", [128, KC, 128], BF16)
        wB = bufW
        utm = [sb(f"b_utm{i}", [128, 512], BF16) for i in range(10)]
        Gp = [sb(f"b_gp{i}", [128, 4, 512], BF16) for i in range(2)]
        pooled = [sb(f"b_pl{i}", [128, 4, 512], BF16) for i in range(2)]
        wpl = sb("b_wpool", [128, 4, 2, 256], BF16)
        bmats = sb("b_bm", [128, 3, 4, 128], BF16)
        ps = sb("b_ps", [128, 8], F32)
        bank = [es.enter_context(nc.psum_tensor(f"b_b{i}", [128, 512], F32)) for i in range(8)]

        sch.add("pool", lambda e: e.dma_start(out=wpl[:], in_=w_pool.rearrange("g (cc p) d -> p g cc d", p=128)),
                writes=["wpl"], dma="constg")
        sch.add("pool", lambda e: e.dma_start(out=bmats[:].rearrange("p k g b -> p (k g b)"), in_=bm_d),
                writes=["bmats"], dma="constg", no_waw=True)
        sch.add("sp", lambda e: e.dma_start(out=ps[:], in_=pscale), writes=["ps"], dma="const")
        CONST = ["wpl", "bmats", "ps"]
        BM, BP, BF = 0, 1, 2
        cnt = {"xc": 0, "u": 0, "gp": 0, "pp": 0, "po": 0, "ch": 0}

        if prefetched:
            for nm in ("wp", "xc0", "xc1"):
                sch.add("pe", lambda e, nm=nm, v=sch.dma_cnt[nm]: e.wait_ge(sch.dma_sems[nm], v) and None)

        def prefetch_wo_a():
            for piece in range(4):
                def f(e, piece=piece):
                    src = w_out.rearrange("(ec p) d -> p ec d", p=128)[:, piece * 2:(piece + 1) * 2, :]
                    return e.dma_start(out=wo_a[:, piece * 2:(piece + 1) * 2, :], in_=src)
                sch.add("pool", f, writes=["wo_a"], dma="woa", no_waw=(piece > 0))

        for bp in range(2):
            pre = prefetched and bp == 0
            if not pre:
                for piece in range(4):
                    def f(e, piece=piece, bp=bp):
                        src = w_perm.rearrange("(kc p) c -> p kc c", p=128)[:, piece * 4:(piece + 1) * 4,
                                                                            4096 + bp * 1024:4096 + (bp + 1) * 1024]
                        return e.dma_start(out=wB[:, piece * 4:(piece + 1) * 4, :], in_=src)
                    sch.add("pool", f, writes=["wB"], dma="wp", no_waw=(piece > 0))
                sch.add("pool", lambda e: e.dma_start(out=xh[:], in_=xchunk_src(xT_all, 1920, 128)),
                        writes=["xh"], dma="xh")
            uslot = {}
            tails = []

            def u_tile(j, lhs_fn, rd):
                b = cnt["u"] % 2
                cnt["u"] += 1
                s = (j + 1) % 10
                uslot[j] = s

                def f(e):
                    for kc in range(KC):
                        i = e.matmul(bank[b][:, :], lhsT=lhs_fn(kc), rhs=wB[:, kc, 0:512],
                                     start=(kc == 0), stop=(kc == KC - 1))
                    return i
                sch.add("pe", f, reads=[rd, "wB"], writes=[("bank", b)])
                sch.add("act", lambda e: e.copy(out=utm[s][:], in_=bank[b][:, :]),
                        reads=[("bank", b)], writes=[("utm", s)])

            if pre:
                u_tile(-1, lambda kc: xc[1][:, kc, 0:128], ("xc", 1))
            else:
                u_tile(-1, lambda kc: xh[:, kc, :], "xh")
            for ci in range(4):
                slot = cnt["xc"] % 2
                cnt["xc"] += 1
                for half in range(2):
                    if pre and ci == 0:
                        continue
                    def f(e, slot=slot, ci=ci, half=half):
                        return e.dma_start(out=xc[slot][:, half * 8:(half + 1) * 8, :],
                                           in_=xchunk_src(xT_all, 2048 + ci * 512, 512)[:, half * 8:(half + 1) * 8, :])
                    sch.add("pool", f, writes=[("xc", slot)], dma=f"xc{slot}", no_waw=(half > 0))
                if bp == 0 and ci == 1:
                    prefetch_wo_a()
                for tt in range(4):
                    u_tile(ci * 4 + tt, lambda kc, slot=slot, tt=tt: xc[slot][:, kc, tt * 128:(tt + 1) * 128], ("xc", slot))
                cs_ = cnt["ch"] % 2
                cnt["ch"] += 1
                if tails:
                    tails[0][0]()
                for cbi in range(4):
                    b = 2 + cnt["gp"] % 2
                    cnt["gp"] += 1

                    def f(e, slot=slot, cbi=cbi, b=b):
                        for kc in range(KC):
                            i = e.matmul(bank[b][:, :], lhsT=wB[:, kc, 512 + cbi * 128:512 + (cbi + 1) * 128],
                                         rhs=xc[slot][:, kc, :], start=(kc == 0), stop=(kc == KC - 1))
                        return i
                    sch.add("pe", f, reads=[("xc", slot), "wB"], writes=[("bank", b)])
                    sch.add("act", lambda e, cbi=cbi, b=b, cs_=cs_: e.activation(out=Gp[cs_][:, cbi, :], in_=bank[b][:, :],
                                                                                 func=AF.Silu),
                            reads=[("bank", b)], writes=[("gp", cs_)])
                if tails:
                    tails.pop(0)[1]()
                def tail_pooled(ci=ci, cs_=cs_, bp=bp, uslot=uslot):
                    for cbl in range(4):
                        gq = 2 * bp + cbl // 2
                        b = 4 + cnt["pp"] % 2
                        cnt["pp"] += 1

                        def f(e, cbl=cbl, gq=gq, b=b, ci=ci, uslot=uslot):
                            for jj in range(4):
                                j = ci * 4 + jj
                                e.matmul(bank[b][:, jj * 128:(jj + 1) * 128], lhsT=utm[uslot[j - 1]][:, cbl * 128:(cbl + 1) * 128],
                                         rhs=bmats[:, BP, gq, :], start=True, stop=False, skip_group_check=True)
                                i = e.matmul(bank[b][:, jj * 128:(jj + 1) * 128], lhsT=utm[uslot[j]][:, cbl * 128:(cbl + 1) * 128],
                                             rhs=bmats[:, (BF if j == 0 else BM), gq, :], start=False, stop=(jj == 3),
                                             skip_group_check=True)
                            return i
                        rds = [("utm", uslot[ci * 4 + jj]) for jj in range(-1, 4)]
                        sch.add("pe", f, reads=rds + CONST, writes=[("bank", b)])
                        sch.add("dve", lambda e, cbl=cbl, b=b, cs_=cs_: e.tensor_copy(out=pooled[cs_][:, cbl, :], in_=bank[b][:, :]),
                                reads=[("bank", b)], writes=[("pooled", cs_)])

                def tail_po(ci=ci, cs_=cs_, bp=bp):
                    for obl in range(4):
                        ob = 4 * bp + obl
                        gq = ob // 2
                        hf = obl % 2
                        b = 6 + cnt["po"] % 2
                        cnt["po"] += 1

                        def f(e, obl=obl, gq=gq, hf=hf, b=b, cs_=cs_):
                            for cc in range(2):
                                i = e.matmul(bank[b][:, :], lhsT=wpl[:, gq, cc, hf * 128:(hf + 1) * 128],
                                             rhs=pooled[cs_][:, (obl // 2) * 2 + cc, :], start=(cc == 0), stop=(cc == 1))
                            return i
                        sch.add("pe", f, reads=[("pooled", cs_)] + CONST, writes=[("bank", b)])
                        sch.add("dve", lambda e, obl=obl, ob=ob, b=b, cs_=cs_, ci=ci: e.scalar_tensor_tensor(
                            out=yT_pool[:, ob, ci * 512:(ci + 1) * 512], in0=bank[b][:, :], scalar=ps[:, ob:ob + 1],
                            in1=Gp[cs_][:, obl, :], op0=ALU.mult, op1=ALU.mult),
                            reads=[("bank", b), ("gp", cs_)] + CONST, writes=["yT_pool"])

                tails.append((tail_pooled, tail_po))
            if bp == 1:
                for piece in range(4, 8):
                    def f(e, piece=piece):
                        src = w_out.rearrange("(ec p) d -> p ec d", p=128)[:, piece * 2:(piece + 1) * 2, :]
                        dst = wB[:].rearrange("p a b -> p (a b)").rearrange("p (e d) -> p e d", d=D)
                        return e.dma_start(out=dst[:, (piece - 4) * 2:(piece - 3) * 2, :], in_=src)
                    sch.add("pool", f, writes=["wB"], dma="wp", no_waw=(piece > 4))
            while tails:
                tp_, to_ = tails.pop(0)
                tp_()
                to_()
        run_block(nc, sch)


def phase_c(nc, sch, x_own, w_out, ln_g, ln_b, out_d, yT_attn, yT_pool, wo_a, bufW, bufX, prefetched):
    import contextlib
    with contextlib.ExitStack() as es:
        sb = lambda name, shape, dtype: es.enter_context(nc.sbuf_tensor(name, shape, dtype))
        wo_b = bufW[:].rearrange("p a b -> p (a b)").rearrange("p (e d) -> p e d", d=D)
        fx = [b_[:].rearrange("p a b -> p (a b)").bitcast(F32) for b_ in bufX]
        xr = [fx[0][:, 0:D], fx[0][:, D:2 * D]]
        z = [fx[1][:, 0:D], fx[1][:, D:2 * D], sb("c_z2", [128, D], F32)[:]]
        junk = sb("c_junk", [128, D], BF16)
        gbc = sb("c_g", [128, D], F32)
        bbc = sb("c_b", [128, D], F32)
        st = [sb(f"c_st{i}", [128, 16], F32) for i in range(3)]
        bank = [es.enter_context(nc.psum_tensor(f"c_b{i}", [128, 512], F32)) for i in range(8)]

        for piece in range(8):
            if prefetched:
                continue
            tgt = wo_a if piece < 4 else wo_b

            def f(e, piece=piece, tgt=tgt):
                src = w_out.rearrange("(ec p) d -> p ec d", p=128)[:, piece * 2:(piece + 1) * 2, :]
                return e.dma_start(out=tgt[:, (piece % 4) * 2:(piece % 4 + 1) * 2, :], in_=src)
            sch.add("pool", f, writes=["wo_a" if piece < 4 else "wo_b"], dma=("woa" if piece < 4 else "wp"),
                    no_waw=(piece % 4 > 0))
        if prefetched:
            for nm in ("woa", "wp"):
                sch.add("pe", lambda e, nm=nm, v=sch.dma_cnt[nm]: e.wait_ge(sch.dma_sems[nm], v) and None)
        sch.add("sp", lambda e: e.dma_start(out=gbc[:], in_=ln_g.partition_broadcast(128)), writes=["gbc"], dma="const")
        sch.add("sp", lambda e: e.dma_start(out=bbc[:], in_=ln_b.partition_broadcast(128)), writes=["bbc"], dma="const",
                no_waw=True)
        CONST = ["gbc", "bbc"]

        def load_x(j):
            s = j % 2
            sch.add("sp", lambda e: e.dma_start(out=xr[s], in_=x_own[j * 128:(j + 1) * 128, :]),
                    writes=[("xr", s)], dma=f"xr{s}")

        load_x(0)
        for j in range(16):
            s = j % 2
            zi = j % 3
            Z = z[zi]
            S = st[zi]
            if j + 1 < 16:
                load_x(j + 1)

            for part in range(2):
                def f(e, j=j, s=s, part=part):
                    for ec in range(8 * part, 8 * part + 8):
                        lhs = (yT_attn[:, ec, j * 128:(j + 1) * 128] if ec < 8 else yT_pool[:, ec - 8, j * 128:(j + 1) * 128])
                        for nb in range(4):
                            i = e.matmul(bank[4 * s + nb][:, :], lhsT=lhs,
                                         rhs=(wo_a if ec < 8 else wo_b)[:, ec % 8, nb * 512:(nb + 1) * 512],
                                         start=(ec == 0), stop=(ec == KC - 1))
                    return i
                sch.add("pe", f, reads=["wo_a" if part == 0 else "wo_b"], writes=[("bank", 4 * s + nb) for nb in range(4)])
            for nb in range(4):
                sch.add("dve", lambda e, s=s, nb=nb, Z=Z: e.scalar_tensor_tensor(
                    out=Z[:, nb * 512:(nb + 1) * 512], in0=xr[s][:, nb * 512:(nb + 1) * 512], scalar=ALPHA,
                    in1=bank[4 * s + nb][:, :], op0=ALU.mult, op1=ALU.add),
                    reads=[("xr", s), ("bank", 4 * s + nb)], writes=[("z", zi, nb)])
            zr = [("z", zi, nb) for nb in range(4)]
            sch.add("act", lambda e, Z=Z, S=S: e.activation(out=junk[:], in_=Z[:], func=AF.Identity, accum_out=S[:, 5:6]),
                    reads=zr, writes=["junk", ("tot", zi)])
            sch.add("act", lambda e, Z=Z, S=S: e.activation(out=junk[:], in_=Z[:], func=AF.Square, accum_out=S[:, 4:5]),
                    reads=zr, writes=["junk", ("s2", zi)])
            sch.add("dve", lambda e, S=S: e.tensor_scalar(out=S[:, 6:7], in0=S[:, 5:6], scalar1=1.0 / D, scalar2=None,
                                                          op0=ALU.mult),
                    reads=[("tot", zi)], writes=[("mean", zi)])
            sch.add("dve", lambda e, S=S: e.tensor_tensor(out=S[:, 7:8], in0=S[:, 6:7], in1=S[:, 6:7], op=ALU.mult),
                    reads=[("mean", zi)], writes=[("msq", zi)])
            sch.add("dve", lambda e, S=S: e.scalar_tensor_tensor(out=S[:, 8:9], in0=S[:, 4:5], scalar=1.0 / D, in1=S[:, 7:8],
                                                                 op0=ALU.mult, op1=ALU.subtract),
                    reads=[("s2", zi), ("msq", zi)], writes=[("var", zi)])
            sch.add("dve", lambda e, S=S: e.tensor_scalar(out=S[:, 9:10], in0=S[:, 8:9], scalar1=EPS, scalar2=None,
                                                          op0=ALU.add),
                    reads=[("var", zi)], writes=[("vare", zi)])
            sch.add("act", lambda e, S=S: e.activation(out=S[:, 10:11], in_=S[:, 9:10], func=AF.Sqrt),
                    reads=[("vare", zi)], writes=[("sd", zi)])
            sch.add("dve", lambda e, S=S: e.reciprocal(out=S[:, 11:12], in_=S[:, 10:11]),
                    reads=[("sd", zi)], writes=[("rstd", zi)])
            sch.add("dve", lambda e, S=S: e.scalar_tensor_tensor(out=S[:, 12:13], in0=S[:, 6:7], scalar=-1.0, in1=S[:, 11:12],
                                                                 op0=ALU.mult, op1=ALU.mult),
                    reads=[("mean", zi), ("rstd", zi)], writes=[("nbias", zi)])
            zall = ("zall", zi)
            sch.add("act", lambda e, Z=Z, S=S: e.activation(out=Z[:], in_=Z[:], func=AF.Identity,
                                                            scale=S[:, 11:12], bias=S[:, 12:13]),
                    reads=zr + [("rstd", zi), ("nbias", zi), ("tot", zi), ("s2", zi)], writes=zr + [zall])
            sch.add("pool", lambda e, Z=Z: e.tensor_tensor(out=Z[:], in0=Z[:], in1=gbc[:], op=ALU.mult),
                    reads=[zall] + CONST, writes=zr + [zall])
            sch.add("pool", lambda e, Z=Z: e.tensor_tensor(out=Z[:], in0=Z[:], in1=bbc[:], op=ALU.add),
                    reads=[zall] + CONST, writes=zr + [zall])
            sch.add("sp", lambda e, Z=Z, j=j: e.dma_start(out=out_d[j * 128:(j + 1) * 128, :], in_=Z[:]),
                    reads=[zall] + zr, writes=[("outd", j)], dma=f"out{zi}")
        sch.add("sp", None, reads=[("outd", 13), ("outd", 14), ("outd", 15)])
        run_block(nc, sch)


def _perm_cols():
    cols = []
    for p in range(4):
        for base in (0, 1024, 2048, 4096):
            for h in (2 * p, 2 * p + 1):
                cols.extend(range(base + h * 128, base + (h + 1) * 128))
    for bp in range(2):
        cols.extend(range(3072 + bp * 512, 3072 + (bp + 1) * 512))
        cols.extend(range(5120 + bp * 512, 5120 + (bp + 1) * 512))
    return np.asarray(cols, dtype=np.int64)


def _tables(half):
    lt = np.arange(LT, dtype=np.int64)
    pos = np.maximum(lt - 2048 + 2048 * half, 0).astype(np.float32)
    inv = (np.float32(500000.0) ** (-(2.0 * np.arange(16, dtype=np.float32)) / np.float32(32.0))).astype(np.float32)
    ang = (pos[:, None] * inv[None, :]).astype(np.float32)
    c, s = np.cos(ang).astype(np.float32), np.sin(ang).astype(np.float32)
    cs = np.concatenate([c, -s, s], axis=1).astype(np.float32)
    cs = np.ascontiguousarray(cs.reshape(32, 128, 48).transpose(1, 0, 2).reshape(128, 32 * 48))
    k = np.arange(128)[:, None]
    q = np.arange(128)[None, :]
    mcur = np.where(k <= q, 0.0, NEG).astype(np.float32)
    mprev = np.where(k >= q, 0.0, NEG).astype(np.float32)
    mprevh = mprev if half == 1 else np.full((128, 128), NEG, np.float32)
    masks = np.ascontiguousarray(np.stack([mcur, mprev, mprevh]).astype(np.float32).transpose(1, 0, 2).reshape(128, 384))
    bm = np.zeros((3, 4, 128, 128), np.float32)
    tp = np.arange(128)[:, None]
    t = np.arange(128)[None, :]
    for g, w in enumerate(POOL_W):
        main = ((tp <= t) & (tp > t - w)).astype(np.float32) / w - (tp == t)
        prev = ((tp - 128) > (t - w)).astype(np.float32) / w
        cntf = np.minimum(t + 1, w).astype(np.float32)
        first = ((tp <= t) & (tp > t - w)).astype(np.float32) / cntf - (tp == t)
        bm[0, g] = main
        bm[1, g] = prev
        bm[2, g] = first if half == 0 else main
    bm = np.ascontiguousarray(bm.transpose(2, 0, 1, 3).reshape(128, 3 * 4 * 128))
    return cs, masks, bm


_NC_CACHE = {}


def kernel(x, w_in, w_pool, pool_scale, w_out, ln_gain, ln_bias):
    x = np.asarray(x, dtype=np.float32)
    w_perm = np.ascontiguousarray(np.asarray(w_in, np.float32)[0][:, _perm_cols()])
    w_pool0 = np.ascontiguousarray(np.asarray(w_pool, np.float32)[0])
    pscale = np.ascontiguousarray(np.asarray(pool_scale, np.float32)[0].reshape(8, 128).T)
    w_out0 = np.ascontiguousarray(np.asarray(w_out, np.float32)[0])
    g0 = np.ascontiguousarray(np.asarray(ln_gain, np.float32)[0])
    b0 = np.ascontiguousarray(np.asarray(ln_bias, np.float32)[0])
    ident = np.eye(128, dtype=np.float32)
    if "nc" not in _NC_CACHE:
        _NC_CACHE["nc"] = build_program()
    nc = _NC_CACHE["nc"]
    in_maps = []
    for c in range(8):
        b, half = c // 2, c % 2
        own = x[b, half * T:(half + 1) * T, :]
        halo = x[b, 0:T, :] if half == 1 else np.zeros((T, D), np.float32)
        xT_all = np.ascontiguousarray(np.concatenate([halo, own], axis=0).T)
        cs, masks, bm = _tables(half)
        in_maps.append({
            "xT_all": xT_all, "x_own": np.ascontiguousarray(own), "w_perm": w_perm, "w_pool": w_pool0,
            "pscale": pscale, "w_out": w_out0, "ln_g": g0, "ln_b": b0, "cs": cs, "masks": masks,
            "ident": ident, "bmats": bm,
        })
    res = run_bass_kernel_spmd(nc, in_maps, core_ids=list(range(8)))
    out = np.empty((4, 4096, D), np.float32)
    for c in range(8):
        b, half = c // 2, c % 2
        out[b, half * T:(half + 1) * T, :] = res.results[c]["out"]
    return out
```

```python
import numpy as np
import concourse.bass as bass
import concourse.mybir as mybir
from concourse.bass_utils import run_bass_kernel_spmd

F32 = mybir.dt.float32
BF16 = mybir.dt.bfloat16
AF = mybir.ActivationFunctionType
ALU = mybir.AluOpType
AX = mybir.AxisListType

D = 2048
T = 2048
LT = 4096
KC = 16
NEG = -30000.0
ALPHA = 2.0 ** 0.25
EPS = 1e-5
SCALE = 128.0 ** -0.5
POOL_W = (2, 4, 8, 16)

COMPUTE = ("pe", "act", "dve", "pool")


class Op:
    __slots__ = ("eng", "fn", "deps", "is_dma", "sem", "val", "need_sig", "seq")


class Sched:
    def __init__(self, prog, counters, dma_sems, dma_cnt):
        self.prog = prog
        self.counters = counters
        self.dma_sems = dma_sems
        self.dma_cnt = dma_cnt
        self.ops = []
        self.lw = {}
        self.rd = {}

    def add(self, eng, fn, reads=(), writes=(), dma=None, no_waw=False):
        idx = len(self.ops)
        deps = {}
        for r in reads:
            w = self.lw.get(r)
            if w is not None:
                deps[w] = "raw"
        for w_ in writes:
            for ridx in self.rd.get(w_, {}).values():
                deps.setdefault(ridx, "war")
            lw = self.lw.get(w_)
            if lw is not None and not no_waw:
                deps.setdefault(lw, "waw")
        op = Op()
        op.eng = eng
        op.fn = fn
        op.deps = deps
        op.is_dma = dma is not None
        op.sem = dma
        op.val = 0
        op.need_sig = False
        op.seq = 0
        if dma is not None:
            self.dma_cnt[dma] += 16
            op.val = self.dma_cnt[dma]
        key = ("dma", dma) if dma is not None else eng
        for r in reads:
            self.rd.setdefault(r, {})[key] = idx
        for w_ in writes:
            self.lw[w_] = idx
            self.rd[w_] = {}
        self.ops.append(op)
        return idx

    def _needs_wait(self, op, dop, kind):
        if dop.is_dma:
            return True
        if dop.eng != op.eng:
            return True
        if op.is_dma:
            return True
        if op.eng in ("act", "dve", "pool"):
            return True
        return False

    def finalize(self):
        for op in self.ops:
            for d, kind in op.deps.items():
                dop = self.ops[d]
                if not dop.is_dma and self._needs_wait(op, dop, kind):
                    dop.need_sig = True
        for op in self.ops:
            if op.need_sig:
                self.counters[op.eng] += 1
                op.seq = self.counters[op.eng]

    def emit(self, eng, eh):
        waited = {}
        for op in self.ops:
            if op.eng != eng:
                continue
            waits = {}
            for d, kind in op.deps.items():
                dop = self.ops[d]
                if not self._needs_wait(op, dop, kind):
                    continue
                if dop.is_dma:
                    k, v = ("d", dop.sem), dop.val
                else:
                    k, v = ("p", dop.eng), dop.seq
                if waits.get(k, 0) < v:
                    waits[k] = v
            for k, v in waits.items():
                if waited.get(k, 0) >= v:
                    continue
                waited[k] = v
                sem = self.dma_sems[k[1]] if k[0] == "d" else self.prog[k[1]]
                eh.wait_ge(sem, v)
            if op.fn is None:
                continue
            inst = op.fn(eh)
            if op.is_dma:
                inst.then_inc(self.dma_sems[op.sem], 16)
            elif op.need_sig:
                inst.then_inc(self.prog[eng], 1)


def run_block(nc, sch):
    sch.finalize()
    with nc.Block() as block:
        @block.tensor
        def _(e):
            sch.emit("pe", e)

        @block.scalar
        def _(e):
            sch.emit("act", e)

        @block.vector
        def _(e):
            sch.emit("dve", e)

        @block.gpsimd
        def _(e):
            sch.emit("pool", e)

        @block.sync
        def _(e):
            sch.emit("sp", e)


DMA_SEM_NAMES = ["const", "constg", "wp", "woa", "xc0", "xc1", "xh", "xr0", "xr1", "out0", "out1", "out2"]


def build_program():
    nc = bass.Bass("TRN2", target_bir_lowering=False)
    dt = lambda name, shape, kind="ExternalInput": nc.dram_tensor(name, shape, F32, kind=kind).ap()
    xT_all = dt("xT_all", [D, LT])
    x_own = dt("x_own", [T, D])
    w_perm = dt("w_perm", [D, 6144])
    w_pool = dt("w_pool", [4, 256, 256])
    pscale = dt("pscale", [128, 8])
    w_out = dt("w_out", [D, D])
    ln_g = dt("ln_g", [D])
    ln_b = dt("ln_b", [D])
    cs_d = dt("cs", [128, 32 * 48])
    mask_d = dt("masks", [128, 3 * 128])
    ident_d = dt("ident", [128, 128])
    bm_d = dt("bmats", [128, 3 * 4 * 128])
    out_d = dt("out", [T, D], kind="ExternalOutput")

    import contextlib
    es = contextlib.ExitStack()
    with es:
        prog = {e: es.enter_context(nc.semaphore("prog_" + e)) for e in COMPUTE}
        dma_sems = {n: es.enter_context(nc.semaphore("dma_" + n)) for n in DMA_SEM_NAMES}
        counters = {e: 0 for e in COMPUTE}
        dma_cnt = {n: 0 for n in DMA_SEM_NAMES}
        new_sched = lambda: Sched(prog, counters, dma_sems, dma_cnt)

        import os
        PH = os.environ.get("KPHASES", "abc")
        bufW = es.enter_context(nc.sbuf_tensor("bufW", [128, KC, 1024], BF16))
        bufX = [es.enter_context(nc.sbuf_tensor(f"bufX{i}", [128, KC, 512], BF16)) for i in range(2)]
        yT_attn = es.enter_context(nc.sbuf_tensor("yT_attn", [128, 8, T], BF16))
        if "a" in PH:
            phase_a(nc, new_sched(), xT_all, w_perm, cs_d, mask_d, ident_d, yT_attn, bufW, bufX, "b" in PH)
        yT_pool = es.enter_context(nc.sbuf_tensor("yT_pool", [128, 8, T], BF16))
        wo_a = es.enter_context(nc.sbuf_tensor("wo_a", [128, 8, D], BF16))
        if "b" in PH:
            phase_b(nc, new_sched(), xT_all, w_perm, w_pool, pscale, bm_d, yT_pool, w_out, wo_a, bufW, bufX, "a" in PH)
        if "c" in PH:
            phase_c(nc, new_sched(), x_own, w_out, ln_g, ln_b, out_d, yT_attn, yT_pool, wo_a, bufW, bufX, "b" in PH)
    return nc


def xchunk_src(xT_all, lt0, n):
    return xT_all.rearrange("(kc p) t -> p kc t", p=128)[:, :, lt0:lt0 + n]


def phase_a(nc, sch, xT_all, w_perm, cs_d, mask_d, ident_d, yT_attn, bufW, bufX, prefetch_b):
    import contextlib
    with contextlib.ExitStack() as es:
        sb = lambda name, shape, dtype: es.enter_context(nc.sbuf_tensor(name, shape, dtype))
        xc = bufX
        wp = bufW
        QT = sb("a_QT", [128, 2, T], BF16)
        QT2 = sb("a_QT2", [128, 2, 4, 512], BF16)
        QT3 = sb("a_QT3", [128, 2, 16, 128], BF16)
        KT = sb("a_KT", [128, 2, LT], BF16)
        VT = sb("a_VT", [128, 2, LT], BF16)
        Gs = sb("a_Gs", [128, 2, T], BF16)
        Vblk = sb("a_Vblk", [128, 69, 128], BF16)
        PT = [sb(f"a_PT{i}", [128, 512], BF16) for i in range(2)]
        PTP = [sb(f"a_PTP{i}", [128, 4, 130], BF16) for i in range(2)]
        PT3 = sb("a_PTh3", [128, 2, 16, 130], BF16)
        PTN = [sb(f"a_PTN{i}", [128, 512], BF16) for i in range(4)]
        qktm = [sb(f"a_qktm{i}", [128, 4, 128], BF16) for i in range(2)]
        rt1 = [sb("a_rt1_0", [128, 4, 32], F32)] * 2
        rt2 = [sb("a_rt2_0", [128, 4, 32], F32)] * 2
        cs = sb("a_cs", [128, 32, 48], F32)
        masks = sb("a_masks", [128, 3, 128], BF16)
        ident = sb("a_ident", [128, 128], BF16)
        ones = sb("a_ones", [128, 128], BF16)
        zeros = sb("a_zeros", [128, 128], BF16)
        m1g0 = sb("a_m1g0", [128, 4, 128], BF16)
        lden = [sb("a_lden0", [128, 512], F32)] * 2
        bank = [es.enter_context(nc.psum_tensor(f"a_b{i}", [128, 512], F32)) for i in range(6)]
        tpb = [es.enter_context(nc.psum_tensor(f"a_tp{i}", [128, 1024], BF16)) for i in range(2)]

        MCUR, MPREV, MPREVH = 0, 1, 2

        sch.add("sp", lambda e: e.dma_start(out=cs[:].rearrange("p t c -> p (t c)"), in_=cs_d),
                writes=["cs"], dma="const")
        sch.add("pool", lambda e: e.dma_start(out=masks[:].rearrange("p m q -> p (m q)"), in_=mask_d),
                writes=["masks"], dma="constg")
        sch.add("pool", lambda e: e.dma_start(out=ident[:], in_=ident_d),
                writes=["ident"], dma="constg", no_waw=True)
        sch.add("dve", lambda e: e.memset(ones[:], 1.0), writes=["ones"])
        sch.add("dve", lambda e: e.memset(zeros[:], 0.0), writes=["zeros"])
        CONST = ["cs", "masks", "ident"]
        sch.add("dve", lambda e: e.tensor_copy(out=m1g0[:, 0, :], in_=masks[:, 2, :]), reads=CONST, writes=["m1g0a"])
        sch.add("dve", lambda e: e.tensor_copy(out=m1g0[:, 1:4, :],
                                               in_=masks[:, 1, :].unsqueeze(1).broadcast_to([128, 3, 128])),
                reads=CONST, writes=["m1g0b"])

        cnt = {"xc": 0, "pqk": 0, "pf": 0, "tp": 0, "qk": 0, "st": 0, "pt": 0, "ptp": 0, "ptn": 0, "grp": 0}
        PQK_BANKS = (0, 1, 2)
        PF_BANKS = (3, 4, 5)

        import os
        DBG_PAIRS = int(os.environ.get("KA_PAIRS", "4"))
        DBG_ATTN = int(os.environ.get("KA_ATTN", "9"))
        DBG_INPROJ = int(os.environ.get("KA_INPROJ", "1"))
        for p in range(DBG_PAIRS):
            for piece in range(4):
                def f(e, piece=piece, p=p):
                    src = w_perm.rearrange("(kc p) c -> p kc c", p=128)[:, piece * 4:(piece + 1) * 4,
                                                                        p * 1024:(p + 1) * 1024]
                    return e.dma_start(out=wp[:, piece * 4:(piece + 1) * 4, :], in_=src)
                sch.add("pool", f, writes=["wp"], dma="wp", no_waw=(piece > 0))

            pending = []

            def flush(keep):
                while len(pending) > keep:
                    pending.pop(0)()

            for ci in range(8 if DBG_INPROJ else 0):
                own = ci >= 4
                slot = cnt["xc"] % 2
                cnt["xc"] += 1
                for half in range(2):
                    def f(e, slot=slot, ci=ci, half=half):
                        return e.dma_start(out=xc[slot][:, half * 8:(half + 1) * 8, :],
                                           in_=xchunk_src(xT_all, ci * 512, 512)[:, half * 8:(half + 1) * 8, :])
                    sch.add("pool", f, writes=[("xc", slot)], dma=f"xc{slot}", no_waw=(half > 0))

                nb = 4 if own else 2
                c0 = 0 if own else 256
                for tt in range(4):
                    tl = ci * 4 + tt
                    b = PQK_BANKS[cnt["pqk"] % 3]
                    cnt["pqk"] += 1

                    def f(e, slot=slot, tt=tt, b=b, nb=nb, c0=c0):
                        for kc in range(KC):
                            i = e.matmul(bank[b][:, 0:nb * 128], lhsT=xc[slot][:, kc, tt * 128:(tt + 1) * 128],
                                         rhs=wp[:, kc, c0:c0 + nb * 128], start=(kc == 0), stop=(kc == KC - 1))
                        return i
                    sch.add("pe", f, reads=[("xc", slot), "wp"], writes=[("bank", b)])
                    qs = cnt["qk"] % 2
                    cnt["qk"] += 1
                    P = bank[b][:, 0:nb * 128].rearrange("p (n d) -> p n d", n=nb)
                    cc = cs[:, tl, 0:16].unsqueeze(1).unsqueeze(1).broadcast_to([128, nb, 2, 16])
                    sn = cs[:, tl, 16:32].unsqueeze(1).broadcast_to([128, nb, 16])
                    sp_ = cs[:, tl, 32:48].unsqueeze(1).broadcast_to([128, nb, 16])
                    sch.add("dve", lambda e, P=P, qs=qs, nb=nb, cc=cc: e.tensor_tensor(
                        out=rt1[qs][:, 0:nb, :].rearrange("p n (h d) -> p n h d", h=2),
                        in0=P[:, :, 0:32].rearrange("p n (h d) -> p n h d", h=2), in1=cc, op=ALU.mult),
                        reads=[("bank", b)] + CONST, writes=["rt1"])
                    sch.add("dve", lambda e, P=P, qs=qs, nb=nb, sn=sn: e.tensor_tensor(
                        out=rt2[qs][:, 0:nb, 0:16], in0=P[:, :, 16:32], in1=sn, op=ALU.mult),
                        reads=[("bank", b)] + CONST, writes=["rt2a"])
                    sch.add("dve", lambda e, P=P, qs=qs, nb=nb, sp_=sp_: e.tensor_tensor(
                        out=rt2[qs][:, 0:nb, 16:32], in0=P[:, :, 0:16], in1=sp_, op=ALU.mult),
                        reads=[("bank", b)] + CONST, writes=["rt2b"])
                    sch.add("act", lambda e, P=P, qs=qs, nb=nb: e.copy(out=qktm[qs][:, 0:nb, 32:128], in_=P[:, :, 32:128]),
                            reads=[("bank", b), "rt1", "rt2a", "rt2b"], writes=[("qk_nr", qs)])
                    sch.add("dve", lambda e, qs=qs, nb=nb: e.tensor_tensor(
                        out=qktm[qs][:, 0:nb, 0:32], in0=rt1[qs][:, 0:nb, :], in1=rt2[qs][:, 0:nb, :], op=ALU.add),
                        reads=["rt1", "rt2a", "rt2b"], writes=[("qk_rot", qs)])

                    def tr(qs=qs, nb=nb, own=own, tl=tl):
                        tb = cnt["tp"] % 2
                        cnt["tp"] += 1

                        def f(e):
                            for blk in range(nb):
                                i = e.transpose(tpb[tb][:, blk * 128:(blk + 1) * 128], qktm[qs][:, blk, :], ident[:])
                            return i
                        sch.add("pe", f, reads=[("qk_nr", qs), ("qk_rot", qs)] + CONST, writes=[("tp", tb)])
                        if own:
                            q0 = (tl - 16) * 128
                            sch.add("act", lambda e: e.copy(out=QT[:, :, q0:q0 + 128],
                                                            in_=tpb[tb][:, 0:256].rearrange("p (n t) -> p n t", n=2)),
                                    reads=[("tp", tb)], writes=["QT"])
                            t16 = tl - 16
                            for n_ in range(2):
                                sch.add("act", lambda e, n_=n_: e.copy(
                                    out=QT2[:, n_, :, 32 * t16:32 * t16 + 32],
                                    in_=tpb[tb][:, n_ * 128:(n_ + 1) * 128].rearrange("p (v r) -> p r v", r=4)),
                                    reads=[("tp", tb)], writes=["QT"])
                                sch.add("act", lambda e, n_=n_: e.copy(
                                    out=QT3[:, n_, :, 8 * t16:8 * t16 + 8],
                                    in_=tpb[tb][:, n_ * 128:(n_ + 1) * 128].rearrange("p (u r) -> p r u", r=16)),
                                    reads=[("tp", tb)], writes=["QT"])
                            sch.add("act", lambda e: e.copy(out=KT[:, :, tl * 128:(tl + 1) * 128],
                                                            in_=tpb[tb][:, 256:512].rearrange("p (n t) -> p n t", n=2)),
                                    reads=[("tp", tb)], writes=["KT"])
                        else:
                            sch.add("act", lambda e: e.copy(out=KT[:, :, tl * 128:(tl + 1) * 128],
                                                            in_=tpb[tb][:, 0:256].rearrange("p (n t) -> p n t", n=2)),
                                    reads=[("tp", tb)], writes=["KT"])
                    pending.append(tr)
                    flush(1)

                for cb in range(4 if own else 2):
                    b = PF_BANKS[cnt["pf"] % 3]
                    cnt["pf"] += 1

                    def f(e, slot=slot, cb=cb, b=b):
                        for kc in range(KC):
                            i = e.matmul(bank[b][:, :], lhsT=wp[:, kc, 512 + cb * 128:512 + (cb + 1) * 128],
                                         rhs=xc[slot][:, kc, :], start=(kc == 0), stop=(kc == KC - 1))
                        return i
                    sch.add("pe", f, reads=[("xc", slot), "wp"], writes=[("bank", b)])
                    if cb < 2:
                        sch.add("dve", lambda e, cb=cb, b=b, ci=ci: e.tensor_copy(
                            out=VT[:, cb, ci * 512:(ci + 1) * 512], in_=bank[b][:, :]),
                            reads=[("bank", b)], writes=["VT"])
                    else:
                        sch.add("act", lambda e, cb=cb, b=b, ci=ci: e.activation(
                            out=Gs[:, cb - 2, (ci - 4) * 512:(ci - 3) * 512], in_=bank[b][:, :], func=AF.Silu),
                            reads=[("bank", b)], writes=["Gs"])
                    flush(0)
            flush(0)
            if p == DBG_PAIRS - 1 and prefetch_b:
                for piece in range(4):
                    def f(e, piece=piece):
                        src = w_perm.rearrange("(kc p) c -> p kc c", p=128)[:, piece * 4:(piece + 1) * 4, 4096:5120]
                        return e.dma_start(out=wp[:, piece * 4:(piece + 1) * 4, :], in_=src)
                    sch.add("pool", f, writes=["wp"], dma="wp", no_waw=(piece > 0))
                for half in range(2):
                    def f(e, half=half):
                        return e.dma_start(out=xc[0][:, half * 8:(half + 1) * 8, :],
                                           in_=xchunk_src(xT_all, 2048, 512)[:, half * 8:(half + 1) * 8, :])
                    sch.add("pool", f, writes=[("xc", 0)], dma="xc0", no_waw=(half > 0))
                sch.add("pool", lambda e: e.dma_start(out=xc[1][:, :, 0:128], in_=xchunk_src(xT_all, 1920, 128)),
                        writes=[("xc", 1)], dma="xc1")

            for hh in range(2 if DBG_ATTN else 0):
                h = 2 * p + hh
                vsrc = []
                for nbk in range(15, 32):
                    vsrc.append(VT[:, hh, nbk * 128:(nbk + 1) * 128])
                for r4 in range(4):
                    for cbk in range(3, 8):
                        vsrc.append(VT[:, hh, 512 * cbk + r4:512 * (cbk + 1):4])
                for r in range(16):
                    for kb in range(2):
                        vsrc.append(VT[:, hh, 2048 * kb + r:2048 * (kb + 1):16])
                v1 = lambda nbk: nbk - 15
                v2 = lambda r4, cbk: 17 + r4 * 5 + (cbk - 3)
                v3 = lambda r, kb: 37 + r * 2 + kb
                for i0 in range(0, 69, 8):
                    n = min(8, 69 - i0)
                    tb = cnt["tp"] % 2
                    cnt["tp"] += 1

                    def f(e, i0=i0, n=n, tb=tb, vsrc=vsrc):
                        for k in range(n):
                            i = e.transpose(tpb[tb][:, k * 128:(k + 1) * 128], vsrc[i0 + k], ident[:])
                        return i
                    sch.add("pe", f, reads=["VT"] + CONST, writes=[("tp", tb)])
                    sch.add("dve", lambda e, i0=i0, n=n, tb=tb: e.tensor_copy(
                        out=Vblk[:, i0:i0 + n, :], in_=tpb[tb][:, 0:n * 128].rearrange("p (n d) -> p n d", n=n)),
                        reads=[("tp", tb)], writes=["Vblk"])

                for rq in range(4 if DBG_ATTN > 1 else 0):
                    for kb in range(2):
                        sbk = cnt["st"] % 2
                        cnt["st"] += 1
                        ST = bank[sbk]

                        def f(e, rq=rq, kb=kb, ST=ST, hh=hh):
                            m = MCUR if kb == 1 else MPREVH
                            e.matmul(ST[:, :].rearrange("p (n q) -> p n q", n=4), lhsT=ident[:],
                                     rhs=masks[:, m, :].unsqueeze(1).broadcast_to([128, 4, 128]), start=True, stop=False)
                            for rr in range(4):
                                r = 4 * rq + rr
                                i = e.matmul(ST[:, rr * 128:(rr + 1) * 128],
                                             lhsT=KT[:, hh, 2048 * kb + r:2048 * (kb + 1):16],
                                             rhs=QT3[:, hh, r, :], start=False, stop=(rr == 3))
                            return i
                        sch.add("pe", f, reads=["QT", "KT"] + CONST, writes=[("bank", sbk)])
                        sch.add("act", lambda e, rq=rq, kb=kb, ST=ST: e.activation(
                            out=PT3[:, kb, 4 * rq:4 * rq + 4, 0:128],
                            in_=ST[:, :].rearrange("p (n q) -> p n q", n=4), func=AF.Exp, scale=SCALE),
                            reads=[("bank", sbk)], writes=[("pt3", kb, rq)])
                PT3R = [[("pt3", kb, rq) for rq in range(4)] for kb in range(2)]
                fin_pending = []

                for g in range(4 if DBG_ATTN > 1 else 0):
                    qb = 512 * g
                    gi = cnt["grp"]
                    cnt["grp"] += 1
                    NUM = bank[2 + gi % 2]
                    DEN = bank[4 + gi % 2]
                    numr = ("bank", 2 + gi % 2)
                    denr = ("bank", 4 + gi % 2)
                    cb = 4 + g
                    tiles = []

                    def qk(ti, g=g, qb=qb, hh=hh, cb=cb):
                        sbk = cnt["st"] % 2
                        cnt["st"] += 1
                        pts = cnt["pt"] % 2 if ti < 2 else 3 + cnt["ptp"] % 2
                        cnt["pt" if ti < 2 else "ptp"] += 1
                        tiles.append(pts)
                        ST = bank[sbk]

                        def f(e):
                            bc = lambda m, n_: masks[:, m, :].unsqueeze(1).broadcast_to([128, n_, 128])
                            if ti == 0:
                                e.matmul(ST[:, :].rearrange("p (n q) -> p n q", n=4), lhsT=ident[:],
                                         rhs=(m1g0[:] if g == 0 else bc(MPREV, 4)), start=True, stop=False)
                                for j in range(4):
                                    nbk = 16 + 4 * g + j - 1
                                    i = e.matmul(ST[:, j * 128:(j + 1) * 128], lhsT=KT[:, hh, nbk * 128:(nbk + 1) * 128],
                                                 rhs=QT[:, hh, qb + j * 128:qb + (j + 1) * 128], start=False, stop=(j == 3))
                            elif ti == 1:
                                e.matmul(ST[:, :].rearrange("p (n q) -> p n q", n=4), lhsT=ident[:],
                                         rhs=bc(MCUR, 4), start=True, stop=False)
                                for j in range(4):
                                    nbk = 16 + 4 * g + j
                                    i = e.matmul(ST[:, j * 128:(j + 1) * 128], lhsT=KT[:, hh, nbk * 128:(nbk + 1) * 128],
                                                 rhs=QT[:, hh, qb + j * 128:qb + (j + 1) * 128], start=False, stop=(j == 3))
                            else:
                                m = MCUR if ti == 3 else (MPREVH if g == 0 else MPREV)
                                cbk = cb if ti == 3 else cb - 1
                                e.matmul(ST[:, :].rearrange("p (n q) -> p n q", n=4), lhsT=ident[:],
                                         rhs=bc(m, 4), start=True, stop=False)
                                for r4 in range(4):
                                    i = e.matmul(ST[:, r4 * 128:(r4 + 1) * 128],
                                                 lhsT=KT[:, hh, 512 * cbk + r4:512 * (cbk + 1):4],
                                                 rhs=QT2[:, hh, r4, 128 * g:128 * g + 128], start=False, stop=(r4 == 3))
                            return i
                        sch.add("pe", f, reads=["QT", "KT", "m1g0a", "m1g0b"] + CONST, writes=[("bank", sbk)])
                        if ti < 2:
                            sch.add("act", lambda e: e.activation(out=PT[pts][:], in_=ST[:, :], func=AF.Exp, scale=SCALE),
                                    reads=[("bank", sbk)], writes=[("pt", pts)])
                        else:
                            sch.add("act", lambda e: e.activation(
                                out=PTP[pts - 3][:, :, 0:128],
                                in_=ST[:, :].rearrange("p (n q) -> p n q", n=4), func=AF.Exp, scale=SCALE),
                                reads=[("bank", sbk)], writes=[("pt", pts)])

                    def pv(ti, g=g, qb=qb, hh=hh, cb=cb, NUM=NUM, DEN=DEN, numr=numr, denr=denr):
                        kb = ti - 4
                        if ti < 2:
                            pts = tiles[ti]
                            P_ = PT[pts][:, :]
                            rds = [("pt", pts)]
                        elif ti < 4:
                            pts = tiles[ti]
                            P_ = PTP[pts - 3]
                            rds = [("pt", pts)]
                        else:
                            kb = ti - 4
                            P_ = PT3
                            rds = PT3R[kb]

                        def f(e):
                            if ti < 2:
                                e.matmul(DEN[:, :], lhsT=ones[:], rhs=P_, start=(ti == 0), stop=False)
                            if ti == 0:
                                e.matmul(NUM[:, :], lhsT=zeros[:], rhs=P_, start=True, stop=False)
                            if ti < 2:
                                for j in range(4):
                                    nbk = 16 + 4 * g + j - (1 if ti == 0 else 0)
                                    i = e.matmul(NUM[:, j * 128:(j + 1) * 128], lhsT=Vblk[:, v1(nbk), :],
                                                 rhs=P_[:, j * 128:(j + 1) * 128], start=False, stop=False)
                            elif ti < 4:
                                cbk = cb if ti == 3 else cb - 1
                                for r4 in range(4):
                                    i = e.matmul(NUM[:, r4:512:4], lhsT=Vblk[:, v2(r4, cbk), :], rhs=P_[:, r4, 0:128],
                                                 start=False, stop=(ti == 3 and r4 == 3))
                            else:
                                for r in range(16):
                                    i = e.matmul(NUM[:, r:512:16], lhsT=Vblk[:, v3(r, kb), :],
                                                 rhs=P_[:, kb, r, 32 * g:32 * g + 32],
                                                 start=False, stop=False)
                            return i
                        sch.add("pe", f, reads=rds + ["Vblk", "ones", "zeros"], writes=[numr, denr])

                    def ptn_copy(ti, g=g):
                        ns = ti - 2
                        if ti < 4:
                            pts = tiles[ti]
                            src = PTP[pts - 3][:, :, 0:128].rearrange("p r i -> p i r")
                            dst = PTN[ns][:, :].rearrange("p (i r) -> p i r", r=4)
                            rds = [("pt", pts)]
                        else:
                            kb = ti - 4
                            src = PT3[:, kb, :, 32 * g:32 * g + 32].rearrange("p r j -> p j r")
                            dst = PTN[ns][:, :].rearrange("p (j r) -> p j r", r=16)
                            rds = PT3R[kb]
                        sch.add("dve", lambda e, src=src, dst=dst: e.tensor_copy(out=dst, in_=src),
                                reads=rds, writes=[("ptn", ns)])

                    def den_tail(DEN=DEN, denr=denr):
                        def f(e):
                            for ns in range(4):
                                i = e.matmul(DEN[:, :], lhsT=ones[:], rhs=PTN[ns][:, :], start=False, stop=(ns == 3))
                            return i
                        sch.add("pe", f, reads=[("ptn", ns) for ns in range(4)] + ["ones"], writes=[denr])

                    ptn_copy(4); ptn_copy(5)
                    qk(0); qk(1)
                    if fin_pending:
                        fin_pending.pop()()
                    pv(0); qk(2); ptn_copy(2); pv(1); qk(3); ptn_copy(3); pv(4); pv(2); pv(5); pv(3); den_tail()

                    def finalize(gi=gi, NUM=NUM, DEN=DEN, numr=numr, denr=denr, h=h, hh=hh, qb=qb):
                        fs = gi % 2
                        sch.add("act", lambda e, fs=fs, DEN=DEN: e.activation(out=lden[fs][:], in_=DEN[:, :], func=AF.Ln),
                                reads=[denr], writes=["lden"])
                        sch.add("act", lambda e, fs=fs: e.activation(out=lden[fs][:], in_=lden[fs][:], func=AF.Exp, scale=-1.0),
                                reads=["lden"], writes=["lden"])
                        sch.add("dve", lambda e, fs=fs, NUM=NUM: e.tensor_tensor(out=lden[fs][:], in0=NUM[:, :], in1=lden[fs][:],
                                                                                 op=ALU.mult),
                                reads=[numr, "lden"], writes=["lden"])
                        sch.add("dve", lambda e, fs=fs, h=h, hh=hh, qb=qb: e.tensor_tensor(
                            out=yT_attn[:, h, qb:qb + 512], in0=lden[fs][:], in1=Gs[:, hh, qb:qb + 512], op=ALU.mult),
                            reads=["lden", "Gs"], writes=["yT_attn"])

                    fin_pending.append(finalize)
                if fin_pending:
                    fin_pending.pop()()
        run_block(nc, sch)


def phase_b(nc, sch, xT_all, w_perm, w_pool, pscale, bm_d, yT_pool, w_out, wo_a, bufW, bufX, prefetched):
    import contextlib
    with contextlib.ExitStack() as es:
        sb = lambda name, shape, dtype: es.enter_context(nc.sbuf_tensor(name, shape, dtype))
        xc = bufX
        xh = sb("b_xh", [128, KC, 128], BF16)
        wB = bufW
        utm = [sb(f"b_utm{i}", [128, 512], BF16) for i in range(10)]
        Gp = [sb(f"b_gp{i}", [128, 4, 512], BF16) for i in range(2)]
        pooled = [sb(f"b_pl{i}", [128, 4, 512], BF16) for i in range(2)]
        wpl = sb("b_wpool", [128, 4, 2, 256], BF16)
        bmats = sb("b_bm", [128, 3, 4, 128], BF16)
        ps = sb("b_ps", [128, 8], F32)
        bank = [es.enter_context(nc.psum_tensor(f"b_b{i}", [128, 512], F32)) for i in range(8)]

        sch.add("pool", lambda e: e.dma_start(out=wpl[:], in_=w_pool.rearrange("g (cc p) d -> p g cc d", p=128)),
                writes=["wpl"], dma="constg")
        sch.add("pool", lambda e: e.dma_start(out=bmats[:].rearrange("p k g b -> p (k g b)"), in_=bm_d),
                writes=["bmats"], dma="constg", no_waw=True)
        sch.add("sp", lambda e: e.dma_start(out=ps[:], in_=pscale), writes=["ps"], dma="const")
        CONST = ["wpl", "bmats", "ps"]
        BM, BP, BF = 0, 1, 2
        cnt = {"xc": 0, "u": 0, "gp": 0, "pp": 0, "po": 0, "ch": 0}

        if prefetched:
            for nm in ("wp", "xc0", "xc1"):
                sch.add("pe", lambda e, nm=nm, v=sch.dma_cnt[nm]: e.wait_ge(sch.dma_sems[nm], v) and None)

        def prefetch_wo_a():
            for piece in range(4):
                def f(e, piece=piece):
                    src = w_out.rearrange("(ec p) d -> p ec d", p=128)[:, piece * 2:(piece + 1) * 2, :]
                    return e.dma_start(out=wo_a[:, piece * 2:(piece + 1) * 2, :], in_=src)
                sch.add("pool", f, writes=["wo_a"], dma="woa", no_waw=(piece > 0))

        for bp in range(2):
            pre = prefetched and bp == 0
            if not pre:
                for piece in range(4):
                    def f(e, piece=piece, bp=bp):
                        src = w_perm.rearrange("(kc p) c -> p kc c", p=128)[:, piece * 4:(piece + 1) * 4,
                                                                            4096 + bp * 1024:4096 + (bp + 1) * 1024]
                        return e.dma_start(out=wB[:, piece * 4:(piece + 1) * 4, :], in_=src)
                    sch.add("pool", f, writes=["wB"], dma="wp", no_waw=(piece > 0))
                sch.add("pool", lambda e: e.dma_start(out=xh[:], in_=xchunk_src(xT_all, 1920, 128)),
                        writes=["xh"], dma="xh")
            uslot = {}
            tails = []

            def u_tile(j, lhs_fn, rd):
                b = cnt["u"] % 2
                cnt["u"] += 1
                s = (j + 1) % 10
                uslot[j] = s

                def f(e):
                    for kc in range(KC):
                        i = e.matmul(bank[b][:, :], lhsT=lhs_fn(kc), rhs=wB[:, kc, 0:512],
                                     start=(kc == 0), stop=(kc == KC - 1))
                    return i
                sch.add("pe", f, reads=[rd, "wB"], writes=[("bank", b)])
                sch.add("act", lambda e: e.copy(out=utm[s][:], in_=bank[b][:, :]),
                        reads=[("bank", b)], writes=[("utm", s)])

            if pre:
                u_tile(-1, lambda kc: xc[1][:, kc, 0:128], ("xc", 1))
            else:
                u_tile(-1, lambda kc: xh[:, kc, :], "xh")
            for ci in range(4):
                slot = cnt["xc"] % 2
                cnt["xc"] += 1
                for half in range(2):
                    if pre and ci == 0:
                        continue
                    def f(e, slot=slot, ci=ci, half=half):
                        return e.dma_start(out=xc[slot][:, half * 8:(half + 1) * 8, :],
                                           in_=xchunk_src(xT_all, 2048 + ci * 512, 512)[:, half * 8:(half + 1) * 8, :])
                    sch.add("pool", f, writes=[("xc", slot)], dma=f"xc{slot}", no_waw=(half > 0))
                if bp == 0 and ci == 1:
                    prefetch_wo_a()
                for tt in range(4):
                    u_tile(ci * 4 + tt, lambda kc, slot=slot, tt=tt: xc[slot][:, kc, tt * 128:(tt + 1) * 128], ("xc", slot))
                cs_ = cnt["ch"] % 2
                cnt["ch"] += 1
                if tails:
                    tails[0][0]()
                for cbi in range(4):
                    b = 2 + cnt["gp"] % 2
                    cnt["gp"] += 1

                    def f(e, slot=slot, cbi=cbi, b=b):
                        for kc in range(KC):
                            i = e.matmul(bank[b][:, :], lhsT=wB[:, kc, 512 + cbi * 128:512 + (cbi + 1) * 128],
                                         rhs=xc[slot][:, kc, :], start=(kc == 0), stop=(kc == KC - 1))
                        return i
                    sch.add("pe", f, reads=[("xc", slot), "wB"], writes=[("bank", b)])
                    sch.add("act", lambda e, cbi=cbi, b=b, cs_=cs_: e.activation(out=Gp[cs_][:, cbi, :], in_=bank[b][:, :],
                                                                                 func=AF.Silu),
                            reads=[("bank", b)], writes=[("gp", cs_)])
                if tails:
                    tails.pop(0)[1]()
                def tail_pooled(ci=ci, cs_=cs_, bp=bp, uslot=uslot):
                    for cbl in range(4):
                        gq = 2 * bp + cbl // 2
                        b = 4 + cnt["pp"] % 2
                        cnt["pp"] += 1

                        def f(e, cbl=cbl, gq=gq, b=b, ci=ci, uslot=uslot):
                            for jj in range(4):
                                j = ci * 4 + jj
                                e.matmul(bank[b][:, jj * 128:(jj + 1) * 128], lhsT=utm[uslot[j - 1]][:, cbl * 128:(cbl + 1) * 128],
                                         rhs=bmats[:, BP, gq, :], start=True, stop=False, skip_group_check=True)
                                i = e.matmul(bank[b][:, jj * 128:(jj + 1) * 128], lhsT=utm[uslot[j]][:, cbl * 128:(cbl + 1) * 128],
                                             rhs=bmats[:, (BF if j == 0 else BM), gq, :], start=False, stop=(jj == 3),
                                             skip_group_check=True)
                            return i
                        rds = [("utm", uslot[ci * 4 + jj]) for jj in range(-1, 4)]
                        sch.add("pe", f, reads=rds + CONST, writes=[("bank", b)])
                        sch.add("dve", lambda e, cbl=cbl, b=b, cs_=cs_: e.tensor_copy(out=pooled[cs_][:, cbl, :], in_=bank[b][:, :]),
                                reads=[("bank", b)], writes=[("pooled", cs_)])

                def tail_po(ci=ci, cs_=cs_, bp=bp):
                    for obl in range(4):
                        ob = 4 * bp + obl
                        gq = ob // 2
                        hf = obl % 2
                        b = 6 + cnt["po"] % 2
                        cnt["po"] += 1

                        def f(e, obl=obl, gq=gq, hf=hf, b=b, cs_=cs_):
                            for cc in range(2):
                                i = e.matmul(bank[b][:, :], lhsT=wpl[:, gq, cc, hf * 128:(hf + 1) * 128],
                                             rhs=pooled[cs_][:, (obl // 2) * 2 + cc, :], start=(cc == 0), stop=(cc == 1))
                            return i
                        sch.add("pe", f, reads=[("pooled", cs_)] + CONST, writes=[("bank", b)])
                        sch.add("dve", lambda e, obl=obl, ob=ob, b=b, cs_=cs_, ci=ci: e.scalar_tensor_tensor(
                            out=yT_pool[:, ob, ci * 512:(ci + 1) * 512], in0=bank[b][:, :], scalar=ps[:, ob:ob + 1],
                            in1=Gp[cs_][:, obl, :], op0=ALU.mult, op1=ALU.mult),
                            reads=[("bank", b), ("gp", cs_)] + CONST, writes=["yT_pool"])

                tails.append((tail_pooled, tail_po))
            if bp == 1:
                for piece in range(4, 8):
                    def f(e, piece=piece):
                        src = w_out.rearrange("(ec p) d -> p ec d", p=128)[:, piece * 2:(piece + 1) * 2, :]
                        dst = wB[:].rearrange("p a b -> p (a b)").rearrange("p (e d) -> p e d", d=D)
                        return e.dma_start(out=dst[:, (piece - 4) * 2:(piece - 3) * 2, :], in_=src)
                    sch.add("pool", f, writes=["wB"], dma="wp", no_waw=(piece > 4))
            while tails:
                tp_, to_ = tails.pop(0)
                tp_()
                to_()
        run_block(nc, sch)


def phase_c(nc, sch, x_own, w_out, ln_g, ln_b, out_d, yT_attn, yT_pool, wo_a, bufW, bufX, prefetched):
    import contextlib
    with contextlib.ExitStack() as es:
        sb = lambda name, shape, dtype: es.enter_context(nc.sbuf_tensor(name, shape, dtype))
        wo_b = bufW[:].rearrange("p a b -> p (a b)").rearrange("p (e d) -> p e d", d=D)
        fx = [b_[:].rearrange("p a b -> p (a b)").bitcast(F32) for b_ in bufX]
        xr = [fx[0][:, 0:D], fx[0][:, D:2 * D]]
        z = [fx[1][:, 0:D], fx[1][:, D:2 * D], sb("c_z2", [128, D], F32)[:]]
        junk = sb("c_junk", [128, D], BF16)
        gbc = sb("c_g", [128, D], F32)
        bbc = sb("c_b", [128, D], F32)
        st = [sb(f"c_st{i}", [128, 16], F32) for i in range(3)]
        bank = [es.enter_context(nc.psum_tensor(f"c_b{i}", [128, 512], F32)) for i in range(8)]

        for piece in range(8):
            if prefetched:
                continue
            tgt = wo_a if piece < 4 else wo_b

            def f(e, piece=piece, tgt=tgt):
                src = w_out.rearrange("(ec p) d -> p ec d", p=128)[:, piece * 2:(piece + 1) * 2, :]
                return e.dma_start(out=tgt[:, (piece % 4) * 2:(piece % 4 + 1) * 2, :], in_=src)
            sch.add("pool", f, writes=["wo_a" if piece < 4 else "wo_b"], dma=("woa" if piece < 4 else "wp"),
                    no_waw=(piece % 4 > 0))
        if prefetched:
            for nm in ("woa", "wp"):
                sch.add("pe", lambda e, nm=nm, v=sch.dma_cnt[nm]: e.wait_ge(sch.dma_sems[nm], v) and None)
        sch.add("sp", lambda e: e.dma_start(out=gbc[:], in_=ln_g.partition_broadcast(128)), writes=["gbc"], dma="const")
        sch.add("sp", lambda e: e.dma_start(out=bbc[:], in_=ln_b.partition_broadcast(128)), writes=["bbc"], dma="const",
                no_waw=True)
        CONST = ["gbc", "bbc"]

        def load_x(j):
            s = j % 2
            sch.add("sp", lambda e: e.dma_start(out=xr[s], in_=x_own[j * 128:(j + 1) * 128, :]),
                    writes=[("xr", s)], dma=f"xr{s}")

        load_x(0)
        for j in range(16):
            s = j % 2
            zi = j % 3
            Z = z[zi]
            S = st[zi]
            if j + 1 < 16:
                load_x(j + 1)

            for part in range(2):
                def f(e, j=j, s=s, part=part):
                    for ec in range(8 * part, 8 * part + 8):
                        lhs = (yT_attn[:, ec, j * 128:(j + 1) * 128] if ec < 8 else yT_pool[:, ec - 8, j * 128:(j + 1) * 128])
                        for nb in range(4):
                            i = e.matmul(bank[4 * s + nb][:, :], lhsT=lhs,
                                         rhs=(wo_a if ec < 8 else wo_b)[:, ec % 8, nb * 512:(nb + 1) * 512],
                                         start=(ec == 0), stop=(ec == KC - 1))
                    return i
                sch.add("pe", f, reads=["wo_a" if part == 0 else "wo_b"], writes=[("bank", 4 * s + nb) for nb in range(4)])
            for nb in range(4):
                sch.add("dve", lambda e, s=s, nb=nb, Z=Z: e.scalar_tensor_tensor(
                    out=Z[:, nb * 512:(nb + 1) * 512], in0=xr[s][:, nb * 512:(nb + 1) * 512], scalar=ALPHA,
                    in1=bank[4 * s + nb][:, :], op0=ALU.mult, op1=ALU.add),
                    reads=[("xr", s), ("bank", 4 * s + nb)], writes=[("z", zi, nb)])
            zr = [("z", zi, nb) for nb in range(4)]
            sch.add("act", lambda e, Z=Z, S=S: e.activation(out=junk[:], in_=Z[:], func=AF.Identity, accum_out=S[:, 5:6]),
                    reads=zr, writes=["junk", ("tot", zi)])
            sch.add("act", lambda e, Z=Z, S=S: e.activation(out=junk[:], in_=Z[:], func=AF.Square, accum_out=S[:, 4:5]),
                    reads=zr, writes=["junk", ("s2", zi)])
            sch.add("dve", lambda e, S=S: e.tensor_scalar(out=S[:, 6:7], in0=S[:, 5:6], scalar1=1.0 / D, scalar2=None,
                                                          op0=ALU.mult),
                    reads=[("tot", zi)], writes=[("mean", zi)])
            sch.add("dve", lambda e, S=S: e.tensor_tensor(out=S[:, 7:8], in0=S[:, 6:7], in1=S[:, 6:7], op=ALU.mult),
                    reads=[("mean", zi)], writes=[("msq", zi)])
            sch.add("dve", lambda e, S=S: e.scalar_tensor_tensor(out=S[:, 8:9], in0=S[:, 4:5], scalar=1.0 / D, in1=S[:, 7:8],
                                                                 op0=ALU.mult, op1=ALU.subtract),
                    reads=[("s2", zi), ("msq", zi)], writes=[("var", zi)])
            sch.add("dve", lambda e, S=S: e.tensor_scalar(out=S[:, 9:10], in0=S[:, 8:9], scalar1=EPS, scalar2=None,
                                                          op0=ALU.add),
                    reads=[("var", zi)], writes=[("vare", zi)])
            sch.add("act", lambda e, S=S: e.activation(out=S[:, 10:11], in_=S[:, 9:10], func=AF.Sqrt),
                    reads=[("vare", zi)], writes=[("sd", zi)])
            sch.add("dve", lambda e, S=S: e.reciprocal(out=S[:, 11:12], in_=S[:, 10:11]),
                    reads=[("sd", zi)], writes=[("rstd", zi)])
            sch.add("dve", lambda e, S=S: e.scalar_tensor_tensor(out=S[:, 12:13], in0=S[:, 6:7], scalar=-1.0, in1=S[:, 11:12],
                                                                 op0=ALU.mult, op1=ALU.mult),
                    reads=[("mean", zi), ("rstd", zi)], writes=[("nbias", zi)])
            zall = ("zall", zi)
            sch.add("act", lambda e, Z=Z, S=S: e.activation(out=Z[:], in_=Z[:], func=AF.Identity,
                                                            scale=S[:, 11:12], bias=S[:, 12:13]),
                    reads=zr + [("rstd", zi), ("nbias", zi), ("tot", zi), ("s2", zi)], writes=zr + [zall])
            sch.add("pool", lambda e, Z=Z: e.tensor_tensor(out=Z[:], in0=Z[:], in1=gbc[:], op=ALU.mult),
                    reads=[zall] + CONST, writes=zr + [zall])
            sch.add("pool", lambda e, Z=Z: e.tensor_tensor(out=Z[:], in0=Z[:], in1=bbc[:], op=ALU.add),
                    reads=[zall] + CONST, writes=zr + [zall])
            sch.add("sp", lambda e, Z=Z, j=j: e.dma_start(out=out_d[j * 128:(j + 1) * 128, :], in_=Z[:]),
                    reads=[zall] + zr, writes=[("outd", j)], dma=f"out{zi}")
        sch.add("sp", None, reads=[("outd", 13), ("outd", 14), ("outd", 15)])
        run_block(nc, sch)


def _perm_cols():
    cols = []
    for p in range(4):
        for base in (0, 1024, 2048, 4096):
            for h in (2 * p, 2 * p + 1):
                cols.extend(range(base + h * 128, base + (h + 1) * 128))
    for bp in range(2):
        cols.extend(range(3072 + bp * 512, 3072 + (bp + 1) * 512))
        cols.extend(range(5120 + bp * 512, 5120 + (bp + 1) * 512))
    return np.asarray(cols, dtype=np.int64)


def _tables(half):
    lt = np.arange(LT, dtype=np.int64)
    pos = np.maximum(lt - 2048 + 2048 * half, 0).astype(np.float32)
    inv = (np.float32(500000.0) ** (-(2.0 * np.arange(16, dtype=np.float32)) / np.float32(32.0))).astype(np.float32)
    ang = (pos[:, None] * inv[None, :]).astype(np.float32)
    c, s = np.cos(ang).astype(np.float32), np.sin(ang).astype(np.float32)
    cs = np.concatenate([c, -s, s], axis=1).astype(np.float32)
    cs = np.ascontiguousarray(cs.reshape(32, 128, 48).transpose(1, 0, 2).reshape(128, 32 * 48))
    k = np.arange(128)[:, None]
    q = np.arange(128)[None, :]
    mcur = np.where(k <= q, 0.0, NEG).astype(np.float32)
    mprev = np.where(k >= q, 0.0, NEG).astype(np.float32)
    mprevh = mprev if half == 1 else np.full((128, 128), NEG, np.float32)
    masks = np.ascontiguousarray(np.stack([mcur, mprev, mprevh]).astype(np.float32).transpose(1, 0, 2).reshape(128, 384))
    bm = np.zeros((3, 4, 128, 128), np.float32)
    tp = np.arange(128)[:, None]
    t = np.arange(128)[None, :]
    for g, w in enumerate(POOL_W):
        main = ((tp <= t) & (tp > t - w)).astype(np.float32) / w - (tp == t)
        prev = ((tp - 128) > (t - w)).astype(np.float32) / w
        cntf = np.minimum(t + 1, w).astype(np.float32)
        first = ((tp <= t) & (tp > t - w)).astype(np.float32) / cntf - (tp == t)
        bm[0, g] = main
        bm[1, g] = prev
        bm[2, g] = first if half == 0 else main
    bm = np.ascontiguousarray(bm.transpose(2, 0, 1, 3).reshape(128, 3 * 4 * 128))
    return cs, masks, bm


_NC_CACHE = {}


def kernel(x, w_in, w_pool, pool_scale, w_out, ln_gain, ln_bias):
    x = np.asarray(x, dtype=np.float32)
    w_perm = np.ascontiguousarray(np.asarray(w_in, np.float32)[0][:, _perm_cols()])
    w_pool0 = np.ascontiguousarray(np.asarray(w_pool, np.float32)[0])
    pscale = np.ascontiguousarray(np.asarray(pool_scale, np.float32)[0].reshape(8, 128).T)
    w_out0 = np.ascontiguousarray(np.asarray(w_out, np.float32)[0])
    g0 = np.ascontiguousarray(np.asarray(ln_gain, np.float32)[0])
    b0 = np.ascontiguousarray(np.asarray(ln_bias, np.float32)[0])
    ident = np.eye(128, dtype=np.float32)
    if "nc" not in _NC_CACHE:
        _NC_CACHE["nc"] = build_program()
    nc = _NC_CACHE["nc"]
    in_maps = []
    for c in range(8):
        b, half = c // 2, c % 2
        own = x[b, half * T:(half + 1) * T, :]
        halo = x[b, 0:T, :] if half == 1 else np.zeros((T, D), np.float32)
        xT_all = np.ascontiguousarray(np.concatenate([halo, own], axis=0).T)
        cs, masks, bm = _tables(half)
        in_maps.append({
            "xT_all": xT_all, "x_own": np.ascontiguousarray(own), "w_perm": w_perm, "w_pool": w_pool0,
            "pscale": pscale, "w_out": w_out0, "ln_g": g0, "ln_b": b0, "cs": cs, "masks": masks,
            "ident": ident, "bmats": bm,
        })
    res = run_bass_kernel_spmd(nc, in_maps, core_ids=list(range(8)))
    out = np.empty((4, 4096, D), np.float32)
    for c in range(8):
        b, half = c // 2, c % 2
        out[b, half * T:(half + 1) * T, :] = res.results[c]["out"]
    return out
```
